# Optimizing a Trainium2 kernel written in Bass

```python
import math
import jax
import jax.numpy as jnp
from jax import lax
import numpy as np

D_MODEL = 1024
BATCH = 8
SEQ = 4096
DEPTH = 1

GRID_W = 64
CTX_LEN = 256
MIX_W = D_MODEL
NA_W = MIX_W // 2
NA_DH = 64
NA_HEADS = NA_W // NA_DH
NA_KR = 8
NA_KC = 16
S5_W = MIX_W - NA_W
S5_CG = 16
S5_G = S5_W // S5_CG
S5_P = 64
D_FF = ((8 * D_MODEL // 3 + 255) // 256) * 256
IN_COLS = 3 * NA_W + S5_W
DN_ALPHA = (2.0 * DEPTH) ** 0.25
DN_BETA = (8.0 * DEPTH) ** -0.25
LN_EPS = 1e-6
DT_MIN = 1e-3
DT_MAX = 1e-1

kernel_name = 'hybrid_na_s5_macaron_dit_layer'


def layer_norm(x, g, b):
    xf = x.astype(jnp.float32)
    mu = jnp.mean(xf, axis=-1, keepdims=True)
    var = jnp.mean(jnp.square(xf - mu), axis=-1, keepdims=True)
    return ((xf - mu) * lax.rsqrt(var + LN_EPS)).astype(x.dtype) * g + b


def post_norm(stream, update, g, b):
    return layer_norm(DN_ALPHA * stream + update, g, b)


def modulate(h, shift, scale):
    return h * (1 + scale) + shift


def swiglu(h, w_up, w_down):
    a, g = jnp.split(h @ w_up, 2, axis=-1)
    return (jax.nn.silu(g) * a) @ w_down


def neighborhood_attention(q, k, v, k_ctx, v_ctx, rpb):
    B, S, H, Dh = q.shape
    rows = S // GRID_W
    kr = min(NA_KR, rows)
    kc = NA_KC
    n_win = kr * kc
    scale = Dh ** -0.5
    qg = q.reshape(B, rows, GRID_W, H, Dh)
    kg = k.reshape(B, rows, GRID_W, H, Dh)
    vg = v.reshape(B, rows, GRID_W, H, Dh)
    cols = jnp.arange(GRID_W)
    col_start = jnp.clip(cols - kc // 2, 0, GRID_W - kc)
    col_idx = col_start[:, None] + jnp.arange(kc)[None, :]
    col_off = col_idx - cols[:, None] + (NA_KC - 1)
    rpb_c = rpb[:, :, col_off]

    def row_block(r):
        r0 = jnp.clip(r - NA_KR // 2, 0, rows - kr)
        qr = lax.dynamic_index_in_dim(qg, r, axis=1, keepdims=False)
        kb = lax.dynamic_slice_in_dim(kg, r0, kr, axis=1)[:, :, col_idx]
        vb = lax.dynamic_slice_in_dim(vg, r0, kr, axis=1)[:, :, col_idx]
        row_off = r0 + jnp.arange(kr) - r + (NA_KR - 1)
        bias = jnp.transpose(rpb_c[:, row_off], (0, 2, 1, 3))
        s_win = jnp.einsum('bwhd,biwjhd->bhwij', qr, kb) * scale + bias
        s_ctx = jnp.einsum('bwhd,bchd->bhwc', qr, k_ctx) * scale
        s = jnp.concatenate([s_win.reshape(B, H, GRID_W, n_win), s_ctx], axis=-1)
        p = jax.nn.softmax(s.astype(jnp.float32), axis=-1).astype(q.dtype)
        p_win = p[..., :n_win].reshape(B, H, GRID_W, kr, kc)
        return (jnp.einsum('bhwij,biwjhd->bwhd', p_win, vb)
                + jnp.einsum('bhwc,bchd->bwhd', p[..., n_win:], v_ctx))

    o = lax.map(row_block, jnp.arange(rows))
    return jnp.moveaxis(o, 0, 1).reshape(B, S, H * Dh)


def context_attention(q, k, v):
    B, L, H, Dh = q.shape
    s = jnp.einsum('bqhd,bkhd->bhqk', q, k) * (Dh ** -0.5)
    p = jax.nn.softmax(s.astype(jnp.float32), axis=-1).astype(q.dtype)
    return jnp.einsum('bhqk,bkhd->bqhd', p, v).reshape(B, L, H * Dh)


def s5_discretise(a_re, a_im, log_dt, b_re, b_im):
    lam = lax.complex(a_re.astype(jnp.float32), a_im.astype(jnp.float32))
    dt = jnp.exp(log_dt.astype(jnp.float32))[:, None]
    a_bar = jnp.exp(lam * dt)
    b = lax.complex(b_re.astype(jnp.float32), b_im.astype(jnp.float32))
    b_bar = ((a_bar - 1.0) / lam)[..., None] * b
    return a_bar, b_bar


def _linear_combine(left, right):
    a_l, b_l = left
    a_r, b_r = right
    return a_r * a_l, a_r * b_l + b_r


def s5_scan(u, a_bar, b_bar, h0, reverse):
    bu = jnp.einsum('gpc,blgc->blgp', b_bar, u.astype(jnp.complex64))
    if h0 is not None:
        edge = -1 if reverse else 0
        bu = bu.at[:, edge].add(a_bar * h0)
    a_seq = jnp.broadcast_to(a_bar, (1, u.shape[1]) + a_bar.shape)
    _, h = lax.associative_scan(_linear_combine, (a_seq, bu), axis=1, reverse=reverse)
    return h


def s5_readout(c_mat, h):
    return jnp.einsum('gcp,blgp->blgc', c_mat, h).real


def s5_glu(y, w_glu, b_glu):
    g = jax.nn.gelu(y)
    return g * jax.nn.sigmoid(g @ w_glu + b_glu)


def s5_mixer(u, u_c, a_re, a_im, log_dt, b_re, b_im, c_re, c_im, d, w_glu, b_glu, with_ctx_out):
    B, S, _ = u.shape
    L = u_c.shape[1]
    ug = u.reshape(B, S, S5_G, S5_CG)
    ucg = u_c.reshape(B, L, S5_G, S5_CG)
    y = d * ug
    y_c = d * ucg if with_ctx_out else None
    for direction in range(2):
        rev = direction == 1
        a_bar, b_bar = s5_discretise(a_re[direction], a_im[direction], log_dt[direction],
                                     b_re[direction], b_im[direction])
        c_mat = lax.complex(c_re[direction].astype(jnp.float32), c_im[direction].astype(jnp.float32))
        h_ctx = s5_scan(ucg, a_bar, b_bar, None, rev)
        h0 = h_ctx[:, 0] if rev else h_ctx[:, -1]
        h_lat = s5_scan(ug, a_bar, b_bar, h0, rev)
        y = y + s5_readout(c_mat, h_lat).astype(y.dtype)
        if with_ctx_out:
            y_c = y_c + s5_readout(c_mat, h_ctx).astype(y_c.dtype)
    y = s5_glu(y.reshape(B, S, S5_W), w_glu, b_glu)
    y_c = s5_glu(y_c.reshape(B, L, S5_W), w_glu, b_glu) if with_ctx_out else None
    return y, y_c


def parallel_mixer(h, h_c, w_in, rpb, a_re, a_im, log_dt, b_re, b_im, c_re, c_im, d,
                   w_glu, b_glu, w_out, with_ctx_out):
    B, S, _ = h.shape
    L = h_c.shape[1]
    splits = [NA_W, 2 * NA_W, 3 * NA_W]
    q, k, v, u = jnp.split(h @ w_in, splits, axis=-1)
    q_c, k_c, v_c, u_c = jnp.split(h_c @ w_in, splits, axis=-1)

    def heads(t):
        return t.reshape(t.shape[0], t.shape[1], NA_HEADS, NA_DH)

    y_na = neighborhood_attention(heads(q), heads(k), heads(v), heads(k_c), heads(v_c), rpb)
    y_s5, y_s5_c = s5_mixer(u, u_c, a_re, a_im, log_dt, b_re, b_im, c_re, c_im, d,
                            w_glu, b_glu, with_ctx_out)
    y = jnp.concatenate([y_na, y_s5], axis=-1) @ w_out
    if with_ctx_out:
        y_na_c = context_attention(heads(q_c), heads(k_c), heads(v_c))
        y_c = jnp.concatenate([y_na_c, y_s5_c], axis=-1) @ w_out
    else:
        y_c = None
    return y, y_c


def setup_inputs(seed: int = 0) -> dict:
    key = jax.random.key(seed)
    ks = jax.random.split(key, 26)
    f32 = jnp.float32

    def nrm(k, shape, s):
        return jax.random.normal(k, shape, f32) * s

    D = D_MODEL
    return {
        'x': nrm(ks[0], (BATCH, SEQ, D), 1.0),
        'c': nrm(ks[1], (BATCH, D), 1.0),
        'ctx': nrm(ks[2], (BATCH, CTX_LEN, D), 1.0),
        'c_ctx': nrm(ks[3], (D,), 1.0),
        'w_ada': nrm(ks[4], (DEPTH, D, 9 * D), 0.5 * D ** -0.5),
        'b_ada': nrm(ks[5], (DEPTH, 9 * D), 0.02),
        'ln_g': 1.0 + nrm(ks[6], (DEPTH, 3, D), 0.02),
        'ln_b': nrm(ks[7], (DEPTH, 3, D), 0.02),
        'ffn1_w_up': nrm(ks[8], (DEPTH, D, 2 * D_FF), D ** -0.5),
        'ffn1_w_down': nrm(ks[9], (DEPTH, D_FF, D), DN_BETA * D_FF ** -0.5),
        'w_in': nrm(ks[10], (DEPTH, D, IN_COLS), D ** -0.5),
        'na_rpb': nrm(ks[11], (DEPTH, NA_HEADS, 2 * NA_KR - 1, 2 * NA_KC - 1), 0.02),
        's5_a_re': -0.5 + nrm(ks[12], (DEPTH, 2, S5_G, S5_P), 0.01),
        's5_a_im': math.pi * jnp.arange(S5_P, dtype=f32) + nrm(ks[13], (DEPTH, 2, S5_G, S5_P), 0.01),
        's5_log_dt': jax.random.uniform(ks[14], (DEPTH, 2, S5_G), f32, math.log(DT_MIN), math.log(DT_MAX)),
        's5_b_re': nrm(ks[15], (DEPTH, 2, S5_G, S5_P, S5_CG), (2 * S5_CG) ** -0.5),
        's5_b_im': nrm(ks[16], (DEPTH, 2, S5_G, S5_P, S5_CG), (2 * S5_CG) ** -0.5),
        's5_c_re': nrm(ks[17], (DEPTH, 2, S5_G, S5_CG, S5_P), (2 * S5_P) ** -0.5),
        's5_c_im': nrm(ks[18], (DEPTH, 2, S5_G, S5_CG, S5_P), (2 * S5_P) ** -0.5),
        's5_d': nrm(ks[19], (DEPTH, S5_G, S5_CG), 1.0),
        's5_w_glu': nrm(ks[20], (DEPTH, S5_W, S5_W), S5_W ** -0.5),
        's5_b_glu': nrm(ks[21], (DEPTH, S5_W), 0.02),
        'w_out': nrm(ks[22], (DEPTH, MIX_W, D), DN_BETA * MIX_W ** -0.5),
        'ffn2_w_up': nrm(ks[23], (DEPTH, D, 2 * D_FF), D ** -0.5),
        'ffn2_w_down': nrm(ks[24], (DEPTH, D_FF, D), DN_BETA * D_FF ** -0.5),
    }


def reference(x, c, ctx, c_ctx, w_ada, b_ada, ln_g, ln_b, ffn1_w_up, ffn1_w_down, w_in, na_rpb,
              s5_a_re, s5_a_im, s5_log_dt, s5_b_re, s5_b_im, s5_c_re, s5_c_im, s5_d,
              s5_w_glu, s5_b_glu, w_out, ffn2_w_up, ffn2_w_down):
    h_c = ctx
    for layer in range(DEPTH):
        last = layer == DEPTH - 1
        mod = jnp.split((jax.nn.silu(c) @ w_ada[layer] + b_ada[layer])[:, None, :], 9, axis=-1)
        mod_c = jnp.split(jax.nn.silu(c_ctx) @ w_ada[layer] + b_ada[layer], 9, axis=-1)

        x = post_norm(x, 0.5 * mod[2] * swiglu(modulate(x, mod[0], mod[1]), ffn1_w_up[layer], ffn1_w_down[layer]),
                      ln_g[layer, 0], ln_b[layer, 0])
        h_c = post_norm(h_c, 0.5 * mod_c[2] * swiglu(modulate(h_c, mod_c[0], mod_c[1]),
                                                     ffn1_w_up[layer], ffn1_w_down[layer]),
                        ln_g[layer, 0], ln_b[layer, 0])

        y, y_c = parallel_mixer(modulate(x, mod[3], mod[4]), modulate(h_c, mod_c[3], mod_c[4]),
                                w_in[layer], na_rpb[layer], s5_a_re[layer], s5_a_im[layer], s5_log_dt[layer],
                                s5_b_re[layer], s5_b_im[layer], s5_c_re[layer], s5_c_im[layer], s5_d[layer],
                                s5_w_glu[layer], s5_b_glu[layer], w_out[layer], not last)
        x = post_norm(x, mod[5] * y, ln_g[layer, 1], ln_b[layer, 1])

        x = post_norm(x, 0.5 * mod[8] * swiglu(modulate(x, mod[6], mod[7]), ffn2_w_up[layer], ffn2_w_down[layer]),
                      ln_g[layer, 2], ln_b[layer, 2])
        if not last:
            h_c = post_norm(h_c, mod_c[5] * y_c, ln_g[layer, 1], ln_b[layer, 1])
            h_c = post_norm(h_c, 0.5 * mod_c[8] * swiglu(modulate(h_c, mod_c[6], mod_c[7]),
                                                         ffn2_w_up[layer], ffn2_w_down[layer]),
                            ln_g[layer, 2], ln_b[layer, 2])
    return x
```

```python
import contextlib
import math
import numpy as np
import concourse.bass as bass
import concourse.mybir as mybir
from concourse.bass_utils import run_bass_kernel_spmd
from concourse.ap import AP

F32 = mybir.dt.float32
BF16 = mybir.dt.bfloat16
I32 = mybir.dt.int32
AF = mybir.ActivationFunctionType
ALU = mybir.AluOpType
AX = mybir.AxisListType

ENGS = ("pe", "act", "dve", "pool", "sp")


class Op:
    __slots__ = ("eng", "fn", "reads", "writes", "pos", "is_dma", "dsem", "dcount",
                 "waits", "signal", "count", "vc")

    def __init__(self, eng, fn, reads, writes, is_dma=False, dsem=None):
        self.eng = eng
        self.fn = fn
        self.reads = tuple(reads)
        self.writes = tuple(writes)
        self.is_dma = is_dma
        self.dsem = dsem
        self.dcount = 0
        self.waits = {}
        self.signal = False
        self.count = 0
        self.vc = None


class Prog:
    def __init__(self):
        self.streams = {e: [] for e in ENGS}
        self.last_writer = {}
        self.readers = {}
        self.dma_counts = {}
        self.key2phys = {}
        self.free_phys = []
        self.free_sw = []
        self.sw_phys = set()
        self.nphys = 0
        self.final_waits = []

    def op(self, eng, fn, reads=(), writes=()):
        ex = [r for r in reads if isinstance(r, str) and r.startswith("ps") and r not in writes]
        if ex:
            writes = list(writes) + ex
        o = Op(eng, fn, reads, writes)
        self._add(o, ())
        return o

    def dma(self, eng, fn, reads=(), writes=(), sem=None, soft=()):
        if sem is None:
            sem = "d_" + str(writes[0])
        if sem not in self.key2phys:
            fl = self.free_sw if eng == "pool" else self.free_phys
            if fl:
                self.key2phys[sem] = fl.pop(0)
            else:
                self.key2phys[sem] = self.nphys
                self.nphys += 1
                if eng == "pool":
                    self.sw_phys.add(self.key2phys[sem])
        sem = self.key2phys[sem]
        assert (sem in self.sw_phys) == (eng == "pool"), "semaphore shared between SW and HW DGE"
        o = Op(eng, fn, reads, writes, is_dma=True, dsem=sem)
        self.dma_counts[sem] = self.dma_counts.get(sem, 0) + 16
        o.dcount = self.dma_counts[sem]
        self._add(o, soft)
        return o

    @staticmethod
    def _tok(o):
        if o.is_dma:
            return (o.dsem, o.dcount)
        return (o.eng, o.pos)

    def _add(self, o, soft):
        st = self.streams[o.eng]
        o.pos = len(st) + 1
        st.append(o)
        vc = dict(st[-2].vc) if len(st) > 1 else {}
        deps = []
        for r in o.reads:
            w = self.last_writer.get(r)
            if w is not None:
                deps.append(w)
        for w_ in o.writes:
            w = self.last_writer.get(w_)
            if w is not None:
                deps.append(w)
            deps.extend(self.readers.get(w_, ()))
        deps.sort(key=lambda d_: -self._tok(d_)[1])
        for d in deps:
            if d is o:
                continue
            k, v = self._tok(d)
            if (not d.is_dma) and d.eng == o.eng and o.eng == "pe":
                continue
            if vc.get(k, 0) >= v:
                continue
            if o.waits.get(k, 0) < v:
                o.waits[k] = v
            if not d.is_dma:
                d.signal = True
            for kk, vv in d.vc.items():
                if vc.get(kk, 0) < vv:
                    vc[kk] = vv
            if vc.get(k, 0) < v:
                vc[k] = v
        o.vc = vc
        for r in o.reads:
            self.readers.setdefault(r, []).append(o)
        for w_ in o.writes:
            self.last_writer[w_] = o
            self.readers[w_] = []
        for w_ in soft:
            self.last_writer[w_] = o

    def barrier(self):
        lasts = []
        for e in ENGS:
            if e == "sp":
                continue
            st = self.streams[e]
            for o_ in reversed(st):
                if not o_.is_dma:
                    lasts.append(o_)
                    break
        dmas = dict(self.dma_counts)
        for e in ENGS:
            o = Op(e, None, (), ())
            st = self.streams[e]
            o.pos = len(st) + 1
            vc = dict(st[-1].vc) if st else {}
            for d in lasts:
                if d.eng == e:
                    if e == "pe":
                        continue
                k, v = self._tok(d)
                if vc.get(k, 0) < v:
                    o.waits[k] = v
                    d.signal = True
                    vc[k] = v
            for k, v in dmas.items():
                if vc.get(k, 0) < v:
                    o.waits[k] = v
                    vc[k] = v
            o.vc = vc
            st.append(o)
        self.last_writer = {}
        self.readers = {}
        allp = set(self.free_phys) | set(self.free_sw) | set(self.key2phys.values())
        self.free_phys = sorted(p for p in allp if p not in self.sw_phys)
        self.free_sw = sorted(p for p in allp if p in self.sw_phys)
        self.key2phys = {}

    def finish_wait(self, eng, sem_keys):
        self.final_waits.append((eng, list(sem_keys)))

    def emit(self, nc):
        for e in ENGS:
            c = 0
            for o in self.streams[e]:
                if o.signal and not o.is_dma:
                    c += 1
                o.count = c
        pos2count = {e: {o.pos: o.count for o in self.streams[e]} for e in ENGS}
        with contextlib.ExitStack() as es:
            sems = {}
            for e in ENGS:
                sems[e] = es.enter_context(nc.semaphore("s_" + e))
            for k in range(self.nphys):
                sems[k] = es.enter_context(nc.semaphore("q%d" % k))
            block = es.enter_context(nc.Block())
            engmap = {"pe": "tensor", "act": "scalar", "dve": "vector", "pool": "gpsimd",
                      "sp": "sync"}

            def make(e):
                def body(engine):
                    for o in self.streams[e]:
                        for k, v in o.waits.items():
                            if k in ENGS:
                                engine.wait_ge(sems[k], pos2count[k][v])
                            else:
                                engine.wait_ge(sems[k], v)
                        if o.fn is None:
                            if o.signal:
                                engine.nop().then_inc(sems[e], 1)
                            continue
                        ins = o.fn(engine)
                        if o.is_dma:
                            ins.then_inc(sems[o.dsem], 16)
                        elif o.signal:
                            ins.then_inc(sems[e], 1)
                    for (fe, keys) in self.final_waits:
                        if fe == e:
                            for k in keys:
                                engine.wait_ge(sems[k], self.dma_counts[k])
                return body

            for e in ENGS:
                if self.streams[e] or any(fe == e for fe, _ in self.final_waits):
                    getattr(block, engmap[e])(make(e))


D = 1024
NLAT = 4096
NCTX = 256
NT = NLAT + NCTX
DFF = 2816
NJ = DFF // 128
NCH = NT // 8
ALPHA = 2.0 ** 0.25
EPS = 1e-6
AW = 51200
MASKV = -240000.0
TWO_PI = 2.0 * math.pi


class KB:
    def __init__(self, debug=None):
        self.debug = debug or ()
        self.nc = bass.Bass("TRN2", target_bir_lowering=False)
        self.P = Prog()
        self.es = contextlib.ExitStack()
        self.uid = 0

    def dram_in(self, name, shape, dt=F32):
        return self.nc.dram_tensor(name, list(shape), dt, kind="ExternalInput").ap()

    def dram_out(self, name, shape, dt=F32):
        return self.nc.dram_tensor(name, list(shape), dt, kind="ExternalOutput").ap()

    def dram_tmp(self, name, shape, dt=F32):
        if name in self.debug:
            return self.nc.dram_tensor(name, list(shape), dt, kind="ExternalOutput").ap()
        return self.nc.dram_tensor(name, list(shape), dt).ap()

    def V(self, off, n, dt=F32, pat=None, **kw):
        v = self.arena[:, off:off + n]
        if dt != F32:
            v = v.bitcast(dt)
        if pat is not None:
            v = v.rearrange(pat, **kw)
        return v

    def name(self, s):
        self.uid += 1
        return "%s%d" % (s, self.uid)


def bc_free(ap2, n):
    return AP(ap2.tensor, ap2.offset, [list(a) for a in ap2.ap] + [[0, n]])


def mk_ap(base, dims, off=0):
    return AP(base.tensor, base.offset + off, [list(base.ap[0])] + [list(d) for d in dims])


O_IDENT = 0
O_IDENTB = 128
O_ONESB = 192
O_MODT = 224
O_BADA = 368
O_LNG = 440
O_LNB = 464
O_BGLU = 488
O_M05 = 496
O_ST = 500
O_SCL = 520
O_SCC = 648
PERS = 1024
S_SC1P, S_SH0, S_HG2, S_G1S, S_B1S, S_AG0, S_AB0, S_M5, S_G2S, S_B2S, S_AG1, S_AB1, S_HG8, S_G2, S_B2 = range(15)


def kb_setup(self, ins):
    nc, P = self.nc, self.P
    self.arena = self.es.enter_context(nc.sbuf_tensor("arena", [128, AW], F32))
    self.psum = self.es.enter_context(nc.psum_tensor("psum", [128, 8, 512], F32))
    ps = self.psum
    V = self.V
    self.ident = ident = V(O_IDENT, 128)
    self.identb = identb = V(O_IDENTB, 64, BF16)
    self.onesb = onesb = V(O_ONESB, 32, BF16)
    self.modT = modT = V(O_MODT, 144, F32, "p (c v) -> p c v", v=2)
    badaT = V(O_BADA, 72)
    self.lngT = lngT = V(O_LNG, 24)
    self.lnbT = lnbT = V(O_LNB, 24)
    self.bgluT = bgluT = V(O_BGLU, 4)
    self.m05 = m05 = V(O_M05, 1)
    sT = V(O_ST, 16)
    self.scl = scl = V(O_SCL, 128, F32, "p (s k) -> p s k", k=8)
    self.scc = scc = V(O_SCC, 64, F32, "p (s k) -> p s k", k=8)

    P.op("pool", lambda e: e.memset(ident, 0.0), writes=["ident"])
    P.op("pool", lambda e: e.affine_select(out=ident, in_=ident, pattern=[[-1, 128]],
                                           compare_op=ALU.not_equal, fill=1.0, base=0,
                                           channel_multiplier=1), reads=["ident"], writes=["ident"])
    P.op("pool", lambda e: e.tensor_copy(out=identb, in_=ident), reads=["ident"], writes=["identb"])
    P.op("pool", lambda e: e.memset(onesb, 1.0), writes=["onesb"])
    P.op("pool", lambda e: e.memset(m05, -0.5), writes=["m05"])

    T0 = PERS
    st1 = V(T0, 128)
    st2 = V(T0 + 128, 128)
    P.dma("sp", lambda e: e.dma_start(out=st1[0:72, :], in_=ins["b_ada"]), writes=["st1"])
    P.dma("sp", lambda e: e.dma_start(out=st2[0:24, :], in_=ins["ln_g"]), writes=["st2a"])
    P.dma("sp", lambda e: e.dma_start(out=st2[24:48, :], in_=ins["ln_b"]), writes=["st2b"])
    P.dma("sp", lambda e: e.dma_start(out=st2[48:52, :], in_=ins["s5_b_glu"]), writes=["st2c"])
    P.dma("sp", lambda e: e.dma_start(out=st2[52:60, :], in_=ins["c"]), writes=["st2d"])
    P.dma("sp", lambda e: e.dma_start(out=st2[60:68, :], in_=ins["c_ctx"]), writes=["st2e"])
    pt = ps[:, 6, 0:256]
    P.op("pe", lambda e: e.transpose(out=pt[:, 0:72], in_=st1[0:72, :], identity=ident[0:72, 0:72]),
         reads=["st1", "ident"], writes=["psT6"])
    P.op("pe", lambda e: e.transpose(out=pt[:, 128:196], in_=st2[0:68, :], identity=ident[0:68, 0:68]),
         reads=["st2a", "st2b", "st2c", "st2d", "st2e", "ident"], writes=["psT6"])
    P.op("dve", lambda e: e.tensor_copy(out=badaT, in_=pt[:, 0:72]), reads=["psT6"], writes=["badaT"])
    P.op("dve", lambda e: e.tensor_copy(out=lngT, in_=pt[:, 128:152]), reads=["psT6"], writes=["lngT"])
    P.op("dve", lambda e: e.tensor_copy(out=lnbT, in_=pt[:, 152:176]), reads=["psT6"], writes=["lnbT"])
    P.op("dve", lambda e: e.tensor_copy(out=bgluT, in_=pt[:, 176:180]), reads=["psT6"], writes=["bgluT"])
    P.op("act", lambda e: e.activation(out=sT, in_=pt[:, 180:196], func=AF.Silu), reads=["psT6"], writes=["sT"])

    wb = [V(T0 + 256 + i * 4096, 4096, F32, "p (k c) -> p k c", k=8) for i in range(2)]
    w_ada = ins["w_ada"].rearrange("(k p) c -> p k c", p=128)
    modps = ps[:, 0, 0:144]
    for bi in range(18):
        t = wb[bi % 2]
        P.dma("sp", lambda e, t=t, bi=bi: e.dma_start(out=t, in_=w_ada[:, :, bi * 512:(bi + 1) * 512]),
              writes=["wada%d" % (bi % 2)])
        for fc in range(4):
            col = (bi * 4 + fc) * 2
            for kk in range(8):
                rhs = mk_ap(sT, [[8, 2]], off=kk)
                P.op("pe", lambda e, t=t, fc=fc, kk=kk, col=col, rhs=rhs: e.matmul(
                    modps[:, col:col + 2], lhsT=t[:, kk, fc * 128:(fc + 1) * 128], rhs=rhs,
                    start=(kk == 0), stop=(kk == 7)),
                    reads=["wada%d" % (bi % 2), "sT"], writes=["ps0"])
    P.op("dve", lambda e: e.tensor_tensor(out=modT, in0=modps.rearrange("p (c v) -> p c v", v=2),
                                          in1=bc_free(badaT, 2), op=ALU.add),
         reads=["ps0", "badaT"], writes=["modT"])

    def m(i, v):
        return modT[:, 8 * i:8 * i + 8, v]

    def G(i):
        return lngT[:, 8 * i:8 * i + 8]

    def B(i):
        return lnbT[:, 8 * i:8 * i + 8]

    def dv(fn, reads=("modT", "lngT", "lnbT", "scal")):
        P.op("dve", fn, reads=list(reads), writes=["scal"])

    for v, sc in ((0, scl), (1, scc)):
        dv(lambda e, sc=sc, v=v: e.tensor_scalar(out=sc[:, S_SC1P, :], in0=m(1, v), scalar1=1.0, scalar2=None, op0=ALU.add))
        dv(lambda e, sc=sc, v=v: e.tensor_copy(out=sc[:, S_SH0, :], in_=m(0, v)))
        dv(lambda e, sc=sc, v=v: e.tensor_scalar(out=sc[:, S_HG2, :], in0=m(2, v), scalar1=0.5, scalar2=None, op0=ALU.mult))
        dv(lambda e, sc=sc, v=v: e.tensor_scalar(out=sc[:, S_G1S, :], in0=m(4, v), scalar1=1.0, scalar2=None, op0=ALU.add))
        dv(lambda e, sc=sc, v=v: e.tensor_tensor(out=sc[:, S_B1S, :], in0=sc[:, S_G1S, :], in1=B(0), op=ALU.mult))
        dv(lambda e, sc=sc, v=v: e.tensor_tensor(out=sc[:, S_B1S, :], in0=sc[:, S_B1S, :], in1=m(3, v), op=ALU.add))
        dv(lambda e, sc=sc, v=v: e.tensor_tensor(out=sc[:, S_G1S, :], in0=sc[:, S_G1S, :], in1=G(0), op=ALU.mult))
    sc = scl
    dv(lambda e: e.tensor_scalar(out=sc[:, S_AG0, :], in0=G(0), scalar1=ALPHA, scalar2=None, op0=ALU.mult))
    dv(lambda e: e.tensor_scalar(out=sc[:, S_AB0, :], in0=B(0), scalar1=ALPHA, scalar2=None, op0=ALU.mult))
    dv(lambda e: e.tensor_copy(out=sc[:, S_M5, :], in_=m(5, 0)))
    dv(lambda e: e.tensor_scalar(out=sc[:, S_G2S, :], in0=m(7, 0), scalar1=1.0, scalar2=None, op0=ALU.add))
    dv(lambda e: e.tensor_tensor(out=sc[:, S_B2S, :], in0=sc[:, S_G2S, :], in1=B(1), op=ALU.mult))
    dv(lambda e: e.tensor_tensor(out=sc[:, S_B2S, :], in0=sc[:, S_B2S, :], in1=m(6, 0), op=ALU.add))
    dv(lambda e: e.tensor_tensor(out=sc[:, S_G2S, :], in0=sc[:, S_G2S, :], in1=G(1), op=ALU.mult))
    dv(lambda e: e.tensor_scalar(out=sc[:, S_AG1, :], in0=G(1), scalar1=ALPHA, scalar2=None, op0=ALU.mult))
    dv(lambda e: e.tensor_scalar(out=sc[:, S_AB1, :], in0=B(1), scalar1=ALPHA, scalar2=None, op0=ALU.mult))
    dv(lambda e: e.tensor_scalar(out=sc[:, S_HG8, :], in0=m(8, 0), scalar1=0.5, scalar2=None, op0=ALU.mult))
    dv(lambda e: e.tensor_copy(out=sc[:, S_G2, :], in_=G(2)))
    dv(lambda e: e.tensor_copy(out=sc[:, S_B2, :], in_=B(2)))


KB.setup = kb_setup


def kb_convert(self, ins):
    nc, P, V = self.nc, self.P, self.V
    W = self.W = {}
    W["up1"] = self.dram_tmp("wup1_d", [NJ, 128, 8, 256], BF16)
    W["up2"] = self.dram_tmp("wup2_d", [NJ, 128, 8, 256], BF16)
    W["dn1"] = self.dram_tmp("wdn1_d", [8, 128, NJ, 128], BF16)
    W["dn2"] = self.dram_tmp("wdn2_d", [8, 128, NJ, 128], BF16)
    W["in"] = self.dram_tmp("win_d", [16, 128, 8, 128], BF16)
    W["out"] = self.dram_tmp("wout_d", [8, 128, 8, 128], BF16)
    W["glu"] = self.dram_tmp("wglu_d", [4, 128, 4, 128], BF16)
    if "noconv" in self.debug:
        return
    T0 = PERS
    CMAX = 5632
    srcb = [V(T0 + i * CMAX, CMAX) for i in range(2)]
    dstb = [V(T0 + 2 * CMAX + i * (CMAX // 2), CMAX // 2, BF16) for i in range(2)]
    self.cv_n = 0
    engs = ["act", "dve", "pool"]
    if "nopool" in self.debug:
        engs = ["act", "dve", "dve"]

    def conv(src, nrt, C, dst, mode):
        for rt in range(nrt):
            n = self.cv_n
            self.cv_n += 1
            sb = srcb[n % 2][:, 0:C]
            db = dstb[n % 2][:, 0:C]
            P.dma("sp", lambda e, sb=sb, rt=rt: e.dma_start(out=sb, in_=src[rt * 128:(rt + 1) * 128, :]),
                  writes=["cvs%d" % (n % 2)])
            eng = engs[n % 3]
            if mode == "up":
                i_ap = sb.rearrange("p (h j c) -> p h j c", h=2, j=NJ)
                o_ap = db.rearrange("p (j h c) -> p h j c", h=2, j=NJ)
            else:
                i_ap, o_ap = sb, db
            if eng == "act":
                P.op("act", lambda e, i_ap=i_ap, o_ap=o_ap: e.activation(out=o_ap, in_=i_ap, func=AF.Copy),
                     reads=["cvs%d" % (n % 2)], writes=["cvd%d" % (n % 2)])
            else:
                P.op(eng, lambda e, i_ap=i_ap, o_ap=o_ap: e.tensor_copy(out=o_ap, in_=i_ap),
                     reads=["cvs%d" % (n % 2)], writes=["cvd%d" % (n % 2)])
            cw = 256 if mode == "up" else 128
            d_ap = dst[:, :, rt, :].rearrange("m p c -> p m c")
            s_ap = db.rearrange("p (m c) -> p m c", c=cw)
            P.dma("sp", lambda e, d_ap=d_ap, s_ap=s_ap: e.dma_start(out=d_ap, in_=s_ap),
                  reads=["cvd%d" % (n % 2)], writes=[("W", id(dst), n)], sem="wconv%d" % (n % 2), soft=["Wall"])

    conv(ins["ffn1_w_up"], 8, 5632, W["up1"], "up")
    if "cv1" in self.debug:
        return
    conv(ins["ffn1_w_down"], NJ, 1024, W["dn1"], "plain")
    if "cv2" in self.debug:
        return
    conv(ins["w_in"], 8, 2048, W["in"], "plain")
    if "cv3" in self.debug:
        return
    conv(ins["w_out"], 8, 1024, W["out"], "plain")
    if "cv4" in self.debug:
        return
    conv(ins["s5_w_glu"], 4, 512, W["glu"], "plain")
    if "cv5" in self.debug:
        return
    conv(ins["ffn2_w_up"], 8, 5632, W["up2"], "up")
    conv(ins["ffn2_w_down"], NJ, 1024, W["dn2"], "plain")


KB.convert = kb_convert


def kb_bank(self, b, n=512):
    return self.psum[:, b, 0:n]


def kb_ffn(self, key, hT, nb, xsT, wup_d, wdn_d, hg, R):
    P = self.P
    actT = R["actT"]
    for j in range(NJ):
        slot = self.wup_cnt % 4
        self.wup_cnt += 1
        wt = R["wup"][slot]
        P.dma("sp", lambda e, wt=wt, j=j: e.dma_start(out=wt, in_=wup_d[j]), reads=["Wall"],
              writes=["wup%d" % slot])
        pa = self.bank(j % 2, nb)
        pg = self.bank(2 + j % 2, nb)
        for kk in range(8):
            P.op("pe", lambda e, wt=wt, kk=kk, pa=pa: e.matmul(pa, lhsT=wt[:, kk, 0:128], rhs=hT[:, kk, :],
                                                              start=(kk == 0), stop=(kk == 7)),
                 reads=["wup%d" % slot, key + "hT%d" % kk], writes=["ps%d" % (j % 2)])
        for kk in range(8):
            P.op("pe", lambda e, wt=wt, kk=kk, pg=pg: e.matmul(pg, lhsT=wt[:, kk, 128:256], rhs=hT[:, kk, :],
                                                              start=(kk == 0), stop=(kk == 7)),
                 reads=["wup%d" % slot, key + "hT%d" % kk], writes=["ps%d" % (2 + j % 2)])
        sgb = R["sg"][j % 2][:, 0:nb]
        P.op("act", lambda e, sgb=sgb, pg=pg: e.activation(out=sgb, in_=pg, func=AF.Silu),
             reads=["ps%d" % (2 + j % 2)], writes=["sg%d" % (j % 2)])
        P.op("dve", lambda e, sgb=sgb, pa=pa, j=j: e.tensor_tensor(out=actT[:, j, 0:nb], in0=pa, in1=sgb, op=ALU.mult),
             reads=["ps%d" % (j % 2), "sg%d" % (j % 2)], writes=["actT%d" % j])
    for i in range(8):
        slot = self.wdn_cnt % 3
        self.wdn_cnt += 1
        wd = R["wdn"][slot]
        P.dma("sp", lambda e, wd=wd, i=i: e.dma_start(out=wd, in_=wdn_d[i]), reads=["Wall"],
              writes=["wdn%d" % slot])
        po = self.bank(4 + i % 2, nb)
        for j in range(NJ):
            P.op("pe", lambda e, wd=wd, j=j, po=po: e.matmul(po, lhsT=wd[:, j, :], rhs=actT[:, j, 0:nb],
                                                            start=(j == 0), stop=(j == NJ - 1)),
                 reads=["wdn%d" % slot, "actT%d" % j], writes=["ps%d" % (4 + i % 2)])
        P.op("dve", lambda e, po=po, i=i: e.scalar_tensor_tensor(out=xsT[:, i, :], in0=po, scalar=hg[:, i:i + 1],
                                                                in1=xsT[:, i, :], op0=ALU.mult, op1=ALU.add),
             reads=["ps%d" % (4 + i % 2), key + "xs%d" % i, "scal"], writes=[key + "xs%d" % i])


def kb_ln(self, key, zT, nb, R, outs):
    P = self.P
    S = nb // 128
    psT = self.psum[:, 6:8, :].rearrange("p a b -> p (a b)")
    mv = R["mv"]
    for s in range(S):
        zn = R["zn"][s % 2]
        znr = "zn%d" % (s % 2)
        for kk in range(8):
            P.op("pe", lambda e, kk=kk, s=s: e.transpose(out=psT[:, kk * 128:(kk + 1) * 128],
                                                         in_=zT[:, kk, s * 128:(s + 1) * 128], identity=self.ident),
                 reads=[key + "xs%d" % kk, "ident"], writes=["psT%d" % (6 + kk // 4)])
        st = R["bst"]
        P.op("dve", lambda e, st=st: e.bn_stats(out=st[:, 0:6], in_=psT[:, 0:512]), reads=["psT6"], writes=["bst"])
        P.op("dve", lambda e, st=st: e.bn_stats(out=st[:, 6:12], in_=psT[:, 512:1024]), reads=["psT7"], writes=["bst"])
        P.op("dve", lambda e, st=st, s=s: e.bn_aggr(out=mv[:, s, 0:2], in_=st[:, 0:12]), reads=["bst"], writes=["mv"])
        P.op("dve", lambda e, s=s: e.tensor_scalar(out=mv[:, s, 2:3], in0=mv[:, s, 1:2], scalar1=EPS, scalar2=None, op0=ALU.add),
             reads=["mv"], writes=["mv"])
        P.op("pool", lambda e, s=s: e.tensor_tensor(out=mv[:, s, 3:4], in0=mv[:, s, 2:3], in1=self.m05, op=ALU.pow),
             reads=["mv", "m05"], writes=["mv"])
        P.op("dve", lambda e, s=s: e.tensor_scalar(out=mv[:, s, 4:5], in0=mv[:, s, 0:1], scalar1=mv[:, s, 3:4], scalar2=-1.0,
                                                   op0=ALU.mult, op1=ALU.mult), reads=["mv"], writes=["mv"])
        for hb in range(2):
            P.op("act", lambda e, s=s, zn=zn, hb=hb: e.activation(out=zn[:, hb * 512:(hb + 1) * 512], in_=psT[:, hb * 512:(hb + 1) * 512],
                                                                 func=AF.Identity, scale=mv[:, s, 3:4], bias=mv[:, s, 4:5]),
                 reads=["psT%d" % (6 + hb), "mv"], writes=[znr + "h%d" % hb])
        if outs is None:
            R["tok_cb"](s, zn, znr)
            continue
        for kk in range(8):
            P.op("pe", lambda e, kk=kk, zn=zn: e.transpose(out=psT[:, kk * 128:(kk + 1) * 128],
                                                           in_=zn[:, kk * 128:(kk + 1) * 128], identity=self.ident),
                 reads=[znr + "h%d" % (kk // 4), "ident"], writes=["psT%d" % (6 + kk // 4)])
        for (dst, gcol, bcol, wres) in outs:
            for kk in range(8):
                o_ap = dst[:, kk, s * 128:(s + 1) * 128]
                i_ap = psT[:, kk * 128:(kk + 1) * 128]
                pr = "psT%d" % (6 + kk // 4)
                g1 = gcol[:, kk:kk + 1]
                b1 = bcol[:, kk:kk + 1]
                if kk < 4:
                    P.op("act", lambda e, o_ap=o_ap, i_ap=i_ap, g1=g1, b1=b1: e.activation(
                        out=o_ap, in_=i_ap, func=AF.Identity, scale=g1, bias=b1),
                        reads=[pr, "scal"], writes=[wres + "%d" % kk])
                else:
                    P.op("dve", lambda e, o_ap=o_ap, i_ap=i_ap, g1=g1, b1=b1: e.tensor_scalar(
                        out=o_ap, in0=i_ap, scalar1=g1, scalar2=b1, op0=ALU.mult, op1=ALU.add),
                        reads=[pr, "scal"], writes=[wres + "%d" % kk])


KB.bank = kb_bank
KB.ffn = kb_ffn
KB.ln = kb_ln


def kb_phaseA(self, ins):
    nc, P, V = self.nc, self.P, self.V
    ps = self.psum
    S = self.S = {}
    S["xs1"] = self.dram_tmp("xs1_d", [8, 128, NLAT], F32)
    S["q"] = self.dram_tmp("q_d", [4, 128, NLAT], BF16)
    S["k"] = self.dram_tmp("k_d", [4, 128, NT], BF16)
    S["v"] = self.dram_tmp("v_d", [NT, 512], BF16)
    S["u"] = self.dram_tmp("u_d", [32, 16, 8, NCH], BF16)
    o = PERS
    R = {}

    def al(n):
        nonlocal o
        r = o
        o += n
        return r
    xtok = [V(al(1024), 1024) for _ in range(2)]
    xsT = V(al(4096), 4096, F32, "p (k t) -> p k t", k=8)
    hT = V(al(2048), 2048, BF16, "p (k t) -> p k t", k=8)
    R["actT"] = V(al(5632), 5632, BF16, "p (j t) -> p j t", j=NJ)
    R["sg"] = [V(al(512), 512) for _ in range(2)]
    R["zn"] = [V(al(1024), 1024) for _ in range(2)]
    xs1st = V(al(4096), 4096, F32, "p (k t) -> p k t", k=8)
    h2T = V(al(2048), 2048, BF16, "p (k t) -> p k t", k=8)
    qkst = V(al(2048), 2048, BF16, "p (m t) -> p m t", m=8)
    vst = [V(al(256), 256, BF16) for _ in range(2)]
    ust = V(al(1024), 1024, BF16, "p (m t) -> p m t", m=4)
    R["mv"] = V(al(32), 32, F32, "p (s c) -> p s c", c=8)
    R["bst"] = V(al(16), 16)
    R["wup"] = [V(al(1024), 1024, BF16, "p (k c) -> p k c", k=8) for _ in range(4)]
    R["wdn"] = [V(al(1408), 1408, BF16, "p (j c) -> p j c", j=NJ) for _ in range(3)]
    wint = [V(al(512), 512, BF16, "p (k c) -> p k c", k=8) for _ in range(4)]
    wv = V(al(2048), 2048, BF16, "p (k m c) -> p k m c", k=8, m=4)
    assert o <= AW, o
    self.wup_cnt = 0
    self.wdn_cnt = 0
    win_cnt = 0
    psT = ps[:, 6:8, :].rearrange("p a b -> p (a b)")
    Wd = self.W
    for m in range(4):
        P.dma("sp", lambda e, m=m: e.dma_start(out=wv[:, :, m, :], in_=Wd["in"][8 + m]), reads=["Wall"], writes=["wv%d" % m])

    if "A0" in self.debug:
        return
    blocks = [(0, NCTX, True)] + [(NCTX + i * 512, 512, False) for i in range(NLAT // 512)]
    if "short" in self.debug:
        blocks = blocks[:2]
    for (t0, nb, isctx) in blocks:
        sc = self.scc if isctx else self.scl
        Sn = nb // 128
        src = ins["ctx"] if isctx else ins["x"]
        r0 = t0 if isctx else t0 - NCTX
        for s in range(Sn):
            xt = xtok[s % 2]
            x_ap = src[r0 + s * 128:r0 + (s + 1) * 128, :]
            P.dma("sp", lambda e, xt=xt, x_ap=x_ap: e.dma_start(out=xt, in_=x_ap),
                  writes=["xtok%d" % (s % 2)])
            if "L1" in self.debug:
                continue
            for kk in range(8):
                P.op("pe", lambda e, kk=kk, xt=xt: e.transpose(out=psT[:, kk * 128:(kk + 1) * 128],
                                                               in_=xt[:, kk * 128:(kk + 1) * 128], identity=self.ident),
                     reads=["xtok%d" % (s % 2), "ident"], writes=["psT%d" % (6 + kk // 4)])
            if "L2" in self.debug:
                continue
            for kk in range(8):
                i_ap = psT[:, kk * 128:(kk + 1) * 128]
                pr = "psT%d" % (6 + kk // 4)
                o1 = xsT[:, kk, s * 128:(s + 1) * 128]
                o2 = hT[:, kk, s * 128:(s + 1) * 128]
                s1 = sc[:, S_SC1P, kk:kk + 1]
                s2 = sc[:, S_SH0, kk:kk + 1]
                if kk < 4:
                    P.op("act", lambda e, i_ap=i_ap, o1=o1: e.activation(out=o1, in_=i_ap, func=AF.Identity, scale=ALPHA),
                         reads=[pr], writes=["Axs%d" % kk])
                    P.op("act", lambda e, i_ap=i_ap, o2=o2, s1=s1, s2=s2: e.activation(out=o2, in_=i_ap, func=AF.Identity, scale=s1, bias=s2),
                         reads=[pr, "scal"], writes=["AhT%d" % kk])
                else:
                    P.op("dve", lambda e, i_ap=i_ap, o1=o1: e.tensor_scalar(out=o1, in0=i_ap, scalar1=ALPHA, scalar2=None, op0=ALU.mult),
                         reads=[pr], writes=["Axs%d" % kk])
                    P.op("dve", lambda e, i_ap=i_ap, o2=o2, s1=s1, s2=s2: e.tensor_scalar(out=o2, in0=i_ap, scalar1=s1, scalar2=s2, op0=ALU.mult, op1=ALU.add),
                         reads=[pr, "scal"], writes=["AhT%d" % kk])
        if "A1" in self.debug:
            return
        self.ffn("A", hT[:, :, 0:nb], nb, xsT[:, :, 0:nb], Wd["up1"], Wd["dn1"], sc[:, S_HG2, :], R)
        if "A2" in self.debug:
            return
        outs = [(h2T, sc[:, S_G1S, :], sc[:, S_B1S, :], "Ah2T")]
        if not isctx:
            outs.append((xs1st, sc[:, S_AG0, :], sc[:, S_AB0, :], "Axs1"))
        self.ln("A", xsT, nb, R, outs)
        if not isctx:
            for kk in range(8):
                P.dma("pool", lambda e, kk=kk, r0=r0, nb=nb: e.dma_start(out=S["xs1"][kk, :, r0:r0 + nb], in_=xs1st[:, kk, 0:nb]),
                      reads=["Axs1%d" % kk], writes=[("xs1d", t0, kk)], sem="st_xs1_%d" % kk, soft=["xs1all"])
        if "A3" in self.debug:
            return
        for mi, m in enumerate([0, 1, 2, 3, 4, 5, 6, 7, 12, 13, 14, 15]):
            if isctx and m < 4:
                continue
            slot = win_cnt % 4
            win_cnt += 1
            wt = wint[slot]
            P.dma("sp", lambda e, wt=wt, m=m: e.dma_start(out=wt, in_=Wd["in"][m]), reads=["Wall"], writes=["win%d" % slot])
            pb = self.bank(mi % 4, nb)
            for kk in range(8):
                P.op("pe", lambda e, wt=wt, kk=kk, pb=pb, nb=nb: e.matmul(pb, lhsT=wt[:, kk, :], rhs=h2T[:, kk, 0:nb],
                                                                  start=(kk == 0), stop=(kk == 7)),
                     reads=["win%d" % slot, "Ah2T%d" % kk], writes=["ps%d" % (mi % 4)])
            if m < 8:
                eng = ("act", "dve")[mi % 2]
                o_ap = qkst[:, m, 0:nb]
                if eng == "act":
                    P.op("act", lambda e, o_ap=o_ap, pb=pb: e.activation(out=o_ap, in_=pb, func=AF.Copy),
                         reads=["ps%d" % (mi % 4)], writes=["qkst%d" % m])
                else:
                    P.op("dve", lambda e, o_ap=o_ap, pb=pb: e.tensor_copy(out=o_ap, in_=pb),
                         reads=["ps%d" % (mi % 4)], writes=["qkst%d" % m])
                if m < 4:
                    P.dma("pool", lambda e, m=m, r0=r0, o_ap=o_ap, nb=nb: e.dma_start(out=S["q"][m, :, r0:r0 + nb], in_=o_ap),
                          reads=["qkst%d" % m], writes=[("qd", t0, m)], sem="st_qk%d" % m, soft=["qall"])
                else:
                    P.dma("pool", lambda e, m=m, t0=t0, o_ap=o_ap, nb=nb: e.dma_start(out=S["k"][m - 4, :, t0:t0 + nb], in_=o_ap),
                          reads=["qkst%d" % m], writes=[("kd", t0, m)], sem="st_qk%d" % m, soft=["kall"])
            else:
                mu = m - 12
                nc8 = nb // 8
                i_ap = pb.rearrange("p (c s) -> p s c", s=8)
                o_ap = ust[:, mu, 0:nb].rearrange("p (s c) -> p s c", s=8)
                P.op("dve", lambda e, o_ap=o_ap, i_ap=i_ap: e.tensor_copy(out=o_ap, in_=i_ap),
                     reads=["ps%d" % (mi % 4)], writes=["ust%d" % mu])
                c0 = t0 // 8
                d_ap = S["u"][mu * 8:(mu + 1) * 8, :, :, c0:c0 + nc8].rearrange("g i s c -> (g i) s c")
                P.dma("pool", lambda e, d_ap=d_ap, o_ap=o_ap: e.dma_start(out=d_ap, in_=o_ap),
                      reads=["ust%d" % mu], writes=[("ud", t0, mu)], sem="st_u%d" % mu, soft=["uall"])
        for s in range(Sn):
            pv = self.bank(4 + s % 2, 512)
            for kk in range(8):
                P.op("pe", lambda e, kk=kk, s=s, pv=pv: e.matmul(pv, lhsT=h2T[:, kk, s * 128:(s + 1) * 128],
                                                                rhs=wv[:, kk, :, :].rearrange("p m c -> p (m c)"),
                                                                start=(kk == 0), stop=(kk == 7)),
                     reads=["wv0", "wv1", "wv2", "wv3", "Ah2T%d" % kk], writes=["ps%d" % (4 + s % 2)])
            vs = vst[s % 2]
            P.op("act", lambda e, vs=vs, pv=pv: e.activation(out=vs, in_=pv, func=AF.Copy),
                 reads=["ps%d" % (4 + s % 2)], writes=["vst%d" % (s % 2)])
            P.dma("pool", lambda e, vs=vs, s=s, t0=t0: e.dma_start(out=S["v"][t0 + s * 128:t0 + (s + 1) * 128, :], in_=vs),
                  reads=["vst%d" % (s % 2)], writes=[("vd", t0, s)], sem="st_v%d" % (s % 2), soft=["vall"])


KB.phaseA = kb_phaseA


IN_SHAPES = {
    "x": [NLAT, D], "c": [8, 128], "ctx": [NCTX, D], "c_ctx": [8, 128],
    "w_ada": [D, 9 * D], "b_ada": [72, 128], "ln_g": [24, 128], "ln_b": [24, 128],
    "ffn1_w_up": [D, 2 * DFF], "ffn1_w_down": [DFF, D], "w_in": [D, 2048],
    "na_rpb": [8, 15, 31], "s5_a_re": [2, 32, 64], "s5_a_im": [2, 32, 64], "s5_log_dt": [2, 32],
    "s5_b_re": [2, 32, 64, 16], "s5_b_im": [2, 32, 64, 16], "s5_c_re": [2, 32, 16, 64],
    "s5_c_im": [2, 32, 16, 64], "s5_d": [32, 16], "s5_w_glu": [512, 512], "s5_b_glu": [4, 128],
    "w_out": [D, D], "ffn2_w_up": [D, 2 * DFF], "ffn2_w_down": [DFF, D],
}


def build_nc(debug=None, stages=("setup", "convert", "A", "S5", "ATT", "C")):
    kb = KB(debug)
    ins = {k: kb.dram_in(k, shp) for k, shp in IN_SHAPES.items()}
    out = kb.dram_out("out", [NLAT, D])
    kb.out = out
    with kb.es:
        kb.setup(ins)
        kb.P.barrier()
        if "convert" in stages:
            kb.convert(ins)
            kb.P.barrier()
        if "A" in stages:
            kb.phaseA(ins)
            kb.P.barrier()
        if "S5" in stages:
            kb.phaseS5(ins)
            kb.P.barrier()
        if "ATT" in stages:
            kb.phaseATT(ins)
            kb.P.barrier()
        if "C" in stages:
            kb.phaseC(ins)
        if "dump_s5" in kb.debug:
            s5 = kb.s5
            for nm, apx, dt_ in (("dbg_W1re", s5["W1b"][0], BF16), ("dbg_W1im", s5["W1b"][1], BF16), ("dbg_T0", s5["T0b"], BF16),
                                 ("dbg_W2re", s5["W2b"][0], BF16), ("dbg_W2im", s5["W2b"][1], BF16)):
                dm = kb.dram_out(nm, [128, 32, 128], dt_)
                kb.P.dma("sp", lambda e, dm=dm, apx=apx: e.dma_start(out=dm, in_=apx), reads=["W1b", "W2b", "T0b", "s5"], writes=[nm], sem="out")
            for nm, apx in (("dbg_rho8", s5["rho8"]), ("dbg_wre", s5["w_re"]), ("dbg_wim", s5["w_im"])):
                dm = kb.dram_out(nm, [128, 32], F32)
                kb.P.dma("sp", lambda e, dm=dm, apx=apx: e.dma_start(out=dm, in_=apx), reads=["s5"], writes=[nm], sem="out")
        if "dump_mod" in kb.debug:
            dm = kb.dram_out("dbg_mod", [128, 144])
            kb.P.dma("sp", lambda e: e.dma_start(out=dm, in_=kb.V(O_MODT, 144)), reads=["modT"], writes=["dbgmod"], sem="out")
        if "C" not in stages:
            z = kb.V(AW - 1024, 1024)
            kb.P.op("pool", lambda e: e.memset(z, 0.0), writes=["zz"])
            kb.P.dma("sp", lambda e: e.dma_start(out=out[0:128, :], in_=z), reads=["zz"], writes=["outz"], sem="out")
        kb.P.finish_wait("sp", list(range(kb.P.nphys)))
        kb.P.emit(kb.nc)
    return kb.nc


def make_in_maps(inputs):
    f = lambda a: np.ascontiguousarray(np.asarray(a, dtype=np.float32))
    shared = {}
    for k in IN_SHAPES:
        if k in ("x", "c", "ctx"):
            continue
        a = f(inputs[k])
        if a.ndim >= 1 and a.shape[0] == 1 and k not in ("c_ctx",):
            a = a[0]
        shared[k] = np.ascontiguousarray(a.reshape(IN_SHAPES[k]))
    maps = []
    for b in range(8):
        m = dict(shared)
        m["x"] = f(inputs["x"][b])
        m["ctx"] = f(inputs["ctx"][b])
        m["c"] = f(inputs["c"][b]).reshape(8, 128)
        maps.append(m)
    return maps


_NC_CACHE = {}


def kernel(**inputs):
    if "nc" not in _NC_CACHE:
        _NC_CACHE["nc"] = build_nc()
    nc = _NC_CACHE["nc"]
    maps = make_in_maps(inputs)
    res = run_bass_kernel_spmd(nc, maps, core_ids=list(range(8)))
    out = np.stack([np.asarray(r["out"], dtype=np.float32) for r in res.results], axis=0)
    return out


def kb_phaseS5(self, ins):
    nc, P, V = self.nc, self.P, self.V
    ps = self.psum
    if not hasattr(self, "S"):
        self.S = {}
    S = self.S
    if "u_in" in self.debug:
        S["u"] = self.dram_in("u_d", [32, 16, 8, NCH], BF16)
    S["y"] = self.dram_tmp("y_d", [32, 16, 8, 512], F32)
    tab_d = self.dram_tmp("tab_d", [32, 128, 2, NCH], F32)
    o = PERS

    def al(n):
        nonlocal o
        r = o
        o += n
        return r
    ident = self.ident
    L = "s5"

    def dv(fn, eng="dve", extra_r=(), extra_w=()):
        P.op(eng, fn, reads=[L] + list(extra_r), writes=[L] + list(extra_w))

    def TT(out, a, b, op, eng="dve"):
        dv(lambda e: e.tensor_tensor(out=out, in0=a, in1=b, op=op), eng)

    def TS(out, a, s1, op0, s2=None, op1=None, eng="dve"):
        if op1 is None:
            dv(lambda e: e.tensor_scalar(out=out, in0=a, scalar1=s1, scalar2=None, op0=op0), eng)
        else:
            dv(lambda e: e.tensor_scalar(out=out, in0=a, scalar1=s1, scalar2=s2, op0=op0, op1=op1), eng)

    def ACT(out, a, func, scale=1.0, bias=0.0):
        dv(lambda e: e.activation(out=out, in_=a, func=func, scale=scale, bias=bias), "act")

    W1b = [V(al(2048), 2048, BF16, "p (k c) -> p k c", k=32) for _ in range(2)]
    T0b = V(al(2048), 2048, BF16, "p (g c) -> p g c", g=32)
    W2b = [V(al(2048), 2048, BF16, "p (k c) -> p k c", k=32) for _ in range(2)]
    rho8 = V(al(32), 32)
    o_mark = o
    st = V(al(384), 384)
    A = V(al(32 * 12), 32 * 12, F32, "p (n k) -> p n k", k=32)
    a_re, a_im, ldt, dtt, x1, er, th, y_, fr, cs, sn, tmp = [A[:, i, :] for i in range(12)]
    A2 = V(al(32 * 12), 32 * 12, F32, "p (n k) -> p n k", k=32)
    ab_re, ab_im, k_re, k_im, den, t1, t2, t3, w_re, w_im, iv_re, iv_im = [A2[:, i, :] for i in range(12)]
    yi = V(al(32), 32).bitcast(I32)
    PaR = V(al(512), 512, F32, "p (k j) -> p k j", j=16)
    PaI = V(al(512), 512, F32, "p (k j) -> p k j", j=16)
    Bre = V(al(512), 512, F32, "p (k c) -> p k c", c=16)
    Bim = V(al(512), 512, F32, "p (k c) -> p k c", c=16)
    Bbre = V(al(512), 512, F32, "p (k c) -> p k c", c=16)
    Bbim = V(al(512), 512, F32, "p (k c) -> p k c", c=16)
    Cre = V(al(512), 512, F32, "p (k c) -> p k c", c=16)
    Cim = V(al(512), 512, F32, "p (k c) -> p k c", c=16)
    tB = V(al(512), 512, F32, "p (k c) -> p k c", c=16)
    cst = V(al(128), 128)
    dcol = V(al(32), 32)
    maskt = [V(al(512), 512, F32, "p (r c) -> p r c", r=4) for _ in range(2)]
    identt = V(al(512), 512, F32, "p (r c) -> p r c", r=4)

    are_v = ins["s5_a_re"].rearrange("d (k m) p -> (d k) (m p)", m=2)
    aim_v = ins["s5_a_im"].rearrange("d (k m) p -> (d k) (m p)", m=2)
    P.dma("sp", lambda e: e.dma_start(out=st[0:32, 0:128], in_=are_v), writes=["s5st0"])
    P.dma("sp", lambda e: e.dma_start(out=st[0:32, 128:256], in_=aim_v), writes=["s5st1"])
    ldt2 = V(al(2), 2)
    P.dma("sp", lambda e: e.dma_start(out=ldt2[0:32, :], in_=ins["s5_log_dt"].rearrange("d (k m) -> (d k) m", m=2)), writes=["s5st2"])
    P.op("dve", lambda e: e.tensor_copy(out=st[0:32, 256:384].rearrange("p (m q) -> p m q", m=2), in_=bc_free(ldt2[0:32, :], 64)),
         reads=["s5st2", L], writes=[L])
    for mi in range(2):
        bre_v = ins["s5_b_re"].rearrange("d (k m) p c -> m p (d k) c", m=2)
        bim_v = ins["s5_b_im"].rearrange("d (k m) p c -> m p (d k) c", m=2)
        P.dma("sp", lambda e, mi=mi, bre_v=bre_v: e.dma_start(out=Bre[mi * 64:(mi + 1) * 64], in_=bre_v[mi]), writes=["s5b%d" % mi])
        P.dma("sp", lambda e, mi=mi, bim_v=bim_v: e.dma_start(out=Bim[mi * 64:(mi + 1) * 64], in_=bim_v[mi]), writes=["s5bi%d" % mi])
    dst16 = V(al(16), 16)
    dst128 = V(al(128), 128)
    P.dma("sp", lambda e: e.dma_start(out=dst16[0:32, :], in_=ins["s5_d"]), writes=["s5d16"])
    P.op("dve", lambda e: e.tensor_copy(out=dst128[0:32, :].rearrange("p (s c) -> p s c", s=8),
                                        in_=AP(dst16.tensor, dst16.offset, [[dst16.ap[0][0], 32], [0, 8], [1, 16]])),
         reads=["s5d16", L], writes=[L])
    pt = ps[:, 6, :]
    P.op("pe", lambda e: e.transpose(out=pt[:, 256:288], in_=dst128[0:32, :], identity=ident[0:32, 0:32]),
         reads=[L, "ident"], writes=["psT6"])
    P.op("dve", lambda e: e.tensor_copy(out=dcol, in_=pt[:, 256:288]), reads=["psT6", L], writes=[L])
    for i in range(3):
        P.op("pe", lambda e, i=i: e.transpose(out=pt[:, i * 32:(i + 1) * 32], in_=st[0:32, i * 128:(i + 1) * 128], identity=ident[0:32, 0:32]),
             reads=["s5st0", "s5st1", L, "ident"], writes=["psT6"])
    P.op("dve", lambda e: e.tensor_copy(out=A[:, 0:3, :], in_=pt[:, 0:96].rearrange("p (n k) -> p n k", n=3)), reads=["psT6", L], writes=[L])
    for ri, (src, dst) in enumerate(((ins["s5_c_re"], Cre), (ins["s5_c_im"], Cim))):
        cv = src.rearrange("d (k m) c p -> d k c m p", m=2)
        for d_ in range(2):
            for j in range(2):
                for kl in range(8):
                    P.dma("sp", lambda e, cv=cv, d_=d_, j=j, kl=kl: e.dma_start(
                        out=cst[kl * 16:(kl + 1) * 16, :].rearrange("p (m q) -> p m q", m=2), in_=cv[d_, 8 * j + kl]),
                        reads=[L], writes=["s5cst%d" % kl])
                P.op("pe", lambda e: e.transpose(out=pt[:, 128:256], in_=cst, identity=ident),
                     reads=["s5cst%d" % kl for kl in range(8)] + ["ident"], writes=["psT6"])
                dk0 = d_ * 16 + 8 * j
                P.op("dve", lambda e, dst=dst, dk0=dk0: e.tensor_copy(out=dst[:, dk0:dk0 + 8, :].rearrange("p k c -> p (k c)"), in_=pt[:, 128:256]),
                     reads=["psT6", L], writes=[L] + ["s5cst%d" % kl for kl in range(8)])
    ACT(dtt, ldt, AF.Exp)
    TT(x1, a_re, dtt, ALU.mult)
    ACT(er, x1, AF.Exp)
    ACT(rho8, x1, AF.Exp, scale=8.0)
    TT(th, a_im, dtt, ALU.mult)
    for (dstt, off) in ((sn, 0.5), (cs, 0.75)):
        TS(y_, th, 1.0 / TWO_PI, ALU.mult, off, ALU.add)
        dv(lambda e: e.tensor_copy(out=yi, in_=y_))
        dv(lambda e: e.tensor_copy(out=fr, in_=yi))
        TT(fr, y_, fr, ALU.subtract)
        TS(tmp, fr, 0.0, ALU.is_lt)
        TT(fr, fr, tmp, ALU.add)
        TS(tmp, fr, 1.0, ALU.is_ge)
        TT(fr, fr, tmp, ALU.subtract)
        TS(fr, fr, TWO_PI, ALU.mult, -math.pi, ALU.add)
        TS(fr, fr, math.pi, ALU.min, -math.pi, ALU.max)
        ACT(dstt, fr, AF.Sin)
    TT(ab_re, er, cs, ALU.mult)
    TT(ab_im, er, sn, ALU.mult)
    TS(t1, ab_re, -1.0, ALU.add)
    TT(den, a_re, a_re, ALU.mult)
    TT(t2, a_im, a_im, ALU.mult)
    TT(den, den, t2, ALU.add)
    dv(lambda e: e.reciprocal(out=den, in_=den))
    TT(k_re, t1, a_re, ALU.mult)
    TT(t2, ab_im, a_im, ALU.mult)
    TT(k_re, k_re, t2, ALU.add)
    TT(k_re, k_re, den, ALU.mult)
    TT(k_im, ab_im, a_re, ALU.mult)
    TT(t2, t1, a_im, ALU.mult)
    TT(k_im, k_im, t2, ALU.subtract)
    TT(k_im, k_im, den, ALU.mult)
    TT(t1, er, er, ALU.mult)
    dv(lambda e: e.reciprocal(out=t1, in_=t1))
    TT(iv_re, ab_re, t1, ALU.mult)
    TT(iv_im, ab_im, t1, ALU.mult)
    TS(iv_im, iv_im, -1.0, ALU.mult)

    def cmul(o_re, o_im, x_re, x_im, y_re, y_im):
        TT(t1, x_re, y_re, ALU.mult)
        TT(t2, x_im, y_im, ALU.mult)
        TT(t3, x_re, y_im, ALU.mult)
        TT(o_im, x_im, y_re, ALU.mult)
        TT(o_im, o_im, t3, ALU.add)
        TT(o_re, t1, t2, ALU.subtract)
    dv(lambda e: e.memset(PaR[:, :, 7], 1.0))
    dv(lambda e: e.memset(PaI[:, :, 7], 0.0))
    dv(lambda e: e.tensor_copy(out=PaR[:, :, 8], in_=ab_re))
    dv(lambda e: e.tensor_copy(out=PaI[:, :, 8], in_=ab_im))
    for jj in range(9, 16):
        cmul(PaR[:, :, jj], PaI[:, :, jj], PaR[:, :, jj - 1], PaI[:, :, jj - 1], ab_re, ab_im)
    for jj in range(6, -1, -1):
        cmul(PaR[:, :, jj], PaI[:, :, jj], PaR[:, :, jj + 1], PaI[:, :, jj + 1], iv_re, iv_im)
    dv(lambda e: e.reciprocal(out=t1, in_=rho8))
    TT(w_re, PaR[:, :, 15], t1, ALU.mult)
    TT(w_im, PaI[:, :, 15], t1, ALU.mult)
    TS(w_im, w_im, -1.0, ALU.mult)
    for (o_, x_, kx, y_b, ky, op) in ((Bbre, Bre, k_re, Bim, k_im, ALU.subtract), (Bbim, Bim, k_re, Bre, k_im, ALU.add)):
        dv(lambda e, o_=o_, x_=x_, kx=kx: e.tensor_tensor(out=o_, in0=x_, in1=bc_free(kx, 16), op=ALU.mult),
           extra_r=["s5b0", "s5b1", "s5bi0", "s5bi1"])
        dv(lambda e, y_b=y_b, ky=ky: e.tensor_tensor(out=tB, in0=y_b, in1=bc_free(ky, 16), op=ALU.mult))
        dv(lambda e, o_=o_, op=op: e.tensor_tensor(out=o_, in0=o_, in1=tB, op=op))
    for d_ in range(2):
        dv(lambda e, d_=d_: e.memset(maskt[d_], 1.0), "pool")
        if d_ == 0:
            pat, base, cm = [[0, 4], [16, 8], [0, 16]], 15, -1
        else:
            pat, base, cm = [[0, 4], [-16, 8], [0, 16]], 0, 1
        dv(lambda e, d_=d_, pat=pat, base=base, cm=cm: e.affine_select(
            out=maskt[d_], in_=maskt[d_], pattern=pat, compare_op=ALU.is_ge, fill=0.0, base=base, channel_multiplier=cm), "pool")
    dv(lambda e: e.tensor_copy(out=identt, in_=AP(ident.tensor, ident.offset, [list(ident.ap[0]), [0, 4], [1, 128]])), "pool", extra_r=["ident"])

    o_fm = o
    Fm = [V(al(2048), 2048, F32, "p (k s c) -> p k s c", k=16, s=8) for _ in range(2)]
    W2m = [V(al(2048), 2048, F32, "p (k t c) -> p k t c", k=16, t=8) for _ in range(2)]
    W2p = [V(al(2048), 2048, F32, "p (k t c) -> p k t c", k=16, t=8) for _ in range(2)]
    tF = V(al(2048), 2048, F32, "p (k s c) -> p k s c", k=16, s=8)
    T0acc = V(al(4096), 4096, F32, "p (g c) -> p g c", g=32)
    tM = V(al(512), 512)

    def pw(Pt, dk0, j0, jstep):
        b = Pt[:, dk0:dk0 + 16, :]
        return AP(b.tensor, b.offset + j0, [list(b.ap[0]), [16, 16], [jstep, 8], [0, 16]])

    def bx(Bt, dk0):
        b = Bt[:, dk0:dk0 + 16, :]
        return AP(b.tensor, b.offset, [list(b.ap[0]), [16, 16], [0, 8], [1, 16]])

    def outer(o_re, o_im, dk0, j0, jstep, Xre, Xim, neg_im):
        dv(lambda e: e.tensor_tensor(out=o_re, in0=pw(PaR, dk0, j0, jstep), in1=bx(Xre, dk0), op=ALU.mult))
        dv(lambda e: e.tensor_tensor(out=tF, in0=pw(PaI, dk0, j0, jstep), in1=bx(Xim, dk0), op=ALU.mult))
        dv(lambda e: e.tensor_tensor(out=o_re, in0=o_re, in1=tF, op=ALU.subtract))
        dv(lambda e: e.tensor_tensor(out=o_im, in0=pw(PaR, dk0, j0, jstep), in1=bx(Xim, dk0), op=ALU.mult))
        dv(lambda e: e.tensor_tensor(out=tF, in0=pw(PaI, dk0, j0, jstep), in1=bx(Xre, dk0), op=ALU.mult))
        dv(lambda e: e.tensor_tensor(out=o_im, in0=o_im, in1=tF, op=ALU.add))
        if neg_im:
            dv(lambda e: e.tensor_scalar(out=o_im, in0=o_im, scalar1=-1.0, scalar2=None, op0=ALU.mult))

    for d_ in range(2):
        dk0 = d_ * 16
        if d_ == 0:
            outer(Fm[0], Fm[1], dk0, 14, -1, Bbre, Bbim, False)
            outer(W2m[0], W2m[1], dk0, 8, 1, Cre, Cim, True)
            outer(W2p[0], W2p[1], dk0, 0, 1, Cre, Cim, True)
        else:
            outer(Fm[0], Fm[1], dk0, 7, 1, Bbre, Bbim, False)
            outer(W2m[0], W2m[1], dk0, 15, -1, Cre, Cim, True)
            outer(W2p[0], W2p[1], dk0, 7, -1, Cre, Cim, True)
        for r in range(2):
            dv(lambda e, r=r, dk0=dk0: e.tensor_copy(out=W2b[r][:, dk0:dk0 + 16, :].rearrange("p k (c t) -> p k t c", t=8), in_=W2m[r]),
               "pool", extra_w=["W2b"])
        for r in range(2):
            for kb4 in range(4):
                bank = ps[:, kb4 % 2, :]
                for q in range(4):
                    k_ = kb4 * 4 + q
                    P.op("pe", lambda e, r=r, k_=k_, q=q, bank=bank: e.transpose(
                        out=bank[:, q * 128:(q + 1) * 128], in_=Fm[r][:, k_, :, :].rearrange("p s c -> p (s c)"), identity=ident),
                        reads=[L, "ident"], writes=["ps%d" % (kb4 % 2)])
                P.op("act", lambda e, r=r, kb4=kb4, dk0=dk0, bank=bank: e.activation(
                    out=W1b[r][:, dk0 + kb4 * 4:dk0 + kb4 * 4 + 4, :].rearrange("p k c -> p (k c)"), in_=bank, func=AF.Copy),
                    reads=["ps%d" % (kb4 % 2)], writes=["W1b"])
        for gb in range(8):
            mi = gb // 4
            bank = ps[:, 2 + gb % 2, :]
            g0 = 2 * 4 * (gb % 4) + mi
            for q in range(4):
                k_ = 4 * (gb % 4) + q
                for r in range(2):
                    P.op("pe", lambda e, r=r, k_=k_, mi=mi, q=q, bank=bank: e.matmul(
                        bank[:, q * 128:(q + 1) * 128],
                        lhsT=Fm[r][mi * 64:(mi + 1) * 64, k_, :, :].rearrange("p s c -> p (s c)"),
                        rhs=W2p[r][mi * 64:(mi + 1) * 64, k_, :, :].rearrange("p t c -> p (t c)"),
                        start=(r == 0), stop=(r == 1)),
                        reads=[L], writes=["ps%d" % (2 + gb % 2)])
            tb0 = T0acc[:, g0, :]
            acc = AP(tb0.tensor, tb0.offset, [list(tb0.ap[0]), [256, 4], [1, 128]])
            bk = bank.rearrange("p (r c) -> p r c", r=4)
            tM3 = tM.rearrange("p (r c) -> p r c", r=4)
            if d_ == 0:
                P.op("dve", lambda e, acc=acc, bk=bk: e.tensor_tensor(out=acc, in0=bk, in1=maskt[0], op=ALU.mult),
                     reads=["ps%d" % (2 + gb % 2), L], writes=["T0acc"])
            else:
                P.op("dve", lambda e, bk=bk, tM3=tM3: e.tensor_tensor(out=tM3, in0=bk, in1=maskt[1], op=ALU.mult),
                     reads=["ps%d" % (2 + gb % 2), L, "T0acc"], writes=["tM"])
                P.op("dve", lambda e, acc=acc, tM3=tM3: e.tensor_tensor(out=acc, in0=acc, in1=tM3, op=ALU.add),
                     reads=["tM", "T0acc"], writes=["T0acc"])
                dc0 = dcol[:, g0:g0 + 1]
                dcb = AP(dc0.tensor, dc0.offset, [list(dc0.ap[0]), [2, 4], [0, 128]])
                P.op("dve", lambda e, dcb=dcb, tM3=tM3: e.tensor_tensor(out=tM3, in0=identt, in1=dcb, op=ALU.mult),
                     reads=["tM", L], writes=["tM"])
                t0b = T0b[:, g0, :]
                o4 = AP(t0b.tensor, t0b.offset, [list(t0b.ap[0]), [256, 4], [1, 8], [8, 16]])
                a4 = AP(tb0.tensor, tb0.offset, [list(tb0.ap[0]), [256, 4], [16, 8], [1, 16]])
                P.op("dve", lambda e, o4=o4, a4=a4: e.tensor_tensor(out=o4, in0=a4, in1=tM.rearrange("p (r t c) -> p r t c", r=4, t=8), op=ALU.add),
                     reads=["tM", "T0acc"], writes=["T0b", "tM"])
        P.op("dve", lambda e: e.memset(tM[:, 0:1], 0.0), reads=["ps0", "ps1", "ps2", "ps3", "tM", L], writes=[L, "tM"])

    self.P.barrier()
    o = o_fm
    wp = [V(al(64), 64, F32, "p (r k) -> p r k", r=2) for _ in range(10)]
    dv(lambda e: e.tensor_copy(out=wp[0][:, 0, :], in_=w_re))
    dv(lambda e: e.tensor_copy(out=wp[0][:, 1, :], in_=w_im))
    for q in range(1, 10):
        a_, b_ = wp[q - 1][:, 0, :], wp[q - 1][:, 1, :]
        TT(t1, a_, a_, ALU.mult)
        TT(t2, b_, b_, ALU.mult)
        TT(wp[q][:, 0, :], t1, t2, ALU.subtract)
        TT(t1, a_, b_, ALU.mult)
        TS(wp[q][:, 1, :], t1, 2.0, ALU.mult)
    TBL = V(al(8 * 2 * NCH), 8 * 2 * NCH, F32, "p (k r n) -> p k r n", k=8, r=2)
    tq = [V(al(2048), 2048) for _ in range(2)]
    for bt in range(4):
        k0 = bt * 8
        dv(lambda e: e.memset(TBL[:, :, 0, 0:1], 1.0))
        dv(lambda e: e.memset(TBL[:, :, 1, 0:1], 0.0))
        dv(lambda e, k0=k0: e.tensor_copy(out=TBL[:, :, 0, 1], in_=wp[0][:, 0, k0:k0 + 8]))
        dv(lambda e, k0=k0: e.tensor_copy(out=TBL[:, :, 1, 1], in_=wp[0][:, 1, k0:k0 + 8]))
        for q in range(1, 10):
            sz = 2 ** q
            n = min(sz, NCH - sz)
            lo_r, lo_i = TBL[:, :, 0, 0:n], TBL[:, :, 1, 0:n]
            hi_r, hi_i = TBL[:, :, 0, sz:sz + n], TBL[:, :, 1, sz:sz + n]
            wr = bc_free(wp[q][:, 0, k0:k0 + 8], n)
            wi = bc_free(wp[q][:, 1, k0:k0 + 8], n)
            u1 = tq[0][:, 0:8 * n].rearrange("p (k n) -> p k n", k=8)
            u2 = tq[1][:, 0:8 * n].rearrange("p (k n) -> p k n", k=8)
            TT(u1, lo_r, wr, ALU.mult)
            TT(u2, lo_i, wi, ALU.mult)
            TT(hi_r, u1, u2, ALU.subtract)
            TT(u1, lo_r, wi, ALU.mult)
            TT(u2, lo_i, wr, ALU.mult)
            TT(hi_i, u1, u2, ALU.add)
        P.dma("sp", lambda e, k0=k0: e.dma_start(out=tab_d[k0:k0 + 8].rearrange("k p r n -> p k r n"), in_=TBL),
              reads=[L], writes=[("tabd", bt)], sem="st_tab", soft=["taball"])
        dv(lambda e: e.memset(tq[0][:, 0:1], 0.0), extra_r=[("tabd", bt)])
    self.P.barrier()

    o = o_mark
    U = V(al(32 * NCH // 2), 32 * NCH // 2, BF16, "p (g c) -> p g c", g=32)
    Ur = V(al(32 * NCH // 2), 32 * NCH // 2, BF16, "p (g c) -> p g c", g=32)
    tb = [V(al(2 * NCH), 2 * NCH, F32, "p (r n) -> p r n", r=2) for _ in range(2)]
    Gin = V(al(2 * NCH), 2 * NCH, F32, "p (r n) -> p r n", r=2)
    Gs = V(al(2 * NCH), 2 * NCH, F32, "p (r n) -> p r n", r=2)
    tt_ = [V(al(NCH), NCH) for _ in range(4)]
    Hb = [[V(al(256), 256, BF16) for _ in range(2)] for _ in range(2)]
    ysb = [V(al(512), 512) for _ in range(2)]
    assert o <= AW, o
    for s_ in range(8):
        P.dma("sp", lambda e, s_=s_: e.dma_start(out=U[16 * s_:16 * s_ + 16], in_=S["u"][:, :, s_, :].rearrange("g i c -> i g c")),
              reads=["uall"], writes=["U%d" % s_])
    Ures = ["U%d" % s_ for s_ in range(8)]
    Ub = U[:, :, :]
    P.op("pool", lambda e: e.tensor_copy(out=Ur[:, :, 0:32], in_=AP(Ub.tensor, Ub.offset + 31, [list(Ub.ap[0]), [NCH, 32], [-1, 32]])),
         reads=Ures, writes=["Ur0"])
    P.op("pool", lambda e: e.tensor_copy(out=Ur[:, :, 32:NCH], in_=AP(Ub.tensor, Ub.offset + NCH - 1, [list(Ub.ap[0]), [NCH, 32], [-1, 512]])),
         reads=Ures, writes=["Ur1"])
    for k in range(16):
        for d_ in range(2):
            dk = d_ * 16 + k
            T = tb[dk % 2]
            tr = "tb%d" % (dk % 2)
            P.dma("sp", lambda e, T=T, dk=dk: e.dma_start(out=T, in_=tab_d[dk]), reads=["taball"], writes=[tr])
            Us = U if d_ == 0 else Ur
            ures = Ures if d_ == 0 else ["Ur0", "Ur1"]
            for mi in range(2):
                g = 2 * k + mi
                for r in range(2):
                    lh = W1b[r][:, dk, mi * 64:(mi + 1) * 64]
                    P.op("pe", lambda e, lh=lh, r=r, mi=mi, g=g, Us=Us: e.matmul(ps[mi * 64:(mi + 1) * 64, r, :], lhsT=lh, rhs=Us[:, g, 0:512], start=True, stop=True),
                         reads=ures + ["W1b"], writes=["ps%d" % r])
                    P.op("pe", lambda e, lh=lh, r=r, mi=mi, g=g, Us=Us: e.matmul(ps[mi * 64:(mi + 1) * 64, 2, r * 32:(r + 1) * 32], lhsT=lh, rhs=Us[:, g, 512:NCH], start=True, stop=True),
                         reads=ures + ["W1b"], writes=["ps2"])
            for (c0, c1, sre, sim, pr) in ((0, 512, ps[:, 0, :], ps[:, 1, :], ["ps0", "ps1"]), (512, NCH, ps[:, 2, 0:32], ps[:, 2, 32:64], ["ps2"])):
                TRs, TIs = T[:, 0, c0:c1], T[:, 1, c0:c1]
                t1_, t2_, t3_, t4_ = [x[:, c0:c1] for x in tt_]
                P.op("dve", lambda e, t1_=t1_, sre=sre, TRs=TRs: e.tensor_tensor(out=t1_, in0=sre, in1=TRs, op=ALU.mult), reads=pr + [tr], writes=["tt0"])
                P.op("dve", lambda e, t2_=t2_, sim=sim, TIs=TIs: e.tensor_tensor(out=t2_, in0=sim, in1=TIs, op=ALU.mult), reads=pr + [tr], writes=["tt1"])
                P.op("dve", lambda e, t3_=t3_, sim=sim, TRs=TRs: e.tensor_tensor(out=t3_, in0=sim, in1=TRs, op=ALU.mult), reads=pr + [tr], writes=["tt2"])
                P.op("dve", lambda e, t4_=t4_, sre=sre, TIs=TIs: e.tensor_tensor(out=t4_, in0=sre, in1=TIs, op=ALU.mult), reads=pr + [tr], writes=["tt3"])
            P.op("pool", lambda e: e.tensor_tensor(out=Gin[:, 0, :], in0=tt_[0], in1=tt_[1], op=ALU.subtract), reads=["tt0", "tt1"], writes=["Gin0"])
            P.op("pool", lambda e: e.tensor_tensor(out=Gin[:, 1, :], in0=tt_[2], in1=tt_[3], op=ALU.add), reads=["tt2", "tt3"], writes=["Gin1"])
            rb = bc_free(rho8[:, dk:dk + 1], NCH).rearrange("p a n -> p (a n)") if False else AP(rho8.tensor, rho8.offset + dk, [list(rho8.ap[0]), [0, NCH]])
            for r in range(2):
                P.op("dve", lambda e, r=r, rb=rb: e.tensor_tensor_scan(out=Gs[:, r, :], data0=rb, data1=Gin[:, r, :], initial=0.0, op0=ALU.mult, op1=ALU.add),
                     reads=["Gin%d" % r, "s5"], writes=["Gs%d" % r])
            TRs, TIs = T[:, 0, 31:543], T[:, 1, 31:543]
            gr, gi = Gs[:, 0, 31:543], Gs[:, 1, 31:543]
            u1, u2, u3, u4 = [x[:, 0:512] for x in tt_]
            P.op("pool", lambda e, TRs=TRs, gr=gr, u1=u1: e.tensor_tensor(out=u1, in0=TRs, in1=gr, op=ALU.mult), reads=[tr, "Gs0", "Gin0", "Gin1"], writes=["tt0"])
            P.op("pool", lambda e, TIs=TIs, gi=gi, u2=u2: e.tensor_tensor(out=u2, in0=TIs, in1=gi, op=ALU.mult), reads=[tr, "Gs1", "Gin0", "Gin1"], writes=["tt1"])
            P.op("pool", lambda e, TRs=TRs, gi=gi, u3=u3: e.tensor_tensor(out=u3, in0=TRs, in1=gi, op=ALU.mult), reads=[tr, "Gs1", "Gin0", "Gin1"], writes=["tt2"])
            P.op("pool", lambda e, TIs=TIs, gr=gr, u4=u4: e.tensor_tensor(out=u4, in0=TIs, in1=gr, op=ALU.mult), reads=[tr, "Gs0", "Gin0", "Gin1"], writes=["tt3"])
            outs_ = []
            for r in range(2):
                hb = Hb[d_][r]
                if d_ == 0:
                    outs_.append(hb)
                else:
                    outs_.append(AP(hb.tensor, hb.offset + 511, [list(hb.ap[0]), [-1, 512]]))
            P.op("pool", lambda e, o_=outs_[0], u1=u1, u2=u2: e.tensor_tensor(out=o_, in0=u1, in1=u2, op=ALU.add), reads=["tt0", "tt1"], writes=["Hb%d0" % d_])
            P.op("pool", lambda e, o_=outs_[1], u3=u3, u4=u4: e.tensor_tensor(out=o_, in0=u3, in1=u4, op=ALU.subtract), reads=["tt2", "tt3"], writes=["Hb%d1" % d_])
        for mi in range(2):
            g = 2 * k + mi
            yb = ps[:, 4 + g % 2, :]
            yr = "ps%d" % (4 + g % 2)
            P.op("pe", lambda e, g=g, yb=yb: e.matmul(yb, lhsT=T0b[:, g, :], rhs=U[:, g, 32:NCH], start=True, stop=False),
                 reads=Ures + ["T0b"], writes=[yr])
            n_ = 0
            for d_ in range(2):
                for r in range(2):
                    n_ += 1
                    lh = W2b[r][mi * 64:(mi + 1) * 64, d_ * 16 + k, :]
                    rh = Hb[d_][r][mi * 64:(mi + 1) * 64, :]
                    P.op("pe", lambda e, lh=lh, rh=rh, yb=yb, n_=n_: e.matmul(yb, lhsT=lh, rhs=rh, start=False, stop=(n_ == 4)),
                         reads=["W2b", "Hb%d%d" % (d_, r)], writes=[yr])
            yt = ysb[g % 2]
            P.op("act", lambda e, yt=yt, yb=yb: e.activation(out=yt, in_=yb, func=AF.Copy), reads=[yr], writes=["ysb%d" % (g % 2)])
            P.dma("pool", lambda e, yt=yt, g=g: e.dma_start(out=S["y"][g].rearrange("c t n -> (c t) n"), in_=yt),
                  reads=["ysb%d" % (g % 2)], writes=[("yd", g)], sem="st_y%d" % (g % 2), soft=["yall"])
    self.s5 = dict(W1b=W1b, T0b=T0b, W2b=W2b, rho8=rho8, w_re=w_re, w_im=w_im, tab_d=tab_d, o_mark=o_mark, o_end=o, L=L)


KB.phaseS5 = kb_phaseS5


MASK_E = -4000.0


def att_plan():
    sets = {}
    plan = []
    for a in range(32):
        rows = (2 * a, 2 * a + 1)
        r0 = [min(max(r - 4, 0), 56) for r in rows]
        lo, hi = min(r0), max(r0) + 7
        tiles = []
        for kt in range(lo // 2, hi // 2 + 1):
            ent = []
            for il in range(2):
                for rl in range(2):
                    i_abs = 2 * kt + il
                    ok = r0[rl] <= i_abs <= r0[rl] + 7
                    ent.append((i_abs - rows[rl] + 7) if ok else None)
            key = tuple(ent)
            if key not in sets:
                sets[key] = len(sets)
            tiles.append((kt, sets[key]))
        plan.append(tiles)
    return plan, sets


def kb_phaseATT(self, ins):
    nc, P, V = self.nc, self.P, self.V
    ps = self.psum
    if not hasattr(self, "S"):
        self.S = {}
    S = self.S
    for nm, shp in (("q", [4, 128, NLAT]), ("k", [4, 128, NT]), ("v", [NT, 512])):
        if nm + "_d_in" in self.debug:
            S[nm] = self.dram_in(nm + "_d", shp, BF16)
    S["yna"] = self.dram_tmp("yna_d", [4, 128, NLAT], BF16)
    tt_d = self.dram_tmp("tt_d", [8, 15, 64, 64], BF16)
    ttm_d = self.dram_tmp("ttm_d", [64, 8, 64], BF16)
    plan, sets = att_plan()
    NS = len(sets)
    o = PERS

    def al(n):
        nonlocal o
        r = o
        o += n
        return r
    ident, identb, onesb = self.ident, self.identb, self.onesb
    kT = V(al(4 * NT // 2), 4 * NT // 2, BF16, "p (m t) -> p m t", m=4)
    Vt = V(al(34 * 256), 34 * 256, BF16, "p (k f) -> p k f", k=34)
    TB = V(al(NS * 8 * 64), NS * 8 * 64, BF16, "p (s h c) -> p s h c", s=NS, h=8)
    qT = [V(al(256), 256, BF16, "p (m t) -> p m t", m=4) for _ in range(2)]
    PT = [V(al(448), 448, BF16, "p (i t) -> p i t", i=7) for _ in range(2)]
    rc = V(al(128), 128)
    yst = [V(al(256), 256, BF16, "p (m t) -> p m t", m=4) for _ in range(2)]
    o_tmp = o
    for m in range(4):
        P.dma("sp", lambda e, m=m: e.dma_start(out=kT[:, m, :], in_=S["k"][m]), reads=["kall"], writes=["kT%d" % m])
    vv = S["v"].rearrange("(k p) f -> p k f", p=128)
    for i in range(2):
        P.dma("sp", lambda e, i=i: e.dma_start(out=Vt[:, i * 17:(i + 1) * 17, :], in_=vv[:, i * 17:(i + 1) * 17, :]), reads=["vall"], writes=["Vt%d" % i])
    L = "att"
    stg = V(al(32), 32)
    Lm = V(al(120), 120)
    E = V(al(4096), 4096, F32, "p (j c) -> p j c", j=64)
    E2 = V(al(4096), 4096, F32, "p (j c) -> p j c", j=64)
    E3 = V(al(4096), 4096, F32, "p (j c) -> p j c", j=64)
    rsel = V(al(2), 2)
    ttb = [V(al(256), 256, BF16) for _ in range(2)]
    mtile = V(al(256), 256, BF16)
    assert o <= AW, o

    def dv(fn, eng="dve", extra_r=(), extra_w=()):
        P.op(eng, fn, reads=[L] + list(extra_r), writes=[L] + list(extra_w))
    dv(lambda e: e.memset(stg, 1.0), "pool")
    P.dma("sp", lambda e: e.dma_start(out=stg[0:120, 0:31], in_=ins["na_rpb"].rearrange("h a b -> (h a) b")), reads=[L], writes=["attstg"])
    P.op("pe", lambda e: e.transpose(out=ps[0:32, 6, 0:120], in_=stg[0:120, :], identity=ident[0:120, 0:120]), reads=["attstg", L, "ident"], writes=["psT6"])
    P.op("dve", lambda e: e.tensor_copy(out=Lm[0:32, :], in_=ps[0:32, 6, 0:120]), reads=["psT6", L], writes=[L])
    E_, E2_, E3_ = E[0:32], E2[0:32], E3[0:32]
    dv(lambda e: e.iota(E_, pattern=[[1, 64], [-1, 64]], base=15, channel_multiplier=-1, allow_small_or_imprecise_dtypes=True), "pool")
    dv(lambda e: e.tensor_scalar(out=E_, in0=E_, scalar1=0.0, scalar2=None, op0=ALU.is_equal))
    dv(lambda e: e.iota(E2_, pattern=[[0, 64], [1, 64]], base=0, channel_multiplier=0, allow_small_or_imprecise_dtypes=True), "pool")
    dv(lambda e: e.tensor_scalar(out=E2_, in0=E2_, scalar1=-8.0, scalar2=0.0, op0=ALU.add, op1=ALU.max))
    dv(lambda e: e.tensor_scalar(out=E2_, in0=E2_, scalar1=48.0, scalar2=None, op0=ALU.min))
    dv(lambda e: e.iota(E3_, pattern=[[1, 64], [0, 64]], base=0, channel_multiplier=0, allow_small_or_imprecise_dtypes=True), "pool")
    dv(lambda e: e.tensor_tensor(out=E3_, in0=E3_, in1=E2_, op=ALU.subtract))
    dv(lambda e: e.tensor_scalar(out=E2_, in0=E3_, scalar1=0.0, scalar2=None, op0=ALU.is_ge))
    dv(lambda e: e.tensor_scalar(out=E3_, in0=E3_, scalar1=15.0, scalar2=None, op0=ALU.is_le))
    dv(lambda e: e.tensor_tensor(out=E2_, in0=E2_, in1=E3_, op=ALU.mult))
    dv(lambda e: e.tensor_tensor(out=E_, in0=E_, in1=E2_, op=ALU.mult))
    dv(lambda e: e.tensor_scalar(out=E2_, in0=E2_, scalar1=-MASK_E, scalar2=MASK_E, op0=ALU.mult, op1=ALU.add))
    rs = rsel[0:32, :]
    dv(lambda e: e.iota(rs[:, 0:1], pattern=[[0, 1]], base=0, channel_multiplier=1, allow_small_or_imprecise_dtypes=True), "pool")
    dv(lambda e: e.tensor_scalar(out=rs[:, 0:1], in0=rs[:, 0:1], scalar1=31.0, scalar2=None, op0=ALU.is_lt))
    dv(lambda e: e.tensor_scalar(out=rs[:, 1:2], in0=rs[:, 0:1], scalar1=-1.0, scalar2=1.0, op0=ALU.mult, op1=ALU.add))
    dv(lambda e: e.tensor_scalar(out=E_, in0=E_, scalar1=rs[:, 0:1], scalar2=None, op0=ALU.mult))
    dv(lambda e: e.scalar_tensor_tensor(out=E_, in0=E2_, scalar=rs[:, 1:2], in1=E_, op0=ALU.mult, op1=ALU.add))
    Ef = E_.rearrange("p j c -> p (j c)")
    ttv = tt_d.rearrange("h a j c -> (h a) (j c)")
    for i in range(8):
        bank = ps[0:120, i % 2, :]
        P.op("pe", lambda e, i=i, bank=bank: e.matmul(bank, lhsT=Lm[0:32, :], rhs=Ef[:, i * 512:(i + 1) * 512], start=True, stop=True),
             reads=[L], writes=["ps%d" % (i % 2)])
        tb_ = ttb[i % 2][0:120, :]
        P.op("act", lambda e, tb_=tb_, bank=bank: e.activation(out=tb_, in_=bank, func=AF.Identity, scale=8.0), reads=["ps%d" % (i % 2)], writes=["ttb%d" % (i % 2)])
        P.dma("sp", lambda e, i=i, tb_=tb_: e.dma_start(out=ttv[:, i * 512:(i + 1) * 512], in_=tb_), reads=["ttb%d" % (i % 2)], writes=[("ttd", i)],
              sem="st_tt%d" % (i % 2), soft=["ttall"])
    dv(lambda e: e.memset(mtile, 8.0 * MASK_E), "pool")
    msrc = AP(mtile.tensor, mtile.offset, [[mtile.ap[0][0], 64], [1, 256]])
    P.dma("sp", lambda e: e.dma_start(out=ttm_d.rearrange("j h c -> j (h c)")[:, 0:256], in_=msrc),
          reads=[L], writes=["ttm_a"], sem="st_ttm")
    P.dma("sp", lambda e: e.dma_start(out=ttm_d.rearrange("j h c -> j (h c)")[:, 256:512], in_=msrc),
          reads=[L], writes=["ttm_b"], sem="st_ttm2")
    self.P.barrier()
    for key, si in sets.items():
        n_ = 0
        for il in range(2):
            for rl in range(2):
                di = key[il * 2 + rl]
                dst = TB[il * 64:(il + 1) * 64, si, :, rl * 64:(rl + 1) * 64]
                if di is None:
                    src = ttm_d
                else:
                    src = tt_d[:, di, :, :].rearrange("h j c -> j h c")
                P.dma("sp", lambda e, dst=dst, src=src: e.dma_start(out=dst, in_=src), writes=[("TB", si, il, rl)], sem="ld_tb%d" % (n_ % 4), soft=["TBall"])
                n_ += 1
    self.P.barrier()

    SCALE = 0.125
    hcnt = 0
    for a in range(32):
        qt = qT[a % 2]
        qr = "qT%d" % (a % 2)
        P.dma("sp", lambda e, a=a, qt=qt: e.dma_start(out=qt, in_=S["q"][:, :, a * 128:(a + 1) * 128].rearrange("m p t -> p m t")), reads=["qall"], writes=[qr])
        tiles = [(256 + kt * 128, 2 + kt, si) for (kt, si) in plan[a]] + [(0, 0, None), (128, 1, None)]
        nt_ = len(tiles)
        ys = yst[a % 2]
        for m in range(4):
            ob = ps[:, 4 + (a * 4 + m) % 2, :]
            obr = "ps%d" % (4 + (a * 4 + m) % 2)
            for hh in range(2):
                h = 2 * m + hh
                sb_ = (hcnt % 2) * 2
                hcnt += 1
                st2 = ps[:, sb_:sb_ + 2, :].rearrange("p a b -> p (a b)")
                pt_ = PT[hcnt % 2]
                ptr = "PT%d" % (hcnt % 2)
                for i, (tok, vt, si) in enumerate(tiles):
                    bres = "ps%d" % (sb_ + i // 4)
                    o_ap = st2[:, i * 128:(i + 1) * 128]
                    P.op("pe", lambda e, o_ap=o_ap, hh=hh, m=m, tok=tok, qt=qt, si=si: e.matmul(
                        o_ap, lhsT=kT[hh * 64:(hh + 1) * 64, m, tok:tok + 128], rhs=qt[hh * 64:(hh + 1) * 64, m, :],
                        start=True, stop=(si is None), skip_group_check=True), reads=["kT%d" % m, qr], writes=[bres])
                    if si is not None:
                        P.op("pe", lambda e, o_ap=o_ap, si=si, h=h: e.matmul(o_ap, lhsT=identb, rhs=TB[:, si, h, :], start=False, stop=True, skip_group_check=True),
                             reads=["identb"], writes=[bres])
                for bnk in range((nt_ + 3) // 4):
                    c0, c1 = bnk * 4, min(nt_, bnk * 4 + 4)
                    P.op("act", lambda e, c0=c0, c1=c1, st2=st2, pt_=pt_: e.activation(
                        out=pt_[:, c0:c1, :].rearrange("p i t -> p (i t)"), in_=st2[:, c0 * 128:c1 * 128], func=AF.Exp, scale=SCALE),
                        reads=["ps%d" % (sb_ + bnk)], writes=[ptr + "b%d" % bnk])
                for i, (tok, vt, si) in enumerate(tiles):
                    P.op("pe", lambda e, i=i, vt=vt, h=h, hh=hh, ob=ob, pt_=pt_: e.matmul(
                        ob[hh * 64:(hh + 1) * 64, 0:128], lhsT=Vt[:, vt, h * 64:(h + 1) * 64], rhs=pt_[:, i, :], start=(i == 0), stop=(i == nt_ - 1), skip_group_check=True),
                        reads=["Vt0", "Vt1", ptr + "b%d" % (i // 4)], writes=[obr])
                for i, (tok, vt, si) in enumerate(tiles):
                    P.op("pe", lambda e, i=i, hh=hh, ob=ob, pt_=pt_: e.matmul(
                        ob[hh * 64:(hh + 1) * 64, 128:256], lhsT=onesb, rhs=pt_[:, i, :], start=(i == 0), stop=(i == nt_ - 1), skip_group_check=True),
                        reads=["onesb", ptr + "b%d" % (i // 4)], writes=[obr])
            P.op("dve", lambda e, ob=ob: e.reciprocal(out=rc, in_=ob[:, 128:256]), reads=[obr], writes=["rc"])
            P.op("dve", lambda e, ob=ob, ys=ys, m=m: e.tensor_tensor(out=ys[:, m, :], in0=ob[:, 0:128], in1=rc, op=ALU.mult), reads=[obr, "rc"], writes=["yst%d_%d" % (a % 2, m)])
        P.dma("pool", lambda e, a=a, ys=ys: e.dma_start(out=S["yna"][:, :, a * 128:(a + 1) * 128].rearrange("m p t -> p m t"), in_=ys),
              reads=["yst%d_%d" % (a % 2, m) for m in range(4)], writes=[("ynad", a)], sem="st_yna%d" % (a % 2), soft=["ynaall"])


KB.phaseATT = kb_phaseATT


def kb_phaseC(self, ins):
    nc, P, V = self.nc, self.P, self.V
    ps = self.psum
    S, Wd = self.S, self.W
    for nm, shp, dt_ in (("xs1", [8, 128, NLAT], F32), ("yna", [4, 128, NLAT], BF16), ("y", [32, 16, 8, 512], F32)):
        if nm + "_d_in" in self.debug:
            S[nm] = self.dram_in(nm + "_d", shp, dt_)
    o = PERS
    R = {}

    def al(n):
        nonlocal o
        r = o
        o += n
        return r
    xsA = V(al(4096), 4096, F32, "p (k t) -> p k t", k=8)
    xsB = V(al(4096), 4096, F32, "p (k t) -> p k t", k=8)
    h3T = V(al(2048), 2048, BF16, "p (k t) -> p k t", k=8)
    R["actT"] = V(al(5632), 5632, BF16, "p (j t) -> p j t", j=NJ)
    R["sg"] = [V(al(512), 512) for _ in range(2)]
    R["zn"] = [V(al(1024), 1024) for _ in range(2)]
    R["mv"] = V(al(32), 32, F32, "p (s c) -> p s c", c=8)
    R["bst"] = V(al(16), 16)
    R["wup"] = [V(al(1024), 1024, BF16, "p (k c) -> p k c", k=8) for _ in range(4)]
    R["wdn"] = [V(al(1408), 1408, BF16, "p (j c) -> p j c", j=NJ) for _ in range(3)]
    ypre = V(al(2048), 2048, F32, "p (m t) -> p m t", m=4)
    gT = V(al(2048), 2048, F32, "p (m t) -> p m t", m=4)
    gtmp = V(al(2048), 2048, F32, "p (m t) -> p m t", m=4)
    gTb = V(al(1024), 1024, BF16, "p (m t) -> p m t", m=4)
    sig = [V(al(512), 512) for _ in range(2)]
    ymix = V(al(2048), 2048, BF16, "p (k t) -> p k t", k=8)
    wout = [V(al(512), 512, BF16, "p (k c) -> p k c", k=8) for _ in range(8)]
    wglu = [V(al(256), 256, BF16, "p (k c) -> p k c", k=4) for _ in range(4)]
    G2f = V(al(1024), 1024)
    B2f = V(al(1024), 1024)
    ost = [V(al(1024), 1024) for _ in range(2)]
    assert o <= AW, o
    self.wup_cnt = 0
    self.wdn_cnt = 0
    scl = self.scl
    for i in range(8):
        P.dma("sp", lambda e, i=i: e.dma_start(out=wout[i], in_=Wd["out"][i]), reads=["Wall"], writes=["wout%d" % i])
    for i in range(4):
        P.dma("sp", lambda e, i=i: e.dma_start(out=wglu[i], in_=Wd["glu"][i]), reads=["Wall"], writes=["wglu%d" % i])
    lg, lb = ins["ln_g"], ins["ln_b"]
    P.dma("sp", lambda e: e.dma_start(out=G2f, in_=AP(lg.tensor, lg.offset + 16 * 128, [[0, 128], [1, 1024]])), writes=["G2f"])
    P.dma("sp", lambda e: e.dma_start(out=B2f, in_=AP(lb.tensor, lb.offset + 16 * 128, [[0, 128], [1, 1024]])), writes=["B2f"])
    GC = 2.0 * math.sqrt(2.0 / math.pi)
    nblk = NLAT // 512
    if "short" in self.debug:
        nblk = 1
    for blk in range(nblk):
        t0 = blk * 512
        c0 = t0 // 8
        for m in range(4):
            P.dma("sp", lambda e, m=m, c0=c0: e.dma_start(out=ypre[:, m, :].rearrange("p (t n) -> p t n", t=8),
                                                          in_=S["y"][m * 8:(m + 1) * 8, :, :, c0:c0 + 64].rearrange("g c t n -> (g c) t n")),
                  reads=["yall"], writes=["ypre%d" % m])
        for m in range(4):
            xin = ypre[:, m, :].rearrange("p (t n) -> p t n", t=8)
            xo = gT[:, m, :].rearrange("p (n t) -> p t n", t=8)
            tm = gtmp[:, m, :].rearrange("p (n t) -> p t n", t=8)
            P.op("pool", lambda e, xin=xin, xo=xo: e.tensor_copy(out=xo, in_=xin), reads=["ypre%d" % m], writes=["gT%d" % m])
            xt_ = gT[:, m, :]
            tm_ = gtmp[:, m, :]
            P.op("dve", lambda e, xt_=xt_, tm_=tm_: e.tensor_tensor(out=tm_, in0=xt_, in1=xt_, op=ALU.mult), reads=["gT%d" % m], writes=["gtmp%d" % m])
            P.op("dve", lambda e, tm_=tm_: e.tensor_scalar(out=tm_, in0=tm_, scalar1=0.044715, scalar2=1.0, op0=ALU.mult, op1=ALU.add), reads=["gtmp%d" % m], writes=["gtmp%d" % m])
            P.op("dve", lambda e, xt_=xt_, tm_=tm_: e.tensor_tensor(out=tm_, in0=tm_, in1=xt_, op=ALU.mult), reads=["gtmp%d" % m, "gT%d" % m], writes=["gtmp%d" % m])
            P.op("act", lambda e, tm_=tm_: e.activation(out=tm_, in_=tm_, func=AF.Sigmoid, scale=GC), reads=["gtmp%d" % m], writes=["gtmp%d" % m])
            P.op("dve", lambda e, xt_=xt_, tm_=tm_: e.tensor_tensor(out=xt_, in0=xt_, in1=tm_, op=ALU.mult), reads=["gtmp%d" % m, "gT%d" % m], writes=["gT%d" % m])
            P.op("pool", lambda e, xt_=xt_, m=m: e.tensor_copy(out=gTb[:, m, :], in_=xt_), reads=["gT%d" % m], writes=["gTb%d" % m])
        for mo in range(4):
            pb = self.bank(mo % 2)
            for m in range(4):
                P.op("pe", lambda e, mo=mo, m=m, pb=pb: e.matmul(pb, lhsT=wglu[mo][:, m, :], rhs=gTb[:, m, :], start=(m == 0), stop=(m == 3)),
                     reads=["wglu%d" % mo, "gTb%d" % m], writes=["ps%d" % (mo % 2)])
            sg_ = sig[mo % 2]
            P.op("act", lambda e, sg_=sg_, pb=pb, mo=mo: e.activation(out=sg_, in_=pb, func=AF.Sigmoid, bias=self.bgluT[:, mo:mo + 1], scale=1.0),
                 reads=["ps%d" % (mo % 2), "bgluT"], writes=["sig%d" % (mo % 2)])
            P.op("dve", lambda e, sg_=sg_, mo=mo: e.tensor_tensor(out=ymix[:, 4 + mo, :], in0=gT[:, mo, :], in1=sg_, op=ALU.mult),
                 reads=["sig%d" % (mo % 2), "gT%d" % mo], writes=["ymix%d" % (4 + mo)])
        for m in range(4):
            P.dma("sp", lambda e, m=m, t0=t0: e.dma_start(out=ymix[:, m, :], in_=S["yna"][m, :, t0:t0 + 512]), reads=["ynaall"], writes=["ymix%d" % m])
        for kk in range(8):
            P.dma("sp", lambda e, kk=kk, t0=t0: e.dma_start(out=xsA[:, kk, :], in_=S["xs1"][kk, :, t0:t0 + 512]), reads=["xs1all"], writes=["Cxs%d" % kk])
        for i in range(8):
            pb = self.bank(2 + i % 2)
            for kc in range(8):
                P.op("pe", lambda e, i=i, kc=kc, pb=pb: e.matmul(pb, lhsT=wout[i][:, kc, :], rhs=ymix[:, kc, :], start=(kc == 0), stop=(kc == 7)),
                     reads=["wout%d" % i, "ymix%d" % kc], writes=["ps%d" % (2 + i % 2)])
            P.op("dve", lambda e, i=i, pb=pb: e.scalar_tensor_tensor(out=xsA[:, i, :], in0=pb, scalar=scl[:, S_M5, i:i + 1], in1=xsA[:, i, :],
                                                                    op0=ALU.mult, op1=ALU.add),
                 reads=["ps%d" % (2 + i % 2), "Cxs%d" % i, "scal"], writes=["Cxs%d" % i])
        self.ln("C", xsA, 512, R, [(h3T, scl[:, S_G2S, :], scl[:, S_B2S, :], "DhT"), (xsB, scl[:, S_AG1, :], scl[:, S_AB1, :], "Dxs")])
        self.ffn("D", h3T, 512, xsB, Wd["up2"], Wd["dn2"], scl[:, S_HG8, :], R)

        def tok_cb(s, zn, znr, t0=t0):
            ot = ost[s % 2]
            P.op("dve", lambda e, ot=ot, zn=zn: e.tensor_tensor(out=ot, in0=zn, in1=G2f, op=ALU.mult), reads=[znr + "h0", znr + "h1", "G2f"], writes=["ost%d" % (s % 2)])
            P.op("pool", lambda e, ot=ot: e.tensor_tensor(out=ot, in0=ot, in1=B2f, op=ALU.add), reads=["ost%d" % (s % 2), "B2f"], writes=["ost%d" % (s % 2)])
            P.dma("pool", lambda e, ot=ot, s=s, t0=t0: e.dma_start(out=self.out[t0 + s * 128:t0 + (s + 1) * 128, :], in_=ot),
                  reads=["ost%d" % (s % 2)], writes=[("outd", t0, s)], sem="out%d" % (s % 2))
        R["tok_cb"] = tok_cb
        self.ln("D", xsB, 512, R, None)


KB.phaseC = kb_phaseC
```

```python
import contextlib
import math
import numpy as np
import concourse.bass as bass
import concourse.mybir as mybir
from concourse.bass_utils import run_bass_kernel_spmd
from concourse.ap import AP

F32 = mybir.dt.float32
BF16 = mybir.dt.bfloat16
I32 = mybir.dt.int32
AF = mybir.ActivationFunctionType
ALU = mybir.AluOpType
AX = mybir.AxisListType

ENGS = ("pe", "act", "dve", "pool", "sp")


class Op:
    __slots__ = ("eng", "fn", "reads", "writes", "pos", "is_dma", "dsem", "dcount",
                 "waits", "signal", "count", "vc")

    def __init__(self, eng, fn, reads, writes, is_dma=False, dsem=None):
        self.eng = eng
        self.fn = fn
        self.reads = tuple(reads)
        self.writes = tuple(writes)
        self.is_dma = is_dma
        self.dsem = dsem
        self.dcount = 0
        self.waits = {}
        self.signal = False
        self.count = 0
        self.vc = None


class Prog:
    def __init__(self):
        self.streams = {e: [] for e in ENGS}
        self.last_writer = {}
        self.readers = {}
        self.dma_counts = {}
        self.key2phys = {}
        self.free_phys = []
        self.free_sw = []
        self.sw_phys = set()
        self.nphys = 0
        self.final_waits = []

    def op(self, eng, fn, reads=(), writes=()):
        ex = [r for r in reads if isinstance(r, str) and r.startswith("ps") and r not in writes]
        if ex:
            writes = list(writes) + ex
        o = Op(eng, fn, reads, writes)
        self._add(o, ())
        return o

    def dma(self, eng, fn, reads=(), writes=(), sem=None, soft=()):
        if sem is None:
            sem = "d_" + str(writes[0])
        if sem not in self.key2phys:
            fl = self.free_sw if eng == "pool" else self.free_phys
            if fl:
                self.key2phys[sem] = fl.pop(0)
            else:
                self.key2phys[sem] = self.nphys
                self.nphys += 1
                if eng == "pool":
                    self.sw_phys.add(self.key2phys[sem])
        sem = self.key2phys[sem]
        assert (sem in self.sw_phys) == (eng == "pool"), "semaphore shared between SW and HW DGE"
        o = Op(eng, fn, reads, writes, is_dma=True, dsem=sem)
        self.dma_counts[sem] = self.dma_counts.get(sem, 0) + 16
        o.dcount = self.dma_counts[sem]
        self._add(o, soft)
        return o

    @staticmethod
    def _tok(o):
        if o.is_dma:
            return (o.dsem, o.dcount)
        return (o.eng, o.pos)

    def _add(self, o, soft):
        st = self.streams[o.eng]
        o.pos = len(st) + 1
        st.append(o)
        vc = dict(st[-2].vc) if len(st) > 1 else {}
        deps = []
        for r in o.reads:
            w = self.last_writer.get(r)
            if w is not None:
                deps.append(w)
        for w_ in o.writes:
            w = self.last_writer.get(w_)
            if w is not None:
                deps.append(w)
            deps.extend(self.readers.get(w_, ()))
        deps.sort(key=lambda d_: -self._tok(d_)[1])
        for d in deps:
            if d is o:
                continue
            k, v = self._tok(d)
            if (not d.is_dma) and d.eng == o.eng and o.eng == "pe":
                continue
            if vc.get(k, 0) >= v:
                continue
            if o.waits.get(k, 0) < v:
                o.waits[k] = v
            if not d.is_dma:
                d.signal = True
            for kk, vv in d.vc.items():
                if vc.get(kk, 0) < vv:
                    vc[kk] = vv
            if vc.get(k, 0) < v:
                vc[k] = v
        o.vc = vc
        for r in o.reads:
            self.readers.setdefault(r, []).append(o)
        for w_ in o.writes:
            self.last_writer[w_] = o
            self.readers[w_] = []
        for w_ in soft:
            self.last_writer[w_] = o

    def barrier(self):
        lasts = []
        for e in ENGS:
            if e == "sp":
                continue
            st = self.streams[e]
            for o_ in reversed(st):
                if not o_.is_dma:
                    lasts.append(o_)
                    break
        dmas = dict(self.dma_counts)
        for e in ENGS:
            o = Op(e, None, (), ())
            st = self.streams[e]
            o.pos = len(st) + 1
            vc = dict(st[-1].vc) if st else {}
            for d in lasts:
                if d.eng == e:
                    if e == "pe":
                        continue
                k, v = self._tok(d)
                if vc.get(k, 0) < v:
                    o.waits[k] = v
                    d.signal = True
                    vc[k] = v
            for k, v in dmas.items():
                if vc.get(k, 0) < v:
                    o.waits[k] = v
                    vc[k] = v
            o.vc = vc
            st.append(o)
        self.last_writer = {}
        self.readers = {}
        allp = set(self.free_phys) | set(self.free_sw) | set(self.key2phys.values())
        self.free_phys = sorted(p for p in allp if p not in self.sw_phys)
        self.free_sw = sorted(p for p in allp if p in self.sw_phys)
        self.key2phys = {}

    def finish_wait(self, eng, sem_keys):
        self.final_waits.append((eng, list(sem_keys)))

    def emit(self, nc):
        for e in ENGS:
            c = 0
            for o in self.streams[e]:
                if o.signal and not o.is_dma:
                    c += 1
                o.count = c
        pos2count = {e: {o.pos: o.count for o in self.streams[e]} for e in ENGS}
        with contextlib.ExitStack() as es:
            sems = {}
            for e in ENGS:
                sems[e] = es.enter_context(nc.semaphore("s_" + e))
            for k in range(self.nphys):
                sems[k] = es.enter_context(nc.semaphore("q%d" % k))
            block = es.enter_context(nc.Block())
            engmap = {"pe": "tensor", "act": "scalar", "dve": "vector", "pool": "gpsimd",
                      "sp": "sync"}

            def make(e):
                def body(engine):
                    for o in self.streams[e]:
                        for k, v in o.waits.items():
                            if k in ENGS:
                                engine.wait_ge(sems[k], pos2count[k][v])
                            else:
                                engine.wait_ge(sems[k], v)
                        if o.fn is None:
                            if o.signal:
                                engine.nop().then_inc(sems[e], 1)
                            continue
                        ins = o.fn(engine)
                        if o.is_dma:
                            ins.then_inc(sems[o.dsem], 16)
                        elif o.signal:
                            ins.then_inc(sems[e], 1)
                    for (fe, keys) in self.final_waits:
                        if fe == e:
                            for k in keys:
                                engine.wait_ge(sems[k], self.dma_counts[k])
                return body

            for e in ENGS:
                if self.streams[e] or any(fe == e for fe, _ in self.final_waits):
                    getattr(block, engmap[e])(make(e))


D = 1024
NLAT = 4096
NCTX = 256
NT = NLAT + NCTX
DFF = 2816
NJ = DFF // 128
NCH = NT // 8
ALPHA = 2.0 ** 0.25
EPS = 1e-6
AW = 51200
MASKV = -240000.0
TWO_PI = 2.0 * math.pi


class KB:
    def __init__(self, debug=None):
        self.debug = debug or ()
        self.nc = bass.Bass("TRN2", target_bir_lowering=False)
        self.P = Prog()
        self.es = contextlib.ExitStack()
        self.uid = 0

    def dram_in(self, name, shape, dt=F32):
        return self.nc.dram_tensor(name, list(shape), dt, kind="ExternalInput").ap()

    def dram_out(self, name, shape, dt=F32):
        return self.nc.dram_tensor(name, list(shape), dt, kind="ExternalOutput").ap()

    def dram_tmp(self, name, shape, dt=F32):
        if name in self.debug:
            return self.nc.dram_tensor(name, list(shape), dt, kind="ExternalOutput").ap()
        return self.nc.dram_tensor(name, list(shape), dt).ap()

    def V(self, off, n, dt=F32, pat=None, **kw):
        v = self.arena[:, off:off + n]
        if dt != F32:
            v = v.bitcast(dt)
        if pat is not None:
            v = v.rearrange(pat, **kw)
        return v

    def name(self, s):
        self.uid += 1
        return "%s%d" % (s, self.uid)


def bc_free(ap2, n):
    return AP(ap2.tensor, ap2.offset, [list(a) for a in ap2.ap] + [[0, n]])


def mk_ap(base, dims, off=0):
    return AP(base.tensor, base.offset + off, [list(base.ap[0])] + [list(d) for d in dims])


O_IDENT = 0
O_IDENTB = 128
O_ONESB = 192
O_MODT = 224
O_BADA = 368
O_LNG = 440
O_LNB = 464
O_BGLU = 488
O_M05 = 496
O_ST = 500
O_SCL = 520
O_SCC = 648
PERS = 1024
S_SC1P, S_SH0, S_HG2, S_G1S, S_B1S, S_AG0, S_AB0, S_M5, S_G2S, S_B2S, S_AG1, S_AB1, S_HG8, S_G2, S_B2 = range(15)


def kb_setup(self, ins):
    nc, P = self.nc, self.P
    self.arena = self.es.enter_context(nc.sbuf_tensor("arena", [128, AW], F32))
    self.psum = self.es.enter_context(nc.psum_tensor("psum", [128, 8, 512], F32))
    ps = self.psum
    V = self.V
    self.ident = ident = V(O_IDENT, 128)
    self.identb = identb = V(O_IDENTB, 64, BF16)
    self.onesb = onesb = V(O_ONESB, 32, BF16)
    self.modT = modT = V(O_MODT, 144, F32, "p (c v) -> p c v", v=2)
    badaT = V(O_BADA, 72)
    self.lngT = lngT = V(O_LNG, 24)
    self.lnbT = lnbT = V(O_LNB, 24)
    self.bgluT = bgluT = V(O_BGLU, 4)
    self.m05 = m05 = V(O_M05, 1)
    sT = V(O_ST, 16)
    self.scl = scl = V(O_SCL, 128, F32, "p (s k) -> p s k", k=8)
    self.scc = scc = V(O_SCC, 64, F32, "p (s k) -> p s k", k=8)

    P.op("pool", lambda e: e.memset(ident, 0.0), writes=["ident"])
    P.op("pool", lambda e: e.affine_select(out=ident, in_=ident, pattern=[[-1, 128]],
                                           compare_op=ALU.not_equal, fill=1.0, base=0,
                                           channel_multiplier=1), reads=["ident"], writes=["ident"])
    P.op("pool", lambda e: e.tensor_copy(out=identb, in_=ident), reads=["ident"], writes=["identb"])
    P.op("pool", lambda e: e.memset(onesb, 1.0), writes=["onesb"])
    P.op("pool", lambda e: e.memset(m05, -0.5), writes=["m05"])

    T0 = PERS
    st1 = V(T0, 128)
    st2 = V(T0 + 128, 128)
    P.dma("sp", lambda e: e.dma_start(out=st1[0:72, :], in_=ins["b_ada"]), writes=["st1"])
    P.dma("sp", lambda e: e.dma_start(out=st2[0:24, :], in_=ins["ln_g"]), writes=["st2a"])
    P.dma("sp", lambda e: e.dma_start(out=st2[24:48, :], in_=ins["ln_b"]), writes=["st2b"])
    P.dma("sp", lambda e: e.dma_start(out=st2[48:52, :], in_=ins["s5_b_glu"]), writes=["st2c"])
    P.dma("sp", lambda e: e.dma_start(out=st2[52:60, :], in_=ins["c"]), writes=["st2d"])
    P.dma("sp", lambda e: e.dma_start(out=st2[60:68, :], in_=ins["c_ctx"]), writes=["st2e"])
    pt = ps[:, 6, 0:256]
    P.op("pe", lambda e: e.transpose(out=pt[:, 0:72], in_=st1[0:72, :], identity=ident[0:72, 0:72]),
         reads=["st1", "ident"], writes=["psT6"])
    P.op("pe", lambda e: e.transpose(out=pt[:, 128:196], in_=st2[0:68, :], identity=ident[0:68, 0:68]),
         reads=["st2a", "st2b", "st2c", "st2d", "st2e", "ident"], writes=["psT6"])
    P.op("dve", lambda e: e.tensor_copy(out=badaT, in_=pt[:, 0:72]), reads=["psT6"], writes=["badaT"])
    P.op("dve", lambda e: e.tensor_copy(out=lngT, in_=pt[:, 128:152]), reads=["psT6"], writes=["lngT"])
    P.op("dve", lambda e: e.tensor_copy(out=lnbT, in_=pt[:, 152:176]), reads=["psT6"], writes=["lnbT"])
    P.op("dve", lambda e: e.tensor_copy(out=bgluT, in_=pt[:, 176:180]), reads=["psT6"], writes=["bgluT"])
    P.op("act", lambda e: e.activation(out=sT, in_=pt[:, 180:196], func=AF.Silu), reads=["psT6"], writes=["sT"])

    wb = [V(T0 + 256 + i * 4096, 4096, F32, "p (k c) -> p k c", k=8) for i in range(2)]
    w_ada = ins["w_ada"].rearrange("(k p) c -> p k c", p=128)
    modps = ps[:, 0, 0:144]
    for bi in range(18):
        t = wb[bi % 2]
        P.dma("sp", lambda e, t=t, bi=bi: e.dma_start(out=t, in_=w_ada[:, :, bi * 512:(bi + 1) * 512]),
              writes=["wada%d" % (bi % 2)])
        for fc in range(4):
            col = (bi * 4 + fc) * 2
            for kk in range(8):
                rhs = mk_ap(sT, [[8, 2]], off=kk)
                P.op("pe", lambda e, t=t, fc=fc, kk=kk, col=col, rhs=rhs: e.matmul(
                    modps[:, col:col + 2], lhsT=t[:, kk, fc * 128:(fc + 1) * 128], rhs=rhs,
                    start=(kk == 0), stop=(kk == 7)),
                    reads=["wada%d" % (bi % 2), "sT"], writes=["ps0"])
    P.op("dve", lambda e: e.tensor_tensor(out=modT, in0=modps.rearrange("p (c v) -> p c v", v=2),
                                          in1=bc_free(badaT, 2), op=ALU.add),
         reads=["ps0", "badaT"], writes=["modT"])

    def m(i, v):
        return modT[:, 8 * i:8 * i + 8, v]

    def G(i):
        return lngT[:, 8 * i:8 * i + 8]

    def B(i):
        return lnbT[:, 8 * i:8 * i + 8]

    def dv(fn, reads=("modT", "lngT", "lnbT", "scal")):
        P.op("dve", fn, reads=list(reads), writes=["scal"])

    for v, sc in ((0, scl), (1, scc)):
        dv(lambda e, sc=sc, v=v: e.tensor_scalar(out=sc[:, S_SC1P, :], in0=m(1, v), scalar1=1.0, scalar2=None, op0=ALU.add))
        dv(lambda e, sc=sc, v=v: e.tensor_copy(out=sc[:, S_SH0, :], in_=m(0, v)))
        dv(lambda e, sc=sc, v=v: e.tensor_scalar(out=sc[:, S_HG2, :], in0=m(2, v), scalar1=0.5, scalar2=None, op0=ALU.mult))
        dv(lambda e, sc=sc, v=v: e.tensor_scalar(out=sc[:, S_G1S, :], in0=m(4, v), scalar1=1.0, scalar2=None, op0=ALU.add))
        dv(lambda e, sc=sc, v=v: e.tensor_tensor(out=sc[:, S_B1S, :], in0=sc[:, S_G1S, :], in1=B(0), op=ALU.mult))
        dv(lambda e, sc=sc, v=v: e.tensor_tensor(out=sc[:, S_B1S, :], in0=sc[:, S_B1S, :], in1=m(3, v), op=ALU.add))
        dv(lambda e, sc=sc, v=v: e.tensor_tensor(out=sc[:, S_G1S, :], in0=sc[:, S_G1S, :], in1=G(0), op=ALU.mult))
    sc = scl
    dv(lambda e: e.tensor_scalar(out=sc[:, S_AG0, :], in0=G(0), scalar1=ALPHA, scalar2=None, op0=ALU.mult))
    dv(lambda e: e.tensor_scalar(out=sc[:, S_AB0, :], in0=B(0), scalar1=ALPHA, scalar2=None, op0=ALU.mult))
    dv(lambda e: e.tensor_copy(out=sc[:, S_M5, :], in_=m(5, 0)))
    dv(lambda e: e.tensor_scalar(out=sc[:, S_G2S, :], in0=m(7, 0), scalar1=1.0, scalar2=None, op0=ALU.add))
    dv(lambda e: e.tensor_tensor(out=sc[:, S_B2S, :], in0=sc[:, S_G2S, :], in1=B(1), op=ALU.mult))
    dv(lambda e: e.tensor_tensor(out=sc[:, S_B2S, :], in0=sc[:, S_B2S, :], in1=m(6, 0), op=ALU.add))
    dv(lambda e: e.tensor_tensor(out=sc[:, S_G2S, :], in0=sc[:, S_G2S, :], in1=G(1), op=ALU.mult))
    dv(lambda e: e.tensor_scalar(out=sc[:, S_AG1, :], in0=G(1), scalar1=ALPHA, scalar2=None, op0=ALU.mult))
    dv(lambda e: e.tensor_scalar(out=sc[:, S_AB1, :], in0=B(1), scalar1=ALPHA, scalar2=None, op0=ALU.mult))
    dv(lambda e: e.tensor_scalar(out=sc[:, S_HG8, :], in0=m(8, 0), scalar1=0.5, scalar2=None, op0=ALU.mult))
    dv(lambda e: e.tensor_copy(out=sc[:, S_G2, :], in_=G(2)))
    dv(lambda e: e.tensor_copy(out=sc[:, S_B2, :], in_=B(2)))


KB.setup = kb_setup


def kb_convert(self, ins):
    nc, P, V = self.nc, self.P, self.V
    W = self.W = {}
    W["up1"] = self.dram_tmp("wup1_d", [NJ, 128, 8, 256], BF16)
    W["up2"] = self.dram_tmp("wup2_d", [NJ, 128, 8, 256], BF16)
    W["dn1"] = self.dram_tmp("wdn1_d", [8, 128, NJ, 128], BF16)
    W["dn2"] = self.dram_tmp("wdn2_d", [8, 128, NJ, 128], BF16)
    W["in"] = self.dram_tmp("win_d", [16, 128, 8, 128], BF16)
    W["out"] = self.dram_tmp("wout_d", [8, 128, 8, 128], BF16)
    W["glu"] = self.dram_tmp("wglu_d", [4, 128, 4, 128], BF16)
    if "noconv" in self.debug:
        return
    T0 = getattr(self, "conv_off", PERS)
    CMAX = 2816
    assert T0 + 3 * CMAX <= AW
    srcb = [V(T0 + i * CMAX, CMAX) for i in range(2)]
    dstb = [V(T0 + 2 * CMAX + i * (CMAX // 2), CMAX // 2, BF16) for i in range(2)]
    self.cv_n = 0
    engs = ["act", "act", "act"]
    if "nopool" in self.debug:
        engs = ["act", "dve", "dve"]

    def conv(src, nrt, C, dst, mode):
        for rt in range(nrt):
            n = self.cv_n
            self.cv_n += 1
            sb = srcb[n % 2][:, 0:C]
            db = dstb[n % 2][:, 0:C]
            P.dma("sp", lambda e, sb=sb, rt=rt: e.dma_start(out=sb, in_=src[rt * 128:(rt + 1) * 128, :]),
                  writes=["cvs%d" % (n % 2)])
            eng = engs[n % 3]
            i_ap, o_ap = sb, db
            if eng == "act":
                P.op("act", lambda e, i_ap=i_ap, o_ap=o_ap: e.activation(out=o_ap, in_=i_ap, func=AF.Copy),
                     reads=["cvs%d" % (n % 2)], writes=["cvd%d" % (n % 2)])
            else:
                P.op(eng, lambda e, i_ap=i_ap, o_ap=o_ap: e.tensor_copy(out=o_ap, in_=i_ap),
                     reads=["cvs%d" % (n % 2)], writes=["cvd%d" % (n % 2)])
            cw = 128
            if mode == "upa":
                d_ap = dst[:, :, rt, 0:128].rearrange("m p c -> p m c")
            elif mode == "upg":
                d_ap = dst[:, :, rt, 128:256].rearrange("m p c -> p m c")
            else:
                d_ap = dst[:, :, rt, :].rearrange("m p c -> p m c")
            s_ap = db.rearrange("p (m c) -> p m c", c=cw)
            P.dma("sp", lambda e, d_ap=d_ap, s_ap=s_ap: e.dma_start(out=d_ap, in_=s_ap),
                  reads=["cvd%d" % (n % 2)], writes=[("W", id(dst), n)], sem="wconv%d" % (n % 2), soft=["Wall"])

    conv(ins["ffn1_w_up"][:, 0:DFF], 8, DFF, W["up1"], "upa")
    conv(ins["ffn1_w_up"][:, DFF:2 * DFF], 8, DFF, W["up1"], "upg")
    if "cv1" in self.debug:
        return
    conv(ins["ffn1_w_down"], NJ, 1024, W["dn1"], "plain")
    if "cv2" in self.debug:
        return
    conv(ins["w_in"], 8, 2048, W["in"], "plain")
    if "cv3" in self.debug:
        return
    conv(ins["w_out"], 8, 1024, W["out"], "plain")
    if "cv4" in self.debug:
        return
    conv(ins["s5_w_glu"], 4, 512, W["glu"], "plain")
    if "cv5" in self.debug:
        return
    conv(ins["ffn2_w_up"][:, 0:DFF], 8, DFF, W["up2"], "upa")
    conv(ins["ffn2_w_up"][:, DFF:2 * DFF], 8, DFF, W["up2"], "upg")
    conv(ins["ffn2_w_down"], NJ, 1024, W["dn2"], "plain")


KB.convert = kb_convert


def kb_bank(self, b, n=512):
    return self.psum[:, b, 0:n]


def kb_ffn(self, key, hT, nb, xsT, wup_d, wdn_d, hg, R):
    P = self.P
    actT = R["actT"]
    for j in range(NJ):
        slot = self.wup_cnt % 4
        self.wup_cnt += 1
        wt = R["wup"][slot]
        P.dma("sp", lambda e, wt=wt, j=j: e.dma_start(out=wt, in_=wup_d[j]), reads=["Wall"],
              writes=["wup%d" % slot])
        pa = self.bank(j % 2, nb)
        pg = self.bank(2 + j % 2, nb)
        for kk in range(8):
            P.op("pe", lambda e, wt=wt, kk=kk, pa=pa: e.matmul(pa, lhsT=wt[:, kk, 0:128], rhs=hT[:, kk, :],
                                                              start=(kk == 0), stop=(kk == 7)),
                 reads=["wup%d" % slot, key + "hT%d" % kk], writes=["ps%d" % (j % 2)])
        for kk in range(8):
            P.op("pe", lambda e, wt=wt, kk=kk, pg=pg: e.matmul(pg, lhsT=wt[:, kk, 128:256], rhs=hT[:, kk, :],
                                                              start=(kk == 0), stop=(kk == 7)),
                 reads=["wup%d" % slot, key + "hT%d" % kk], writes=["ps%d" % (2 + j % 2)])
        sgb = R["sg"][j % 2][:, 0:nb]
        P.op("act", lambda e, sgb=sgb, pg=pg: e.activation(out=sgb, in_=pg, func=AF.Silu),
             reads=["ps%d" % (2 + j % 2)], writes=["sg%d" % (j % 2)])
        P.op("dve", lambda e, sgb=sgb, pa=pa, j=j: e.tensor_tensor(out=actT[:, j, 0:nb], in0=pa, in1=sgb, op=ALU.mult),
             reads=["ps%d" % (j % 2), "sg%d" % (j % 2)], writes=["actT%d" % j])
    for i in range(8):
        slot = self.wdn_cnt % 3
        self.wdn_cnt += 1
        wd = R["wdn"][slot]
        P.dma("sp", lambda e, wd=wd, i=i: e.dma_start(out=wd, in_=wdn_d[i]), reads=["Wall"],
              writes=["wdn%d" % slot])
        po = self.bank(4 + i % 2, nb)
        for j in range(NJ):
            P.op("pe", lambda e, wd=wd, j=j, po=po: e.matmul(po, lhsT=wd[:, j, :], rhs=actT[:, j, 0:nb],
                                                            start=(j == 0), stop=(j == NJ - 1)),
                 reads=["wdn%d" % slot, "actT%d" % j], writes=["ps%d" % (4 + i % 2)])
        P.op("dve", lambda e, po=po, i=i: e.scalar_tensor_tensor(out=xsT[:, i, :], in0=po, scalar=hg[:, i:i + 1],
                                                                in1=xsT[:, i, :], op0=ALU.mult, op1=ALU.add),
             reads=["ps%d" % (4 + i % 2), key + "xs%d" % i, "scal"], writes=[key + "xs%d" % i])


def kb_ln(self, key, zT, nb, R, outs):
    P = self.P
    S = nb // 128
    psT = self.psum[:, 6:8, :].rearrange("p a b -> p (a b)")
    mv = R["mv"]
    for s in range(S):
        zn = R["zn"][s % 2]
        znr = "zn%d" % (s % 2)
        for kk in range(8):
            P.op("pe", lambda e, kk=kk, s=s: e.transpose(out=psT[:, kk * 128:(kk + 1) * 128],
                                                         in_=zT[:, kk, s * 128:(s + 1) * 128], identity=self.ident),
                 reads=[key + "xs%d" % kk, "ident"], writes=["psT%d" % (6 + kk // 4)])
        st = R["bst"]
        P.op("dve", lambda e, st=st: e.bn_stats(out=st[:, 0:6], in_=psT[:, 0:512]), reads=["psT6"], writes=["bst"])
        P.op("dve", lambda e, st=st: e.bn_stats(out=st[:, 6:12], in_=psT[:, 512:1024]), reads=["psT7"], writes=["bst"])
        P.op("dve", lambda e, st=st, s=s: e.bn_aggr(out=mv[:, s, 0:2], in_=st[:, 0:12]), reads=["bst"], writes=["mv"])
        P.op("dve", lambda e, s=s: e.tensor_scalar(out=mv[:, s, 2:3], in0=mv[:, s, 1:2], scalar1=EPS, scalar2=None, op0=ALU.add),
             reads=["mv"], writes=["mv"])
        P.op("pool", lambda e, s=s: e.tensor_tensor(out=mv[:, s, 3:4], in0=mv[:, s, 2:3], in1=self.m05, op=ALU.pow),
             reads=["mv", "m05"], writes=["mv"])
        P.op("dve", lambda e, s=s: e.tensor_scalar(out=mv[:, s, 4:5], in0=mv[:, s, 0:1], scalar1=mv[:, s, 3:4], scalar2=-1.0,
                                                   op0=ALU.mult, op1=ALU.mult), reads=["mv"], writes=["mv"])
        for hb in range(2):
            P.op("act", lambda e, s=s, zn=zn, hb=hb: e.activation(out=zn[:, hb * 512:(hb + 1) * 512], in_=psT[:, hb * 512:(hb + 1) * 512],
                                                                 func=AF.Identity, scale=mv[:, s, 3:4], bias=mv[:, s, 4:5]),
                 reads=["psT%d" % (6 + hb), "mv"], writes=[znr + "h%d" % hb])
        if outs is None:
            R["tok_cb"](s, zn, znr)
            continue
        for kk in range(8):
            P.op("pe", lambda e, kk=kk, zn=zn: e.transpose(out=psT[:, kk * 128:(kk + 1) * 128],
                                                           in_=zn[:, kk * 128:(kk + 1) * 128], identity=self.ident),
                 reads=[znr + "h%d" % (kk // 4), "ident"], writes=["psT%d" % (6 + kk // 4)])
        for (dst, gcol, bcol, wres) in outs:
            for kk in range(8):
                o_ap = dst[:, kk, s * 128:(s + 1) * 128]
                i_ap = psT[:, kk * 128:(kk + 1) * 128]
                pr = "psT%d" % (6 + kk // 4)
                g1 = gcol[:, kk:kk + 1]
                b1 = bcol[:, kk:kk + 1]
                if kk < 4:
                    P.op("act", lambda e, o_ap=o_ap, i_ap=i_ap, g1=g1, b1=b1: e.activation(
                        out=o_ap, in_=i_ap, func=AF.Identity, scale=g1, bias=b1),
                        reads=[pr, "scal"], writes=[wres + "%d" % kk])
                else:
                    P.op("dve", lambda e, o_ap=o_ap, i_ap=i_ap, g1=g1, b1=b1: e.tensor_scalar(
                        out=o_ap, in0=i_ap, scalar1=g1, scalar2=b1, op0=ALU.mult, op1=ALU.add),
                        reads=[pr, "scal"], writes=[wres + "%d" % kk])


KB.bank = kb_bank
KB.ffn = kb_ffn
KB.ln = kb_ln


def kb_phaseA(self, ins):
    nc, P, V = self.nc, self.P, self.V
    ps = self.psum
    S = self.S = {}
    S["xs1"] = self.dram_tmp("xs1_d", [8, 128, NLAT], F32)
    S["q"] = self.dram_tmp("q_d", [4, 128, NLAT], BF16)
    S["k"] = self.dram_tmp("k_d", [4, 128, NT], BF16)
    S["v"] = self.dram_tmp("v_d", [NT, 512], BF16)
    S["u"] = self.dram_tmp("u_d", [32, 16, 8, NCH], BF16)
    o = self.s5["o_mark"] if hasattr(self, "s5") else PERS
    R = {}

    def al(n):
        nonlocal o
        r = o
        o += n
        return r
    xtok = [V(al(1024), 1024) for _ in range(2)]
    xsT = V(al(4096), 4096, F32, "p (k t) -> p k t", k=8)
    hT = V(al(2048), 2048, BF16, "p (k t) -> p k t", k=8)
    R["actT"] = V(al(5632), 5632, BF16, "p (j t) -> p j t", j=NJ)
    R["sg"] = [V(al(512), 512) for _ in range(2)]
    R["zn"] = [V(al(1024), 1024) for _ in range(2)]
    xs1st = V(al(4096), 4096, F32, "p (k t) -> p k t", k=8)
    h2T = V(al(2048), 2048, BF16, "p (k t) -> p k t", k=8)
    qkst = V(al(2048), 2048, BF16, "p (m t) -> p m t", m=8)
    vst = [V(al(256), 256, BF16) for _ in range(2)]
    ust = V(al(1024), 1024, BF16, "p (m t) -> p m t", m=4)
    R["mv"] = V(al(32), 32, F32, "p (s c) -> p s c", c=8)
    R["bst"] = V(al(16), 16)
    R["wup"] = [V(al(1024), 1024, BF16, "p (k c) -> p k c", k=8) for _ in range(4)]
    R["wdn"] = [V(al(1408), 1408, BF16, "p (j c) -> p j c", j=NJ) for _ in range(3)]
    wint = [V(al(512), 512, BF16, "p (k c) -> p k c", k=8) for _ in range(4)]
    wv = V(al(2048), 2048, BF16, "p (k m c) -> p k m c", k=8, m=4)
    assert o <= AW, o
    self.wup_cnt = 0
    self.wdn_cnt = 0
    win_cnt = 0
    psT = ps[:, 6:8, :].rearrange("p a b -> p (a b)")
    Wd = self.W
    for m in range(4):
        P.dma("sp", lambda e, m=m: e.dma_start(out=wv[:, :, m, :], in_=Wd["in"][8 + m]), reads=["Wall"], writes=["wv%d" % m])

    if "A0" in self.debug:
        return
    blocks = [(0, NCTX, True)] + [(NCTX + i * 512, 512, False) for i in range(NLAT // 512)]
    if "short" in self.debug:
        blocks = blocks[:2]
    for (t0, nb, isctx) in blocks:
        sc = self.scc if isctx else self.scl
        Sn = nb // 128
        src = ins["ctx"] if isctx else ins["x"]
        r0 = t0 if isctx else t0 - NCTX
        for s in range(Sn):
            xt = xtok[s % 2]
            x_ap = src[r0 + s * 128:r0 + (s + 1) * 128, :]
            P.dma("sp", lambda e, xt=xt, x_ap=x_ap: e.dma_start(out=xt, in_=x_ap),
                  writes=["xtok%d" % (s % 2)])
            if "L1" in self.debug:
                continue
            for kk in range(8):
                P.op("pe", lambda e, kk=kk, xt=xt: e.transpose(out=psT[:, kk * 128:(kk + 1) * 128],
                                                               in_=xt[:, kk * 128:(kk + 1) * 128], identity=self.ident),
                     reads=["xtok%d" % (s % 2), "ident"], writes=["psT%d" % (6 + kk // 4)])
            if "L2" in self.debug:
                continue
            for kk in range(8):
                i_ap = psT[:, kk * 128:(kk + 1) * 128]
                pr = "psT%d" % (6 + kk // 4)
                o1 = xsT[:, kk, s * 128:(s + 1) * 128]
                o2 = hT[:, kk, s * 128:(s + 1) * 128]
                s1 = sc[:, S_SC1P, kk:kk + 1]
                s2 = sc[:, S_SH0, kk:kk + 1]
                if kk < 4:
                    P.op("act", lambda e, i_ap=i_ap, o1=o1: e.activation(out=o1, in_=i_ap, func=AF.Identity, scale=ALPHA),
                         reads=[pr], writes=["Axs%d" % kk])
                    P.op("act", lambda e, i_ap=i_ap, o2=o2, s1=s1, s2=s2: e.activation(out=o2, in_=i_ap, func=AF.Identity, scale=s1, bias=s2),
                         reads=[pr, "scal"], writes=["AhT%d" % kk])
                else:
                    P.op("dve", lambda e, i_ap=i_ap, o1=o1: e.tensor_scalar(out=o1, in0=i_ap, scalar1=ALPHA, scalar2=None, op0=ALU.mult),
                         reads=[pr], writes=["Axs%d" % kk])
                    P.op("dve", lambda e, i_ap=i_ap, o2=o2, s1=s1, s2=s2: e.tensor_scalar(out=o2, in0=i_ap, scalar1=s1, scalar2=s2, op0=ALU.mult, op1=ALU.add),
                         reads=[pr, "scal"], writes=["AhT%d" % kk])
        if "A1" in self.debug:
            return
        self.ffn("A", hT[:, :, 0:nb], nb, xsT[:, :, 0:nb], Wd["up1"], Wd["dn1"], sc[:, S_HG2, :], R)
        if "A2" in self.debug:
            return
        outs = [(h2T, sc[:, S_G1S, :], sc[:, S_B1S, :], "Ah2T")]
        if not isctx:
            outs.append((xs1st, sc[:, S_AG0, :], sc[:, S_AB0, :], "Axs1"))
        self.ln("A", xsT, nb, R, outs)
        if not isctx:
            for kk in range(8):
                P.dma("pool", lambda e, kk=kk, r0=r0, nb=nb: e.dma_start(out=S["xs1"][kk, :, r0:r0 + nb], in_=xs1st[:, kk, 0:nb]),
                      reads=["Axs1%d" % kk], writes=[("xs1d", t0, kk)], sem="st_xs1_%d" % kk, soft=["xs1all"])
        if "A3" in self.debug:
            return
        for mi, m in enumerate([0, 1, 2, 3, 4, 5, 6, 7, 12, 13, 14, 15]):
            if isctx and m < 4:
                continue
            slot = win_cnt % 4
            win_cnt += 1
            wt = wint[slot]
            P.dma("sp", lambda e, wt=wt, m=m: e.dma_start(out=wt, in_=Wd["in"][m]), reads=["Wall"], writes=["win%d" % slot])
            pb = self.bank(mi % 4, nb)
            for kk in range(8):
                P.op("pe", lambda e, wt=wt, kk=kk, pb=pb, nb=nb: e.matmul(pb, lhsT=wt[:, kk, :], rhs=h2T[:, kk, 0:nb],
                                                                  start=(kk == 0), stop=(kk == 7)),
                     reads=["win%d" % slot, "Ah2T%d" % kk], writes=["ps%d" % (mi % 4)])
            if m < 8:
                eng = ("act", "dve")[mi % 2]
                o_ap = qkst[:, m, 0:nb]
                if eng == "act":
                    P.op("act", lambda e, o_ap=o_ap, pb=pb: e.activation(out=o_ap, in_=pb, func=AF.Copy),
                         reads=["ps%d" % (mi % 4)], writes=["qkst%d" % m])
                else:
                    P.op("dve", lambda e, o_ap=o_ap, pb=pb: e.tensor_copy(out=o_ap, in_=pb),
                         reads=["ps%d" % (mi % 4)], writes=["qkst%d" % m])
                if m < 4:
                    P.dma("pool", lambda e, m=m, r0=r0, o_ap=o_ap, nb=nb: e.dma_start(out=S["q"][m, :, r0:r0 + nb], in_=o_ap),
                          reads=["qkst%d" % m], writes=[("qd", t0, m)], sem="st_qk%d" % m, soft=["qall"])
                else:
                    P.dma("pool", lambda e, m=m, t0=t0, o_ap=o_ap, nb=nb: e.dma_start(out=S["k"][m - 4, :, t0:t0 + nb], in_=o_ap),
                          reads=["qkst%d" % m], writes=[("kd", t0, m)], sem="st_qk%d" % m, soft=["kall"])
            else:
                mu = m - 12
                nc8 = nb // 8
                i_ap = pb.rearrange("p (c s) -> p s c", s=8)
                o_ap = ust[:, mu, 0:nb].rearrange("p (s c) -> p s c", s=8)
                P.op("dve", lambda e, o_ap=o_ap, i_ap=i_ap: e.tensor_copy(out=o_ap, in_=i_ap),
                     reads=["ps%d" % (mi % 4)], writes=["ust%d" % mu])
                c0 = t0 // 8
                d_ap = S["u"][mu * 8:(mu + 1) * 8, :, :, c0:c0 + nc8].rearrange("g i s c -> (g i) s c")
                P.dma("pool", lambda e, d_ap=d_ap, o_ap=o_ap: e.dma_start(out=d_ap, in_=o_ap),
                      reads=["ust%d" % mu], writes=[("ud", t0, mu)], sem="st_u%d" % mu, soft=["uall"])
        for s in range(Sn):
            pv = self.bank(4 + s % 2, 512)
            for kk in range(8):
                P.op("pe", lambda e, kk=kk, s=s, pv=pv: e.matmul(pv, lhsT=h2T[:, kk, s * 128:(s + 1) * 128],
                                                                rhs=wv[:, kk, :, :].rearrange("p m c -> p (m c)"),
                                                                start=(kk == 0), stop=(kk == 7)),
                     reads=["wv0", "wv1", "wv2", "wv3", "Ah2T%d" % kk], writes=["ps%d" % (4 + s % 2)])
            vs = vst[s % 2]
            P.op("act", lambda e, vs=vs, pv=pv: e.activation(out=vs, in_=pv, func=AF.Copy),
                 reads=["ps%d" % (4 + s % 2)], writes=["vst%d" % (s % 2)])
            P.dma("pool", lambda e, vs=vs, s=s, t0=t0: e.dma_start(out=S["v"][t0 + s * 128:t0 + (s + 1) * 128, :], in_=vs),
                  reads=["vst%d" % (s % 2)], writes=[("vd", t0, s)], sem="st_v%d" % (s % 2), soft=["vall"])


KB.phaseA = kb_phaseA


IN_SHAPES = {
    "x": [NLAT, D], "c": [8, 128], "ctx": [NCTX, D], "c_ctx": [8, 128],
    "w_ada": [D, 9 * D], "b_ada": [72, 128], "ln_g": [24, 128], "ln_b": [24, 128],
    "ffn1_w_up": [D, 2 * DFF], "ffn1_w_down": [DFF, D], "w_in": [D, 2048],
    "na_rpb": [8, 15, 31], "s5_a_re": [2, 32, 64], "s5_a_im": [2, 32, 64], "s5_log_dt": [2, 32],
    "s5_b_re": [2, 32, 64, 16], "s5_b_im": [2, 32, 64, 16], "s5_c_re": [2, 32, 16, 64],
    "s5_c_im": [2, 32, 16, 64], "s5_d": [32, 16], "s5_w_glu": [512, 512], "s5_b_glu": [4, 128],
    "w_out": [D, D], "ffn2_w_up": [D, 2 * DFF], "ffn2_w_down": [DFF, D],
}


def build_nc(debug=None, stages=("setup", "convert", "A", "S5", "ATT", "C")):
    kb = KB(debug)
    ins = {k: kb.dram_in(k, shp) for k, shp in IN_SHAPES.items()}
    out = kb.dram_out("out", [NLAT, D])
    kb.out = out
    with kb.es:
        kb.setup(ins)
        kb.P.barrier()
        kb.conv_off = AW - 3 * 2816
        if "S5" in stages:
            kb.phaseS5(ins, mid_cb=(lambda: kb.convert(ins)) if "convert" in stages else None)
            assert kb.s5["o_end"] <= kb.conv_off, kb.s5["o_end"]
        elif "convert" in stages:
            kb.convert(ins)
        kb.P.barrier()
        if "A" in stages:
            kb.phaseA(ins)
            kb.P.barrier()
        if "S5" in stages:
            kb.phaseS5main(ins)
            kb.P.barrier()
        if "ATT" in stages:
            kb.phaseATT(ins)
            kb.P.barrier()
        if "C" in stages:
            kb.phaseC(ins)
        if "dump_s5" in kb.debug:
            s5 = kb.s5
            for nm, apx, dt_ in (("dbg_W1re", s5["W1b"][0], BF16), ("dbg_W1im", s5["W1b"][1], BF16), ("dbg_T0", s5["T0b"], BF16),
                                 ("dbg_W2re", s5["W2b"][0], BF16), ("dbg_W2im", s5["W2b"][1], BF16)):
                dm = kb.dram_out(nm, [128, 32, 128], dt_)
                kb.P.dma("sp", lambda e, dm=dm, apx=apx: e.dma_start(out=dm, in_=apx), reads=["W1b", "W2b", "T0b", "s5"], writes=[nm], sem="out")
            for nm, apx in (("dbg_rho8", s5["rho8"]), ("dbg_wre", s5["w_re"]), ("dbg_wim", s5["w_im"])):
                dm = kb.dram_out(nm, [128, 32], F32)
                kb.P.dma("sp", lambda e, dm=dm, apx=apx: e.dma_start(out=dm, in_=apx), reads=["s5"], writes=[nm], sem="out")
        if "dump_mod" in kb.debug:
            dm = kb.dram_out("dbg_mod", [128, 144])
            kb.P.dma("sp", lambda e: e.dma_start(out=dm, in_=kb.V(O_MODT, 144)), reads=["modT"], writes=["dbgmod"], sem="out")
        if "C" not in stages:
            z = kb.V(AW - 1024, 1024)
            kb.P.op("pool", lambda e: e.memset(z, 0.0), writes=["zz"])
            kb.P.dma("sp", lambda e: e.dma_start(out=out[0:128, :], in_=z), reads=["zz"], writes=["outz"], sem="out")
        kb.P.finish_wait("sp", list(range(kb.P.nphys)))
        kb.P.emit(kb.nc)
    return kb.nc


def make_in_maps(inputs):
    f = lambda a: np.ascontiguousarray(np.asarray(a, dtype=np.float32))
    shared = {}
    for k in IN_SHAPES:
        if k in ("x", "c", "ctx"):
            continue
        a = f(inputs[k])
        if a.ndim >= 1 and a.shape[0] == 1 and k not in ("c_ctx",):
            a = a[0]
        shared[k] = np.ascontiguousarray(a.reshape(IN_SHAPES[k]))
    maps = []
    for b in range(8):
        m = dict(shared)
        m["x"] = f(inputs["x"][b])
        m["ctx"] = f(inputs["ctx"][b])
        m["c"] = f(inputs["c"][b]).reshape(8, 128)
        maps.append(m)
    return maps


_NC_CACHE = {}


def kernel(**inputs):
    if "nc" not in _NC_CACHE:
        _NC_CACHE["nc"] = build_nc()
    nc = _NC_CACHE["nc"]
    maps = make_in_maps(inputs)
    res = run_bass_kernel_spmd(nc, maps, core_ids=list(range(8)))
    out = np.stack([np.asarray(r["out"], dtype=np.float32) for r in res.results], axis=0)
    return out


def kb_phaseS5(self, ins, mid_cb=None):
    nc, P, V = self.nc, self.P, self.V
    ps = self.psum
    if not hasattr(self, "S"):
        self.S = {}
    S = self.S
    tab_d = self.dram_tmp("tab_d", [32, 128, 2, NCH], F32)
    o = PERS

    def al(n):
        nonlocal o
        r = o
        o += n
        return r
    ident = self.ident
    L = "s5"

    def dv(fn, eng="dve", extra_r=(), extra_w=()):
        P.op(eng, fn, reads=[L] + list(extra_r), writes=[L] + list(extra_w))

    def TT(out, a, b, op, eng="dve"):
        dv(lambda e: e.tensor_tensor(out=out, in0=a, in1=b, op=op), eng)

    def TS(out, a, s1, op0, s2=None, op1=None, eng="dve"):
        if op1 is None:
            dv(lambda e: e.tensor_scalar(out=out, in0=a, scalar1=s1, scalar2=None, op0=op0), eng)
        else:
            dv(lambda e: e.tensor_scalar(out=out, in0=a, scalar1=s1, scalar2=s2, op0=op0, op1=op1), eng)

    def ACT(out, a, func, scale=1.0, bias=0.0):
        dv(lambda e: e.activation(out=out, in_=a, func=func, scale=scale, bias=bias), "act")

    W1b = [V(al(2048), 2048, BF16, "p (k c) -> p k c", k=32) for _ in range(2)]
    T0b = V(al(2048), 2048, BF16, "p (g c) -> p g c", g=32)
    W2b = [V(al(2048), 2048, BF16, "p (k c) -> p k c", k=32) for _ in range(2)]
    rho8 = V(al(32), 32)
    o_mark = o
    st = V(al(384), 384)
    A = V(al(32 * 12), 32 * 12, F32, "p (n k) -> p n k", k=32)
    a_re, a_im, ldt, dtt, x1, er, th, y_, fr, cs, sn, tmp = [A[:, i, :] for i in range(12)]
    A2 = V(al(32 * 12), 32 * 12, F32, "p (n k) -> p n k", k=32)
    ab_re, ab_im, k_re, k_im, den, t1, t2, t3, w_re, w_im, iv_re, iv_im = [A2[:, i, :] for i in range(12)]
    yi = V(al(32), 32).bitcast(I32)
    PaR = V(al(512), 512, F32, "p (k j) -> p k j", j=16)
    PaI = V(al(512), 512, F32, "p (k j) -> p k j", j=16)
    Bre = V(al(512), 512, F32, "p (k c) -> p k c", c=16)
    Bim = V(al(512), 512, F32, "p (k c) -> p k c", c=16)
    Bbre = V(al(512), 512, F32, "p (k c) -> p k c", c=16)
    Bbim = V(al(512), 512, F32, "p (k c) -> p k c", c=16)
    Cre = V(al(512), 512, F32, "p (k c) -> p k c", c=16)
    Cim = V(al(512), 512, F32, "p (k c) -> p k c", c=16)
    tB = V(al(512), 512, F32, "p (k c) -> p k c", c=16)
    cst = V(al(128), 128)
    dcol = V(al(32), 32)
    maskt = [V(al(512), 512, F32, "p (r c) -> p r c", r=4) for _ in range(2)]
    identt = V(al(512), 512, F32, "p (r c) -> p r c", r=4)

    are_v = ins["s5_a_re"].rearrange("d (k m) p -> (d k) (m p)", m=2)
    aim_v = ins["s5_a_im"].rearrange("d (k m) p -> (d k) (m p)", m=2)
    P.dma("sp", lambda e: e.dma_start(out=st[0:32, 0:128], in_=are_v), writes=["s5st0"])
    P.dma("sp", lambda e: e.dma_start(out=st[0:32, 128:256], in_=aim_v), writes=["s5st1"])
    ldt2 = V(al(2), 2)
    P.dma("sp", lambda e: e.dma_start(out=ldt2[0:32, :], in_=ins["s5_log_dt"].rearrange("d (k m) -> (d k) m", m=2)), writes=["s5st2"])
    P.op("dve", lambda e: e.tensor_copy(out=st[0:32, 256:384].rearrange("p (m q) -> p m q", m=2), in_=bc_free(ldt2[0:32, :], 64)),
         reads=["s5st2", L], writes=[L])
    for mi in range(2):
        bre_v = ins["s5_b_re"].rearrange("d (k m) p c -> m p (d k) c", m=2)
        bim_v = ins["s5_b_im"].rearrange("d (k m) p c -> m p (d k) c", m=2)
        P.dma("sp", lambda e, mi=mi, bre_v=bre_v: e.dma_start(out=Bre[mi * 64:(mi + 1) * 64], in_=bre_v[mi]), writes=["s5b%d" % mi])
        P.dma("sp", lambda e, mi=mi, bim_v=bim_v: e.dma_start(out=Bim[mi * 64:(mi + 1) * 64], in_=bim_v[mi]), writes=["s5bi%d" % mi])
    dst16 = V(al(16), 16)
    dst128 = V(al(128), 128)
    P.dma("sp", lambda e: e.dma_start(out=dst16[0:32, :], in_=ins["s5_d"]), writes=["s5d16"])
    P.op("dve", lambda e: e.tensor_copy(out=dst128[0:32, :].rearrange("p (s c) -> p s c", s=8),
                                        in_=AP(dst16.tensor, dst16.offset, [[dst16.ap[0][0], 32], [0, 8], [1, 16]])),
         reads=["s5d16", L], writes=[L])
    pt = ps[:, 6, :]
    P.op("pe", lambda e: e.transpose(out=pt[:, 256:288], in_=dst128[0:32, :], identity=ident[0:32, 0:32]),
         reads=[L, "ident"], writes=["psT6"])
    P.op("dve", lambda e: e.tensor_copy(out=dcol, in_=pt[:, 256:288]), reads=["psT6", L], writes=[L])
    for i in range(3):
        P.op("pe", lambda e, i=i: e.transpose(out=pt[:, i * 32:(i + 1) * 32], in_=st[0:32, i * 128:(i + 1) * 128], identity=ident[0:32, 0:32]),
             reads=["s5st0", "s5st1", L, "ident"], writes=["psT6"])
    P.op("dve", lambda e: e.tensor_copy(out=A[:, 0:3, :], in_=pt[:, 0:96].rearrange("p (n k) -> p n k", n=3)), reads=["psT6", L], writes=[L])
    for ri, (src, dst) in enumerate(((ins["s5_c_re"], Cre), (ins["s5_c_im"], Cim))):
        cv = src.rearrange("d (k m) c p -> d k c m p", m=2)
        for d_ in range(2):
            for j in range(2):
                for kl in range(8):
                    P.dma("sp", lambda e, cv=cv, d_=d_, j=j, kl=kl: e.dma_start(
                        out=cst[kl * 16:(kl + 1) * 16, :].rearrange("p (m q) -> p m q", m=2), in_=cv[d_, 8 * j + kl]),
                        reads=[L], writes=["s5cst%d" % kl])
                P.op("pe", lambda e: e.transpose(out=pt[:, 128:256], in_=cst, identity=ident),
                     reads=["s5cst%d" % kl for kl in range(8)] + ["ident"], writes=["psT6"])
                dk0 = d_ * 16 + 8 * j
                P.op("dve", lambda e, dst=dst, dk0=dk0: e.tensor_copy(out=dst[:, dk0:dk0 + 8, :].rearrange("p k c -> p (k c)"), in_=pt[:, 128:256]),
                     reads=["psT6", L], writes=[L] + ["s5cst%d" % kl for kl in range(8)])
    ACT(dtt, ldt, AF.Exp)
    TT(x1, a_re, dtt, ALU.mult)
    ACT(er, x1, AF.Exp)
    ACT(rho8, x1, AF.Exp, scale=8.0)
    TT(th, a_im, dtt, ALU.mult)
    for (dstt, off) in ((sn, 0.5), (cs, 0.75)):
        TS(y_, th, 1.0 / TWO_PI, ALU.mult, off, ALU.add)
        dv(lambda e: e.tensor_copy(out=yi, in_=y_))
        dv(lambda e: e.tensor_copy(out=fr, in_=yi))
        TT(fr, y_, fr, ALU.subtract)
        TS(tmp, fr, 0.0, ALU.is_lt)
        TT(fr, fr, tmp, ALU.add)
        TS(tmp, fr, 1.0, ALU.is_ge)
        TT(fr, fr, tmp, ALU.subtract)
        TS(fr, fr, TWO_PI, ALU.mult, -math.pi, ALU.add)
        TS(fr, fr, math.pi, ALU.min, -math.pi, ALU.max)
        ACT(dstt, fr, AF.Sin)
    if mid_cb is not None:
        mid_cb()
    TT(ab_re, er, cs, ALU.mult)
    TT(ab_im, er, sn, ALU.mult)
    TS(t1, ab_re, -1.0, ALU.add)
    TT(den, a_re, a_re, ALU.mult)
    TT(t2, a_im, a_im, ALU.mult)
    TT(den, den, t2, ALU.add)
    dv(lambda e: e.reciprocal(out=den, in_=den))
    TT(k_re, t1, a_re, ALU.mult)
    TT(t2, ab_im, a_im, ALU.mult)
    TT(k_re, k_re, t2, ALU.add)
    TT(k_re, k_re, den, ALU.mult)
    TT(k_im, ab_im, a_re, ALU.mult)
    TT(t2, t1, a_im, ALU.mult)
    TT(k_im, k_im, t2, ALU.subtract)
    TT(k_im, k_im, den, ALU.mult)
    TT(t1, er, er, ALU.mult)
    dv(lambda e: e.reciprocal(out=t1, in_=t1))
    TT(iv_re, ab_re, t1, ALU.mult)
    TT(iv_im, ab_im, t1, ALU.mult)
    TS(iv_im, iv_im, -1.0, ALU.mult)

    def cmul(o_re, o_im, x_re, x_im, y_re, y_im):
        TT(t1, x_re, y_re, ALU.mult)
        TT(t2, x_im, y_im, ALU.mult)
        TT(t3, x_re, y_im, ALU.mult)
        TT(o_im, x_im, y_re, ALU.mult)
        TT(o_im, o_im, t3, ALU.add)
        TT(o_re, t1, t2, ALU.subtract)
    dv(lambda e: e.memset(PaR[:, :, 7], 1.0))
    dv(lambda e: e.memset(PaI[:, :, 7], 0.0))
    dv(lambda e: e.tensor_copy(out=PaR[:, :, 8], in_=ab_re))
    dv(lambda e: e.tensor_copy(out=PaI[:, :, 8], in_=ab_im))
    for jj in range(9, 16):
        cmul(PaR[:, :, jj], PaI[:, :, jj], PaR[:, :, jj - 1], PaI[:, :, jj - 1], ab_re, ab_im)
    for jj in range(6, -1, -1):
        cmul(PaR[:, :, jj], PaI[:, :, jj], PaR[:, :, jj + 1], PaI[:, :, jj + 1], iv_re, iv_im)
    dv(lambda e: e.reciprocal(out=t1, in_=rho8))
    TT(w_re, PaR[:, :, 15], t1, ALU.mult)
    TT(w_im, PaI[:, :, 15], t1, ALU.mult)
    TS(w_im, w_im, -1.0, ALU.mult)
    for (o_, x_, kx, y_b, ky, op) in ((Bbre, Bre, k_re, Bim, k_im, ALU.subtract), (Bbim, Bim, k_re, Bre, k_im, ALU.add)):
        dv(lambda e, o_=o_, x_=x_, kx=kx: e.tensor_tensor(out=o_, in0=x_, in1=bc_free(kx, 16), op=ALU.mult),
           extra_r=["s5b0", "s5b1", "s5bi0", "s5bi1"])
        dv(lambda e, y_b=y_b, ky=ky: e.tensor_tensor(out=tB, in0=y_b, in1=bc_free(ky, 16), op=ALU.mult))
        dv(lambda e, o_=o_, op=op: e.tensor_tensor(out=o_, in0=o_, in1=tB, op=op))
    for d_ in range(2):
        dv(lambda e, d_=d_: e.memset(maskt[d_], 1.0), "pool")
        if d_ == 0:
            pat, base, cm = [[0, 4], [16, 8], [0, 16]], 15, -1
        else:
            pat, base, cm = [[0, 4], [-16, 8], [0, 16]], 0, 1
        dv(lambda e, d_=d_, pat=pat, base=base, cm=cm: e.affine_select(
            out=maskt[d_], in_=maskt[d_], pattern=pat, compare_op=ALU.is_ge, fill=0.0, base=base, channel_multiplier=cm), "pool")
    dv(lambda e: e.tensor_copy(out=identt, in_=AP(ident.tensor, ident.offset, [list(ident.ap[0]), [0, 4], [1, 128]])), "pool", extra_r=["ident"])

    o_fm = o
    Fm = [V(al(2048), 2048, F32, "p (k s c) -> p k s c", k=16, s=8) for _ in range(2)]
    W2m = [V(al(2048), 2048, F32, "p (k t c) -> p k t c", k=16, t=8) for _ in range(2)]
    W2p = [V(al(2048), 2048, F32, "p (k t c) -> p k t c", k=16, t=8) for _ in range(2)]
    tF = V(al(2048), 2048, F32, "p (k s c) -> p k s c", k=16, s=8)
    T0acc = V(al(4096), 4096, F32, "p (g c) -> p g c", g=32)
    tM = V(al(512), 512)

    def pw(Pt, dk0, j0, jstep):
        b = Pt[:, dk0:dk0 + 16, :]
        return AP(b.tensor, b.offset + j0, [list(b.ap[0]), [16, 16], [jstep, 8], [0, 16]])

    def bx(Bt, dk0):
        b = Bt[:, dk0:dk0 + 16, :]
        return AP(b.tensor, b.offset, [list(b.ap[0]), [16, 16], [0, 8], [1, 16]])

    def outer(o_re, o_im, dk0, j0, jstep, Xre, Xim, neg_im):
        dv(lambda e: e.tensor_tensor(out=o_re, in0=pw(PaR, dk0, j0, jstep), in1=bx(Xre, dk0), op=ALU.mult))
        dv(lambda e: e.tensor_tensor(out=tF, in0=pw(PaI, dk0, j0, jstep), in1=bx(Xim, dk0), op=ALU.mult))
        dv(lambda e: e.tensor_tensor(out=o_re, in0=o_re, in1=tF, op=ALU.subtract))
        dv(lambda e: e.tensor_tensor(out=o_im, in0=pw(PaR, dk0, j0, jstep), in1=bx(Xim, dk0), op=ALU.mult))
        dv(lambda e: e.tensor_tensor(out=tF, in0=pw(PaI, dk0, j0, jstep), in1=bx(Xre, dk0), op=ALU.mult))
        dv(lambda e: e.tensor_tensor(out=o_im, in0=o_im, in1=tF, op=ALU.add))
        if neg_im:
            dv(lambda e: e.tensor_scalar(out=o_im, in0=o_im, scalar1=-1.0, scalar2=None, op0=ALU.mult))

    for d_ in range(2):
        dk0 = d_ * 16
        if d_ == 0:
            outer(Fm[0], Fm[1], dk0, 14, -1, Bbre, Bbim, False)
            outer(W2m[0], W2m[1], dk0, 8, 1, Cre, Cim, True)
            outer(W2p[0], W2p[1], dk0, 0, 1, Cre, Cim, True)
        else:
            outer(Fm[0], Fm[1], dk0, 7, 1, Bbre, Bbim, False)
            outer(W2m[0], W2m[1], dk0, 15, -1, Cre, Cim, True)
            outer(W2p[0], W2p[1], dk0, 7, -1, Cre, Cim, True)
        for r in range(2):
            dv(lambda e, r=r, dk0=dk0: e.tensor_copy(out=W2b[r][:, dk0:dk0 + 16, :].rearrange("p k (c t) -> p k t c", t=8), in_=W2m[r]),
               "pool", extra_w=["W2b"])
        for r in range(2):
            for kb4 in range(4):
                bank = ps[:, kb4 % 2, :]
                for q in range(4):
                    k_ = kb4 * 4 + q
                    P.op("pe", lambda e, r=r, k_=k_, q=q, bank=bank: e.transpose(
                        out=bank[:, q * 128:(q + 1) * 128], in_=Fm[r][:, k_, :, :].rearrange("p s c -> p (s c)"), identity=ident),
                        reads=[L, "ident"], writes=["ps%d" % (kb4 % 2)])
                P.op("dve", lambda e, r=r, kb4=kb4, dk0=dk0, bank=bank: e.tensor_copy(
                    out=W1b[r][:, dk0 + kb4 * 4:dk0 + kb4 * 4 + 4, :].rearrange("p k c -> p (k c)"), in_=bank),
                    reads=["ps%d" % (kb4 % 2)], writes=["W1b"])
        for gb in range(8):
            mi = gb // 4
            bank = ps[:, 2 + gb % 2, :]
            g0 = 2 * 4 * (gb % 4) + mi
            for q in range(4):
                k_ = 4 * (gb % 4) + q
                for r in range(2):
                    P.op("pe", lambda e, r=r, k_=k_, mi=mi, q=q, bank=bank: e.matmul(
                        bank[:, q * 128:(q + 1) * 128],
                        lhsT=Fm[r][mi * 64:(mi + 1) * 64, k_, :, :].rearrange("p s c -> p (s c)"),
                        rhs=W2p[r][mi * 64:(mi + 1) * 64, k_, :, :].rearrange("p t c -> p (t c)"),
                        start=(r == 0), stop=(r == 1)),
                        reads=[L], writes=["ps%d" % (2 + gb % 2)])
            tb0 = T0acc[:, g0, :]
            acc = AP(tb0.tensor, tb0.offset, [list(tb0.ap[0]), [256, 4], [1, 128]])
            bk = bank.rearrange("p (r c) -> p r c", r=4)
            tM3 = tM.rearrange("p (r c) -> p r c", r=4)
            if d_ == 0:
                P.op("dve", lambda e, acc=acc, bk=bk: e.tensor_tensor(out=acc, in0=bk, in1=maskt[0], op=ALU.mult),
                     reads=["ps%d" % (2 + gb % 2), L], writes=["T0acc"])
            else:
                P.op("dve", lambda e, bk=bk, tM3=tM3: e.tensor_tensor(out=tM3, in0=bk, in1=maskt[1], op=ALU.mult),
                     reads=["ps%d" % (2 + gb % 2), L, "T0acc"], writes=["tM"])
                P.op("dve", lambda e, acc=acc, tM3=tM3: e.tensor_tensor(out=acc, in0=acc, in1=tM3, op=ALU.add),
                     reads=["tM", "T0acc"], writes=["T0acc"])
                dc0 = dcol[:, g0:g0 + 1]
                dcb = AP(dc0.tensor, dc0.offset, [list(dc0.ap[0]), [2, 4], [0, 128]])
                P.op("dve", lambda e, dcb=dcb, tM3=tM3: e.tensor_tensor(out=tM3, in0=identt, in1=dcb, op=ALU.mult),
                     reads=["tM", L], writes=["tM"])
                t0b = T0b[:, g0, :]
                o4 = AP(t0b.tensor, t0b.offset, [list(t0b.ap[0]), [256, 4], [1, 8], [8, 16]])
                a4 = AP(tb0.tensor, tb0.offset, [list(tb0.ap[0]), [256, 4], [16, 8], [1, 16]])
                P.op("dve", lambda e, o4=o4, a4=a4: e.tensor_tensor(out=o4, in0=a4, in1=tM.rearrange("p (r t c) -> p r t c", r=4, t=8), op=ALU.add),
                     reads=["tM", "T0acc"], writes=["T0b", "tM"])
        P.op("dve", lambda e: e.memset(tM[:, 0:1], 0.0), reads=["ps0", "ps1", "ps2", "ps3", "tM", L], writes=[L, "tM"])

    P.op("dve", lambda e: e.memset(tM[:, 0:1], 0.0), reads=["ps0", "ps1", "ps2", "ps3", "tM", "T0acc", "T0b", "W1b", "W2b", L], writes=[L, "tM", "T0acc"])
    o = o_fm
    wp = [V(al(64), 64, F32, "p (r k) -> p r k", r=2) for _ in range(10)]
    dv(lambda e: e.tensor_copy(out=wp[0][:, 0, :], in_=w_re))
    dv(lambda e: e.tensor_copy(out=wp[0][:, 1, :], in_=w_im))
    for q in range(1, 10):
        a_, b_ = wp[q - 1][:, 0, :], wp[q - 1][:, 1, :]
        TT(t1, a_, a_, ALU.mult)
        TT(t2, b_, b_, ALU.mult)
        TT(wp[q][:, 0, :], t1, t2, ALU.subtract)
        TT(t1, a_, b_, ALU.mult)
        TS(wp[q][:, 1, :], t1, 2.0, ALU.mult)
    TBL = V(al(8 * 2 * NCH), 8 * 2 * NCH, F32, "p (k r n) -> p k r n", k=8, r=2)
    tq = [V(al(2048), 2048) for _ in range(2)]
    for bt in range(4):
        k0 = bt * 8
        dv(lambda e: e.memset(TBL[:, :, 0, 0:1], 1.0))
        dv(lambda e: e.memset(TBL[:, :, 1, 0:1], 0.0))
        dv(lambda e, k0=k0: e.tensor_copy(out=TBL[:, :, 0, 1], in_=wp[0][:, 0, k0:k0 + 8]))
        dv(lambda e, k0=k0: e.tensor_copy(out=TBL[:, :, 1, 1], in_=wp[0][:, 1, k0:k0 + 8]))
        for q in range(1, 10):
            sz = 2 ** q
            n = min(sz, NCH - sz)
            lo_r, lo_i = TBL[:, :, 0, 0:n], TBL[:, :, 1, 0:n]
            hi_r, hi_i = TBL[:, :, 0, sz:sz + n], TBL[:, :, 1, sz:sz + n]
            wr = bc_free(wp[q][:, 0, k0:k0 + 8], n)
            wi = bc_free(wp[q][:, 1, k0:k0 + 8], n)
            u1 = tq[0][:, 0:8 * n].rearrange("p (k n) -> p k n", k=8)
            u2 = tq[1][:, 0:8 * n].rearrange("p (k n) -> p k n", k=8)
            TT(u1, lo_r, wr, ALU.mult)
            TT(u2, lo_i, wi, ALU.mult)
            TT(hi_r, u1, u2, ALU.subtract)
            TT(u1, lo_r, wi, ALU.mult)
            TT(u2, lo_i, wr, ALU.mult)
            TT(hi_i, u1, u2, ALU.add)
        P.dma("pool", lambda e, k0=k0: e.dma_start(out=tab_d[k0:k0 + 8].rearrange("k p r n -> p k r n"), in_=TBL),
              reads=[L], writes=[("tabd", bt)], sem="st_tab", soft=["taball"])
        dv(lambda e: e.memset(tq[0][:, 0:1], 0.0), extra_r=[("tabd", bt)])
    self.s5 = dict(W1b=W1b, T0b=T0b, W2b=W2b, rho8=rho8, w_re=w_re, w_im=w_im, tab_d=tab_d, o_mark=o_mark, o_end=o, L=L)


def kb_phaseS5main(self, ins):
    nc, P, V = self.nc, self.P, self.V
    ps = self.psum
    S = self.S
    s5 = self.s5
    W1b, T0b, W2b, rho8, tab_d, o_mark = s5["W1b"], s5["T0b"], s5["W2b"], s5["rho8"], s5["tab_d"], s5["o_mark"]
    if "u_in" in self.debug:
        S["u"] = self.dram_in("u_d", [32, 16, 8, NCH], BF16)
    S["y"] = self.dram_tmp("y_d", [32, 16, 8, 512], F32)

    def al(n):
        nonlocal o
        r = o
        o += n
        return r
    o = o_mark
    U = V(al(32 * NCH // 2), 32 * NCH // 2, BF16, "p (g c) -> p g c", g=32)
    Ur = V(al(32 * NCH // 2), 32 * NCH // 2, BF16, "p (g c) -> p g c", g=32)
    tb = [V(al(2 * NCH), 2 * NCH, F32, "p (r n) -> p r n", r=2) for _ in range(2)]
    Gin = V(al(2 * NCH), 2 * NCH, F32, "p (r n) -> p r n", r=2)
    Gs = V(al(2 * NCH), 2 * NCH, F32, "p (r n) -> p r n", r=2)
    tt_ = [V(al(NCH), NCH) for _ in range(4)]
    Hb = [[V(al(256), 256, BF16) for _ in range(2)] for _ in range(2)]
    ysb = [V(al(512), 512) for _ in range(2)]
    assert o <= AW, o
    for s_ in range(8):
        P.dma("sp", lambda e, s_=s_: e.dma_start(out=U[16 * s_:16 * s_ + 16], in_=S["u"][:, :, s_, :].rearrange("g i c -> i g c")),
              reads=["uall"], writes=["U%d" % s_])
    Ures = ["U%d" % s_ for s_ in range(8)]
    Ub = U[:, :, :]
    P.op("pool", lambda e: e.tensor_copy(out=Ur[:, :, 0:32], in_=AP(Ub.tensor, Ub.offset + 31, [list(Ub.ap[0]), [NCH, 32], [-1, 32]])),
         reads=Ures, writes=["Ur0"])
    P.op("pool", lambda e: e.tensor_copy(out=Ur[:, :, 32:NCH], in_=AP(Ub.tensor, Ub.offset + NCH - 1, [list(Ub.ap[0]), [NCH, 32], [-1, 512]])),
         reads=Ures, writes=["Ur1"])
    for k in range(16):
        for d_ in range(2):
            dk = d_ * 16 + k
            T = tb[dk % 2]
            tr = "tb%d" % (dk % 2)
            P.dma("sp", lambda e, T=T, dk=dk: e.dma_start(out=T, in_=tab_d[dk]), reads=["taball"], writes=[tr])
            Us = U if d_ == 0 else Ur
            ures = Ures if d_ == 0 else ["Ur0", "Ur1"]
            for mi in range(2):
                g = 2 * k + mi
                for r in range(2):
                    lh = W1b[r][:, dk, mi * 64:(mi + 1) * 64]
                    P.op("pe", lambda e, lh=lh, r=r, mi=mi, g=g, Us=Us: e.matmul(ps[mi * 64:(mi + 1) * 64, r, :], lhsT=lh, rhs=Us[:, g, 0:512], start=True, stop=True),
                         reads=ures + ["W1b"], writes=["ps%d" % r])
                    P.op("pe", lambda e, lh=lh, r=r, mi=mi, g=g, Us=Us: e.matmul(ps[mi * 64:(mi + 1) * 64, 2, r * 32:(r + 1) * 32], lhsT=lh, rhs=Us[:, g, 512:NCH], start=True, stop=True),
                         reads=ures + ["W1b"], writes=["ps2"])
            for (c0, c1, sre, sim, pr) in ((0, 512, ps[:, 0, :], ps[:, 1, :], ["ps0", "ps1"]), (512, NCH, ps[:, 2, 0:32], ps[:, 2, 32:64], ["ps2"])):
                TRs, TIs = T[:, 0, c0:c1], T[:, 1, c0:c1]
                t1_, t2_, t3_, t4_ = [x[:, c0:c1] for x in tt_]
                P.op("dve", lambda e, t1_=t1_, sre=sre, TRs=TRs: e.tensor_tensor(out=t1_, in0=sre, in1=TRs, op=ALU.mult), reads=pr + [tr], writes=["tt0"])
                P.op("dve", lambda e, t2_=t2_, sim=sim, TIs=TIs: e.tensor_tensor(out=t2_, in0=sim, in1=TIs, op=ALU.mult), reads=pr + [tr], writes=["tt1"])
                P.op("dve", lambda e, t3_=t3_, sim=sim, TRs=TRs: e.tensor_tensor(out=t3_, in0=sim, in1=TRs, op=ALU.mult), reads=pr + [tr], writes=["tt2"])
                P.op("dve", lambda e, t4_=t4_, sre=sre, TIs=TIs: e.tensor_tensor(out=t4_, in0=sre, in1=TIs, op=ALU.mult), reads=pr + [tr], writes=["tt3"])
            P.op("pool", lambda e: e.tensor_tensor(out=Gin[:, 0, :], in0=tt_[0], in1=tt_[1], op=ALU.subtract), reads=["tt0", "tt1"], writes=["Gin0"])
            P.op("pool", lambda e: e.tensor_tensor(out=Gin[:, 1, :], in0=tt_[2], in1=tt_[3], op=ALU.add), reads=["tt2", "tt3"], writes=["Gin1"])
            rb = bc_free(rho8[:, dk:dk + 1], NCH).rearrange("p a n -> p (a n)") if False else AP(rho8.tensor, rho8.offset + dk, [list(rho8.ap[0]), [0, NCH]])
            for r in range(2):
                P.op("dve", lambda e, r=r, rb=rb: e.tensor_tensor_scan(out=Gs[:, r, :], data0=rb, data1=Gin[:, r, :], initial=0.0, op0=ALU.mult, op1=ALU.add),
                     reads=["Gin%d" % r, "s5"], writes=["Gs%d" % r])
            TRs, TIs = T[:, 0, 31:543], T[:, 1, 31:543]
            gr, gi = Gs[:, 0, 31:543], Gs[:, 1, 31:543]
            u1, u2, u3, u4 = [x[:, 0:512] for x in tt_]
            P.op("pool", lambda e, TRs=TRs, gr=gr, u1=u1: e.tensor_tensor(out=u1, in0=TRs, in1=gr, op=ALU.mult), reads=[tr, "Gs0", "Gin0", "Gin1"], writes=["tt0"])
            P.op("pool", lambda e, TIs=TIs, gi=gi, u2=u2: e.tensor_tensor(out=u2, in0=TIs, in1=gi, op=ALU.mult), reads=[tr, "Gs1", "Gin0", "Gin1"], writes=["tt1"])
            P.op("pool", lambda e, TRs=TRs, gi=gi, u3=u3: e.tensor_tensor(out=u3, in0=TRs, in1=gi, op=ALU.mult), reads=[tr, "Gs1", "Gin0", "Gin1"], writes=["tt2"])
            P.op("pool", lambda e, TIs=TIs, gr=gr, u4=u4: e.tensor_tensor(out=u4, in0=TIs, in1=gr, op=ALU.mult), reads=[tr, "Gs0", "Gin0", "Gin1"], writes=["tt3"])
            outs_ = []
            for r in range(2):
                hb = Hb[d_][r]
                if d_ == 0:
                    outs_.append(hb)
                else:
                    outs_.append(AP(hb.tensor, hb.offset + 511, [list(hb.ap[0]), [-1, 512]]))
            P.op("pool", lambda e, o_=outs_[0], u1=u1, u2=u2: e.tensor_tensor(out=o_, in0=u1, in1=u2, op=ALU.add), reads=["tt0", "tt1"], writes=["Hb%d0" % d_])
            P.op("pool", lambda e, o_=outs_[1], u3=u3, u4=u4: e.tensor_tensor(out=o_, in0=u3, in1=u4, op=ALU.subtract), reads=["tt2", "tt3"], writes=["Hb%d1" % d_])
        for mi in range(2):
            g = 2 * k + mi
            yb = ps[:, 4 + g % 2, :]
            yr = "ps%d" % (4 + g % 2)
            P.op("pe", lambda e, g=g, yb=yb: e.matmul(yb, lhsT=T0b[:, g, :], rhs=U[:, g, 32:NCH], start=True, stop=False),
                 reads=Ures + ["T0b"], writes=[yr])
            n_ = 0
            for d_ in range(2):
                for r in range(2):
                    n_ += 1
                    lh = W2b[r][mi * 64:(mi + 1) * 64, d_ * 16 + k, :]
                    rh = Hb[d_][r][mi * 64:(mi + 1) * 64, :]
                    P.op("pe", lambda e, lh=lh, rh=rh, yb=yb, n_=n_: e.matmul(yb, lhsT=lh, rhs=rh, start=False, stop=(n_ == 4)),
                         reads=["W2b", "Hb%d%d" % (d_, r)], writes=[yr])
            yt = ysb[g % 2]
            P.op("act", lambda e, yt=yt, yb=yb: e.activation(out=yt, in_=yb, func=AF.Copy), reads=[yr], writes=["ysb%d" % (g % 2)])
            P.dma("pool", lambda e, yt=yt, g=g: e.dma_start(out=S["y"][g].rearrange("c t n -> (c t) n"), in_=yt),
                  reads=["ysb%d" % (g % 2)], writes=[("yd", g)], sem="st_y%d" % (g % 2), soft=["yall"])


KB.phaseS5 = kb_phaseS5
KB.phaseS5main = kb_phaseS5main


MASK_E = -4000.0


def att_plan():
    sets = {}
    plan = []
    for a in range(32):
        rows = (2 * a, 2 * a + 1)
        r0 = [min(max(r - 4, 0), 56) for r in rows]
        lo, hi = min(r0), max(r0) + 7
        tiles = []
        for kt in range(lo // 2, hi // 2 + 1):
            ent = []
            for il in range(2):
                for rl in range(2):
                    i_abs = 2 * kt + il
                    ok = r0[rl] <= i_abs <= r0[rl] + 7
                    ent.append((i_abs - rows[rl] + 7) if ok else None)
            key = tuple(ent)
            if key not in sets:
                sets[key] = len(sets)
            tiles.append((kt, sets[key]))
        plan.append(tiles)
    return plan, sets


def kb_phaseATT(self, ins):
    nc, P, V = self.nc, self.P, self.V
    ps = self.psum
    if not hasattr(self, "S"):
        self.S = {}
    S = self.S
    for nm, shp in (("q", [4, 128, NLAT]), ("k", [4, 128, NT]), ("v", [NT, 512])):
        if nm + "_d_in" in self.debug:
            S[nm] = self.dram_in(nm + "_d", shp, BF16)
    S["yna"] = self.dram_tmp("yna_d", [4, 128, NLAT], BF16)
    tt_d = self.dram_tmp("tt_d", [8, 15, 64, 64], BF16)
    ttm_d = self.dram_tmp("ttm_d", [64, 8, 64], BF16)
    plan, sets = att_plan()
    NS = len(sets)
    o = PERS

    def al(n):
        nonlocal o
        r = o
        o += n
        return r
    ident, identb, onesb = self.ident, self.identb, self.onesb
    kT = V(al(4 * NT // 2), 4 * NT // 2, BF16, "p (m t) -> p m t", m=4)
    Vt = V(al(34 * 256), 34 * 256, BF16, "p (k f) -> p k f", k=34)
    TB = V(al(NS * 8 * 64), NS * 8 * 64, BF16, "p (s h c) -> p s h c", s=NS, h=8)
    qT = [V(al(256), 256, BF16, "p (m t) -> p m t", m=4) for _ in range(2)]
    PT = [V(al(448), 448, BF16, "p (i t) -> p i t", i=7) for _ in range(2)]
    rc = V(al(128), 128)
    yst = [V(al(256), 256, BF16, "p (m t) -> p m t", m=4) for _ in range(2)]
    o_tmp = o
    for m in range(4):
        P.dma("sp", lambda e, m=m: e.dma_start(out=kT[:, m, :], in_=S["k"][m]), reads=["kall"], writes=["kT%d" % m])
    vv = S["v"].rearrange("(k p) f -> p k f", p=128)
    for i in range(2):
        P.dma("sp", lambda e, i=i: e.dma_start(out=Vt[:, i * 17:(i + 1) * 17, :], in_=vv[:, i * 17:(i + 1) * 17, :]), reads=["vall"], writes=["Vt%d" % i])
    L = "att"
    stg = V(al(32), 32)
    Lm = V(al(120), 120)
    E = V(al(4096), 4096, F32, "p (j c) -> p j c", j=64)
    E2 = V(al(4096), 4096, F32, "p (j c) -> p j c", j=64)
    E3 = V(al(4096), 4096, F32, "p (j c) -> p j c", j=64)
    rsel = V(al(2), 2)
    ttb = [V(al(256), 256, BF16) for _ in range(2)]
    mtile = V(al(256), 256, BF16)
    assert o <= AW, o

    def dv(fn, eng="dve", extra_r=(), extra_w=()):
        P.op(eng, fn, reads=[L] + list(extra_r), writes=[L] + list(extra_w))
    dv(lambda e: e.memset(stg, 1.0), "pool")
    P.dma("sp", lambda e: e.dma_start(out=stg[0:120, 0:31], in_=ins["na_rpb"].rearrange("h a b -> (h a) b")), reads=[L], writes=["attstg"])
    P.op("pe", lambda e: e.transpose(out=ps[0:32, 6, 0:120], in_=stg[0:120, :], identity=ident[0:120, 0:120]), reads=["attstg", L, "ident"], writes=["psT6"])
    P.op("dve", lambda e: e.tensor_copy(out=Lm[0:32, :], in_=ps[0:32, 6, 0:120]), reads=["psT6", L], writes=[L])
    E_, E2_, E3_ = E[0:32], E2[0:32], E3[0:32]
    dv(lambda e: e.iota(E_, pattern=[[1, 64], [-1, 64]], base=15, channel_multiplier=-1, allow_small_or_imprecise_dtypes=True), "pool")
    dv(lambda e: e.tensor_scalar(out=E_, in0=E_, scalar1=0.0, scalar2=None, op0=ALU.is_equal))
    dv(lambda e: e.iota(E2_, pattern=[[0, 64], [1, 64]], base=0, channel_multiplier=0, allow_small_or_imprecise_dtypes=True), "pool")
    dv(lambda e: e.tensor_scalar(out=E2_, in0=E2_, scalar1=-8.0, scalar2=0.0, op0=ALU.add, op1=ALU.max))
    dv(lambda e: e.tensor_scalar(out=E2_, in0=E2_, scalar1=48.0, scalar2=None, op0=ALU.min))
    dv(lambda e: e.iota(E3_, pattern=[[1, 64], [0, 64]], base=0, channel_multiplier=0, allow_small_or_imprecise_dtypes=True), "pool")
    dv(lambda e: e.tensor_tensor(out=E3_, in0=E3_, in1=E2_, op=ALU.subtract))
    dv(lambda e: e.tensor_scalar(out=E2_, in0=E3_, scalar1=0.0, scalar2=None, op0=ALU.is_ge))
    dv(lambda e: e.tensor_scalar(out=E3_, in0=E3_, scalar1=15.0, scalar2=None, op0=ALU.is_le))
    dv(lambda e: e.tensor_tensor(out=E2_, in0=E2_, in1=E3_, op=ALU.mult))
    dv(lambda e: e.tensor_tensor(out=E_, in0=E_, in1=E2_, op=ALU.mult))
    dv(lambda e: e.tensor_scalar(out=E2_, in0=E2_, scalar1=-MASK_E, scalar2=MASK_E, op0=ALU.mult, op1=ALU.add))
    rs = rsel[0:32, :]
    dv(lambda e: e.iota(rs[:, 0:1], pattern=[[0, 1]], base=0, channel_multiplier=1, allow_small_or_imprecise_dtypes=True), "pool")
    dv(lambda e: e.tensor_scalar(out=rs[:, 0:1], in0=rs[:, 0:1], scalar1=31.0, scalar2=None, op0=ALU.is_lt))
    dv(lambda e: e.tensor_scalar(out=rs[:, 1:2], in0=rs[:, 0:1], scalar1=-1.0, scalar2=1.0, op0=ALU.mult, op1=ALU.add))
    dv(lambda e: e.tensor_scalar(out=E_, in0=E_, scalar1=rs[:, 0:1], scalar2=None, op0=ALU.mult))
    dv(lambda e: e.scalar_tensor_tensor(out=E_, in0=E2_, scalar=rs[:, 1:2], in1=E_, op0=ALU.mult, op1=ALU.add))
    Ef = E_.rearrange("p j c -> p (j c)")
    ttv = tt_d.rearrange("h a j c -> (h a) (j c)")
    for i in range(8):
        bank = ps[0:120, i % 2, :]
        P.op("pe", lambda e, i=i, bank=bank: e.matmul(bank, lhsT=Lm[0:32, :], rhs=Ef[:, i * 512:(i + 1) * 512], start=True, stop=True),
             reads=[L], writes=["ps%d" % (i % 2)])
        tb_ = ttb[i % 2][0:120, :]
        P.op("act", lambda e, tb_=tb_, bank=bank: e.activation(out=tb_, in_=bank, func=AF.Identity, scale=8.0), reads=["ps%d" % (i % 2)], writes=["ttb%d" % (i % 2)])
        P.dma("sp", lambda e, i=i, tb_=tb_: e.dma_start(out=ttv[:, i * 512:(i + 1) * 512], in_=tb_), reads=["ttb%d" % (i % 2)], writes=[("ttd", i)],
              sem="st_tt%d" % (i % 2), soft=["ttall"])
    dv(lambda e: e.memset(mtile, 8.0 * MASK_E), "pool")
    msrc = AP(mtile.tensor, mtile.offset, [[mtile.ap[0][0], 64], [1, 256]])
    P.dma("sp", lambda e: e.dma_start(out=ttm_d.rearrange("j h c -> j (h c)")[:, 0:256], in_=msrc),
          reads=[L], writes=["ttm_a"], sem="st_ttm")
    P.dma("sp", lambda e: e.dma_start(out=ttm_d.rearrange("j h c -> j (h c)")[:, 256:512], in_=msrc),
          reads=[L], writes=["ttm_b"], sem="st_ttm2")
    self.P.barrier()
    for key, si in sets.items():
        n_ = 0
        for il in range(2):
            for rl in range(2):
                di = key[il * 2 + rl]
                dst = TB[il * 64:(il + 1) * 64, si, :, rl * 64:(rl + 1) * 64]
                if di is None:
                    src = ttm_d
                else:
                    src = tt_d[:, di, :, :].rearrange("h j c -> j h c")
                P.dma("sp", lambda e, dst=dst, src=src: e.dma_start(out=dst, in_=src), writes=[("TB", si, il, rl)], sem="ld_tb%d" % (n_ % 4), soft=["TBall"])
                n_ += 1
    self.P.barrier()

    SCALE = 0.125
    hcnt = 0
    for a in range(32):
        qt = qT[a % 2]
        qr = "qT%d" % (a % 2)
        P.dma("sp", lambda e, a=a, qt=qt: e.dma_start(out=qt, in_=S["q"][:, :, a * 128:(a + 1) * 128].rearrange("m p t -> p m t")), reads=["qall"], writes=[qr])
        tiles = [(256 + kt * 128, 2 + kt, si) for (kt, si) in plan[a]] + [(0, 0, None), (128, 1, None)]
        nt_ = len(tiles)
        ys = yst[a % 2]
        for m in range(4):
            ob = ps[:, 4 + (a * 4 + m) % 2, :]
            obr = "ps%d" % (4 + (a * 4 + m) % 2)
            for hh in range(2):
                h = 2 * m + hh
                sb_ = (hcnt % 2) * 2
                hcnt += 1
                st2 = ps[:, sb_:sb_ + 2, :].rearrange("p a b -> p (a b)")
                pt_ = PT[hcnt % 2]
                ptr = "PT%d" % (hcnt % 2)
                for i, (tok, vt, si) in enumerate(tiles):
                    bres = "ps%d" % (sb_ + i // 4)
                    o_ap = st2[:, i * 128:(i + 1) * 128]
                    P.op("pe", lambda e, o_ap=o_ap, hh=hh, m=m, tok=tok, qt=qt, si=si: e.matmul(
                        o_ap, lhsT=kT[hh * 64:(hh + 1) * 64, m, tok:tok + 128], rhs=qt[hh * 64:(hh + 1) * 64, m, :],
                        start=True, stop=(si is None), skip_group_check=True), reads=["kT%d" % m, qr], writes=[bres])
                    if si is not None:
                        P.op("pe", lambda e, o_ap=o_ap, si=si, h=h: e.matmul(o_ap, lhsT=identb, rhs=TB[:, si, h, :], start=False, stop=True, skip_group_check=True),
                             reads=["identb"], writes=[bres])
                for bnk in range((nt_ + 3) // 4):
                    c0, c1 = bnk * 4, min(nt_, bnk * 4 + 4)
                    P.op("act", lambda e, c0=c0, c1=c1, st2=st2, pt_=pt_: e.activation(
                        out=pt_[:, c0:c1, :].rearrange("p i t -> p (i t)"), in_=st2[:, c0 * 128:c1 * 128], func=AF.Exp, scale=SCALE),
                        reads=["ps%d" % (sb_ + bnk)], writes=[ptr + "b%d" % bnk])
                for i, (tok, vt, si) in enumerate(tiles):
                    P.op("pe", lambda e, i=i, vt=vt, h=h, hh=hh, ob=ob, pt_=pt_: e.matmul(
                        ob[hh * 64:(hh + 1) * 64, 0:128], lhsT=Vt[:, vt, h * 64:(h + 1) * 64], rhs=pt_[:, i, :], start=(i == 0), stop=(i == nt_ - 1), skip_group_check=True),
                        reads=["Vt0", "Vt1", ptr + "b%d" % (i // 4)], writes=[obr])
                for i, (tok, vt, si) in enumerate(tiles):
                    P.op("pe", lambda e, i=i, hh=hh, ob=ob, pt_=pt_: e.matmul(
                        ob[hh * 64:(hh + 1) * 64, 128:256], lhsT=onesb, rhs=pt_[:, i, :], start=(i == 0), stop=(i == nt_ - 1), skip_group_check=True),
                        reads=["onesb", ptr + "b%d" % (i // 4)], writes=[obr])
            P.op("dve", lambda e, ob=ob: e.reciprocal(out=rc, in_=ob[:, 128:256]), reads=[obr], writes=["rc"])
            P.op("dve", lambda e, ob=ob, ys=ys, m=m: e.tensor_tensor(out=ys[:, m, :], in0=ob[:, 0:128], in1=rc, op=ALU.mult), reads=[obr, "rc"], writes=["yst%d_%d" % (a % 2, m)])
        P.dma("pool", lambda e, a=a, ys=ys: e.dma_start(out=S["yna"][:, :, a * 128:(a + 1) * 128].rearrange("m p t -> p m t"), in_=ys),
              reads=["yst%d_%d" % (a % 2, m) for m in range(4)], writes=[("ynad", a)], sem="st_yna%d" % (a % 2), soft=["ynaall"])


KB.phaseATT = kb_phaseATT


def kb_phaseC(self, ins):
    nc, P, V = self.nc, self.P, self.V
    ps = self.psum
    S, Wd = self.S, self.W
    for nm, shp, dt_ in (("xs1", [8, 128, NLAT], F32), ("yna", [4, 128, NLAT], BF16), ("y", [32, 16, 8, 512], F32)):
        if nm + "_d_in" in self.debug:
            S[nm] = self.dram_in(nm + "_d", shp, dt_)
    o = PERS
    R = {}

    def al(n):
        nonlocal o
        r = o
        o += n
        return r
    xsA = V(al(4096), 4096, F32, "p (k t) -> p k t", k=8)
    xsB = V(al(4096), 4096, F32, "p (k t) -> p k t", k=8)
    h3T = V(al(2048), 2048, BF16, "p (k t) -> p k t", k=8)
    R["actT"] = V(al(5632), 5632, BF16, "p (j t) -> p j t", j=NJ)
    R["sg"] = [V(al(512), 512) for _ in range(2)]
    R["zn"] = [V(al(1024), 1024) for _ in range(2)]
    R["mv"] = V(al(32), 32, F32, "p (s c) -> p s c", c=8)
    R["bst"] = V(al(16), 16)
    R["wup"] = [V(al(1024), 1024, BF16, "p (k c) -> p k c", k=8) for _ in range(4)]
    R["wdn"] = [V(al(1408), 1408, BF16, "p (j c) -> p j c", j=NJ) for _ in range(3)]
    ypre = V(al(2048), 2048, F32, "p (m t) -> p m t", m=4)
    gT = V(al(2048), 2048, F32, "p (m t) -> p m t", m=4)
    gtmp = V(al(2048), 2048, F32, "p (m t) -> p m t", m=4)
    gTb = V(al(1024), 1024, BF16, "p (m t) -> p m t", m=4)
    sig = [V(al(512), 512) for _ in range(2)]
    ymix = V(al(2048), 2048, BF16, "p (k t) -> p k t", k=8)
    wout = [V(al(512), 512, BF16, "p (k c) -> p k c", k=8) for _ in range(8)]
    wglu = [V(al(256), 256, BF16, "p (k c) -> p k c", k=4) for _ in range(4)]
    G2f = V(al(1024), 1024)
    B2f = V(al(1024), 1024)
    ost = [V(al(1024), 1024) for _ in range(2)]
    assert o <= AW, o
    self.wup_cnt = 0
    self.wdn_cnt = 0
    scl = self.scl
    for i in range(8):
        P.dma("sp", lambda e, i=i: e.dma_start(out=wout[i], in_=Wd["out"][i]), reads=["Wall"], writes=["wout%d" % i])
    for i in range(4):
        P.dma("sp", lambda e, i=i: e.dma_start(out=wglu[i], in_=Wd["glu"][i]), reads=["Wall"], writes=["wglu%d" % i])
    lg, lb = ins["ln_g"], ins["ln_b"]
    P.dma("sp", lambda e: e.dma_start(out=G2f, in_=AP(lg.tensor, lg.offset + 16 * 128, [[0, 128], [1, 1024]])), writes=["G2f"])
    P.dma("sp", lambda e: e.dma_start(out=B2f, in_=AP(lb.tensor, lb.offset + 16 * 128, [[0, 128], [1, 1024]])), writes=["B2f"])
    GC = 2.0 * math.sqrt(2.0 / math.pi)
    nblk = NLAT // 512
    if "short" in self.debug:
        nblk = 1
    for blk in range(nblk):
        t0 = blk * 512
        c0 = t0 // 8
        for m in range(4):
            P.dma("sp", lambda e, m=m, c0=c0: e.dma_start(out=ypre[:, m, :].rearrange("p (t n) -> p t n", t=8),
                                                          in_=S["y"][m * 8:(m + 1) * 8, :, :, c0:c0 + 64].rearrange("g c t n -> (g c) t n")),
                  reads=["yall"], writes=["ypre%d" % m])
        for m in range(4):
            xin = ypre[:, m, :].rearrange("p (t n) -> p t n", t=8)
            xo = gT[:, m, :].rearrange("p (n t) -> p t n", t=8)
            tm = gtmp[:, m, :].rearrange("p (n t) -> p t n", t=8)
            P.op("pool", lambda e, xin=xin, xo=xo: e.tensor_copy(out=xo, in_=xin), reads=["ypre%d" % m], writes=["gT%d" % m])
            xt_ = gT[:, m, :]
            tm_ = gtmp[:, m, :]
            P.op("dve", lambda e, xt_=xt_, tm_=tm_: e.tensor_tensor(out=tm_, in0=xt_, in1=xt_, op=ALU.mult), reads=["gT%d" % m], writes=["gtmp%d" % m])
            P.op("dve", lambda e, tm_=tm_: e.tensor_scalar(out=tm_, in0=tm_, scalar1=0.044715, scalar2=1.0, op0=ALU.mult, op1=ALU.add), reads=["gtmp%d" % m], writes=["gtmp%d" % m])
            P.op("dve", lambda e, xt_=xt_, tm_=tm_: e.tensor_tensor(out=tm_, in0=tm_, in1=xt_, op=ALU.mult), reads=["gtmp%d" % m, "gT%d" % m], writes=["gtmp%d" % m])
            P.op("act", lambda e, tm_=tm_: e.activation(out=tm_, in_=tm_, func=AF.Sigmoid, scale=GC), reads=["gtmp%d" % m], writes=["gtmp%d" % m])
            P.op("dve", lambda e, xt_=xt_, tm_=tm_: e.tensor_tensor(out=xt_, in0=xt_, in1=tm_, op=ALU.mult), reads=["gtmp%d" % m, "gT%d" % m], writes=["gT%d" % m])
            P.op("pool", lambda e, xt_=xt_, m=m: e.tensor_copy(out=gTb[:, m, :], in_=xt_), reads=["gT%d" % m], writes=["gTb%d" % m])
        for mo in range(4):
            pb = self.bank(mo % 2)
            for m in range(4):
                P.op("pe", lambda e, mo=mo, m=m, pb=pb: e.matmul(pb, lhsT=wglu[mo][:, m, :], rhs=gTb[:, m, :], start=(m == 0), stop=(m == 3)),
                     reads=["wglu%d" % mo, "gTb%d" % m], writes=["ps%d" % (mo % 2)])
            sg_ = sig[mo % 2]
            P.op("act", lambda e, sg_=sg_, pb=pb, mo=mo: e.activation(out=sg_, in_=pb, func=AF.Sigmoid, bias=self.bgluT[:, mo:mo + 1], scale=1.0),
                 reads=["ps%d" % (mo % 2), "bgluT"], writes=["sig%d" % (mo % 2)])
            P.op("dve", lambda e, sg_=sg_, mo=mo: e.tensor_tensor(out=ymix[:, 4 + mo, :], in0=gT[:, mo, :], in1=sg_, op=ALU.mult),
                 reads=["sig%d" % (mo % 2), "gT%d" % mo], writes=["ymix%d" % (4 + mo)])
        for m in range(4):
            P.dma("sp", lambda e, m=m, t0=t0: e.dma_start(out=ymix[:, m, :], in_=S["yna"][m, :, t0:t0 + 512]), reads=["ynaall"], writes=["ymix%d" % m])
        for kk in range(8):
            P.dma("sp", lambda e, kk=kk, t0=t0: e.dma_start(out=xsA[:, kk, :], in_=S["xs1"][kk, :, t0:t0 + 512]), reads=["xs1all"], writes=["Cxs%d" % kk])
        for i in range(8):
            pb = self.bank(2 + i % 2)
            for kc in range(8):
                P.op("pe", lambda e, i=i, kc=kc, pb=pb: e.matmul(pb, lhsT=wout[i][:, kc, :], rhs=ymix[:, kc, :], start=(kc == 0), stop=(kc == 7)),
                     reads=["wout%d" % i, "ymix%d" % kc], writes=["ps%d" % (2 + i % 2)])
            P.op("dve", lambda e, i=i, pb=pb: e.scalar_tensor_tensor(out=xsA[:, i, :], in0=pb, scalar=scl[:, S_M5, i:i + 1], in1=xsA[:, i, :],
                                                                    op0=ALU.mult, op1=ALU.add),
                 reads=["ps%d" % (2 + i % 2), "Cxs%d" % i, "scal"], writes=["Cxs%d" % i])
        self.ln("C", xsA, 512, R, [(h3T, scl[:, S_G2S, :], scl[:, S_B2S, :], "DhT"), (xsB, scl[:, S_AG1, :], scl[:, S_AB1, :], "Dxs")])
        self.ffn("D", h3T, 512, xsB, Wd["up2"], Wd["dn2"], scl[:, S_HG8, :], R)

        def tok_cb(s, zn, znr, t0=t0):
            ot = ost[s % 2]
            P.op("dve", lambda e, ot=ot, zn=zn: e.tensor_tensor(out=ot, in0=zn, in1=G2f, op=ALU.mult), reads=[znr + "h0", znr + "h1", "G2f"], writes=["ost%d" % (s % 2)])
            P.op("pool", lambda e, ot=ot: e.tensor_tensor(out=ot, in0=ot, in1=B2f, op=ALU.add), reads=["ost%d" % (s % 2), "B2f"], writes=["ost%d" % (s % 2)])
            P.dma("pool", lambda e, ot=ot, s=s, t0=t0: e.dma_start(out=self.out[t0 + s * 128:t0 + (s + 1) * 128, :], in_=ot),
                  reads=["ost%d" % (s % 2)], writes=[("outd", t0, s)], sem="out%d" % (s % 2))
        R["tok_cb"] = tok_cb
        self.ln("D", xsB, 512, R, None)


KB.phaseC = kb_phaseC
```

```python
import contextlib
import math
import numpy as np
import concourse.bass as bass
import concourse.mybir as mybir
from concourse.bass_utils import run_bass_kernel_spmd
from concourse.ap import AP

F32 = mybir.dt.float32
BF16 = mybir.dt.bfloat16
I32 = mybir.dt.int32
AF = mybir.ActivationFunctionType
ALU = mybir.AluOpType
AX = mybir.AxisListType

ENGS = ("pe", "act", "dve", "pool", "sp")


class Op:
    __slots__ = ("eng", "fn", "reads", "writes", "pos", "is_dma", "dsem", "dcount",
                 "waits", "signal", "count", "vc")

    def __init__(self, eng, fn, reads, writes, is_dma=False, dsem=None):
        self.eng = eng
        self.fn = fn
        self.reads = tuple(reads)
        self.writes = tuple(writes)
        self.is_dma = is_dma
        self.dsem = dsem
        self.dcount = 0
        self.waits = {}
        self.signal = False
        self.count = 0
        self.vc = None


class Prog:
    def __init__(self):
        self.streams = {e: [] for e in ENGS}
        self.last_writer = {}
        self.readers = {}
        self.dma_counts = {}
        self.key2phys = {}
        self.free_phys = []
        self.free_sw = []
        self.sw_phys = set()
        self.nphys = 0
        self.final_waits = []

    def op(self, eng, fn, reads=(), writes=()):
        ex = [r for r in reads if isinstance(r, str) and r.startswith("ps") and r not in writes]
        if ex:
            writes = list(writes) + ex
        o = Op(eng, fn, reads, writes)
        self._add(o, ())
        return o

    def dma(self, eng, fn, reads=(), writes=(), sem=None, soft=()):
        if sem is None:
            sem = "d_" + str(writes[0])
        if sem not in self.key2phys:
            fl = self.free_sw if eng == "pool" else self.free_phys
            if fl:
                self.key2phys[sem] = fl.pop(0)
            else:
                self.key2phys[sem] = self.nphys
                self.nphys += 1
                if eng == "pool":
                    self.sw_phys.add(self.key2phys[sem])
        sem = self.key2phys[sem]
        assert (sem in self.sw_phys) == (eng == "pool"), "semaphore shared between SW and HW DGE"
        o = Op(eng, fn, reads, writes, is_dma=True, dsem=sem)
        self.dma_counts[sem] = self.dma_counts.get(sem, 0) + 16
        o.dcount = self.dma_counts[sem]
        self._add(o, soft)
        return o

    @staticmethod
    def _tok(o):
        if o.is_dma:
            return (o.dsem, o.dcount)
        return (o.eng, o.pos)

    def _add(self, o, soft):
        st = self.streams[o.eng]
        o.pos = len(st) + 1
        st.append(o)
        vc = dict(st[-2].vc) if len(st) > 1 else {}
        deps = []
        for r in o.reads:
            w = self.last_writer.get(r)
            if w is not None:
                deps.append(w)
        for w_ in o.writes:
            w = self.last_writer.get(w_)
            if w is not None:
                deps.append(w)
            deps.extend(self.readers.get(w_, ()))
        deps.sort(key=lambda d_: -self._tok(d_)[1])
        for d in deps:
            if d is o:
                continue
            k, v = self._tok(d)
            if (not d.is_dma) and d.eng == o.eng and o.eng == "pe":
                continue
            if vc.get(k, 0) >= v:
                continue
            if o.waits.get(k, 0) < v:
                o.waits[k] = v
            if not d.is_dma:
                d.signal = True
            for kk, vv in d.vc.items():
                if vc.get(kk, 0) < vv:
                    vc[kk] = vv
            if vc.get(k, 0) < v:
                vc[k] = v
        o.vc = vc
        for r in o.reads:
            self.readers.setdefault(r, []).append(o)
        for w_ in o.writes:
            self.last_writer[w_] = o
            self.readers[w_] = []
        for w_ in soft:
            self.last_writer[w_] = o

    def barrier(self):
        lasts = []
        for e in ENGS:
            if e == "sp":
                continue
            st = self.streams[e]
            for o_ in reversed(st):
                if not o_.is_dma:
                    lasts.append(o_)
                    break
        dmas = dict(self.dma_counts)
        for e in ENGS:
            o = Op(e, None, (), ())
            st = self.streams[e]
            o.pos = len(st) + 1
            vc = dict(st[-1].vc) if st else {}
            for d in lasts:
                if d.eng == e:
                    if e == "pe":
                        continue
                k, v = self._tok(d)
                if vc.get(k, 0) < v:
                    o.waits[k] = v
                    d.signal = True
                    vc[k] = v
            for k, v in dmas.items():
                if vc.get(k, 0) < v:
                    o.waits[k] = v
                    vc[k] = v
            o.vc = vc
            st.append(o)
        self.last_writer = {}
        self.readers = {}
        allp = set(self.free_phys) | set(self.free_sw) | set(self.key2phys.values())
        self.free_phys = sorted(p for p in allp if p not in self.sw_phys)
        self.free_sw = sorted(p for p in allp if p in self.sw_phys)
        self.key2phys = {}

    def finish_wait(self, eng, sem_keys):
        self.final_waits.append((eng, list(sem_keys)))

    def emit(self, nc):
        for e in ENGS:
            c = 0
            for o in self.streams[e]:
                if o.signal and not o.is_dma:
                    c += 1
                o.count = c
        pos2count = {e: {o.pos: o.count for o in self.streams[e]} for e in ENGS}
        with contextlib.ExitStack() as es:
            sems = {}
            for e in ENGS:
                sems[e] = es.enter_context(nc.semaphore("s_" + e))
            for k in range(self.nphys):
                sems[k] = es.enter_context(nc.semaphore("q%d" % k))
            block = es.enter_context(nc.Block())
            engmap = {"pe": "tensor", "act": "scalar", "dve": "vector", "pool": "gpsimd",
                      "sp": "sync"}

            def make(e):
                def body(engine):
                    for o in self.streams[e]:
                        for k, v in o.waits.items():
                            if k in ENGS:
                                engine.wait_ge(sems[k], pos2count[k][v])
                            else:
                                engine.wait_ge(sems[k], v)
                        if o.fn is None:
                            if o.signal:
                                engine.nop().then_inc(sems[e], 1)
                            continue
                        ins = o.fn(engine)
                        if o.is_dma:
                            ins.then_inc(sems[o.dsem], 16)
                        elif o.signal:
                            ins.then_inc(sems[e], 1)
                    for (fe, keys) in self.final_waits:
                        if fe == e:
                            for k in keys:
                                engine.wait_ge(sems[k], self.dma_counts[k])
                return body

            for e in ENGS:
                if self.streams[e] or any(fe == e for fe, _ in self.final_waits):
                    getattr(block, engmap[e])(make(e))


D = 1024
NLAT = 4096
NCTX = 256
NT = NLAT + NCTX
DFF = 2816
NJ = DFF // 128
NCH = NT // 8
ALPHA = 2.0 ** 0.25
EPS = 1e-6
AW = 51200
MASKV = -240000.0
TWO_PI = 2.0 * math.pi


class KB:
    def __init__(self, debug=None):
        self.debug = debug or ()
        self.nc = bass.Bass("TRN2", target_bir_lowering=False)
        self.P = Prog()
        self.es = contextlib.ExitStack()
        self.uid = 0

    def dram_in(self, name, shape, dt=F32):
        return self.nc.dram_tensor(name, list(shape), dt, kind="ExternalInput").ap()

    def dram_out(self, name, shape, dt=F32):
        return self.nc.dram_tensor(name, list(shape), dt, kind="ExternalOutput").ap()

    def dram_tmp(self, name, shape, dt=F32):
        if name in self.debug:
            return self.nc.dram_tensor(name, list(shape), dt, kind="ExternalOutput").ap()
        return self.nc.dram_tensor(name, list(shape), dt).ap()

    def V(self, off, n, dt=F32, pat=None, **kw):
        v = self.arena[:, off:off + n]
        if dt != F32:
            v = v.bitcast(dt)
        if pat is not None:
            v = v.rearrange(pat, **kw)
        return v

    def name(self, s):
        self.uid += 1
        return "%s%d" % (s, self.uid)


def bc_free(ap2, n):
    return AP(ap2.tensor, ap2.offset, [list(a) for a in ap2.ap] + [[0, n]])


def mk_ap(base, dims, off=0):
    return AP(base.tensor, base.offset + off, [list(base.ap[0])] + [list(d) for d in dims])


O_IDENT = 0
O_IDENTB = 128
O_ONESB = 192
O_MODT = 224
O_BADA = 368
O_LNG = 440
O_LNB = 464
O_BGLU = 488
O_M05 = 496
O_ST = 500
O_SCL = 520
O_SCC = 648
PERS = 1024
S_SC1P, S_SH0, S_HG2, S_G1S, S_B1S, S_AG0, S_AB0, S_M5, S_G2S, S_B2S, S_AG1, S_AB1, S_HG8, S_G2, S_B2 = range(15)


def kb_setup(self, ins):
    nc, P = self.nc, self.P
    self.arena = self.es.enter_context(nc.sbuf_tensor("arena", [128, AW], F32))
    self.psum = self.es.enter_context(nc.psum_tensor("psum", [128, 8, 512], F32))
    ps = self.psum
    V = self.V
    self.ident = ident = V(O_IDENT, 128)
    self.identb = identb = V(O_IDENTB, 64, BF16)
    self.onesb = onesb = V(O_ONESB, 32, BF16)
    self.modT = modT = V(O_MODT, 144, F32, "p (c v) -> p c v", v=2)
    badaT = V(O_BADA, 72)
    self.lngT = lngT = V(O_LNG, 24)
    self.lnbT = lnbT = V(O_LNB, 24)
    self.bgluT = bgluT = V(O_BGLU, 4)
    self.m05 = m05 = V(O_M05, 1)
    sT = V(O_ST, 16)
    self.scl = scl = V(O_SCL, 128, F32, "p (s k) -> p s k", k=8)
    self.scc = scc = V(O_SCC, 64, F32, "p (s k) -> p s k", k=8)

    P.op("pool", lambda e: e.memset(ident, 0.0), writes=["ident"])
    P.op("pool", lambda e: e.affine_select(out=ident, in_=ident, pattern=[[-1, 128]],
                                           compare_op=ALU.not_equal, fill=1.0, base=0,
                                           channel_multiplier=1), reads=["ident"], writes=["ident"])
    P.op("pool", lambda e: e.tensor_copy(out=identb, in_=ident), reads=["ident"], writes=["identb"])
    P.op("pool", lambda e: e.memset(onesb, 1.0), writes=["onesb"])
    P.op("pool", lambda e: e.memset(m05, -0.5), writes=["m05"])

    T0 = PERS
    st1 = V(T0, 128)
    st2 = V(T0 + 128, 128)
    P.dma("sp", lambda e: e.dma_start(out=st1[0:72, :], in_=ins["b_ada"]), writes=["st1"])
    P.dma("sp", lambda e: e.dma_start(out=st2[0:24, :], in_=ins["ln_g"]), writes=["st2a"])
    P.dma("sp", lambda e: e.dma_start(out=st2[24:48, :], in_=ins["ln_b"]), writes=["st2b"])
    P.dma("sp", lambda e: e.dma_start(out=st2[48:52, :], in_=ins["s5_b_glu"]), writes=["st2c"])
    P.dma("sp", lambda e: e.dma_start(out=st2[52:60, :], in_=ins["c"]), writes=["st2d"])
    P.dma("sp", lambda e: e.dma_start(out=st2[60:68, :], in_=ins["c_ctx"]), writes=["st2e"])
    pt = ps[:, 6, 0:256]
    P.op("pe", lambda e: e.transpose(out=pt[:, 0:72], in_=st1[0:72, :], identity=ident[0:72, 0:72]),
         reads=["st1", "ident"], writes=["psT6"])
    P.op("pe", lambda e: e.transpose(out=pt[:, 128:196], in_=st2[0:68, :], identity=ident[0:68, 0:68]),
         reads=["st2a", "st2b", "st2c", "st2d", "st2e", "ident"], writes=["psT6"])
    P.op("dve", lambda e: e.tensor_copy(out=badaT, in_=pt[:, 0:72]), reads=["psT6"], writes=["badaT"])
    P.op("dve", lambda e: e.tensor_copy(out=lngT, in_=pt[:, 128:152]), reads=["psT6"], writes=["lngT"])
    P.op("dve", lambda e: e.tensor_copy(out=lnbT, in_=pt[:, 152:176]), reads=["psT6"], writes=["lnbT"])
    P.op("dve", lambda e: e.tensor_copy(out=bgluT, in_=pt[:, 176:180]), reads=["psT6"], writes=["bgluT"])
    P.op("act", lambda e: e.activation(out=sT, in_=pt[:, 180:196], func=AF.Silu), reads=["psT6"], writes=["sT"])

    wb = [V(T0 + 256 + i * 4096, 4096, F32, "p (k c) -> p k c", k=8) for i in range(2)]
    w_ada = ins["w_ada"].rearrange("(k p) c -> p k c", p=128)
    modps = ps[:, 0, 0:144]
    for bi in range(18):
        t = wb[bi % 2]
        P.dma("sp", lambda e, t=t, bi=bi: e.dma_start(out=t, in_=w_ada[:, :, bi * 512:(bi + 1) * 512]),
              writes=["wada%d" % (bi % 2)])
        for fc in range(4):
            col = (bi * 4 + fc) * 2
            for kk in range(8):
                rhs = mk_ap(sT, [[8, 2]], off=kk)
                P.op("pe", lambda e, t=t, fc=fc, kk=kk, col=col, rhs=rhs: e.matmul(
                    modps[:, col:col + 2], lhsT=t[:, kk, fc * 128:(fc + 1) * 128], rhs=rhs,
                    start=(kk == 0), stop=(kk == 7)),
                    reads=["wada%d" % (bi % 2), "sT"], writes=["ps0"])
    P.op("dve", lambda e: e.tensor_tensor(out=modT, in0=modps.rearrange("p (c v) -> p c v", v=2),
                                          in1=bc_free(badaT, 2), op=ALU.add),
         reads=["ps0", "badaT"], writes=["modT"])

    def m(i, v):
        return modT[:, 8 * i:8 * i + 8, v]

    def G(i):
        return lngT[:, 8 * i:8 * i + 8]

    def B(i):
        return lnbT[:, 8 * i:8 * i + 8]

    def dv(fn, reads=("modT", "lngT", "lnbT", "scal")):
        P.op("dve", fn, reads=list(reads), writes=["scal"])

    for v, sc in ((0, scl), (1, scc)):
        dv(lambda e, sc=sc, v=v: e.tensor_scalar(out=sc[:, S_SC1P, :], in0=m(1, v), scalar1=1.0, scalar2=None, op0=ALU.add))
        dv(lambda e, sc=sc, v=v: e.tensor_copy(out=sc[:, S_SH0, :], in_=m(0, v)))
        dv(lambda e, sc=sc, v=v: e.tensor_scalar(out=sc[:, S_HG2, :], in0=m(2, v), scalar1=0.5, scalar2=None, op0=ALU.mult))
        dv(lambda e, sc=sc, v=v: e.tensor_scalar(out=sc[:, S_G1S, :], in0=m(4, v), scalar1=1.0, scalar2=None, op0=ALU.add))
        dv(lambda e, sc=sc, v=v: e.tensor_tensor(out=sc[:, S_B1S, :], in0=sc[:, S_G1S, :], in1=B(0), op=ALU.mult))
        dv(lambda e, sc=sc, v=v: e.tensor_tensor(out=sc[:, S_B1S, :], in0=sc[:, S_B1S, :], in1=m(3, v), op=ALU.add))
        dv(lambda e, sc=sc, v=v: e.tensor_tensor(out=sc[:, S_G1S, :], in0=sc[:, S_G1S, :], in1=G(0), op=ALU.mult))
    sc = scl
    dv(lambda e: e.tensor_scalar(out=sc[:, S_AG0, :], in0=G(0), scalar1=ALPHA, scalar2=None, op0=ALU.mult))
    dv(lambda e: e.tensor_scalar(out=sc[:, S_AB0, :], in0=B(0), scalar1=ALPHA, scalar2=None, op0=ALU.mult))
    dv(lambda e: e.tensor_copy(out=sc[:, S_M5, :], in_=m(5, 0)))
    dv(lambda e: e.tensor_scalar(out=sc[:, S_G2S, :], in0=m(7, 0), scalar1=1.0, scalar2=None, op0=ALU.add))
    dv(lambda e: e.tensor_tensor(out=sc[:, S_B2S, :], in0=sc[:, S_G2S, :], in1=B(1), op=ALU.mult))
    dv(lambda e: e.tensor_tensor(out=sc[:, S_B2S, :], in0=sc[:, S_B2S, :], in1=m(6, 0), op=ALU.add))
    dv(lambda e: e.tensor_tensor(out=sc[:, S_G2S, :], in0=sc[:, S_G2S, :], in1=G(1), op=ALU.mult))
    dv(lambda e: e.tensor_scalar(out=sc[:, S_AG1, :], in0=G(1), scalar1=ALPHA, scalar2=None, op0=ALU.mult))
    dv(lambda e: e.tensor_scalar(out=sc[:, S_AB1, :], in0=B(1), scalar1=ALPHA, scalar2=None, op0=ALU.mult))
    dv(lambda e: e.tensor_scalar(out=sc[:, S_HG8, :], in0=m(8, 0), scalar1=0.5, scalar2=None, op0=ALU.mult))
    dv(lambda e: e.tensor_copy(out=sc[:, S_G2, :], in_=G(2)))
    dv(lambda e: e.tensor_copy(out=sc[:, S_B2, :], in_=B(2)))


KB.setup = kb_setup


def kb_convert(self, ins):
    nc, P, V = self.nc, self.P, self.V
    W = self.W = {}
    W["up1"] = self.dram_tmp("wup1_d", [NJ, 128, 8, 256], BF16)
    W["up2"] = self.dram_tmp("wup2_d", [NJ, 128, 8, 256], BF16)
    W["dn1"] = self.dram_tmp("wdn1_d", [8, 128, NJ, 128], BF16)
    W["dn2"] = self.dram_tmp("wdn2_d", [8, 128, NJ, 128], BF16)
    W["in"] = self.dram_tmp("win_d", [16, 128, 8, 128], BF16)
    W["out"] = self.dram_tmp("wout_d", [8, 128, 8, 128], BF16)
    W["glu"] = self.dram_tmp("wglu_d", [4, 128, 4, 128], BF16)
    if "noconv" in self.debug:
        return
    T0 = getattr(self, "conv_off", PERS)
    CMAX = 2816
    assert T0 + 3 * CMAX <= AW
    srcb = [V(T0 + i * CMAX, CMAX) for i in range(2)]
    dstb = [V(T0 + 2 * CMAX + i * (CMAX // 2), CMAX // 2, BF16) for i in range(2)]
    self.cv_n = 0
    engs = ["act", "act", "act"]
    if "nopool" in self.debug:
        engs = ["act", "dve", "dve"]

    def conv(src, nrt, C, dst, mode):
        for rt in range(nrt):
            n = self.cv_n
            self.cv_n += 1
            sb = srcb[n % 2][:, 0:C]
            db = dstb[n % 2][:, 0:C]
            P.dma("sp", lambda e, sb=sb, rt=rt: e.dma_start(out=sb, in_=src[rt * 128:(rt + 1) * 128, :]),
                  writes=["cvs%d" % (n % 2)])
            eng = engs[n % 3]
            i_ap, o_ap = sb, db
            if eng == "act":
                P.op("act", lambda e, i_ap=i_ap, o_ap=o_ap: e.activation(out=o_ap, in_=i_ap, func=AF.Copy),
                     reads=["cvs%d" % (n % 2)], writes=["cvd%d" % (n % 2)])
            else:
                P.op(eng, lambda e, i_ap=i_ap, o_ap=o_ap: e.tensor_copy(out=o_ap, in_=i_ap),
                     reads=["cvs%d" % (n % 2)], writes=["cvd%d" % (n % 2)])
            cw = 128
            if mode == "upa":
                d_ap = dst[:, :, rt, 0:128].rearrange("m p c -> p m c")
            elif mode == "upg":
                d_ap = dst[:, :, rt, 128:256].rearrange("m p c -> p m c")
            else:
                d_ap = dst[:, :, rt, :].rearrange("m p c -> p m c")
            s_ap = db.rearrange("p (m c) -> p m c", c=cw)
            P.dma("sp", lambda e, d_ap=d_ap, s_ap=s_ap: e.dma_start(out=d_ap, in_=s_ap),
                  reads=["cvd%d" % (n % 2)], writes=[("W", id(dst), n)], sem="wconv%d" % (n % 2), soft=["Wall"])

    conv(ins["ffn1_w_up"][:, 0:DFF], 8, DFF, W["up1"], "upa")
    conv(ins["ffn1_w_up"][:, DFF:2 * DFF], 8, DFF, W["up1"], "upg")
    if "cv1" in self.debug:
        return
    conv(ins["ffn1_w_down"], NJ, 1024, W["dn1"], "plain")
    if "cv2" in self.debug:
        return
    conv(ins["w_in"], 8, 2048, W["in"], "plain")
    if "cv3" in self.debug:
        return
    conv(ins["w_out"], 8, 1024, W["out"], "plain")
    if "cv4" in self.debug:
        return
    conv(ins["s5_w_glu"], 4, 512, W["glu"], "plain")
    if "cv5" in self.debug:
        return
    conv(ins["ffn2_w_up"][:, 0:DFF], 8, DFF, W["up2"], "upa")
    conv(ins["ffn2_w_up"][:, DFF:2 * DFF], 8, DFF, W["up2"], "upg")
    conv(ins["ffn2_w_down"], NJ, 1024, W["dn2"], "plain")


KB.convert = kb_convert


def kb_bank(self, b, n=512):
    return self.psum[:, b, 0:n]


def kb_ffn(self, key, hT, nb, xsT, wup_d, wdn_d, hg, R):
    P = self.P
    actT = R["actT"]
    for j in range(NJ):
        slot = self.wup_cnt % 4
        self.wup_cnt += 1
        wt = R["wup"][slot]
        P.dma("sp", lambda e, wt=wt, j=j: e.dma_start(out=wt, in_=wup_d[j]), reads=["Wall"],
              writes=["wup%d" % slot])
        pa = self.bank(j % 2, nb)
        pg = self.bank(2 + j % 2, nb)
        for kk in range(8):
            P.op("pe", lambda e, wt=wt, kk=kk, pa=pa: e.matmul(pa, lhsT=wt[:, kk, 0:128], rhs=hT[:, kk, :],
                                                              start=(kk == 0), stop=(kk == 7)),
                 reads=["wup%d" % slot, key + "hT%d" % kk], writes=["ps%d" % (j % 2)])
        for kk in range(8):
            P.op("pe", lambda e, wt=wt, kk=kk, pg=pg: e.matmul(pg, lhsT=wt[:, kk, 128:256], rhs=hT[:, kk, :],
                                                              start=(kk == 0), stop=(kk == 7)),
                 reads=["wup%d" % slot, key + "hT%d" % kk], writes=["ps%d" % (2 + j % 2)])
        sgb = R["sg"][j % 2][:, 0:nb]
        P.op("act", lambda e, sgb=sgb, pg=pg: e.activation(out=sgb, in_=pg, func=AF.Silu),
             reads=["ps%d" % (2 + j % 2)], writes=["sg%d" % (j % 2)])
        P.op("dve", lambda e, sgb=sgb, pa=pa, j=j: e.tensor_tensor(out=actT[:, j, 0:nb], in0=pa, in1=sgb, op=ALU.mult),
             reads=["ps%d" % (j % 2), "sg%d" % (j % 2)], writes=["actT%d" % j])
    for i in range(8):
        slot = self.wdn_cnt % 3
        self.wdn_cnt += 1
        wd = R["wdn"][slot]
        P.dma("sp", lambda e, wd=wd, i=i: e.dma_start(out=wd, in_=wdn_d[i]), reads=["Wall"],
              writes=["wdn%d" % slot])
        po = self.bank(4 + i % 2, nb)
        for j in range(NJ):
            P.op("pe", lambda e, wd=wd, j=j, po=po: e.matmul(po, lhsT=wd[:, j, :], rhs=actT[:, j, 0:nb],
                                                            start=(j == 0), stop=(j == NJ - 1)),
                 reads=["wdn%d" % slot, "actT%d" % j], writes=["ps%d" % (4 + i % 2)])
        P.op("dve", lambda e, po=po, i=i: e.scalar_tensor_tensor(out=xsT[:, i, :], in0=po, scalar=hg[:, i:i + 1],
                                                                in1=xsT[:, i, :], op0=ALU.mult, op1=ALU.add),
             reads=["ps%d" % (4 + i % 2), key + "xs%d" % i, "scal"], writes=[key + "xs%d" % i])


def kb_ln(self, key, zT, nb, R, outs):
    P = self.P
    S = nb // 128
    mv = R["mv"]
    FW = [((0, 1), ("ps0", "ps1")), ((2, 3), ("ps2", "ps3"))]
    BK = [((4, 5), ("ps4", "ps5")), ((6, 7), ("psT6", "psT7"))]
    for s in range(S):
        zn = R["zn"][s % 2]
        znr = "zn%d" % (s % 2)
        (fa, _), fr = FW[s % 2]
        (ba, _), br = BK[s % 2]
        psF = self.psum[:, fa:fa + 2, :].rearrange("p a b -> p (a b)")
        psB = self.psum[:, ba:ba + 2, :].rearrange("p a b -> p (a b)")
        for kk in range(8):
            P.op("pe", lambda e, kk=kk, s=s, psF=psF: e.transpose(out=psF[:, kk * 128:(kk + 1) * 128],
                                                                  in_=zT[:, kk, s * 128:(s + 1) * 128], identity=self.ident),
                 reads=[key + "xs%d" % kk, "ident"], writes=[fr[kk // 4]])
        st = R["bst"][s % 2]
        sr = "bst%d" % (s % 2)
        P.op("dve", lambda e, st=st, psF=psF: e.bn_stats(out=st[:, 0:6], in_=psF[:, 0:512]), reads=[fr[0]], writes=[sr])
        P.op("dve", lambda e, st=st, psF=psF: e.bn_stats(out=st[:, 6:12], in_=psF[:, 512:1024]), reads=[fr[1]], writes=[sr])
        mr = "mv%d" % s
        P.op("dve", lambda e, st=st, s=s: e.bn_aggr(out=mv[:, s, 0:2], in_=st[:, 0:12]), reads=[sr], writes=[mr])
        P.op("dve", lambda e, s=s: e.tensor_scalar(out=mv[:, s, 2:3], in0=mv[:, s, 1:2], scalar1=EPS, scalar2=None, op0=ALU.add),
             reads=[mr], writes=[mr])
        P.op("pool", lambda e, s=s: e.tensor_tensor(out=mv[:, s, 3:4], in0=mv[:, s, 2:3], in1=self.m05, op=ALU.pow),
             reads=[mr, "m05"], writes=[mr])
        P.op("dve", lambda e, s=s: e.tensor_scalar(out=mv[:, s, 4:5], in0=mv[:, s, 0:1], scalar1=mv[:, s, 3:4], scalar2=-1.0,
                                                   op0=ALU.mult, op1=ALU.mult), reads=[mr], writes=[mr])
        for hb in range(2):
            P.op("act", lambda e, s=s, zn=zn, hb=hb, psF=psF: e.activation(out=zn[:, hb * 512:(hb + 1) * 512], in_=psF[:, hb * 512:(hb + 1) * 512],
                                                                          func=AF.Identity, scale=mv[:, s, 3:4], bias=mv[:, s, 4:5]),
                 reads=[fr[hb], mr], writes=[znr + "h%d" % hb])
        if outs is None:
            R["tok_cb"](s, zn, znr)
            continue
        for kk in range(8):
            P.op("pe", lambda e, kk=kk, zn=zn, psB=psB: e.transpose(out=psB[:, kk * 128:(kk + 1) * 128],
                                                                    in_=zn[:, kk * 128:(kk + 1) * 128], identity=self.ident),
                 reads=[znr + "h%d" % (kk // 4), "ident"], writes=[br[kk // 4]])
        for (dst, gcol, bcol, wres) in outs:
            for kk in range(8):
                o_ap = dst[:, kk, s * 128:(s + 1) * 128]
                i_ap = psB[:, kk * 128:(kk + 1) * 128]
                pr = br[kk // 4]
                g1 = gcol[:, kk:kk + 1]
                b1 = bcol[:, kk:kk + 1]
                if kk < 4:
                    P.op("act", lambda e, o_ap=o_ap, i_ap=i_ap, g1=g1, b1=b1: e.activation(
                        out=o_ap, in_=i_ap, func=AF.Identity, scale=g1, bias=b1),
                        reads=[pr, "scal"], writes=[wres + "%d" % kk])
                else:
                    P.op("dve", lambda e, o_ap=o_ap, i_ap=i_ap, g1=g1, b1=b1: e.tensor_scalar(
                        out=o_ap, in0=i_ap, scalar1=g1, scalar2=b1, op0=ALU.mult, op1=ALU.add),
                        reads=[pr, "scal"], writes=[wres + "%d" % kk])


KB.bank = kb_bank
KB.ffn = kb_ffn
KB.ln = kb_ln


def kb_phaseA(self, ins):
    nc, P, V = self.nc, self.P, self.V
    ps = self.psum
    S = self.S = {}
    S["xs1"] = self.dram_tmp("xs1_d", [8, 128, NLAT], F32)
    S["q"] = self.dram_tmp("q_d", [4, 128, NLAT], BF16)
    S["k"] = self.dram_tmp("k_d", [4, 128, NT], BF16)
    S["v"] = self.dram_tmp("v_d", [NT, 512], BF16)
    S["u"] = self.dram_tmp("u_d", [32, 16, 8, NCH], BF16)
    o = self.s5["o_mark"] if hasattr(self, "s5") else PERS
    R = {}

    def al(n):
        nonlocal o
        r = o
        o += n
        return r
    xtok = [V(al(1024), 1024) for _ in range(2)]
    xsT = V(al(4096), 4096, F32, "p (k t) -> p k t", k=8)
    hT = V(al(2048), 2048, BF16, "p (k t) -> p k t", k=8)
    R["actT"] = V(al(5632), 5632, BF16, "p (j t) -> p j t", j=NJ)
    R["sg"] = [V(al(512), 512) for _ in range(2)]
    R["zn"] = [V(al(1024), 1024) for _ in range(2)]
    xs1st = V(al(4096), 4096, F32, "p (k t) -> p k t", k=8)
    h2T = V(al(2048), 2048, BF16, "p (k t) -> p k t", k=8)
    qkst = V(al(2048), 2048, BF16, "p (m t) -> p m t", m=8)
    vst = [V(al(256), 256, BF16) for _ in range(2)]
    ust = V(al(1024), 1024, BF16, "p (m t) -> p m t", m=4)
    R["mv"] = V(al(32), 32, F32, "p (s c) -> p s c", c=8)
    R["bst"] = [V(al(16), 16) for _ in range(2)]
    R["wup"] = [V(al(1024), 1024, BF16, "p (k c) -> p k c", k=8) for _ in range(4)]
    R["wdn"] = [V(al(1408), 1408, BF16, "p (j c) -> p j c", j=NJ) for _ in range(3)]
    wint = [V(al(512), 512, BF16, "p (k c) -> p k c", k=8) for _ in range(4)]
    wv = V(al(2048), 2048, BF16, "p (k m c) -> p k m c", k=8, m=4)
    assert o <= AW, o
    self.wup_cnt = 0
    self.wdn_cnt = 0
    win_cnt = 0
    psT = ps[:, 6:8, :].rearrange("p a b -> p (a b)")
    Wd = self.W
    for m in range(4):
        P.dma("sp", lambda e, m=m: e.dma_start(out=wv[:, :, m, :], in_=Wd["in"][8 + m]), reads=["Wall"], writes=["wv%d" % m])

    if "A0" in self.debug:
        return
    blocks = [(0, NCTX, True)] + [(NCTX + i * 512, 512, False) for i in range(NLAT // 512)]
    if "short" in self.debug:
        blocks = blocks[:2]
    for (t0, nb, isctx) in blocks:
        sc = self.scc if isctx else self.scl
        Sn = nb // 128
        src = ins["ctx"] if isctx else ins["x"]
        r0 = t0 if isctx else t0 - NCTX
        for s in range(Sn):
            xt = xtok[s % 2]
            if s % 2 == 0:
                psT = ps[:, 6:8, :].rearrange("p a b -> p (a b)")
                xbr = ("psT6", "psT7")
            else:
                psT = ps[:, 4:6, :].rearrange("p a b -> p (a b)")
                xbr = ("ps4", "ps5")
            x_ap = src[r0 + s * 128:r0 + (s + 1) * 128, :]
            P.dma("sp", lambda e, xt=xt, x_ap=x_ap: e.dma_start(out=xt, in_=x_ap),
                  writes=["xtok%d" % (s % 2)])
            if "L1" in self.debug:
                continue
            for kk in range(8):
                P.op("pe", lambda e, kk=kk, xt=xt, psT=psT: e.transpose(out=psT[:, kk * 128:(kk + 1) * 128],
                                                                        in_=xt[:, kk * 128:(kk + 1) * 128], identity=self.ident),
                     reads=["xtok%d" % (s % 2), "ident"], writes=[xbr[kk // 4]])
            if "L2" in self.debug:
                continue
            for kk in range(8):
                i_ap = psT[:, kk * 128:(kk + 1) * 128]
                pr = xbr[kk // 4]
                o1 = xsT[:, kk, s * 128:(s + 1) * 128]
                o2 = hT[:, kk, s * 128:(s + 1) * 128]
                s1 = sc[:, S_SC1P, kk:kk + 1]
                s2 = sc[:, S_SH0, kk:kk + 1]
                if kk < 4:
                    P.op("act", lambda e, i_ap=i_ap, o1=o1: e.activation(out=o1, in_=i_ap, func=AF.Identity, scale=ALPHA),
                         reads=[pr], writes=["Axs%d" % kk])
                    P.op("act", lambda e, i_ap=i_ap, o2=o2, s1=s1, s2=s2: e.activation(out=o2, in_=i_ap, func=AF.Identity, scale=s1, bias=s2),
                         reads=[pr, "scal"], writes=["AhT%d" % kk])
                else:
                    P.op("dve", lambda e, i_ap=i_ap, o1=o1: e.tensor_scalar(out=o1, in0=i_ap, scalar1=ALPHA, scalar2=None, op0=ALU.mult),
                         reads=[pr], writes=["Axs%d" % kk])
                    P.op("dve", lambda e, i_ap=i_ap, o2=o2, s1=s1, s2=s2: e.tensor_scalar(out=o2, in0=i_ap, scalar1=s1, scalar2=s2, op0=ALU.mult, op1=ALU.add),
                         reads=[pr, "scal"], writes=["AhT%d" % kk])
        if "A1" in self.debug:
            return
        self.ffn("A", hT[:, :, 0:nb], nb, xsT[:, :, 0:nb], Wd["up1"], Wd["dn1"], sc[:, S_HG2, :], R)
        if "A2" in self.debug:
            return
        outs = [(h2T, sc[:, S_G1S, :], sc[:, S_B1S, :], "Ah2T")]
        if not isctx:
            outs.append((xs1st, sc[:, S_AG0, :], sc[:, S_AB0, :], "Axs1"))
        self.ln("A", xsT, nb, R, outs)
        if not isctx:
            for kk in range(8):
                P.dma("pool", lambda e, kk=kk, r0=r0, nb=nb: e.dma_start(out=S["xs1"][kk, :, r0:r0 + nb], in_=xs1st[:, kk, 0:nb]),
                      reads=["Axs1%d" % kk], writes=[("xs1d", t0, kk)], sem="st_xs1_%d" % kk, soft=["xs1all"])
        if "A3" in self.debug:
            return
        for mi, m in enumerate([0, 1, 2, 3, 4, 5, 6, 7, 12, 13, 14, 15]):
            if isctx and m < 4:
                continue
            slot = win_cnt % 4
            win_cnt += 1
            wt = wint[slot]
            P.dma("sp", lambda e, wt=wt, m=m: e.dma_start(out=wt, in_=Wd["in"][m]), reads=["Wall"], writes=["win%d" % slot])
            pb = self.bank(mi % 4, nb)
            for kk in range(8):
                P.op("pe", lambda e, wt=wt, kk=kk, pb=pb, nb=nb: e.matmul(pb, lhsT=wt[:, kk, :], rhs=h2T[:, kk, 0:nb],
                                                                  start=(kk == 0), stop=(kk == 7)),
                     reads=["win%d" % slot, "Ah2T%d" % kk], writes=["ps%d" % (mi % 4)])
            if m < 8:
                eng = ("act", "dve")[mi % 2]
                o_ap = qkst[:, m, 0:nb]
                if eng == "act":
                    P.op("act", lambda e, o_ap=o_ap, pb=pb: e.activation(out=o_ap, in_=pb, func=AF.Copy),
                         reads=["ps%d" % (mi % 4)], writes=["qkst%d" % m])
                else:
                    P.op("dve", lambda e, o_ap=o_ap, pb=pb: e.tensor_copy(out=o_ap, in_=pb),
                         reads=["ps%d" % (mi % 4)], writes=["qkst%d" % m])
                if m < 4:
                    P.dma("pool", lambda e, m=m, r0=r0, o_ap=o_ap, nb=nb: e.dma_start(out=S["q"][m, :, r0:r0 + nb], in_=o_ap),
                          reads=["qkst%d" % m], writes=[("qd", t0, m)], sem="st_qk%d" % m, soft=["qall"])
                else:
                    P.dma("pool", lambda e, m=m, t0=t0, o_ap=o_ap, nb=nb: e.dma_start(out=S["k"][m - 4, :, t0:t0 + nb], in_=o_ap),
                          reads=["qkst%d" % m], writes=[("kd", t0, m)], sem="st_qk%d" % m, soft=["kall"])
            else:
                mu = m - 12
                nc8 = nb // 8
                i_ap = pb.rearrange("p (c s) -> p s c", s=8)
                o_ap = ust[:, mu, 0:nb].rearrange("p (s c) -> p s c", s=8)
                P.op("dve", lambda e, o_ap=o_ap, i_ap=i_ap: e.tensor_copy(out=o_ap, in_=i_ap),
                     reads=["ps%d" % (mi % 4)], writes=["ust%d" % mu])
                c0 = t0 // 8
                d_ap = S["u"][mu * 8:(mu + 1) * 8, :, :, c0:c0 + nc8].rearrange("g i s c -> (g i) s c")
                P.dma("pool", lambda e, d_ap=d_ap, o_ap=o_ap: e.dma_start(out=d_ap, in_=o_ap),
                      reads=["ust%d" % mu], writes=[("ud", t0, mu)], sem="st_u%d" % mu, soft=["uall"])
        for s in range(Sn):
            pv = self.bank(4 + s % 2, 512)
            for kk in range(8):
                P.op("pe", lambda e, kk=kk, s=s, pv=pv: e.matmul(pv, lhsT=h2T[:, kk, s * 128:(s + 1) * 128],
                                                                rhs=wv[:, kk, :, :].rearrange("p m c -> p (m c)"),
                                                                start=(kk == 0), stop=(kk == 7)),
                     reads=["wv0", "wv1", "wv2", "wv3", "Ah2T%d" % kk], writes=["ps%d" % (4 + s % 2)])
            vs = vst[s % 2]
            P.op("act", lambda e, vs=vs, pv=pv: e.activation(out=vs, in_=pv, func=AF.Copy),
                 reads=["ps%d" % (4 + s % 2)], writes=["vst%d" % (s % 2)])
            P.dma("pool", lambda e, vs=vs, s=s, t0=t0: e.dma_start(out=S["v"][t0 + s * 128:t0 + (s + 1) * 128, :], in_=vs),
                  reads=["vst%d" % (s % 2)], writes=[("vd", t0, s)], sem="st_v%d" % (s % 2), soft=["vall"])


KB.phaseA = kb_phaseA


IN_SHAPES = {
    "x": [NLAT, D], "c": [8, 128], "ctx": [NCTX, D], "c_ctx": [8, 128],
    "w_ada": [D, 9 * D], "b_ada": [72, 128], "ln_g": [24, 128], "ln_b": [24, 128],
    "ffn1_w_up": [D, 2 * DFF], "ffn1_w_down": [DFF, D], "w_in": [D, 2048],
    "na_rpb": [8, 15, 31], "s5_a_re": [2, 32, 64], "s5_a_im": [2, 32, 64], "s5_log_dt": [2, 32],
    "s5_b_re": [2, 32, 64, 16], "s5_b_im": [2, 32, 64, 16], "s5_c_re": [2, 32, 16, 64],
    "s5_c_im": [2, 32, 16, 64], "s5_d": [32, 16], "s5_w_glu": [512, 512], "s5_b_glu": [4, 128],
    "w_out": [D, D], "ffn2_w_up": [D, 2 * DFF], "ffn2_w_down": [DFF, D],
}


def build_nc(debug=None, stages=("setup", "convert", "A", "S5", "ATT", "C")):
    kb = KB(debug)
    ins = {k: kb.dram_in(k, shp) for k, shp in IN_SHAPES.items()}
    out = kb.dram_out("out", [NLAT, D])
    kb.out = out
    with kb.es:
        kb.setup(ins)
        kb.P.barrier()
        kb.conv_off = AW - 3 * 2816
        if "S5" in stages:
            kb.phaseS5(ins, mid_cb=(lambda: kb.convert(ins)) if "convert" in stages else None)
            assert kb.s5["o_end"] <= kb.conv_off, kb.s5["o_end"]
        elif "convert" in stages:
            kb.convert(ins)
        kb.P.barrier()
        if "A" in stages:
            kb.phaseA(ins)
            kb.P.barrier()
        if "S5" in stages:
            kb.phaseS5main(ins)
            kb.P.barrier()
        if "ATT" in stages:
            kb.phaseATT(ins)
            kb.P.barrier()
        if "C" in stages:
            kb.phaseC(ins)
        if "dump_s5" in kb.debug:
            s5 = kb.s5
            for nm, apx, dt_ in (("dbg_W1re", s5["W1b"][0], BF16), ("dbg_W1im", s5["W1b"][1], BF16), ("dbg_T0", s5["T0b"], BF16),
                                 ("dbg_W2re", s5["W2b"][0], BF16), ("dbg_W2im", s5["W2b"][1], BF16)):
                dm = kb.dram_out(nm, [128, 32, 128], dt_)
                kb.P.dma("sp", lambda e, dm=dm, apx=apx: e.dma_start(out=dm, in_=apx), reads=["W1b", "W2b", "T0b", "s5"], writes=[nm], sem="out")
            for nm, apx in (("dbg_rho8", s5["rho8"]), ("dbg_wre", s5["w_re"]), ("dbg_wim", s5["w_im"])):
                dm = kb.dram_out(nm, [128, 32], F32)
                kb.P.dma("sp", lambda e, dm=dm, apx=apx: e.dma_start(out=dm, in_=apx), reads=["s5"], writes=[nm], sem="out")
        if "dump_mod" in kb.debug:
            dm = kb.dram_out("dbg_mod", [128, 144])
            kb.P.dma("sp", lambda e: e.dma_start(out=dm, in_=kb.V(O_MODT, 144)), reads=["modT"], writes=["dbgmod"], sem="out")
        if "C" not in stages:
            z = kb.V(AW - 1024, 1024)
            kb.P.op("pool", lambda e: e.memset(z, 0.0), writes=["zz"])
            kb.P.dma("sp", lambda e: e.dma_start(out=out[0:128, :], in_=z), reads=["zz"], writes=["outz"], sem="out")
        kb.P.finish_wait("sp", list(range(kb.P.nphys)))
        kb.P.emit(kb.nc)
    return kb.nc


def make_in_maps(inputs):
    f = lambda a: np.ascontiguousarray(np.asarray(a, dtype=np.float32))
    shared = {}
    for k in IN_SHAPES:
        if k in ("x", "c", "ctx"):
            continue
        a = f(inputs[k])
        if a.ndim >= 1 and a.shape[0] == 1 and k not in ("c_ctx",):
            a = a[0]
        shared[k] = np.ascontiguousarray(a.reshape(IN_SHAPES[k]))
    maps = []
    for b in range(8):
        m = dict(shared)
        m["x"] = f(inputs["x"][b])
        m["ctx"] = f(inputs["ctx"][b])
        m["c"] = f(inputs["c"][b]).reshape(8, 128)
        maps.append(m)
    return maps


_NC_CACHE = {}


def kernel(**inputs):
    if "nc" not in _NC_CACHE:
        _NC_CACHE["nc"] = build_nc()
    nc = _NC_CACHE["nc"]
    maps = make_in_maps(inputs)
    res = run_bass_kernel_spmd(nc, maps, core_ids=list(range(8)))
    out = np.stack([np.asarray(r["out"], dtype=np.float32) for r in res.results], axis=0)
    return out


def kb_phaseS5(self, ins, mid_cb=None):
    nc, P, V = self.nc, self.P, self.V
    ps = self.psum
    if not hasattr(self, "S"):
        self.S = {}
    S = self.S
    tab_d = self.dram_tmp("tab_d", [32, 128, 2, NCH], F32)
    o = PERS

    def al(n):
        nonlocal o
        r = o
        o += n
        return r
    ident = self.ident
    L = "s5"

    def dv(fn, eng="dve", extra_r=(), extra_w=()):
        P.op(eng, fn, reads=[L] + list(extra_r), writes=[L] + list(extra_w))

    def TT(out, a, b, op, eng="dve"):
        dv(lambda e: e.tensor_tensor(out=out, in0=a, in1=b, op=op), eng)

    def TS(out, a, s1, op0, s2=None, op1=None, eng="dve"):
        if op1 is None:
            dv(lambda e: e.tensor_scalar(out=out, in0=a, scalar1=s1, scalar2=None, op0=op0), eng)
        else:
            dv(lambda e: e.tensor_scalar(out=out, in0=a, scalar1=s1, scalar2=s2, op0=op0, op1=op1), eng)

    def ACT(out, a, func, scale=1.0, bias=0.0):
        dv(lambda e: e.activation(out=out, in_=a, func=func, scale=scale, bias=bias), "act")

    W1b = [V(al(2048), 2048, BF16, "p (k c) -> p k c", k=32) for _ in range(2)]
    T0b = V(al(2048), 2048, BF16, "p (g c) -> p g c", g=32)
    W2b = [V(al(2048), 2048, BF16, "p (k c) -> p k c", k=32) for _ in range(2)]
    rho8 = V(al(32), 32)
    o_mark = o
    st = V(al(384), 384)
    A = V(al(32 * 12), 32 * 12, F32, "p (n k) -> p n k", k=32)
    a_re, a_im, ldt, dtt, x1, er, th, y_, fr, cs, sn, tmp = [A[:, i, :] for i in range(12)]
    A2 = V(al(32 * 12), 32 * 12, F32, "p (n k) -> p n k", k=32)
    ab_re, ab_im, k_re, k_im, den, t1, t2, t3, w_re, w_im, iv_re, iv_im = [A2[:, i, :] for i in range(12)]
    yi = V(al(32), 32).bitcast(I32)
    PaR = V(al(512), 512, F32, "p (k j) -> p k j", j=16)
    PaI = V(al(512), 512, F32, "p (k j) -> p k j", j=16)
    Bre = V(al(512), 512, F32, "p (k c) -> p k c", c=16)
    Bim = V(al(512), 512, F32, "p (k c) -> p k c", c=16)
    Bbre = V(al(512), 512, F32, "p (k c) -> p k c", c=16)
    Bbim = V(al(512), 512, F32, "p (k c) -> p k c", c=16)
    Cre = V(al(512), 512, F32, "p (k c) -> p k c", c=16)
    Cim = V(al(512), 512, F32, "p (k c) -> p k c", c=16)
    tB = V(al(512), 512, F32, "p (k c) -> p k c", c=16)
    cst = V(al(128), 128)
    dcol = V(al(32), 32)
    maskt = [V(al(512), 512, F32, "p (r c) -> p r c", r=4) for _ in range(2)]
    identt = V(al(512), 512, F32, "p (r c) -> p r c", r=4)

    are_v = ins["s5_a_re"].rearrange("d (k m) p -> (d k) (m p)", m=2)
    aim_v = ins["s5_a_im"].rearrange("d (k m) p -> (d k) (m p)", m=2)
    P.dma("sp", lambda e: e.dma_start(out=st[0:32, 0:128], in_=are_v), writes=["s5st0"])
    P.dma("sp", lambda e: e.dma_start(out=st[0:32, 128:256], in_=aim_v), writes=["s5st1"])
    ldt2 = V(al(2), 2)
    P.dma("sp", lambda e: e.dma_start(out=ldt2[0:32, :], in_=ins["s5_log_dt"].rearrange("d (k m) -> (d k) m", m=2)), writes=["s5st2"])
    P.op("dve", lambda e: e.tensor_copy(out=st[0:32, 256:384].rearrange("p (m q) -> p m q", m=2), in_=bc_free(ldt2[0:32, :], 64)),
         reads=["s5st2", L], writes=[L])
    for mi in range(2):
        bre_v = ins["s5_b_re"].rearrange("d (k m) p c -> m p (d k) c", m=2)
        bim_v = ins["s5_b_im"].rearrange("d (k m) p c -> m p (d k) c", m=2)
        P.dma("sp", lambda e, mi=mi, bre_v=bre_v: e.dma_start(out=Bre[mi * 64:(mi + 1) * 64], in_=bre_v[mi]), writes=["s5b%d" % mi])
        P.dma("sp", lambda e, mi=mi, bim_v=bim_v: e.dma_start(out=Bim[mi * 64:(mi + 1) * 64], in_=bim_v[mi]), writes=["s5bi%d" % mi])
    dst16 = V(al(16), 16)
    dst128 = V(al(128), 128)
    P.dma("sp", lambda e: e.dma_start(out=dst16[0:32, :], in_=ins["s5_d"]), writes=["s5d16"])
    P.op("dve", lambda e: e.tensor_copy(out=dst128[0:32, :].rearrange("p (s c) -> p s c", s=8),
                                        in_=AP(dst16.tensor, dst16.offset, [[dst16.ap[0][0], 32], [0, 8], [1, 16]])),
         reads=["s5d16", L], writes=[L])
    pt = ps[:, 6, :]
    P.op("pe", lambda e: e.transpose(out=pt[:, 256:288], in_=dst128[0:32, :], identity=ident[0:32, 0:32]),
         reads=[L, "ident"], writes=["psT6"])
    P.op("dve", lambda e: e.tensor_copy(out=dcol, in_=pt[:, 256:288]), reads=["psT6", L], writes=[L])
    for i in range(3):
        P.op("pe", lambda e, i=i: e.transpose(out=pt[:, i * 32:(i + 1) * 32], in_=st[0:32, i * 128:(i + 1) * 128], identity=ident[0:32, 0:32]),
             reads=["s5st0", "s5st1", L, "ident"], writes=["psT6"])
    P.op("dve", lambda e: e.tensor_copy(out=A[:, 0:3, :], in_=pt[:, 0:96].rearrange("p (n k) -> p n k", n=3)), reads=["psT6", L], writes=[L])
    for ri, (src, dst) in enumerate(((ins["s5_c_re"], Cre), (ins["s5_c_im"], Cim))):
        cv = src.rearrange("d (k m) c p -> d k c m p", m=2)
        for d_ in range(2):
            for j in range(2):
                for kl in range(8):
                    P.dma("sp", lambda e, cv=cv, d_=d_, j=j, kl=kl: e.dma_start(
                        out=cst[kl * 16:(kl + 1) * 16, :].rearrange("p (m q) -> p m q", m=2), in_=cv[d_, 8 * j + kl]),
                        reads=[L], writes=["s5cst%d" % kl])
                P.op("pe", lambda e: e.transpose(out=pt[:, 128:256], in_=cst, identity=ident),
                     reads=["s5cst%d" % kl for kl in range(8)] + ["ident"], writes=["psT6"])
                dk0 = d_ * 16 + 8 * j
                P.op("dve", lambda e, dst=dst, dk0=dk0: e.tensor_copy(out=dst[:, dk0:dk0 + 8, :].rearrange("p k c -> p (k c)"), in_=pt[:, 128:256]),
                     reads=["psT6", L], writes=[L] + ["s5cst%d" % kl for kl in range(8)])
    ACT(dtt, ldt, AF.Exp)
    TT(x1, a_re, dtt, ALU.mult)
    ACT(er, x1, AF.Exp)
    ACT(rho8, x1, AF.Exp, scale=8.0)
    TT(th, a_im, dtt, ALU.mult)
    for (dstt, off) in ((sn, 0.5), (cs, 0.75)):
        TS(y_, th, 1.0 / TWO_PI, ALU.mult, off, ALU.add)
        dv(lambda e: e.tensor_copy(out=yi, in_=y_))
        dv(lambda e: e.tensor_copy(out=fr, in_=yi))
        TT(fr, y_, fr, ALU.subtract)
        TS(tmp, fr, 0.0, ALU.is_lt)
        TT(fr, fr, tmp, ALU.add)
        TS(tmp, fr, 1.0, ALU.is_ge)
        TT(fr, fr, tmp, ALU.subtract)
        TS(fr, fr, TWO_PI, ALU.mult, -math.pi, ALU.add)
        TS(fr, fr, math.pi, ALU.min, -math.pi, ALU.max)
        ACT(dstt, fr, AF.Sin)
    if mid_cb is not None:
        mid_cb()
    TT(ab_re, er, cs, ALU.mult)
    TT(ab_im, er, sn, ALU.mult)
    TS(t1, ab_re, -1.0, ALU.add)
    TT(den, a_re, a_re, ALU.mult)
    TT(t2, a_im, a_im, ALU.mult)
    TT(den, den, t2, ALU.add)
    dv(lambda e: e.reciprocal(out=den, in_=den))
    TT(k_re, t1, a_re, ALU.mult)
    TT(t2, ab_im, a_im, ALU.mult)
    TT(k_re, k_re, t2, ALU.add)
    TT(k_re, k_re, den, ALU.mult)
    TT(k_im, ab_im, a_re, ALU.mult)
    TT(t2, t1, a_im, ALU.mult)
    TT(k_im, k_im, t2, ALU.subtract)
    TT(k_im, k_im, den, ALU.mult)
    TT(t1, er, er, ALU.mult)
    dv(lambda e: e.reciprocal(out=t1, in_=t1))
    TT(iv_re, ab_re, t1, ALU.mult)
    TT(iv_im, ab_im, t1, ALU.mult)
    TS(iv_im, iv_im, -1.0, ALU.mult)

    def cmul(o_re, o_im, x_re, x_im, y_re, y_im):
        TT(t1, x_re, y_re, ALU.mult)
        TT(t2, x_im, y_im, ALU.mult)
        TT(t3, x_re, y_im, ALU.mult)
        TT(o_im, x_im, y_re, ALU.mult)
        TT(o_im, o_im, t3, ALU.add)
        TT(o_re, t1, t2, ALU.subtract)
    dv(lambda e: e.memset(PaR[:, :, 7], 1.0))
    dv(lambda e: e.memset(PaI[:, :, 7], 0.0))
    dv(lambda e: e.tensor_copy(out=PaR[:, :, 8], in_=ab_re))
    dv(lambda e: e.tensor_copy(out=PaI[:, :, 8], in_=ab_im))
    for jj in range(9, 16):
        cmul(PaR[:, :, jj], PaI[:, :, jj], PaR[:, :, jj - 1], PaI[:, :, jj - 1], ab_re, ab_im)
    for jj in range(6, -1, -1):
        cmul(PaR[:, :, jj], PaI[:, :, jj], PaR[:, :, jj + 1], PaI[:, :, jj + 1], iv_re, iv_im)
    dv(lambda e: e.reciprocal(out=t1, in_=rho8))
    TT(w_re, PaR[:, :, 15], t1, ALU.mult)
    TT(w_im, PaI[:, :, 15], t1, ALU.mult)
    TS(w_im, w_im, -1.0, ALU.mult)
    for (o_, x_, kx, y_b, ky, op) in ((Bbre, Bre, k_re, Bim, k_im, ALU.subtract), (Bbim, Bim, k_re, Bre, k_im, ALU.add)):
        dv(lambda e, o_=o_, x_=x_, kx=kx: e.tensor_tensor(out=o_, in0=x_, in1=bc_free(kx, 16), op=ALU.mult),
           extra_r=["s5b0", "s5b1", "s5bi0", "s5bi1"])
        dv(lambda e, y_b=y_b, ky=ky: e.tensor_tensor(out=tB, in0=y_b, in1=bc_free(ky, 16), op=ALU.mult))
        dv(lambda e, o_=o_, op=op: e.tensor_tensor(out=o_, in0=o_, in1=tB, op=op))
    for d_ in range(2):
        dv(lambda e, d_=d_: e.memset(maskt[d_], 1.0), "pool")
        if d_ == 0:
            pat, base, cm = [[0, 4], [16, 8], [0, 16]], 15, -1
        else:
            pat, base, cm = [[0, 4], [-16, 8], [0, 16]], 0, 1
        dv(lambda e, d_=d_, pat=pat, base=base, cm=cm: e.affine_select(
            out=maskt[d_], in_=maskt[d_], pattern=pat, compare_op=ALU.is_ge, fill=0.0, base=base, channel_multiplier=cm), "pool")
    dv(lambda e: e.tensor_copy(out=identt, in_=AP(ident.tensor, ident.offset, [list(ident.ap[0]), [0, 4], [1, 128]])), "pool", extra_r=["ident"])

    o_fm = o
    Fm = [V(al(2048), 2048, F32, "p (k s c) -> p k s c", k=16, s=8) for _ in range(2)]
    W2m = [V(al(2048), 2048, F32, "p (k t c) -> p k t c", k=16, t=8) for _ in range(2)]
    W2p = [V(al(2048), 2048, F32, "p (k t c) -> p k t c", k=16, t=8) for _ in range(2)]
    tF = V(al(2048), 2048, F32, "p (k s c) -> p k s c", k=16, s=8)
    T0acc = V(al(4096), 4096, F32, "p (g c) -> p g c", g=32)
    tM = V(al(512), 512)

    def pw(Pt, dk0, j0, jstep):
        b = Pt[:, dk0:dk0 + 16, :]
        return AP(b.tensor, b.offset + j0, [list(b.ap[0]), [16, 16], [jstep, 8], [0, 16]])

    def bx(Bt, dk0):
        b = Bt[:, dk0:dk0 + 16, :]
        return AP(b.tensor, b.offset, [list(b.ap[0]), [16, 16], [0, 8], [1, 16]])

    def outer(o_re, o_im, dk0, j0, jstep, Xre, Xim, neg_im):
        dv(lambda e: e.tensor_tensor(out=o_re, in0=pw(PaR, dk0, j0, jstep), in1=bx(Xre, dk0), op=ALU.mult))
        dv(lambda e: e.tensor_tensor(out=tF, in0=pw(PaI, dk0, j0, jstep), in1=bx(Xim, dk0), op=ALU.mult))
        dv(lambda e: e.tensor_tensor(out=o_re, in0=o_re, in1=tF, op=ALU.subtract))
        dv(lambda e: e.tensor_tensor(out=o_im, in0=pw(PaR, dk0, j0, jstep), in1=bx(Xim, dk0), op=ALU.mult))
        dv(lambda e: e.tensor_tensor(out=tF, in0=pw(PaI, dk0, j0, jstep), in1=bx(Xre, dk0), op=ALU.mult))
        dv(lambda e: e.tensor_tensor(out=o_im, in0=o_im, in1=tF, op=ALU.add))
        if neg_im:
            dv(lambda e: e.tensor_scalar(out=o_im, in0=o_im, scalar1=-1.0, scalar2=None, op0=ALU.mult))

    for d_ in range(2):
        dk0 = d_ * 16
        if d_ == 0:
            outer(Fm[0], Fm[1], dk0, 14, -1, Bbre, Bbim, False)
            outer(W2m[0], W2m[1], dk0, 8, 1, Cre, Cim, True)
            outer(W2p[0], W2p[1], dk0, 0, 1, Cre, Cim, True)
        else:
            outer(Fm[0], Fm[1], dk0, 7, 1, Bbre, Bbim, False)
            outer(W2m[0], W2m[1], dk0, 15, -1, Cre, Cim, True)
            outer(W2p[0], W2p[1], dk0, 7, -1, Cre, Cim, True)
        for r in range(2):
            dv(lambda e, r=r, dk0=dk0: e.tensor_copy(out=W2b[r][:, dk0:dk0 + 16, :].rearrange("p k (c t) -> p k t c", t=8), in_=W2m[r]),
               "pool", extra_w=["W2b"])
        for r in range(2):
            for kb4 in range(4):
                bank = ps[:, kb4 % 2, :]
                for q in range(4):
                    k_ = kb4 * 4 + q
                    P.op("pe", lambda e, r=r, k_=k_, q=q, bank=bank: e.transpose(
                        out=bank[:, q * 128:(q + 1) * 128], in_=Fm[r][:, k_, :, :].rearrange("p s c -> p (s c)"), identity=ident),
                        reads=[L, "ident"], writes=["ps%d" % (kb4 % 2)])
                P.op("dve", lambda e, r=r, kb4=kb4, dk0=dk0, bank=bank: e.tensor_copy(
                    out=W1b[r][:, dk0 + kb4 * 4:dk0 + kb4 * 4 + 4, :].rearrange("p k c -> p (k c)"), in_=bank),
                    reads=["ps%d" % (kb4 % 2)], writes=["W1b"])
        for gb in range(8):
            mi = gb // 4
            bank = ps[:, 2 + gb % 2, :]
            g0 = 2 * 4 * (gb % 4) + mi
            for q in range(4):
                k_ = 4 * (gb % 4) + q
                for r in range(2):
                    P.op("pe", lambda e, r=r, k_=k_, mi=mi, q=q, bank=bank: e.matmul(
                        bank[:, q * 128:(q + 1) * 128],
                        lhsT=Fm[r][mi * 64:(mi + 1) * 64, k_, :, :].rearrange("p s c -> p (s c)"),
                        rhs=W2p[r][mi * 64:(mi + 1) * 64, k_, :, :].rearrange("p t c -> p (t c)"),
                        start=(r == 0), stop=(r == 1)),
                        reads=[L], writes=["ps%d" % (2 + gb % 2)])
            tb0 = T0acc[:, g0, :]
            acc = AP(tb0.tensor, tb0.offset, [list(tb0.ap[0]), [256, 4], [1, 128]])
            bk = bank.rearrange("p (r c) -> p r c", r=4)
            tM3 = tM.rearrange("p (r c) -> p r c", r=4)
            if d_ == 0:
                P.op("dve", lambda e, acc=acc, bk=bk: e.tensor_tensor(out=acc, in0=bk, in1=maskt[0], op=ALU.mult),
                     reads=["ps%d" % (2 + gb % 2), L], writes=["T0acc"])
            else:
                P.op("dve", lambda e, bk=bk, tM3=tM3: e.tensor_tensor(out=tM3, in0=bk, in1=maskt[1], op=ALU.mult),
                     reads=["ps%d" % (2 + gb % 2), L, "T0acc"], writes=["tM"])
                P.op("dve", lambda e, acc=acc, tM3=tM3: e.tensor_tensor(out=acc, in0=acc, in1=tM3, op=ALU.add),
                     reads=["tM", "T0acc"], writes=["T0acc"])
                dc0 = dcol[:, g0:g0 + 1]
                dcb = AP(dc0.tensor, dc0.offset, [list(dc0.ap[0]), [2, 4], [0, 128]])
                P.op("dve", lambda e, dcb=dcb, tM3=tM3: e.tensor_tensor(out=tM3, in0=identt, in1=dcb, op=ALU.mult),
                     reads=["tM", L], writes=["tM"])
                t0b = T0b[:, g0, :]
                o4 = AP(t0b.tensor, t0b.offset, [list(t0b.ap[0]), [256, 4], [1, 8], [8, 16]])
                a4 = AP(tb0.tensor, tb0.offset, [list(tb0.ap[0]), [256, 4], [16, 8], [1, 16]])
                P.op("dve", lambda e, o4=o4, a4=a4: e.tensor_tensor(out=o4, in0=a4, in1=tM.rearrange("p (r t c) -> p r t c", r=4, t=8), op=ALU.add),
                     reads=["tM", "T0acc"], writes=["T0b", "tM"])
        P.op("dve", lambda e: e.memset(tM[:, 0:1], 0.0), reads=["ps0", "ps1", "ps2", "ps3", "tM", L], writes=[L, "tM"])

    P.op("dve", lambda e: e.memset(tM[:, 0:1], 0.0), reads=["ps0", "ps1", "ps2", "ps3", "tM", "T0acc", "T0b", "W1b", "W2b", L], writes=[L, "tM", "T0acc"])
    o = o_fm
    wp = [V(al(64), 64, F32, "p (r k) -> p r k", r=2) for _ in range(10)]
    dv(lambda e: e.tensor_copy(out=wp[0][:, 0, :], in_=w_re))
    dv(lambda e: e.tensor_copy(out=wp[0][:, 1, :], in_=w_im))
    for q in range(1, 10):
        a_, b_ = wp[q - 1][:, 0, :], wp[q - 1][:, 1, :]
        TT(t1, a_, a_, ALU.mult)
        TT(t2, b_, b_, ALU.mult)
        TT(wp[q][:, 0, :], t1, t2, ALU.subtract)
        TT(t1, a_, b_, ALU.mult)
        TS(wp[q][:, 1, :], t1, 2.0, ALU.mult)
    TBL = V(al(8 * 2 * NCH), 8 * 2 * NCH, F32, "p (k r n) -> p k r n", k=8, r=2)
    tq = [V(al(2048), 2048) for _ in range(2)]
    for bt in range(4):
        k0 = bt * 8
        dv(lambda e: e.memset(TBL[:, :, 0, 0:1], 1.0))
        dv(lambda e: e.memset(TBL[:, :, 1, 0:1], 0.0))
        dv(lambda e, k0=k0: e.tensor_copy(out=TBL[:, :, 0, 1], in_=wp[0][:, 0, k0:k0 + 8]))
        dv(lambda e, k0=k0: e.tensor_copy(out=TBL[:, :, 1, 1], in_=wp[0][:, 1, k0:k0 + 8]))
        for q in range(1, 10):
            sz = 2 ** q
            n = min(sz, NCH - sz)
            lo_r, lo_i = TBL[:, :, 0, 0:n], TBL[:, :, 1, 0:n]
            hi_r, hi_i = TBL[:, :, 0, sz:sz + n], TBL[:, :, 1, sz:sz + n]
            wr = bc_free(wp[q][:, 0, k0:k0 + 8], n)
            wi = bc_free(wp[q][:, 1, k0:k0 + 8], n)
            u1 = tq[0][:, 0:8 * n].rearrange("p (k n) -> p k n", k=8)
            u2 = tq[1][:, 0:8 * n].rearrange("p (k n) -> p k n", k=8)
            TT(u1, lo_r, wr, ALU.mult)
            TT(u2, lo_i, wi, ALU.mult)
            TT(hi_r, u1, u2, ALU.subtract)
            TT(u1, lo_r, wi, ALU.mult)
            TT(u2, lo_i, wr, ALU.mult)
            TT(hi_i, u1, u2, ALU.add)
        P.dma("pool", lambda e, k0=k0: e.dma_start(out=tab_d[k0:k0 + 8].rearrange("k p r n -> p k r n"), in_=TBL),
              reads=[L], writes=[("tabd", bt)], sem="st_tab", soft=["taball"])
        dv(lambda e: e.memset(tq[0][:, 0:1], 0.0), extra_r=[("tabd", bt)])
    self.s5 = dict(W1b=W1b, T0b=T0b, W2b=W2b, rho8=rho8, w_re=w_re, w_im=w_im, tab_d=tab_d, o_mark=o_mark, o_end=o, L=L)


def kb_phaseS5main(self, ins):
    nc, P, V = self.nc, self.P, self.V
    ps = self.psum
    S = self.S
    s5 = self.s5
    W1b, T0b, W2b, rho8, tab_d, o_mark = s5["W1b"], s5["T0b"], s5["W2b"], s5["rho8"], s5["tab_d"], s5["o_mark"]
    if "u_in" in self.debug:
        S["u"] = self.dram_in("u_d", [32, 16, 8, NCH], BF16)
    S["y"] = self.dram_tmp("y_d", [32, 16, 8, 512], F32)

    def al(n):
        nonlocal o
        r = o
        o += n
        return r
    o = o_mark
    U = V(al(32 * NCH // 2), 32 * NCH // 2, BF16, "p (g c) -> p g c", g=32)
    Ur = V(al(32 * NCH // 2), 32 * NCH // 2, BF16, "p (g c) -> p g c", g=32)
    tb = [V(al(2 * NCH), 2 * NCH, F32, "p (r n) -> p r n", r=2) for _ in range(2)]
    Gin = V(al(2 * NCH), 2 * NCH, F32, "p (r n) -> p r n", r=2)
    Gs = V(al(2 * NCH), 2 * NCH, F32, "p (r n) -> p r n", r=2)
    tt_ = [V(al(NCH), NCH) for _ in range(4)]
    Hb = [[V(al(256), 256, BF16) for _ in range(2)] for _ in range(2)]
    ysb = [V(al(512), 512) for _ in range(2)]
    assert o <= AW, o
    for s_ in range(8):
        P.dma("sp", lambda e, s_=s_: e.dma_start(out=U[16 * s_:16 * s_ + 16], in_=S["u"][:, :, s_, :].rearrange("g i c -> i g c")),
              reads=["uall"], writes=["U%d" % s_])
    Ures = ["U%d" % s_ for s_ in range(8)]
    Ub = U[:, :, :]
    P.op("pool", lambda e: e.tensor_copy(out=Ur[:, :, 0:32], in_=AP(Ub.tensor, Ub.offset + 31, [list(Ub.ap[0]), [NCH, 32], [-1, 32]])),
         reads=Ures, writes=["Ur0"])
    P.op("pool", lambda e: e.tensor_copy(out=Ur[:, :, 32:NCH], in_=AP(Ub.tensor, Ub.offset + NCH - 1, [list(Ub.ap[0]), [NCH, 32], [-1, 512]])),
         reads=Ures, writes=["Ur1"])
    for k in range(16):
        for d_ in range(2):
            dk = d_ * 16 + k
            T = tb[dk % 2]
            tr = "tb%d" % (dk % 2)
            P.dma("sp", lambda e, T=T, dk=dk: e.dma_start(out=T, in_=tab_d[dk]), reads=["taball"], writes=[tr])
            Us = U if d_ == 0 else Ur
            ures = Ures if d_ == 0 else ["Ur0", "Ur1"]
            for mi in range(2):
                g = 2 * k + mi
                for r in range(2):
                    lh = W1b[r][:, dk, mi * 64:(mi + 1) * 64]
                    P.op("pe", lambda e, lh=lh, r=r, mi=mi, g=g, Us=Us: e.matmul(ps[mi * 64:(mi + 1) * 64, r, :], lhsT=lh, rhs=Us[:, g, 0:512], start=True, stop=True),
                         reads=ures + ["W1b"], writes=["ps%d" % r])
                    P.op("pe", lambda e, lh=lh, r=r, mi=mi, g=g, Us=Us: e.matmul(ps[mi * 64:(mi + 1) * 64, 2, r * 32:(r + 1) * 32], lhsT=lh, rhs=Us[:, g, 512:NCH], start=True, stop=True),
                         reads=ures + ["W1b"], writes=["ps2"])
            for (c0, c1, sre, sim, pr) in ((0, 512, ps[:, 0, :], ps[:, 1, :], ["ps0", "ps1"]), (512, NCH, ps[:, 2, 0:32], ps[:, 2, 32:64], ["ps2"])):
                TRs, TIs = T[:, 0, c0:c1], T[:, 1, c0:c1]
                t1_, t2_, t3_, t4_ = [x[:, c0:c1] for x in tt_]
                P.op("dve", lambda e, t1_=t1_, sre=sre, TRs=TRs: e.tensor_tensor(out=t1_, in0=sre, in1=TRs, op=ALU.mult), reads=pr + [tr], writes=["tt0"])
                P.op("dve", lambda e, t2_=t2_, sim=sim, TIs=TIs: e.tensor_tensor(out=t2_, in0=sim, in1=TIs, op=ALU.mult), reads=pr + [tr], writes=["tt1"])
                P.op("dve", lambda e, t3_=t3_, sim=sim, TRs=TRs: e.tensor_tensor(out=t3_, in0=sim, in1=TRs, op=ALU.mult), reads=pr + [tr], writes=["tt2"])
                P.op("dve", lambda e, t4_=t4_, sre=sre, TIs=TIs: e.tensor_tensor(out=t4_, in0=sre, in1=TIs, op=ALU.mult), reads=pr + [tr], writes=["tt3"])
            P.op("pool", lambda e: e.tensor_tensor(out=Gin[:, 0, :], in0=tt_[0], in1=tt_[1], op=ALU.subtract), reads=["tt0", "tt1"], writes=["Gin0"])
            P.op("pool", lambda e: e.tensor_tensor(out=Gin[:, 1, :], in0=tt_[2], in1=tt_[3], op=ALU.add), reads=["tt2", "tt3"], writes=["Gin1"])
            rb = bc_free(rho8[:, dk:dk + 1], NCH).rearrange("p a n -> p (a n)") if False else AP(rho8.tensor, rho8.offset + dk, [list(rho8.ap[0]), [0, NCH]])
            for r in range(2):
                P.op("dve", lambda e, r=r, rb=rb: e.tensor_tensor_scan(out=Gs[:, r, :], data0=rb, data1=Gin[:, r, :], initial=0.0, op0=ALU.mult, op1=ALU.add),
                     reads=["Gin%d" % r, "s5"], writes=["Gs%d" % r])
            TRs, TIs = T[:, 0, 31:543], T[:, 1, 31:543]
            gr, gi = Gs[:, 0, 31:543], Gs[:, 1, 31:543]
            u1, u2, u3, u4 = [x[:, 0:512] for x in tt_]
            P.op("pool", lambda e, TRs=TRs, gr=gr, u1=u1: e.tensor_tensor(out=u1, in0=TRs, in1=gr, op=ALU.mult), reads=[tr, "Gs0", "Gin0", "Gin1"], writes=["tt0"])
            P.op("pool", lambda e, TIs=TIs, gi=gi, u2=u2: e.tensor_tensor(out=u2, in0=TIs, in1=gi, op=ALU.mult), reads=[tr, "Gs1", "Gin0", "Gin1"], writes=["tt1"])
            P.op("pool", lambda e, TRs=TRs, gi=gi, u3=u3: e.tensor_tensor(out=u3, in0=TRs, in1=gi, op=ALU.mult), reads=[tr, "Gs1", "Gin0", "Gin1"], writes=["tt2"])
            P.op("pool", lambda e, TIs=TIs, gr=gr, u4=u4: e.tensor_tensor(out=u4, in0=TIs, in1=gr, op=ALU.mult), reads=[tr, "Gs0", "Gin0", "Gin1"], writes=["tt3"])
            outs_ = []
            for r in range(2):
                hb = Hb[d_][r]
                if d_ == 0:
                    outs_.append(hb)
                else:
                    outs_.append(AP(hb.tensor, hb.offset + 511, [list(hb.ap[0]), [-1, 512]]))
            P.op("pool", lambda e, o_=outs_[0], u1=u1, u2=u2: e.tensor_tensor(out=o_, in0=u1, in1=u2, op=ALU.add), reads=["tt0", "tt1"], writes=["Hb%d0" % d_])
            P.op("pool", lambda e, o_=outs_[1], u3=u3, u4=u4: e.tensor_tensor(out=o_, in0=u3, in1=u4, op=ALU.subtract), reads=["tt2", "tt3"], writes=["Hb%d1" % d_])
        for mi in range(2):
            g = 2 * k + mi
            yb = ps[:, 4 + g % 2, :]
            yr = "ps%d" % (4 + g % 2)
            P.op("pe", lambda e, g=g, yb=yb: e.matmul(yb, lhsT=T0b[:, g, :], rhs=U[:, g, 32:NCH], start=True, stop=False),
                 reads=Ures + ["T0b"], writes=[yr])
            n_ = 0
            for d_ in range(2):
                for r in range(2):
                    n_ += 1
                    lh = W2b[r][mi * 64:(mi + 1) * 64, d_ * 16 + k, :]
                    rh = Hb[d_][r][mi * 64:(mi + 1) * 64, :]
                    P.op("pe", lambda e, lh=lh, rh=rh, yb=yb, n_=n_: e.matmul(yb, lhsT=lh, rhs=rh, start=False, stop=(n_ == 4)),
                         reads=["W2b", "Hb%d%d" % (d_, r)], writes=[yr])
            yt = ysb[g % 2]
            P.op("act", lambda e, yt=yt, yb=yb: e.activation(out=yt, in_=yb, func=AF.Copy), reads=[yr], writes=["ysb%d" % (g % 2)])
            P.dma("pool", lambda e, yt=yt, g=g: e.dma_start(out=S["y"][g].rearrange("c t n -> (c t) n"), in_=yt),
                  reads=["ysb%d" % (g % 2)], writes=[("yd", g)], sem="st_y%d" % (g % 2), soft=["yall"])


KB.phaseS5 = kb_phaseS5
KB.phaseS5main = kb_phaseS5main


MASK_E = -4000.0


def att_plan():
    sets = {}
    plan = []
    for a in range(32):
        rows = (2 * a, 2 * a + 1)
        r0 = [min(max(r - 4, 0), 56) for r in rows]
        lo, hi = min(r0), max(r0) + 7
        tiles = []
        for kt in range(lo // 2, hi // 2 + 1):
            ent = []
            for il in range(2):
                for rl in range(2):
                    i_abs = 2 * kt + il
                    ok = r0[rl] <= i_abs <= r0[rl] + 7
                    ent.append((i_abs - rows[rl] + 7) if ok else None)
            key = tuple(ent)
            if key not in sets:
                sets[key] = len(sets)
            tiles.append((kt, sets[key]))
        plan.append(tiles)
    return plan, sets


def kb_phaseATT(self, ins):
    nc, P, V = self.nc, self.P, self.V
    ps = self.psum
    if not hasattr(self, "S"):
        self.S = {}
    S = self.S
    for nm, shp in (("q", [4, 128, NLAT]), ("k", [4, 128, NT]), ("v", [NT, 512])):
        if nm + "_d_in" in self.debug:
            S[nm] = self.dram_in(nm + "_d", shp, BF16)
    S["yna"] = self.dram_tmp("yna_d", [4, 128, NLAT], BF16)
    tt_d = self.dram_tmp("tt_d", [8, 15, 64, 64], BF16)
    ttm_d = self.dram_tmp("ttm_d", [64, 8, 64], BF16)
    plan, sets = att_plan()
    NS = len(sets)
    o = PERS

    def al(n):
        nonlocal o
        r = o
        o += n
        return r
    ident, identb, onesb = self.ident, self.identb, self.onesb
    kT = V(al(4 * NT // 2), 4 * NT // 2, BF16, "p (m t) -> p m t", m=4)
    Vt = V(al(34 * 256), 34 * 256, BF16, "p (k f) -> p k f", k=34)
    TB = V(al(NS * 8 * 64), NS * 8 * 64, BF16, "p (s h c) -> p s h c", s=NS, h=8)
    qT = [V(al(256), 256, BF16, "p (m t) -> p m t", m=4) for _ in range(2)]
    PT = [V(al(448), 448, BF16, "p (i t) -> p i t", i=7) for _ in range(2)]
    rc = V(al(128), 128)
    yst = [V(al(256), 256, BF16, "p (m t) -> p m t", m=4) for _ in range(2)]
    o_tmp = o
    for m in range(4):
        P.dma("sp", lambda e, m=m: e.dma_start(out=kT[:, m, :], in_=S["k"][m]), reads=["kall"], writes=["kT%d" % m])
    vv = S["v"].rearrange("(k p) f -> p k f", p=128)
    for i in range(2):
        P.dma("sp", lambda e, i=i: e.dma_start(out=Vt[:, i * 17:(i + 1) * 17, :], in_=vv[:, i * 17:(i + 1) * 17, :]), reads=["vall"], writes=["Vt%d" % i])
    L = "att"
    stg = V(al(32), 32)
    Lm = V(al(120), 120)
    E = V(al(4096), 4096, F32, "p (j c) -> p j c", j=64)
    E2 = V(al(4096), 4096, F32, "p (j c) -> p j c", j=64)
    E3 = V(al(4096), 4096, F32, "p (j c) -> p j c", j=64)
    rsel = V(al(2), 2)
    ttb = [V(al(256), 256, BF16) for _ in range(2)]
    mtile = V(al(256), 256, BF16)
    assert o <= AW, o

    def dv(fn, eng="dve", extra_r=(), extra_w=()):
        P.op(eng, fn, reads=[L] + list(extra_r), writes=[L] + list(extra_w))
    dv(lambda e: e.memset(stg, 1.0), "pool")
    P.dma("sp", lambda e: e.dma_start(out=stg[0:120, 0:31], in_=ins["na_rpb"].rearrange("h a b -> (h a) b")), reads=[L], writes=["attstg"])
    P.op("pe", lambda e: e.transpose(out=ps[0:32, 6, 0:120], in_=stg[0:120, :], identity=ident[0:120, 0:120]), reads=["attstg", L, "ident"], writes=["psT6"])
    P.op("dve", lambda e: e.tensor_copy(out=Lm[0:32, :], in_=ps[0:32, 6, 0:120]), reads=["psT6", L], writes=[L])
    E_, E2_, E3_ = E[0:32], E2[0:32], E3[0:32]
    dv(lambda e: e.iota(E_, pattern=[[1, 64], [-1, 64]], base=15, channel_multiplier=-1, allow_small_or_imprecise_dtypes=True), "pool")
    dv(lambda e: e.tensor_scalar(out=E_, in0=E_, scalar1=0.0, scalar2=None, op0=ALU.is_equal))
    dv(lambda e: e.iota(E2_, pattern=[[0, 64], [1, 64]], base=0, channel_multiplier=0, allow_small_or_imprecise_dtypes=True), "pool")
    dv(lambda e: e.tensor_scalar(out=E2_, in0=E2_, scalar1=-8.0, scalar2=0.0, op0=ALU.add, op1=ALU.max))
    dv(lambda e: e.tensor_scalar(out=E2_, in0=E2_, scalar1=48.0, scalar2=None, op0=ALU.min))
    dv(lambda e: e.iota(E3_, pattern=[[1, 64], [0, 64]], base=0, channel_multiplier=0, allow_small_or_imprecise_dtypes=True), "pool")
    dv(lambda e: e.tensor_tensor(out=E3_, in0=E3_, in1=E2_, op=ALU.subtract))
    dv(lambda e: e.tensor_scalar(out=E2_, in0=E3_, scalar1=0.0, scalar2=None, op0=ALU.is_ge))
    dv(lambda e: e.tensor_scalar(out=E3_, in0=E3_, scalar1=15.0, scalar2=None, op0=ALU.is_le))
    dv(lambda e: e.tensor_tensor(out=E2_, in0=E2_, in1=E3_, op=ALU.mult))
    dv(lambda e: e.tensor_tensor(out=E_, in0=E_, in1=E2_, op=ALU.mult))
    dv(lambda e: e.tensor_scalar(out=E2_, in0=E2_, scalar1=-MASK_E, scalar2=MASK_E, op0=ALU.mult, op1=ALU.add))
    rs = rsel[0:32, :]
    dv(lambda e: e.iota(rs[:, 0:1], pattern=[[0, 1]], base=0, channel_multiplier=1, allow_small_or_imprecise_dtypes=True), "pool")
    dv(lambda e: e.tensor_scalar(out=rs[:, 0:1], in0=rs[:, 0:1], scalar1=31.0, scalar2=None, op0=ALU.is_lt))
    dv(lambda e: e.tensor_scalar(out=rs[:, 1:2], in0=rs[:, 0:1], scalar1=-1.0, scalar2=1.0, op0=ALU.mult, op1=ALU.add))
    dv(lambda e: e.tensor_scalar(out=E_, in0=E_, scalar1=rs[:, 0:1], scalar2=None, op0=ALU.mult))
    dv(lambda e: e.scalar_tensor_tensor(out=E_, in0=E2_, scalar=rs[:, 1:2], in1=E_, op0=ALU.mult, op1=ALU.add))
    Ef = E_.rearrange("p j c -> p (j c)")
    ttv = tt_d.rearrange("h a j c -> (h a) (j c)")
    for i in range(8):
        bank = ps[0:120, i % 2, :]
        P.op("pe", lambda e, i=i, bank=bank: e.matmul(bank, lhsT=Lm[0:32, :], rhs=Ef[:, i * 512:(i + 1) * 512], start=True, stop=True),
             reads=[L], writes=["ps%d" % (i % 2)])
        tb_ = ttb[i % 2][0:120, :]
        P.op("act", lambda e, tb_=tb_, bank=bank: e.activation(out=tb_, in_=bank, func=AF.Identity, scale=8.0), reads=["ps%d" % (i % 2)], writes=["ttb%d" % (i % 2)])
        P.dma("sp", lambda e, i=i, tb_=tb_: e.dma_start(out=ttv[:, i * 512:(i + 1) * 512], in_=tb_), reads=["ttb%d" % (i % 2)], writes=[("ttd", i)],
              sem="st_tt%d" % (i % 2), soft=["ttall"])
    dv(lambda e: e.memset(mtile, 8.0 * MASK_E), "pool")
    msrc = AP(mtile.tensor, mtile.offset, [[mtile.ap[0][0], 64], [1, 256]])
    P.dma("sp", lambda e: e.dma_start(out=ttm_d.rearrange("j h c -> j (h c)")[:, 0:256], in_=msrc),
          reads=[L], writes=["ttm_a"], sem="st_ttm")
    P.dma("sp", lambda e: e.dma_start(out=ttm_d.rearrange("j h c -> j (h c)")[:, 256:512], in_=msrc),
          reads=[L], writes=["ttm_b"], sem="st_ttm2")
    self.P.barrier()
    for key, si in sets.items():
        n_ = 0
        for il in range(2):
            for rl in range(2):
                di = key[il * 2 + rl]
                dst = TB[il * 64:(il + 1) * 64, si, :, rl * 64:(rl + 1) * 64]
                if di is None:
                    src = ttm_d
                else:
                    src = tt_d[:, di, :, :].rearrange("h j c -> j h c")
                P.dma("sp", lambda e, dst=dst, src=src: e.dma_start(out=dst, in_=src), writes=[("TB", si, il, rl)], sem="ld_tb%d" % (n_ % 4), soft=["TBall"])
                n_ += 1
    self.P.barrier()

    SCALE = 0.125
    tasks = []
    for a in range(32):
        tiles = [(256 + kt * 128, 2 + kt, si) for (kt, si) in plan[a]] + [(0, 0, None), (128, 1, None)]
        for m in range(4):
            for hh in range(2):
                tasks.append(dict(a=a, m=m, hh=hh, h=2 * m + hh, tiles=tiles, n=len(tasks)))

    def emit_q(a):
        qt = qT[a % 2]
        P.dma("sp", lambda e, a=a, qt=qt: e.dma_start(out=qt, in_=S["q"][:, :, a * 128:(a + 1) * 128].rearrange("m p t -> p m t")),
              reads=["qall"], writes=["qT%d" % (a % 2)])

    def emit_qk(t):
        a, m, hh, h, tiles, n = t["a"], t["m"], t["hh"], t["h"], t["tiles"], t["n"]
        qt = qT[a % 2]
        qr = "qT%d" % (a % 2)
        sb_ = (n % 2) * 2
        st2 = ps[:, sb_:sb_ + 2, :].rearrange("p a b -> p (a b)")
        pt_ = PT[n % 2]
        ptr = "PT%d" % (n % 2)
        nt_ = len(tiles)
        for i, (tok, vt, si) in enumerate(tiles):
            bres = "ps%d" % (sb_ + i // 4)
            o_ap = st2[:, i * 128:(i + 1) * 128]
            P.op("pe", lambda e, o_ap=o_ap, hh=hh, m=m, tok=tok, qt=qt, si=si: e.matmul(
                o_ap, lhsT=kT[hh * 64:(hh + 1) * 64, m, tok:tok + 128], rhs=qt[hh * 64:(hh + 1) * 64, m, :],
                start=True, stop=(si is None), skip_group_check=True), reads=["kT%d" % m, qr], writes=[bres])
            if si is not None:
                P.op("pe", lambda e, o_ap=o_ap, si=si, h=h: e.matmul(o_ap, lhsT=identb, rhs=TB[:, si, h, :], start=False, stop=True, skip_group_check=True),
                     reads=["identb"], writes=[bres])
        for bnk in range((nt_ + 3) // 4):
            c0, c1 = bnk * 4, min(nt_, bnk * 4 + 4)
            P.op("act", lambda e, c0=c0, c1=c1, st2=st2, pt_=pt_: e.activation(
                out=pt_[:, c0:c1, :].rearrange("p i t -> p (i t)"), in_=st2[:, c0 * 128:c1 * 128], func=AF.Exp, scale=SCALE),
                reads=["ps%d" % (sb_ + bnk)], writes=[ptr + "b%d" % bnk])

    def emit_pv(t):
        a, m, hh, h, tiles, n = t["a"], t["m"], t["hh"], t["h"], t["tiles"], t["n"]
        pt_ = PT[n % 2]
        ptr = "PT%d" % (n % 2)
        nt_ = len(tiles)
        ob = ps[:, 4 + (a * 4 + m) % 2, :]
        obr = "ps%d" % (4 + (a * 4 + m) % 2)
        for i, (tok, vt, si) in enumerate(tiles):
            P.op("pe", lambda e, i=i, vt=vt, h=h, hh=hh, ob=ob, pt_=pt_: e.matmul(
                ob[hh * 64:(hh + 1) * 64, 0:128], lhsT=Vt[:, vt, h * 64:(h + 1) * 64], rhs=pt_[:, i, :], start=(i == 0), stop=(i == nt_ - 1), skip_group_check=True),
                reads=["Vt0", "Vt1", ptr + "b%d" % (i // 4)], writes=[obr])
        for i, (tok, vt, si) in enumerate(tiles):
            P.op("pe", lambda e, i=i, hh=hh, ob=ob, pt_=pt_: e.matmul(
                ob[hh * 64:(hh + 1) * 64, 128:256], lhsT=onesb, rhs=pt_[:, i, :], start=(i == 0), stop=(i == nt_ - 1), skip_group_check=True),
                reads=["onesb", ptr + "b%d" % (i // 4)], writes=[obr])
        if hh == 1:
            ys = yst[a % 2]
            P.op("dve", lambda e, ob=ob: e.reciprocal(out=rc, in_=ob[:, 128:256]), reads=[obr], writes=["rc"])
            P.op("dve", lambda e, ob=ob, ys=ys, m=m: e.tensor_tensor(out=ys[:, m, :], in0=ob[:, 0:128], in1=rc, op=ALU.mult), reads=[obr, "rc"], writes=["yst%d_%d" % (a % 2, m)])
            if m == 3:
                P.dma("pool", lambda e, a=a, ys=ys: e.dma_start(out=S["yna"][:, :, a * 128:(a + 1) * 128].rearrange("m p t -> p m t"), in_=ys),
                      reads=["yst%d_%d" % (a % 2, mm) for mm in range(4)], writes=[("ynad", a)], sem="st_yna%d" % (a % 2), soft=["ynaall"])

    emit_q(0)
    emit_qk(tasks[0])
    for ti, t in enumerate(tasks):
        if ti + 1 < len(tasks):
            tn = tasks[ti + 1]
            if tn["m"] == 0 and tn["hh"] == 0:
                emit_q(tn["a"])
            emit_qk(tn)
        emit_pv(t)


KB.phaseATT = kb_phaseATT


def kb_phaseC(self, ins):
    nc, P, V = self.nc, self.P, self.V
    ps = self.psum
    S, Wd = self.S, self.W
    for nm, shp, dt_ in (("xs1", [8, 128, NLAT], F32), ("yna", [4, 128, NLAT], BF16), ("y", [32, 16, 8, 512], F32)):
        if nm + "_d_in" in self.debug:
            S[nm] = self.dram_in(nm + "_d", shp, dt_)
    o = PERS
    R = {}

    def al(n):
        nonlocal o
        r = o
        o += n
        return r
    xsA = V(al(4096), 4096, F32, "p (k t) -> p k t", k=8)
    xsB = V(al(4096), 4096, F32, "p (k t) -> p k t", k=8)
    h3T = V(al(2048), 2048, BF16, "p (k t) -> p k t", k=8)
    R["actT"] = V(al(5632), 5632, BF16, "p (j t) -> p j t", j=NJ)
    R["sg"] = [V(al(512), 512) for _ in range(2)]
    R["zn"] = [V(al(1024), 1024) for _ in range(2)]
    R["mv"] = V(al(32), 32, F32, "p (s c) -> p s c", c=8)
    R["bst"] = [V(al(16), 16) for _ in range(2)]
    R["wup"] = [V(al(1024), 1024, BF16, "p (k c) -> p k c", k=8) for _ in range(4)]
    R["wdn"] = [V(al(1408), 1408, BF16, "p (j c) -> p j c", j=NJ) for _ in range(3)]
    ypre = V(al(2048), 2048, F32, "p (m t) -> p m t", m=4)
    gT = V(al(2048), 2048, F32, "p (m t) -> p m t", m=4)
    gtmp = V(al(2048), 2048, F32, "p (m t) -> p m t", m=4)
    gTb = V(al(1024), 1024, BF16, "p (m t) -> p m t", m=4)
    sig = [V(al(512), 512) for _ in range(2)]
    ymix = V(al(2048), 2048, BF16, "p (k t) -> p k t", k=8)
    wout = [V(al(512), 512, BF16, "p (k c) -> p k c", k=8) for _ in range(8)]
    wglu = [V(al(256), 256, BF16, "p (k c) -> p k c", k=4) for _ in range(4)]
    G2f = V(al(1024), 1024)
    B2f = V(al(1024), 1024)
    ost = [V(al(1024), 1024) for _ in range(2)]
    assert o <= AW, o
    self.wup_cnt = 0
    self.wdn_cnt = 0
    scl = self.scl
    for i in range(8):
        P.dma("sp", lambda e, i=i: e.dma_start(out=wout[i], in_=Wd["out"][i]), reads=["Wall"], writes=["wout%d" % i])
    for i in range(4):
        P.dma("sp", lambda e, i=i: e.dma_start(out=wglu[i], in_=Wd["glu"][i]), reads=["Wall"], writes=["wglu%d" % i])
    lg, lb = ins["ln_g"], ins["ln_b"]
    P.dma("sp", lambda e: e.dma_start(out=G2f, in_=AP(lg.tensor, lg.offset + 16 * 128, [[0, 128], [1, 1024]])), writes=["G2f"])
    P.dma("sp", lambda e: e.dma_start(out=B2f, in_=AP(lb.tensor, lb.offset + 16 * 128, [[0, 128], [1, 1024]])), writes=["B2f"])
    GC = 2.0 * math.sqrt(2.0 / math.pi)
    nblk = NLAT // 512
    if "short" in self.debug:
        nblk = 1
    for blk in range(nblk):
        t0 = blk * 512
        c0 = t0 // 8
        for m in range(4):
            P.dma("sp", lambda e, m=m, c0=c0: e.dma_start(out=ypre[:, m, :].rearrange("p (t n) -> p t n", t=8),
                                                          in_=S["y"][m * 8:(m + 1) * 8, :, :, c0:c0 + 64].rearrange("g c t n -> (g c) t n")),
                  reads=["yall"], writes=["ypre%d" % m])
        for m in range(4):
            xin = ypre[:, m, :].rearrange("p (t n) -> p t n", t=8)
            xo = gT[:, m, :].rearrange("p (n t) -> p t n", t=8)
            tm = gtmp[:, m, :].rearrange("p (n t) -> p t n", t=8)
            P.op("pool", lambda e, xin=xin, xo=xo: e.tensor_copy(out=xo, in_=xin), reads=["ypre%d" % m], writes=["gT%d" % m])
            xt_ = gT[:, m, :]
            tm_ = gtmp[:, m, :]
            P.op("dve", lambda e, xt_=xt_, tm_=tm_: e.tensor_tensor(out=tm_, in0=xt_, in1=xt_, op=ALU.mult), reads=["gT%d" % m], writes=["gtmp%d" % m])
            P.op("dve", lambda e, tm_=tm_: e.tensor_scalar(out=tm_, in0=tm_, scalar1=0.044715, scalar2=1.0, op0=ALU.mult, op1=ALU.add), reads=["gtmp%d" % m], writes=["gtmp%d" % m])
            P.op("dve", lambda e, xt_=xt_, tm_=tm_: e.tensor_tensor(out=tm_, in0=tm_, in1=xt_, op=ALU.mult), reads=["gtmp%d" % m, "gT%d" % m], writes=["gtmp%d" % m])
            P.op("act", lambda e, tm_=tm_: e.activation(out=tm_, in_=tm_, func=AF.Sigmoid, scale=GC), reads=["gtmp%d" % m], writes=["gtmp%d" % m])
            P.op("dve", lambda e, xt_=xt_, tm_=tm_: e.tensor_tensor(out=xt_, in0=xt_, in1=tm_, op=ALU.mult), reads=["gtmp%d" % m, "gT%d" % m], writes=["gT%d" % m])
            P.op("pool", lambda e, xt_=xt_, m=m: e.tensor_copy(out=gTb[:, m, :], in_=xt_), reads=["gT%d" % m], writes=["gTb%d" % m])
        for mo in range(4):
            pb = self.bank(mo % 2)
            for m in range(4):
                P.op("pe", lambda e, mo=mo, m=m, pb=pb: e.matmul(pb, lhsT=wglu[mo][:, m, :], rhs=gTb[:, m, :], start=(m == 0), stop=(m == 3)),
                     reads=["wglu%d" % mo, "gTb%d" % m], writes=["ps%d" % (mo % 2)])
            sg_ = sig[mo % 2]
            P.op("act", lambda e, sg_=sg_, pb=pb, mo=mo: e.activation(out=sg_, in_=pb, func=AF.Sigmoid, bias=self.bgluT[:, mo:mo + 1], scale=1.0),
                 reads=["ps%d" % (mo % 2), "bgluT"], writes=["sig%d" % (mo % 2)])
            P.op("dve", lambda e, sg_=sg_, mo=mo: e.tensor_tensor(out=ymix[:, 4 + mo, :], in0=gT[:, mo, :], in1=sg_, op=ALU.mult),
                 reads=["sig%d" % (mo % 2), "gT%d" % mo], writes=["ymix%d" % (4 + mo)])
        for m in range(4):
            P.dma("sp", lambda e, m=m, t0=t0: e.dma_start(out=ymix[:, m, :], in_=S["yna"][m, :, t0:t0 + 512]), reads=["ynaall"], writes=["ymix%d" % m])
        for kk in range(8):
            P.dma("sp", lambda e, kk=kk, t0=t0: e.dma_start(out=xsA[:, kk, :], in_=S["xs1"][kk, :, t0:t0 + 512]), reads=["xs1all"], writes=["Cxs%d" % kk])
        for i in range(8):
            pb = self.bank(2 + i % 2)
            for kc in range(8):
                P.op("pe", lambda e, i=i, kc=kc, pb=pb: e.matmul(pb, lhsT=wout[i][:, kc, :], rhs=ymix[:, kc, :], start=(kc == 0), stop=(kc == 7)),
                     reads=["wout%d" % i, "ymix%d" % kc], writes=["ps%d" % (2 + i % 2)])
            P.op("dve", lambda e, i=i, pb=pb: e.scalar_tensor_tensor(out=xsA[:, i, :], in0=pb, scalar=scl[:, S_M5, i:i + 1], in1=xsA[:, i, :],
                                                                    op0=ALU.mult, op1=ALU.add),
                 reads=["ps%d" % (2 + i % 2), "Cxs%d" % i, "scal"], writes=["Cxs%d" % i])
        self.ln("C", xsA, 512, R, [(h3T, scl[:, S_G2S, :], scl[:, S_B2S, :], "DhT"), (xsB, scl[:, S_AG1, :], scl[:, S_AB1, :], "Dxs")])
        self.ffn("D", h3T, 512, xsB, Wd["up2"], Wd["dn2"], scl[:, S_HG8, :], R)

        def tok_cb(s, zn, znr, t0=t0):
            ot = ost[s % 2]
            P.op("dve", lambda e, ot=ot, zn=zn: e.tensor_tensor(out=ot, in0=zn, in1=G2f, op=ALU.mult), reads=[znr + "h0", znr + "h1", "G2f"], writes=["ost%d" % (s % 2)])
            P.op("pool", lambda e, ot=ot: e.tensor_tensor(out=ot, in0=ot, in1=B2f, op=ALU.add), reads=["ost%d" % (s % 2), "B2f"], writes=["ost%d" % (s % 2)])
            P.dma("pool", lambda e, ot=ot, s=s, t0=t0: e.dma_start(out=self.out[t0 + s * 128:t0 + (s + 1) * 128, :], in_=ot),
                  reads=["ost%d" % (s % 2)], writes=[("outd", t0, s)], sem="out%d" % (s % 2))
        R["tok_cb"] = tok_cb
        self.ln("D", xsB, 512, R, None)


KB.phaseC = kb_phaseC
```

```python
import contextlib
import math
import numpy as np
import concourse.bass as bass
import concourse.mybir as mybir
from concourse.bass_utils import run_bass_kernel_spmd
from concourse.ap import AP

F32 = mybir.dt.float32
BF16 = mybir.dt.bfloat16
I32 = mybir.dt.int32
AF = mybir.ActivationFunctionType
ALU = mybir.AluOpType
AX = mybir.AxisListType

ENGS = ("pe", "act", "dve", "pool", "sp")


class Op:
    __slots__ = ("eng", "fn", "reads", "writes", "pos", "is_dma", "dsem", "dcount",
                 "waits", "signal", "count", "vc")

    def __init__(self, eng, fn, reads, writes, is_dma=False, dsem=None):
        self.eng = eng
        self.fn = fn
        self.reads = tuple(reads)
        self.writes = tuple(writes)
        self.is_dma = is_dma
        self.dsem = dsem
        self.dcount = 0
        self.waits = {}
        self.signal = False
        self.count = 0
        self.vc = None


class Prog:
    def __init__(self):
        self.streams = {e: [] for e in ENGS}
        self.last_writer = {}
        self.readers = {}
        self.dma_counts = {}
        self.key2phys = {}
        self.free_phys = []
        self.free_sw = []
        self.sw_phys = set()
        self.nphys = 0
        self.final_waits = []

    def op(self, eng, fn, reads=(), writes=()):
        ex = [r for r in reads if isinstance(r, str) and r.startswith("ps") and r not in writes]
        if ex:
            writes = list(writes) + ex
        o = Op(eng, fn, reads, writes)
        self._add(o, ())
        return o

    def dma(self, eng, fn, reads=(), writes=(), sem=None, soft=()):
        if sem is None:
            sem = "d_" + str(writes[0])
        if sem not in self.key2phys:
            fl = self.free_sw if eng == "pool" else self.free_phys
            if fl:
                self.key2phys[sem] = fl.pop(0)
            else:
                self.key2phys[sem] = self.nphys
                self.nphys += 1
                if eng == "pool":
                    self.sw_phys.add(self.key2phys[sem])
        sem = self.key2phys[sem]
        assert (sem in self.sw_phys) == (eng == "pool"), "semaphore shared between SW and HW DGE"
        o = Op(eng, fn, reads, writes, is_dma=True, dsem=sem)
        self.dma_counts[sem] = self.dma_counts.get(sem, 0) + 16
        o.dcount = self.dma_counts[sem]
        self._add(o, soft)
        return o

    @staticmethod
    def _tok(o):
        if o.is_dma:
            return (o.dsem, o.dcount)
        return (o.eng, o.pos)

    def _add(self, o, soft):
        st = self.streams[o.eng]
        o.pos = len(st) + 1
        st.append(o)
        vc = dict(st[-2].vc) if len(st) > 1 else {}
        deps = []
        for r in o.reads:
            w = self.last_writer.get(r)
            if w is not None:
                deps.append(w)
        for w_ in o.writes:
            w = self.last_writer.get(w_)
            if w is not None:
                deps.append(w)
            deps.extend(self.readers.get(w_, ()))
        deps.sort(key=lambda d_: -self._tok(d_)[1])
        for d in deps:
            if d is o:
                continue
            k, v = self._tok(d)
            if (not d.is_dma) and d.eng == o.eng and o.eng == "pe":
                continue
            if vc.get(k, 0) >= v:
                continue
            if o.waits.get(k, 0) < v:
                o.waits[k] = v
            if not d.is_dma:
                d.signal = True
            for kk, vv in d.vc.items():
                if vc.get(kk, 0) < vv:
                    vc[kk] = vv
            if vc.get(k, 0) < v:
                vc[k] = v
        o.vc = vc
        for r in o.reads:
            self.readers.setdefault(r, []).append(o)
        for w_ in o.writes:
            self.last_writer[w_] = o
            self.readers[w_] = []
        for w_ in soft:
            self.last_writer[w_] = o

    def barrier(self):
        lasts = []
        for e in ENGS:
            if e == "sp":
                continue
            st = self.streams[e]
            for o_ in reversed(st):
                if not o_.is_dma:
                    lasts.append(o_)
                    break
        dmas = dict(self.dma_counts)
        for e in ENGS:
            o = Op(e, None, (), ())
            st = self.streams[e]
            o.pos = len(st) + 1
            vc = dict(st[-1].vc) if st else {}
            for d in lasts:
                if d.eng == e:
                    if e == "pe":
                        continue
                k, v = self._tok(d)
                if vc.get(k, 0) < v:
                    o.waits[k] = v
                    d.signal = True
                    vc[k] = v
            for k, v in dmas.items():
                if vc.get(k, 0) < v:
                    o.waits[k] = v
                    vc[k] = v
            o.vc = vc
            st.append(o)
        self.last_writer = {}
        self.readers = {}
        allp = set(self.free_phys) | set(self.free_sw) | set(self.key2phys.values())
        self.free_phys = sorted(p for p in allp if p not in self.sw_phys)
        self.free_sw = sorted(p for p in allp if p in self.sw_phys)
        self.key2phys = {}

    def finish_wait(self, eng, sem_keys):
        self.final_waits.append((eng, list(sem_keys)))

    def emit(self, nc):
        for e in ENGS:
            c = 0
            for o in self.streams[e]:
                if o.signal and not o.is_dma:
                    c += 1
                o.count = c
        pos2count = {e: {o.pos: o.count for o in self.streams[e]} for e in ENGS}
        with contextlib.ExitStack() as es:
            sems = {}
            for e in ENGS:
                sems[e] = es.enter_context(nc.semaphore("s_" + e))
            for k in range(self.nphys):
                sems[k] = es.enter_context(nc.semaphore("q%d" % k))
            block = es.enter_context(nc.Block())
            engmap = {"pe": "tensor", "act": "scalar", "dve": "vector", "pool": "gpsimd",
                      "sp": "sync"}

            def make(e):
                def body(engine):
                    for o in self.streams[e]:
                        for k, v in o.waits.items():
                            if k in ENGS:
                                engine.wait_ge(sems[k], pos2count[k][v])
                            else:
                                engine.wait_ge(sems[k], v)
                        if o.fn is None:
                            if o.signal:
                                engine.nop().then_inc(sems[e], 1)
                            continue
                        ins = o.fn(engine)
                        if o.is_dma:
                            ins.then_inc(sems[o.dsem], 16)
                        elif o.signal:
                            ins.then_inc(sems[e], 1)
                    for (fe, keys) in self.final_waits:
                        if fe == e:
                            for k in keys:
                                engine.wait_ge(sems[k], self.dma_counts[k])
                return body

            for e in ENGS:
                if self.streams[e] or any(fe == e for fe, _ in self.final_waits):
                    getattr(block, engmap[e])(make(e))


D = 1024
NLAT = 4096
NCTX = 256
NT = NLAT + NCTX
DFF = 2816
NJ = DFF // 128
NCH = NT // 8
ALPHA = 2.0 ** 0.25
EPS = 1e-6
AW = 51200
MASKV = -240000.0
TWO_PI = 2.0 * math.pi


class KB:
    def __init__(self, debug=None):
        self.debug = debug or ()
        self.nc = bass.Bass("TRN2", target_bir_lowering=False)
        self.P = Prog()
        self.es = contextlib.ExitStack()
        self.uid = 0

    def dram_in(self, name, shape, dt=F32):
        return self.nc.dram_tensor(name, list(shape), dt, kind="ExternalInput").ap()

    def dram_out(self, name, shape, dt=F32):
        return self.nc.dram_tensor(name, list(shape), dt, kind="ExternalOutput").ap()

    def dram_tmp(self, name, shape, dt=F32):
        if name in self.debug:
            return self.nc.dram_tensor(name, list(shape), dt, kind="ExternalOutput").ap()
        return self.nc.dram_tensor(name, list(shape), dt).ap()

    def V(self, off, n, dt=F32, pat=None, **kw):
        v = self.arena[:, off:off + n]
        if dt != F32:
            v = v.bitcast(dt)
        if pat is not None:
            v = v.rearrange(pat, **kw)
        return v

    def name(self, s):
        self.uid += 1
        return "%s%d" % (s, self.uid)


def bc_free(ap2, n):
    return AP(ap2.tensor, ap2.offset, [list(a) for a in ap2.ap] + [[0, n]])


def mk_ap(base, dims, off=0):
    return AP(base.tensor, base.offset + off, [list(base.ap[0])] + [list(d) for d in dims])


O_IDENT = 0
O_IDENTB = 128
O_ONESB = 192
O_MODT = 224
O_BADA = 368
O_LNG = 440
O_LNB = 464
O_BGLU = 488
O_M05 = 496
O_ST = 500
O_SCL = 520
O_SCC = 648
PERS = 1024
S_SC1P, S_SH0, S_HG2, S_G1S, S_B1S, S_AG0, S_AB0, S_M5, S_G2S, S_B2S, S_AG1, S_AB1, S_HG8, S_G2, S_B2 = range(15)


def kb_setup(self, ins):
    nc, P = self.nc, self.P
    self.arena = self.es.enter_context(nc.sbuf_tensor("arena", [128, AW], F32))
    self.psum = self.es.enter_context(nc.psum_tensor("psum", [128, 8, 512], F32))
    ps = self.psum
    V = self.V
    self.ident = ident = V(O_IDENT, 128)
    self.identb = identb = V(O_IDENTB, 64, BF16)
    self.onesb = onesb = V(O_ONESB, 32, BF16)
    self.modT = modT = V(O_MODT, 144, F32, "p (c v) -> p c v", v=2)
    badaT = V(O_BADA, 72)
    self.lngT = lngT = V(O_LNG, 24)
    self.lnbT = lnbT = V(O_LNB, 24)
    self.bgluT = bgluT = V(O_BGLU, 4)
    self.m05 = m05 = V(O_M05, 1)
    sT = V(O_ST, 16)
    self.scl = scl = V(O_SCL, 128, F32, "p (s k) -> p s k", k=8)
    self.scc = scc = V(O_SCC, 64, F32, "p (s k) -> p s k", k=8)

    P.op("pool", lambda e: e.memset(ident, 0.0), writes=["ident"])
    P.op("pool", lambda e: e.affine_select(out=ident, in_=ident, pattern=[[-1, 128]],
                                           compare_op=ALU.not_equal, fill=1.0, base=0,
                                           channel_multiplier=1), reads=["ident"], writes=["ident"])
    P.op("pool", lambda e: e.tensor_copy(out=identb, in_=ident), reads=["ident"], writes=["identb"])
    P.op("pool", lambda e: e.memset(onesb, 1.0), writes=["onesb"])
    P.op("pool", lambda e: e.memset(m05, -0.5), writes=["m05"])

    T0 = PERS
    st1 = V(T0, 128)
    st2 = V(T0 + 128, 128)
    P.dma("sp", lambda e: e.dma_start(out=st1[0:72, :], in_=ins["b_ada"]), writes=["st1"])
    P.dma("sp", lambda e: e.dma_start(out=st2[0:24, :], in_=ins["ln_g"]), writes=["st2a"])
    P.dma("sp", lambda e: e.dma_start(out=st2[24:48, :], in_=ins["ln_b"]), writes=["st2b"])
    P.dma("sp", lambda e: e.dma_start(out=st2[48:52, :], in_=ins["s5_b_glu"]), writes=["st2c"])
    P.dma("sp", lambda e: e.dma_start(out=st2[52:60, :], in_=ins["c"]), writes=["st2d"])
    P.dma("sp", lambda e: e.dma_start(out=st2[60:68, :], in_=ins["c_ctx"]), writes=["st2e"])
    pt = ps[:, 6, 0:256]
    P.op("pe", lambda e: e.transpose(out=pt[:, 0:72], in_=st1[0:72, :], identity=ident[0:72, 0:72]),
         reads=["st1", "ident"], writes=["psT6"])
    P.op("pe", lambda e: e.transpose(out=pt[:, 128:196], in_=st2[0:68, :], identity=ident[0:68, 0:68]),
         reads=["st2a", "st2b", "st2c", "st2d", "st2e", "ident"], writes=["psT6"])
    P.op("dve", lambda e: e.tensor_copy(out=badaT, in_=pt[:, 0:72]), reads=["psT6"], writes=["badaT"])
    P.op("dve", lambda e: e.tensor_copy(out=lngT, in_=pt[:, 128:152]), reads=["psT6"], writes=["lngT"])
    P.op("dve", lambda e: e.tensor_copy(out=lnbT, in_=pt[:, 152:176]), reads=["psT6"], writes=["lnbT"])
    P.op("dve", lambda e: e.tensor_copy(out=bgluT, in_=pt[:, 176:180]), reads=["psT6"], writes=["bgluT"])
    P.op("act", lambda e: e.activation(out=sT, in_=pt[:, 180:196], func=AF.Silu), reads=["psT6"], writes=["sT"])

    wb = [V(T0 + 256 + i * 4096, 4096, F32, "p (k c) -> p k c", k=8) for i in range(2)]
    w_ada = ins["w_ada"].rearrange("(k p) c -> p k c", p=128)
    modps = ps[:, 0, 0:144]
    for bi in range(18):
        t = wb[bi % 2]
        P.dma("sp", lambda e, t=t, bi=bi: e.dma_start(out=t, in_=w_ada[:, :, bi * 512:(bi + 1) * 512]),
              writes=["wada%d" % (bi % 2)])
        for fc in range(4):
            col = (bi * 4 + fc) * 2
            for kk in range(8):
                rhs = mk_ap(sT, [[8, 2]], off=kk)
                P.op("pe", lambda e, t=t, fc=fc, kk=kk, col=col, rhs=rhs: e.matmul(
                    modps[:, col:col + 2], lhsT=t[:, kk, fc * 128:(fc + 1) * 128], rhs=rhs,
                    start=(kk == 0), stop=(kk == 7)),
                    reads=["wada%d" % (bi % 2), "sT"], writes=["ps0"])
    P.op("dve", lambda e: e.tensor_tensor(out=modT, in0=modps.rearrange("p (c v) -> p c v", v=2),
                                          in1=bc_free(badaT, 2), op=ALU.add),
         reads=["ps0", "badaT"], writes=["modT"])

    def m(i, v):
        return modT[:, 8 * i:8 * i + 8, v]

    def G(i):
        return lngT[:, 8 * i:8 * i + 8]

    def B(i):
        return lnbT[:, 8 * i:8 * i + 8]

    def dv(fn, reads=("modT", "lngT", "lnbT", "scal")):
        P.op("dve", fn, reads=list(reads), writes=["scal"])

    for v, sc in ((0, scl), (1, scc)):
        dv(lambda e, sc=sc, v=v: e.tensor_scalar(out=sc[:, S_SC1P, :], in0=m(1, v), scalar1=1.0, scalar2=None, op0=ALU.add))
        dv(lambda e, sc=sc, v=v: e.tensor_copy(out=sc[:, S_SH0, :], in_=m(0, v)))
        dv(lambda e, sc=sc, v=v: e.tensor_scalar(out=sc[:, S_HG2, :], in0=m(2, v), scalar1=0.5, scalar2=None, op0=ALU.mult))
        dv(lambda e, sc=sc, v=v: e.tensor_scalar(out=sc[:, S_G1S, :], in0=m(4, v), scalar1=1.0, scalar2=None, op0=ALU.add))
        dv(lambda e, sc=sc, v=v: e.tensor_tensor(out=sc[:, S_B1S, :], in0=sc[:, S_G1S, :], in1=B(0), op=ALU.mult))
        dv(lambda e, sc=sc, v=v: e.tensor_tensor(out=sc[:, S_B1S, :], in0=sc[:, S_B1S, :], in1=m(3, v), op=ALU.add))
        dv(lambda e, sc=sc, v=v: e.tensor_tensor(out=sc[:, S_G1S, :], in0=sc[:, S_G1S, :], in1=G(0), op=ALU.mult))
    sc = scl
    dv(lambda e: e.tensor_scalar(out=sc[:, S_AG0, :], in0=G(0), scalar1=ALPHA, scalar2=None, op0=ALU.mult))
    dv(lambda e: e.tensor_scalar(out=sc[:, S_AB0, :], in0=B(0), scalar1=ALPHA, scalar2=None, op0=ALU.mult))
    dv(lambda e: e.tensor_copy(out=sc[:, S_M5, :], in_=m(5, 0)))
    dv(lambda e: e.tensor_scalar(out=sc[:, S_G2S, :], in0=m(7, 0), scalar1=1.0, scalar2=None, op0=ALU.add))
    dv(lambda e: e.tensor_tensor(out=sc[:, S_B2S, :], in0=sc[:, S_G2S, :], in1=B(1), op=ALU.mult))
    dv(lambda e: e.tensor_tensor(out=sc[:, S_B2S, :], in0=sc[:, S_B2S, :], in1=m(6, 0), op=ALU.add))
    dv(lambda e: e.tensor_tensor(out=sc[:, S_G2S, :], in0=sc[:, S_G2S, :], in1=G(1), op=ALU.mult))
    dv(lambda e: e.tensor_scalar(out=sc[:, S_AG1, :], in0=G(1), scalar1=ALPHA, scalar2=None, op0=ALU.mult))
    dv(lambda e: e.tensor_scalar(out=sc[:, S_AB1, :], in0=B(1), scalar1=ALPHA, scalar2=None, op0=ALU.mult))
    dv(lambda e: e.tensor_scalar(out=sc[:, S_HG8, :], in0=m(8, 0), scalar1=0.5, scalar2=None, op0=ALU.mult))
    dv(lambda e: e.tensor_copy(out=sc[:, S_G2, :], in_=G(2)))
    dv(lambda e: e.tensor_copy(out=sc[:, S_B2, :], in_=B(2)))


KB.setup = kb_setup


def kb_convert(self, ins):
    nc, P, V = self.nc, self.P, self.V
    W = self.W = {}
    W["up1"] = self.dram_tmp("wup1_d", [NJ, 128, 8, 256], BF16)
    W["up2"] = self.dram_tmp("wup2_d", [NJ, 128, 8, 256], BF16)
    W["dn1"] = self.dram_tmp("wdn1_d", [8, 128, NJ, 128], BF16)
    W["dn2"] = self.dram_tmp("wdn2_d", [8, 128, NJ, 128], BF16)
    W["in"] = self.dram_tmp("win_d", [16, 128, 8, 128], BF16)
    W["out"] = self.dram_tmp("wout_d", [8, 128, 8, 128], BF16)
    W["glu"] = self.dram_tmp("wglu_d", [4, 128, 4, 128], BF16)
    if "noconv" in self.debug:
        return
    T0 = getattr(self, "conv_off", PERS)
    CMAX = 2816
    assert T0 + 3 * CMAX <= AW
    srcb = [V(T0 + i * CMAX, CMAX) for i in range(2)]
    dstb = [V(T0 + 2 * CMAX + i * (CMAX // 2), CMAX // 2, BF16) for i in range(2)]
    self.cv_n = 0
    engs = ["act", "act", "act"]
    if "nopool" in self.debug:
        engs = ["act", "dve", "dve"]

    def conv(src, nrt, C, dst, mode):
        for rt in range(nrt):
            n = self.cv_n
            self.cv_n += 1
            sb = srcb[n % 2][:, 0:C]
            db = dstb[n % 2][:, 0:C]
            P.dma("sp", lambda e, sb=sb, rt=rt: e.dma_start(out=sb, in_=src[rt * 128:(rt + 1) * 128, :]),
                  writes=["cvs%d" % (n % 2)])
            eng = engs[n % 3]
            i_ap, o_ap = sb, db
            if eng == "act":
                P.op("act", lambda e, i_ap=i_ap, o_ap=o_ap: e.activation(out=o_ap, in_=i_ap, func=AF.Copy),
                     reads=["cvs%d" % (n % 2)], writes=["cvd%d" % (n % 2)])
            else:
                P.op(eng, lambda e, i_ap=i_ap, o_ap=o_ap: e.tensor_copy(out=o_ap, in_=i_ap),
                     reads=["cvs%d" % (n % 2)], writes=["cvd%d" % (n % 2)])
            cw = 128
            if mode == "upa":
                d_ap = dst[:, :, rt, 0:128].rearrange("m p c -> p m c")
            elif mode == "upg":
                d_ap = dst[:, :, rt, 128:256].rearrange("m p c -> p m c")
            else:
                d_ap = dst[:, :, rt, :].rearrange("m p c -> p m c")
            s_ap = db.rearrange("p (m c) -> p m c", c=cw)
            P.dma("sp", lambda e, d_ap=d_ap, s_ap=s_ap: e.dma_start(out=d_ap, in_=s_ap),
                  reads=["cvd%d" % (n % 2)], writes=[("W", id(dst), n)], sem="wconv%d" % (n % 2), soft=["Wall"])

    conv(ins["ffn1_w_up"][:, 0:DFF], 8, DFF, W["up1"], "upa")
    conv(ins["ffn1_w_up"][:, DFF:2 * DFF], 8, DFF, W["up1"], "upg")
    if "cv1" in self.debug:
        return
    conv(ins["ffn1_w_down"], NJ, 1024, W["dn1"], "plain")
    if "cv2" in self.debug:
        return
    conv(ins["w_in"], 8, 2048, W["in"], "plain")
    if "cv3" in self.debug:
        return
    conv(ins["w_out"], 8, 1024, W["out"], "plain")
    if "cv4" in self.debug:
        return
    conv(ins["s5_w_glu"], 4, 512, W["glu"], "plain")
    if "cv5" in self.debug:
        return
    conv(ins["ffn2_w_up"][:, 0:DFF], 8, DFF, W["up2"], "upa")
    conv(ins["ffn2_w_up"][:, DFF:2 * DFF], 8, DFF, W["up2"], "upg")
    conv(ins["ffn2_w_down"], NJ, 1024, W["dn2"], "plain")


KB.convert = kb_convert


def kb_bank(self, b, n=512):
    return self.psum[:, b, 0:n]


def kb_ffn(self, key, hT, nb, xsT, wup_d, wdn_d, hg, R, pump=None, hkey=None, mid=None):
    P = self.P
    actT = R["actT"]
    if hkey is None:
        hkey = key
    for j in range(NJ):
        if pump is not None and j >= 2:
            next(pump, None)
        slot = self.wup_cnt % 4
        self.wup_cnt += 1
        wt = R["wup"][slot]
        P.dma("sp", lambda e, wt=wt, j=j: e.dma_start(out=wt, in_=wup_d[j]), reads=["Wall"],
              writes=["wup%d" % slot])
        pa = self.bank(j % 2, nb)
        pg = self.bank(2 + j % 2, nb)
        for kk in range(8):
            P.op("pe", lambda e, wt=wt, kk=kk, pa=pa: e.matmul(pa, lhsT=wt[:, kk, 0:128], rhs=hT[:, kk, :],
                                                              start=(kk == 0), stop=(kk == 7)),
                 reads=["wup%d" % slot, hkey + "hT%d" % kk], writes=["ps%d" % (j % 2)])
        for kk in range(8):
            P.op("pe", lambda e, wt=wt, kk=kk, pg=pg: e.matmul(pg, lhsT=wt[:, kk, 128:256], rhs=hT[:, kk, :],
                                                              start=(kk == 0), stop=(kk == 7)),
                 reads=["wup%d" % slot, hkey + "hT%d" % kk], writes=["ps%d" % (2 + j % 2)])
        sgb = R["sg"][j % 2][:, 0:nb]
        P.op("act", lambda e, sgb=sgb, pg=pg: e.activation(out=sgb, in_=pg, func=AF.Silu),
             reads=["ps%d" % (2 + j % 2)], writes=["sg%d" % (j % 2)])
        P.op("dve", lambda e, sgb=sgb, pa=pa, j=j: e.tensor_tensor(out=actT[:, j, 0:nb], in0=pa, in1=sgb, op=ALU.mult),
             reads=["ps%d" % (j % 2), "sg%d" % (j % 2)], writes=["actT%d" % j])
    if pump is not None:
        for _ in pump:
            pass
    if mid is not None:
        mid()
    for i in range(8):
        slot = self.wdn_cnt % 3
        self.wdn_cnt += 1
        wd = R["wdn"][slot]
        P.dma("sp", lambda e, wd=wd, i=i: e.dma_start(out=wd, in_=wdn_d[i]), reads=["Wall"],
              writes=["wdn%d" % slot])
        po = self.bank(4 + i % 2, nb)
        for j in range(NJ):
            P.op("pe", lambda e, wd=wd, j=j, po=po: e.matmul(po, lhsT=wd[:, j, :], rhs=actT[:, j, 0:nb],
                                                            start=(j == 0), stop=(j == NJ - 1)),
                 reads=["wdn%d" % slot, "actT%d" % j], writes=["ps%d" % (4 + i % 2)])
        P.op("dve", lambda e, po=po, i=i: e.scalar_tensor_tensor(out=xsT[:, i, :], in0=po, scalar=hg[:, i:i + 1],
                                                                in1=xsT[:, i, :], op0=ALU.mult, op1=ALU.add),
             reads=["ps%d" % (4 + i % 2), key + "xs%d" % i, "scal"], writes=[key + "xs%d" % i])


def kb_ln(self, key, zT, nb, R, outs):
    for _ in self.ln_gen(key, zT, nb, R, outs, None):
        pass


def kb_ln_gen(self, key, zT, nb, R, outs, fixed):
    P = self.P
    S = nb // 128
    mv = R["mv"]
    FW = [((0, 1), ("ps0", "ps1")), ((2, 3), ("ps2", "ps3"))]
    BK = [((4, 5), ("ps4", "ps5")), ((6, 7), ("psT6", "psT7"))]
    for s in range(S):
        zn = R["zn"][s % 2]
        znr = "zn%d" % (s % 2)
        (fa, _), fr = FW[s % 2]
        (ba, _), br = BK[s % 2]
        if fixed is not None:
            (fa, _), fr = fixed[0]
            (ba, _), br = fixed[1]
        psF = self.psum[:, fa:fa + 2, :].rearrange("p a b -> p (a b)")
        psB = self.psum[:, ba:ba + 2, :].rearrange("p a b -> p (a b)")
        for kk in range(8):
            P.op("pe", lambda e, kk=kk, s=s, psF=psF: e.transpose(out=psF[:, kk * 128:(kk + 1) * 128],
                                                                  in_=zT[:, kk, s * 128:(s + 1) * 128], identity=self.ident),
                 reads=[key + "xs%d" % kk, "ident"], writes=[fr[kk // 4]])
        st = R["bst"][s % 2]
        sr = "bst%d" % (s % 2)
        P.op("dve", lambda e, st=st, psF=psF: e.bn_stats(out=st[:, 0:6], in_=psF[:, 0:512]), reads=[fr[0]], writes=[sr])
        P.op("dve", lambda e, st=st, psF=psF: e.bn_stats(out=st[:, 6:12], in_=psF[:, 512:1024]), reads=[fr[1]], writes=[sr])
        mr = "mv%d" % s
        P.op("dve", lambda e, st=st, s=s: e.bn_aggr(out=mv[:, s, 0:2], in_=st[:, 0:12]), reads=[sr], writes=[mr])
        P.op("dve", lambda e, s=s: e.tensor_scalar(out=mv[:, s, 2:3], in0=mv[:, s, 1:2], scalar1=EPS, scalar2=None, op0=ALU.add),
             reads=[mr], writes=[mr])
        P.op("pool", lambda e, s=s: e.tensor_tensor(out=mv[:, s, 3:4], in0=mv[:, s, 2:3], in1=self.m05, op=ALU.pow),
             reads=[mr, "m05"], writes=[mr])
        P.op("dve", lambda e, s=s: e.tensor_scalar(out=mv[:, s, 4:5], in0=mv[:, s, 0:1], scalar1=mv[:, s, 3:4], scalar2=-1.0,
                                                   op0=ALU.mult, op1=ALU.mult), reads=[mr], writes=[mr])
        for hb in range(2):
            P.op("act", lambda e, s=s, zn=zn, hb=hb, psF=psF: e.activation(out=zn[:, hb * 512:(hb + 1) * 512], in_=psF[:, hb * 512:(hb + 1) * 512],
                                                                          func=AF.Identity, scale=mv[:, s, 3:4], bias=mv[:, s, 4:5]),
                 reads=[fr[hb], mr], writes=[znr + "h%d" % hb])
        if outs is None:
            R["tok_cb"](s, zn, znr)
            yield
            continue
        yield
        for kk in range(8):
            P.op("pe", lambda e, kk=kk, zn=zn, psB=psB: e.transpose(out=psB[:, kk * 128:(kk + 1) * 128],
                                                                    in_=zn[:, kk * 128:(kk + 1) * 128], identity=self.ident),
                 reads=[znr + "h%d" % (kk // 4), "ident"], writes=[br[kk // 4]])
        for (dst, gcol, bcol, wres) in outs:
            for kk in range(8):
                o_ap = dst[:, kk, s * 128:(s + 1) * 128]
                i_ap = psB[:, kk * 128:(kk + 1) * 128]
                pr = br[kk // 4]
                g1 = gcol[:, kk:kk + 1]
                b1 = bcol[:, kk:kk + 1]
                if kk < 4:
                    P.op("act", lambda e, o_ap=o_ap, i_ap=i_ap, g1=g1, b1=b1: e.activation(
                        out=o_ap, in_=i_ap, func=AF.Identity, scale=g1, bias=b1),
                        reads=[pr, "scal"], writes=[wres + "%d" % kk])
                else:
                    P.op("dve", lambda e, o_ap=o_ap, i_ap=i_ap, g1=g1, b1=b1: e.tensor_scalar(
                        out=o_ap, in0=i_ap, scalar1=g1, scalar2=b1, op0=ALU.mult, op1=ALU.add),
                        reads=[pr, "scal"], writes=[wres + "%d" % kk])
        yield


KB.ln_gen = kb_ln_gen
KB.bank = kb_bank
KB.ffn = kb_ffn
KB.ln = kb_ln


def kb_phaseA(self, ins):
    nc, P, V = self.nc, self.P, self.V
    ps = self.psum
    S = self.S = {}
    S["xs1"] = self.dram_tmp("xs1_d", [8, 128, NLAT], F32)
    S["q"] = self.dram_tmp("q_d", [4, 128, NLAT], BF16)
    S["k"] = self.dram_tmp("k_d", [4, 128, NT], BF16)
    S["v"] = self.dram_tmp("v_d", [NT, 512], BF16)
    S["u"] = self.dram_tmp("u_d", [32, 16, 8, NCH], BF16)
    o = PERS
    R = {}

    def al(n):
        nonlocal o
        r = o
        o += n
        return r
    xtok = [V(al(1024), 1024) for _ in range(2)]
    xsT = V(al(4096), 4096, F32, "p (k t) -> p k t", k=8)
    xsTb = V(al(4096), 4096, F32, "p (k t) -> p k t", k=8)
    hT = V(al(2048), 2048, BF16, "p (k t) -> p k t", k=8)
    R["actT"] = V(al(5632), 5632, BF16, "p (j t) -> p j t", j=NJ)
    R["sg"] = [V(al(512), 512) for _ in range(2)]
    R["zn"] = [V(al(1024), 1024) for _ in range(2)]
    xs1st = V(al(4096), 4096, F32, "p (k t) -> p k t", k=8)
    h2T = V(al(2048), 2048, BF16, "p (k t) -> p k t", k=8)
    qkst = V(al(2048), 2048, BF16, "p (m t) -> p m t", m=8)
    vst = [V(al(256), 256, BF16) for _ in range(2)]
    ust = V(al(1024), 1024, BF16, "p (m t) -> p m t", m=4)
    R["mv"] = V(al(32), 32, F32, "p (s c) -> p s c", c=8)
    R["bst"] = [V(al(16), 16) for _ in range(2)]
    R["wup"] = [V(al(1024), 1024, BF16, "p (k c) -> p k c", k=8) for _ in range(4)]
    R["wdn"] = [V(al(1408), 1408, BF16, "p (j c) -> p j c", j=NJ) for _ in range(3)]
    wint = [V(al(512), 512, BF16, "p (k c) -> p k c", k=8) for _ in range(4)]
    wv = V(al(2048), 2048, BF16, "p (k m c) -> p k m c", k=8, m=4)
    assert o <= AW, o
    self.wup_cnt = 0
    self.wdn_cnt = 0
    win_cnt = 0
    psT = ps[:, 6:8, :].rearrange("p a b -> p (a b)")
    Wd = self.W
    for m in range(4):
        P.dma("sp", lambda e, m=m: e.dma_start(out=wv[:, :, m, :], in_=Wd["in"][8 + m]), reads=["Wall"], writes=["wv%d" % m])

    if "A0" in self.debug:
        return
    blocks = [(0, NCTX, True)] + [(NCTX + i * 512, 512, False) for i in range(NLAT // 512)]
    if "short" in self.debug:
        blocks = blocks[:2]
    xsT2 = [xsT, xsTb]
    PUMP_BANKS = (((4, 5), ("ps4", "ps5")), ((6, 7), ("psT6", "psT7")))

    def x_stage(blk, buf):
        (t0, nb, isctx) = blk
        sc = self.scc if isctx else self.scl
        xs = xsT2[buf]
        src = ins["ctx"] if isctx else ins["x"]
        r0 = t0 if isctx else t0 - NCTX
        for s in range(nb // 128):
            xt = xtok[s % 2]
            if s % 2 == 0:
                psT = ps[:, 6:8, :].rearrange("p a b -> p (a b)")
                xbr = ("psT6", "psT7")
            else:
                psT = ps[:, 4:6, :].rearrange("p a b -> p (a b)")
                xbr = ("ps4", "ps5")
            x_ap = src[r0 + s * 128:r0 + (s + 1) * 128, :]
            P.dma("sp", lambda e, xt=xt, x_ap=x_ap: e.dma_start(out=xt, in_=x_ap), writes=["xtok%d" % (s % 2)])
            for kk in range(8):
                P.op("pe", lambda e, kk=kk, xt=xt, psT=psT: e.transpose(out=psT[:, kk * 128:(kk + 1) * 128],
                                                                        in_=xt[:, kk * 128:(kk + 1) * 128], identity=self.ident),
                     reads=["xtok%d" % (s % 2), "ident"], writes=[xbr[kk // 4]])
            for kk in range(8):
                i_ap = psT[:, kk * 128:(kk + 1) * 128]
                pr = xbr[kk // 4]
                o1 = xs[:, kk, s * 128:(s + 1) * 128]
                o2 = hT[:, kk, s * 128:(s + 1) * 128]
                s1 = sc[:, S_SC1P, kk:kk + 1]
                s2 = sc[:, S_SH0, kk:kk + 1]
                if kk < 4:
                    P.op("act", lambda e, i_ap=i_ap, o1=o1: e.activation(out=o1, in_=i_ap, func=AF.Identity, scale=ALPHA),
                         reads=[pr], writes=["A%dxs%d" % (buf, kk)])
                    P.op("act", lambda e, i_ap=i_ap, o2=o2, s1=s1, s2=s2: e.activation(out=o2, in_=i_ap, func=AF.Identity, scale=s1, bias=s2),
                         reads=[pr, "scal"], writes=["AhT%d" % kk])
                else:
                    P.op("dve", lambda e, i_ap=i_ap, o1=o1: e.tensor_scalar(out=o1, in0=i_ap, scalar1=ALPHA, scalar2=None, op0=ALU.mult),
                         reads=[pr], writes=["A%dxs%d" % (buf, kk)])
                    P.op("dve", lambda e, i_ap=i_ap, o2=o2, s1=s1, s2=s2: e.tensor_scalar(out=o2, in0=i_ap, scalar1=s1, scalar2=s2, op0=ALU.mult, op1=ALU.add),
                         reads=[pr, "scal"], writes=["AhT%d" % kk])

    def ln_stage(blk, buf, fixed):
        (t0, nb, isctx) = blk
        sc = self.scc if isctx else self.scl
        outs = [(h2T, sc[:, S_G1S, :], sc[:, S_B1S, :], "Ah2T")]
        if not isctx:
            outs.append((xs1st, sc[:, S_AG0, :], sc[:, S_AB0, :], "Axs1"))
        return self.ln_gen("A%d" % buf, xsT2[buf], nb, R, outs, fixed)

    def win_stage(blk):
        nonlocal win_cnt
        (t0, nb, isctx) = blk
        Sn = nb // 128
        r0 = t0 if isctx else t0 - NCTX
        if not isctx:
            for kk in range(8):
                P.dma("pool", lambda e, kk=kk, r0=r0, nb=nb: e.dma_start(out=S["xs1"][kk, :, r0:r0 + nb], in_=xs1st[:, kk, 0:nb]),
                      reads=["Axs1%d" % kk], writes=[("xs1d", t0, kk)], sem="st_xs1_%d" % kk, soft=["xs1all"])
        for mi, m in enumerate([0, 1, 2, 3, 4, 5, 6, 7, 12, 13, 14, 15]):
            if isctx and m < 4:
                continue
            slot = win_cnt % 4
            win_cnt += 1
            wt = wint[slot]
            P.dma("sp", lambda e, wt=wt, m=m: e.dma_start(out=wt, in_=Wd["in"][m]), reads=["Wall"], writes=["win%d" % slot])
            pb = self.bank(mi % 4, nb)
            for kk in range(8):
                P.op("pe", lambda e, wt=wt, kk=kk, pb=pb, nb=nb: e.matmul(pb, lhsT=wt[:, kk, :], rhs=h2T[:, kk, 0:nb],
                                                                         start=(kk == 0), stop=(kk == 7)),
                     reads=["win%d" % slot, "Ah2T%d" % kk], writes=["ps%d" % (mi % 4)])
            if m < 8:
                eng = ("act", "dve")[mi % 2]
                o_ap = qkst[:, m, 0:nb]
                if eng == "act":
                    P.op("act", lambda e, o_ap=o_ap, pb=pb: e.activation(out=o_ap, in_=pb, func=AF.Copy),
                         reads=["ps%d" % (mi % 4)], writes=["qkst%d" % m])
                else:
                    P.op("dve", lambda e, o_ap=o_ap, pb=pb: e.tensor_copy(out=o_ap, in_=pb),
                         reads=["ps%d" % (mi % 4)], writes=["qkst%d" % m])
                if m < 4:
                    P.dma("pool", lambda e, m=m, r0=r0, o_ap=o_ap, nb=nb: e.dma_start(out=S["q"][m, :, r0:r0 + nb], in_=o_ap),
                          reads=["qkst%d" % m], writes=[("qd", t0, m)], sem="st_qk%d" % m, soft=["qall"])
                else:
                    P.dma("pool", lambda e, m=m, t0=t0, o_ap=o_ap, nb=nb: e.dma_start(out=S["k"][m - 4, :, t0:t0 + nb], in_=o_ap),
                          reads=["qkst%d" % m], writes=[("kd", t0, m)], sem="st_qk%d" % m, soft=["kall"])
            else:
                mu = m - 12
                nc8 = nb // 8
                i_ap = pb.rearrange("p (c s) -> p s c", s=8)
                o_ap = ust[:, mu, 0:nb].rearrange("p (s c) -> p s c", s=8)
                P.op("dve", lambda e, o_ap=o_ap, i_ap=i_ap: e.tensor_copy(out=o_ap, in_=i_ap),
                     reads=["ps%d" % (mi % 4)], writes=["ust%d" % mu])
                c0 = t0 // 8
                d_ap = S["u"][mu * 8:(mu + 1) * 8, :, :, c0:c0 + nc8].rearrange("g i s c -> (g i) s c")
                P.dma("pool", lambda e, d_ap=d_ap, o_ap=o_ap: e.dma_start(out=d_ap, in_=o_ap),
                      reads=["ust%d" % mu], writes=[("ud", t0, mu)], sem="st_u%d" % mu, soft=["uall"])
        for s in range(Sn):
            pv = self.bank(4 + s % 2, 512)
            for kk in range(8):
                P.op("pe", lambda e, kk=kk, s=s, pv=pv: e.matmul(pv, lhsT=h2T[:, kk, s * 128:(s + 1) * 128],
                                                                rhs=wv[:, kk, :, :].rearrange("p m c -> p (m c)"),
                                                                start=(kk == 0), stop=(kk == 7)),
                     reads=["wv0", "wv1", "wv2", "wv3", "Ah2T%d" % kk], writes=["ps%d" % (4 + s % 2)])
            vs = vst[s % 2]
            P.op("act", lambda e, vs=vs, pv=pv: e.activation(out=vs, in_=pv, func=AF.Copy),
                 reads=["ps%d" % (4 + s % 2)], writes=["vst%d" % (s % 2)])
            P.dma("pool", lambda e, vs=vs, s=s, t0=t0: e.dma_start(out=S["v"][t0 + s * 128:t0 + (s + 1) * 128, :], in_=vs),
                  reads=["vst%d" % (s % 2)], writes=[("vd", t0, s)], sem="st_v%d" % (s % 2), soft=["vall"])

    prev = None
    for idx, blk in enumerate(blocks):
        buf = idx % 2
        (t0, nb, isctx) = blk
        sc = self.scc if isctx else self.scl
        x_stage(blk, buf)
        pump = ln_stage(prev[0], prev[1], PUMP_BANKS) if prev is not None else None
        mid = (lambda pb_=prev[0]: win_stage(pb_)) if prev is not None else None
        self.ffn("A%d" % buf, hT[:, :, 0:nb], nb, xsT2[buf][:, :, 0:nb], Wd["up1"], Wd["dn1"], sc[:, S_HG2, :], R,
                 pump=pump, hkey="A", mid=mid)
        prev = (blk, buf)
    for _ in ln_stage(prev[0], prev[1], None):
        pass
    win_stage(prev[0])


KB.phaseA = kb_phaseA


IN_SHAPES = {
    "x": [NLAT, D], "c": [8, 128], "ctx": [NCTX, D], "c_ctx": [8, 128],
    "w_ada": [D, 9 * D], "b_ada": [72, 128], "ln_g": [24, 128], "ln_b": [24, 128],
    "ffn1_w_up": [D, 2 * DFF], "ffn1_w_down": [DFF, D], "w_in": [D, 2048],
    "na_rpb": [8, 15, 31], "s5_a_re": [2, 32, 64], "s5_a_im": [2, 32, 64], "s5_log_dt": [2, 32],
    "s5_b_re": [2, 32, 64, 16], "s5_b_im": [2, 32, 64, 16], "s5_c_re": [2, 32, 16, 64],
    "s5_c_im": [2, 32, 16, 64], "s5_d": [32, 16], "s5_w_glu": [512, 512], "s5_b_glu": [4, 128],
    "w_out": [D, D], "ffn2_w_up": [D, 2 * DFF], "ffn2_w_down": [DFF, D],
}


def build_nc(debug=None, stages=("setup", "convert", "A", "S5", "ATT", "C")):
    kb = KB(debug)
    ins = {k: kb.dram_in(k, shp) for k, shp in IN_SHAPES.items()}
    out = kb.dram_out("out", [NLAT, D])
    kb.out = out
    with kb.es:
        kb.setup(ins)
        kb.P.barrier()
        kb.conv_off = AW - 3 * 2816
        if "S5" in stages:
            kb.phaseS5(ins, mid_cb=(lambda: kb.convert(ins)) if "convert" in stages else None)
            assert kb.s5["o_end"] <= kb.conv_off, kb.s5["o_end"]
        elif "convert" in stages:
            kb.convert(ins)
        kb.P.barrier()
        if "A" in stages:
            kb.phaseA(ins)
            kb.P.barrier()
        if "S5" in stages:
            kb.phaseS5main(ins)
            kb.P.barrier()
        if "ATT" in stages:
            kb.phaseATT(ins)
            kb.P.barrier()
        if "C" in stages:
            kb.phaseC(ins)
        if "dump_mod" in kb.debug:
            dm = kb.dram_out("dbg_mod", [128, 144])
            kb.P.dma("sp", lambda e: e.dma_start(out=dm, in_=kb.V(O_MODT, 144)), reads=["modT"], writes=["dbgmod"], sem="out")
        if "C" not in stages:
            z = kb.V(AW - 1024, 1024)
            kb.P.op("pool", lambda e: e.memset(z, 0.0), writes=["zz"])
            kb.P.dma("sp", lambda e: e.dma_start(out=out[0:128, :], in_=z), reads=["zz"], writes=["outz"], sem="out")
        kb.P.finish_wait("sp", list(range(kb.P.nphys)))
        kb.P.emit(kb.nc)
    return kb.nc


def make_in_maps(inputs):
    f = lambda a: np.ascontiguousarray(np.asarray(a, dtype=np.float32))
    shared = {}
    for k in IN_SHAPES:
        if k in ("x", "c", "ctx"):
            continue
        a = f(inputs[k])
        if a.ndim >= 1 and a.shape[0] == 1 and k not in ("c_ctx",):
            a = a[0]
        shared[k] = np.ascontiguousarray(a.reshape(IN_SHAPES[k]))
    maps = []
    for b in range(8):
        m = dict(shared)
        m["x"] = f(inputs["x"][b])
        m["ctx"] = f(inputs["ctx"][b])
        m["c"] = f(inputs["c"][b]).reshape(8, 128)
        maps.append(m)
    return maps


_NC_CACHE = {}


def kernel(**inputs):
    if "nc" not in _NC_CACHE:
        _NC_CACHE["nc"] = build_nc()
    nc = _NC_CACHE["nc"]
    maps = make_in_maps(inputs)
    res = run_bass_kernel_spmd(nc, maps, core_ids=list(range(8)))
    out = np.stack([np.asarray(r["out"], dtype=np.float32) for r in res.results], axis=0)
    return out


def kb_phaseS5(self, ins, mid_cb=None):
    nc, P, V = self.nc, self.P, self.V
    ps = self.psum
    if not hasattr(self, "S"):
        self.S = {}
    S = self.S
    tab_d = self.dram_tmp("tab_d", [32, 128, 2, NCH], F32)
    o = PERS

    def al(n):
        nonlocal o
        r = o
        o += n
        return r
    ident = self.ident
    L = "s5"

    def dv(fn, eng="dve", extra_r=(), extra_w=()):
        P.op(eng, fn, reads=[L] + list(extra_r), writes=[L] + list(extra_w))

    def TT(out, a, b, op, eng="dve"):
        dv(lambda e: e.tensor_tensor(out=out, in0=a, in1=b, op=op), eng)

    def TS(out, a, s1, op0, s2=None, op1=None, eng="dve"):
        if op1 is None:
            dv(lambda e: e.tensor_scalar(out=out, in0=a, scalar1=s1, scalar2=None, op0=op0), eng)
        else:
            dv(lambda e: e.tensor_scalar(out=out, in0=a, scalar1=s1, scalar2=s2, op0=op0, op1=op1), eng)

    def ACT(out, a, func, scale=1.0, bias=0.0):
        dv(lambda e: e.activation(out=out, in_=a, func=func, scale=scale, bias=bias), "act")

    W1b = [V(al(2048), 2048, BF16, "p (k c) -> p k c", k=32) for _ in range(2)]
    T0b = V(al(2048), 2048, BF16, "p (g c) -> p g c", g=32)
    W2b = [V(al(2048), 2048, BF16, "p (k c) -> p k c", k=32) for _ in range(2)]
    rho8 = V(al(32), 32)
    o_mark = o
    st = V(al(384), 384)
    A = V(al(32 * 12), 32 * 12, F32, "p (n k) -> p n k", k=32)
    a_re, a_im, ldt, dtt, x1, er, th, y_, fr, cs, sn, tmp = [A[:, i, :] for i in range(12)]
    A2 = V(al(32 * 12), 32 * 12, F32, "p (n k) -> p n k", k=32)
    ab_re, ab_im, k_re, k_im, den, t1, t2, t3, w_re, w_im, iv_re, iv_im = [A2[:, i, :] for i in range(12)]
    yi = V(al(32), 32).bitcast(I32)
    PaR = V(al(512), 512, F32, "p (k j) -> p k j", j=16)
    PaI = V(al(512), 512, F32, "p (k j) -> p k j", j=16)
    Bre = V(al(512), 512, F32, "p (k c) -> p k c", c=16)
    Bim = V(al(512), 512, F32, "p (k c) -> p k c", c=16)
    Bbre = V(al(512), 512, F32, "p (k c) -> p k c", c=16)
    Bbim = V(al(512), 512, F32, "p (k c) -> p k c", c=16)
    Cre = V(al(512), 512, F32, "p (k c) -> p k c", c=16)
    Cim = V(al(512), 512, F32, "p (k c) -> p k c", c=16)
    tB = V(al(512), 512, F32, "p (k c) -> p k c", c=16)
    cst = V(al(128), 128)
    dcol = V(al(32), 32)
    maskt = [V(al(512), 512, F32, "p (r c) -> p r c", r=4) for _ in range(2)]
    identt = V(al(512), 512, F32, "p (r c) -> p r c", r=4)

    are_v = ins["s5_a_re"].rearrange("d (k m) p -> (d k) (m p)", m=2)
    aim_v = ins["s5_a_im"].rearrange("d (k m) p -> (d k) (m p)", m=2)
    P.dma("sp", lambda e: e.dma_start(out=st[0:32, 0:128], in_=are_v), writes=["s5st0"])
    P.dma("sp", lambda e: e.dma_start(out=st[0:32, 128:256], in_=aim_v), writes=["s5st1"])
    ldt2 = V(al(2), 2)
    P.dma("sp", lambda e: e.dma_start(out=ldt2[0:32, :], in_=ins["s5_log_dt"].rearrange("d (k m) -> (d k) m", m=2)), writes=["s5st2"])
    P.op("dve", lambda e: e.tensor_copy(out=st[0:32, 256:384].rearrange("p (m q) -> p m q", m=2), in_=bc_free(ldt2[0:32, :], 64)),
         reads=["s5st2", L], writes=[L])
    for mi in range(2):
        bre_v = ins["s5_b_re"].rearrange("d (k m) p c -> m p (d k) c", m=2)
        bim_v = ins["s5_b_im"].rearrange("d (k m) p c -> m p (d k) c", m=2)
        P.dma("sp", lambda e, mi=mi, bre_v=bre_v: e.dma_start(out=Bre[mi * 64:(mi + 1) * 64], in_=bre_v[mi]), writes=["s5b%d" % mi])
        P.dma("sp", lambda e, mi=mi, bim_v=bim_v: e.dma_start(out=Bim[mi * 64:(mi + 1) * 64], in_=bim_v[mi]), writes=["s5bi%d" % mi])
    dst16 = V(al(16), 16)
    dst128 = V(al(128), 128)
    P.dma("sp", lambda e: e.dma_start(out=dst16[0:32, :], in_=ins["s5_d"]), writes=["s5d16"])
    P.op("dve", lambda e: e.tensor_copy(out=dst128[0:32, :].rearrange("p (s c) -> p s c", s=8),
                                        in_=AP(dst16.tensor, dst16.offset, [[dst16.ap[0][0], 32], [0, 8], [1, 16]])),
         reads=["s5d16", L], writes=[L])
    pt = ps[:, 6, :]
    P.op("pe", lambda e: e.transpose(out=pt[:, 256:288], in_=dst128[0:32, :], identity=ident[0:32, 0:32]),
         reads=[L, "ident"], writes=["psT6"])
    P.op("dve", lambda e: e.tensor_copy(out=dcol, in_=pt[:, 256:288]), reads=["psT6", L], writes=[L])
    for i in range(3):
        P.op("pe", lambda e, i=i: e.transpose(out=pt[:, i * 32:(i + 1) * 32], in_=st[0:32, i * 128:(i + 1) * 128], identity=ident[0:32, 0:32]),
             reads=["s5st0", "s5st1", L, "ident"], writes=["psT6"])
    P.op("dve", lambda e: e.tensor_copy(out=A[:, 0:3, :], in_=pt[:, 0:96].rearrange("p (n k) -> p n k", n=3)), reads=["psT6", L], writes=[L])
    for ri, (src, dst) in enumerate(((ins["s5_c_re"], Cre), (ins["s5_c_im"], Cim))):
        cv = src.rearrange("d (k m) c p -> d k c m p", m=2)
        for d_ in range(2):
            for j in range(2):
                for kl in range(8):
                    P.dma("sp", lambda e, cv=cv, d_=d_, j=j, kl=kl: e.dma_start(
                        out=cst[kl * 16:(kl + 1) * 16, :].rearrange("p (m q) -> p m q", m=2), in_=cv[d_, 8 * j + kl]),
                        reads=[L], writes=["s5cst%d" % kl])
                P.op("pe", lambda e: e.transpose(out=pt[:, 128:256], in_=cst, identity=ident),
                     reads=["s5cst%d" % kl for kl in range(8)] + ["ident"], writes=["psT6"])
                dk0 = d_ * 16 + 8 * j
                P.op("dve", lambda e, dst=dst, dk0=dk0: e.tensor_copy(out=dst[:, dk0:dk0 + 8, :].rearrange("p k c -> p (k c)"), in_=pt[:, 128:256]),
                     reads=["psT6", L], writes=[L] + ["s5cst%d" % kl for kl in range(8)])
    ACT(dtt, ldt, AF.Exp)
    TT(x1, a_re, dtt, ALU.mult)
    ACT(er, x1, AF.Exp)
    ACT(rho8, x1, AF.Exp, scale=8.0)
    TT(th, a_im, dtt, ALU.mult)
    for (dstt, off) in ((sn, 0.5), (cs, 0.75)):
        TS(y_, th, 1.0 / TWO_PI, ALU.mult, off, ALU.add)
        dv(lambda e: e.tensor_copy(out=yi, in_=y_))
        dv(lambda e: e.tensor_copy(out=fr, in_=yi))
        TT(fr, y_, fr, ALU.subtract)
        TS(tmp, fr, 0.0, ALU.is_lt)
        TT(fr, fr, tmp, ALU.add)
        TS(tmp, fr, 1.0, ALU.is_ge)
        TT(fr, fr, tmp, ALU.subtract)
        TS(fr, fr, TWO_PI, ALU.mult, -math.pi, ALU.add)
        TS(fr, fr, math.pi, ALU.min, -math.pi, ALU.max)
        ACT(dstt, fr, AF.Sin)
    if mid_cb is not None:
        mid_cb()
    TT(ab_re, er, cs, ALU.mult)
    TT(ab_im, er, sn, ALU.mult)
    TS(t1, ab_re, -1.0, ALU.add)
    TT(den, a_re, a_re, ALU.mult)
    TT(t2, a_im, a_im, ALU.mult)
    TT(den, den, t2, ALU.add)
    dv(lambda e: e.reciprocal(out=den, in_=den))
    TT(k_re, t1, a_re, ALU.mult)
    TT(t2, ab_im, a_im, ALU.mult)
    TT(k_re, k_re, t2, ALU.add)
    TT(k_re, k_re, den, ALU.mult)
    TT(k_im, ab_im, a_re, ALU.mult)
    TT(t2, t1, a_im, ALU.mult)
    TT(k_im, k_im, t2, ALU.subtract)
    TT(k_im, k_im, den, ALU.mult)
    TT(t1, er, er, ALU.mult)
    dv(lambda e: e.reciprocal(out=t1, in_=t1))
    TT(iv_re, ab_re, t1, ALU.mult)
    TT(iv_im, ab_im, t1, ALU.mult)
    TS(iv_im, iv_im, -1.0, ALU.mult)

    def cmul(o_re, o_im, x_re, x_im, y_re, y_im):
        TT(t1, x_re, y_re, ALU.mult)
        TT(t2, x_im, y_im, ALU.mult)
        TT(t3, x_re, y_im, ALU.mult)
        TT(o_im, x_im, y_re, ALU.mult)
        TT(o_im, o_im, t3, ALU.add)
        TT(o_re, t1, t2, ALU.subtract)
    dv(lambda e: e.memset(PaR[:, :, 7], 1.0))
    dv(lambda e: e.memset(PaI[:, :, 7], 0.0))
    dv(lambda e: e.tensor_copy(out=PaR[:, :, 8], in_=ab_re))
    dv(lambda e: e.tensor_copy(out=PaI[:, :, 8], in_=ab_im))
    for jj in range(9, 16):
        cmul(PaR[:, :, jj], PaI[:, :, jj], PaR[:, :, jj - 1], PaI[:, :, jj - 1], ab_re, ab_im)
    for jj in range(6, -1, -1):
        cmul(PaR[:, :, jj], PaI[:, :, jj], PaR[:, :, jj + 1], PaI[:, :, jj + 1], iv_re, iv_im)
    dv(lambda e: e.reciprocal(out=t1, in_=rho8))
    TT(w_re, PaR[:, :, 15], t1, ALU.mult)
    TT(w_im, PaI[:, :, 15], t1, ALU.mult)
    TS(w_im, w_im, -1.0, ALU.mult)
    for (o_, x_, kx, y_b, ky, op) in ((Bbre, Bre, k_re, Bim, k_im, ALU.subtract), (Bbim, Bim, k_re, Bre, k_im, ALU.add)):
        dv(lambda e, o_=o_, x_=x_, kx=kx: e.tensor_tensor(out=o_, in0=x_, in1=bc_free(kx, 16), op=ALU.mult),
           extra_r=["s5b0", "s5b1", "s5bi0", "s5bi1"])
        dv(lambda e, y_b=y_b, ky=ky: e.tensor_tensor(out=tB, in0=y_b, in1=bc_free(ky, 16), op=ALU.mult))
        dv(lambda e, o_=o_, op=op: e.tensor_tensor(out=o_, in0=o_, in1=tB, op=op))
    for d_ in range(2):
        dv(lambda e, d_=d_: e.memset(maskt[d_], 1.0), "pool")
        if d_ == 0:
            pat, base, cm = [[0, 4], [16, 8], [0, 16]], 15, -1
        else:
            pat, base, cm = [[0, 4], [-16, 8], [0, 16]], 0, 1
        dv(lambda e, d_=d_, pat=pat, base=base, cm=cm: e.affine_select(
            out=maskt[d_], in_=maskt[d_], pattern=pat, compare_op=ALU.is_ge, fill=0.0, base=base, channel_multiplier=cm), "pool")
    dv(lambda e: e.tensor_copy(out=identt, in_=AP(ident.tensor, ident.offset, [list(ident.ap[0]), [0, 4], [1, 128]])), "pool", extra_r=["ident"])

    o_fm = o
    Fm = [V(al(2048), 2048, F32, "p (k s c) -> p k s c", k=16, s=8) for _ in range(2)]
    W2m = [V(al(2048), 2048, F32, "p (k t c) -> p k t c", k=16, t=8) for _ in range(2)]
    W2p = [V(al(2048), 2048, F32, "p (k t c) -> p k t c", k=16, t=8) for _ in range(2)]
    tF = V(al(2048), 2048, F32, "p (k s c) -> p k s c", k=16, s=8)
    T0acc = V(al(4096), 4096, F32, "p (g c) -> p g c", g=32)
    tM = V(al(512), 512)

    def pw(Pt, dk0, j0, jstep):
        b = Pt[:, dk0:dk0 + 16, :]
        return AP(b.tensor, b.offset + j0, [list(b.ap[0]), [16, 16], [jstep, 8], [0, 16]])

    def bx(Bt, dk0):
        b = Bt[:, dk0:dk0 + 16, :]
        return AP(b.tensor, b.offset, [list(b.ap[0]), [16, 16], [0, 8], [1, 16]])

    def outer(o_re, o_im, dk0, j0, jstep, Xre, Xim, neg_im):
        dv(lambda e: e.tensor_tensor(out=o_re, in0=pw(PaR, dk0, j0, jstep), in1=bx(Xre, dk0), op=ALU.mult))
        dv(lambda e: e.tensor_tensor(out=tF, in0=pw(PaI, dk0, j0, jstep), in1=bx(Xim, dk0), op=ALU.mult))
        dv(lambda e: e.tensor_tensor(out=o_re, in0=o_re, in1=tF, op=ALU.subtract))
        dv(lambda e: e.tensor_tensor(out=o_im, in0=pw(PaR, dk0, j0, jstep), in1=bx(Xim, dk0), op=ALU.mult))
        dv(lambda e: e.tensor_tensor(out=tF, in0=pw(PaI, dk0, j0, jstep), in1=bx(Xre, dk0), op=ALU.mult))
        dv(lambda e: e.tensor_tensor(out=o_im, in0=o_im, in1=tF, op=ALU.add))
        if neg_im:
            dv(lambda e: e.tensor_scalar(out=o_im, in0=o_im, scalar1=-1.0, scalar2=None, op0=ALU.mult))

    for d_ in range(2):
        dk0 = d_ * 16
        if d_ == 0:
            outer(Fm[0], Fm[1], dk0, 14, -1, Bbre, Bbim, False)
            outer(W2m[0], W2m[1], dk0, 8, 1, Cre, Cim, True)
            outer(W2p[0], W2p[1], dk0, 0, 1, Cre, Cim, True)
        else:
            outer(Fm[0], Fm[1], dk0, 7, 1, Bbre, Bbim, False)
            outer(W2m[0], W2m[1], dk0, 15, -1, Cre, Cim, True)
            outer(W2p[0], W2p[1], dk0, 7, -1, Cre, Cim, True)
        for r in range(2):
            dv(lambda e, r=r, dk0=dk0: e.tensor_copy(out=W2b[r][:, dk0:dk0 + 16, :].rearrange("p k (c t) -> p k t c", t=8), in_=W2m[r]),
               "pool", extra_w=["W2b"])
        for r in range(2):
            for kb4 in range(4):
                bank = ps[:, kb4 % 2, :]
                for q in range(4):
                    k_ = kb4 * 4 + q
                    P.op("pe", lambda e, r=r, k_=k_, q=q, bank=bank: e.transpose(
                        out=bank[:, q * 128:(q + 1) * 128], in_=Fm[r][:, k_, :, :].rearrange("p s c -> p (s c)"), identity=ident),
                        reads=[L, "ident"], writes=["ps%d" % (kb4 % 2)])
                P.op("dve", lambda e, r=r, kb4=kb4, dk0=dk0, bank=bank: e.tensor_copy(
                    out=W1b[r][:, dk0 + kb4 * 4:dk0 + kb4 * 4 + 4, :].rearrange("p k c -> p (k c)"), in_=bank),
                    reads=["ps%d" % (kb4 % 2)], writes=["W1b"])
        for gb in range(8):
            mi = gb // 4
            bank = ps[:, 2 + gb % 2, :]
            g0 = 2 * 4 * (gb % 4) + mi
            for q in range(4):
                k_ = 4 * (gb % 4) + q
                for r in range(2):
                    P.op("pe", lambda e, r=r, k_=k_, mi=mi, q=q, bank=bank: e.matmul(
                        bank[:, q * 128:(q + 1) * 128],
                        lhsT=Fm[r][mi * 64:(mi + 1) * 64, k_, :, :].rearrange("p s c -> p (s c)"),
                        rhs=W2p[r][mi * 64:(mi + 1) * 64, k_, :, :].rearrange("p t c -> p (t c)"),
                        start=(r == 0), stop=(r == 1)),
                        reads=[L], writes=["ps%d" % (2 + gb % 2)])
            tb0 = T0acc[:, g0, :]
            acc = AP(tb0.tensor, tb0.offset, [list(tb0.ap[0]), [256, 4], [1, 128]])
            bk = bank.rearrange("p (r c) -> p r c", r=4)
            tM3 = tM.rearrange("p (r c) -> p r c", r=4)
            if d_ == 0:
                P.op("dve", lambda e, acc=acc, bk=bk: e.tensor_tensor(out=acc, in0=bk, in1=maskt[0], op=ALU.mult),
                     reads=["ps%d" % (2 + gb % 2), L], writes=["T0acc"])
            else:
                P.op("dve", lambda e, bk=bk, tM3=tM3: e.tensor_tensor(out=tM3, in0=bk, in1=maskt[1], op=ALU.mult),
                     reads=["ps%d" % (2 + gb % 2), L, "T0acc"], writes=["tM"])
                P.op("dve", lambda e, acc=acc, tM3=tM3: e.tensor_tensor(out=acc, in0=acc, in1=tM3, op=ALU.add),
                     reads=["tM", "T0acc"], writes=["T0acc"])
                dc0 = dcol[:, g0:g0 + 1]
                dcb = AP(dc0.tensor, dc0.offset, [list(dc0.ap[0]), [2, 4], [0, 128]])
                P.op("dve", lambda e, dcb=dcb, tM3=tM3: e.tensor_tensor(out=tM3, in0=identt, in1=dcb, op=ALU.mult),
                     reads=["tM", L], writes=["tM"])
                t0b = T0b[:, g0, :]
                o4 = AP(t0b.tensor, t0b.offset, [list(t0b.ap[0]), [256, 4], [1, 8], [8, 16]])
                a4 = AP(tb0.tensor, tb0.offset, [list(tb0.ap[0]), [256, 4], [16, 8], [1, 16]])
                P.op("dve", lambda e, o4=o4, a4=a4: e.tensor_tensor(out=o4, in0=a4, in1=tM.rearrange("p (r t c) -> p r t c", r=4, t=8), op=ALU.add),
                     reads=["tM", "T0acc"], writes=["T0b", "tM"])
        P.op("dve", lambda e: e.memset(tM[:, 0:1], 0.0), reads=["ps0", "ps1", "ps2", "ps3", "tM", L], writes=[L, "tM"])

    P.op("dve", lambda e: e.memset(tM[:, 0:1], 0.0), reads=["ps0", "ps1", "ps2", "ps3", "tM", "T0acc", "T0b", "W1b", "W2b", L], writes=[L, "tM", "T0acc"])
    o = o_fm
    wp = [V(al(64), 64, F32, "p (r k) -> p r k", r=2) for _ in range(10)]
    dv(lambda e: e.tensor_copy(out=wp[0][:, 0, :], in_=w_re))
    dv(lambda e: e.tensor_copy(out=wp[0][:, 1, :], in_=w_im))
    for q in range(1, 10):
        a_, b_ = wp[q - 1][:, 0, :], wp[q - 1][:, 1, :]
        TT(t1, a_, a_, ALU.mult)
        TT(t2, b_, b_, ALU.mult)
        TT(wp[q][:, 0, :], t1, t2, ALU.subtract)
        TT(t1, a_, b_, ALU.mult)
        TS(wp[q][:, 1, :], t1, 2.0, ALU.mult)
    TBL = V(al(8 * 2 * NCH), 8 * 2 * NCH, F32, "p (k r n) -> p k r n", k=8, r=2)
    tq = [V(al(2048), 2048) for _ in range(2)]
    for bt in range(4):
        k0 = bt * 8
        dv(lambda e: e.memset(TBL[:, :, 0, 0:1], 1.0))
        dv(lambda e: e.memset(TBL[:, :, 1, 0:1], 0.0))
        dv(lambda e, k0=k0: e.tensor_copy(out=TBL[:, :, 0, 1], in_=wp[0][:, 0, k0:k0 + 8]))
        dv(lambda e, k0=k0: e.tensor_copy(out=TBL[:, :, 1, 1], in_=wp[0][:, 1, k0:k0 + 8]))
        for q in range(1, 10):
            sz = 2 ** q
            n = min(sz, NCH - sz)
            lo_r, lo_i = TBL[:, :, 0, 0:n], TBL[:, :, 1, 0:n]
            hi_r, hi_i = TBL[:, :, 0, sz:sz + n], TBL[:, :, 1, sz:sz + n]
            wr = bc_free(wp[q][:, 0, k0:k0 + 8], n)
            wi = bc_free(wp[q][:, 1, k0:k0 + 8], n)
            u1 = tq[0][:, 0:8 * n].rearrange("p (k n) -> p k n", k=8)
            u2 = tq[1][:, 0:8 * n].rearrange("p (k n) -> p k n", k=8)
            TT(u1, lo_r, wr, ALU.mult)
            TT(u2, lo_i, wi, ALU.mult)
            TT(hi_r, u1, u2, ALU.subtract)
            TT(u1, lo_r, wi, ALU.mult)
            TT(u2, lo_i, wr, ALU.mult)
            TT(hi_i, u1, u2, ALU.add)
        P.dma("pool", lambda e, k0=k0: e.dma_start(out=tab_d[k0:k0 + 8].rearrange("k p r n -> p k r n"), in_=TBL),
              reads=[L], writes=[("tabd", bt)], sem="st_tab", soft=["taball"])
        dv(lambda e: e.memset(tq[0][:, 0:1], 0.0), extra_r=[("tabd", bt)])
    s5m_d = self.dram_tmp("s5m_d", [5, 128, 4096], BF16)
    rho_d = self.dram_tmp("rho_d", [128, 32], F32)
    for i_, m_ in enumerate((W1b[0], W1b[1], T0b, W2b[0], W2b[1])):
        P.dma("pool", lambda e, i_=i_, m_=m_: e.dma_start(out=s5m_d[i_], in_=m_.rearrange("p k c -> p (k c)")),
              reads=["W1b", "T0b", "W2b", L], writes=[("s5md", i_)], sem="st_s5m", soft=["s5mall"])
    P.dma("pool", lambda e: e.dma_start(out=rho_d, in_=rho8), reads=[L], writes=["rhod"], sem="st_s5m", soft=["s5mall"])
    self.s5 = dict(s5m_d=s5m_d, rho_d=rho_d, tab_d=tab_d, o_end=o, L=L)


def kb_phaseS5main(self, ins):
    nc, P, V = self.nc, self.P, self.V
    ps = self.psum
    S = self.S
    s5 = self.s5
    tab_d = s5["tab_d"]
    o_mark = PERS
    if "u_in" in self.debug:
        S["u"] = self.dram_in("u_d", [32, 16, 8, NCH], BF16)
    S["y"] = self.dram_tmp("y_d", [32, 16, 8, 512], F32)

    def al(n):
        nonlocal o
        r = o
        o += n
        return r
    o = o_mark
    W1b = [V(al(2048), 2048, BF16, "p (k c) -> p k c", k=32) for _ in range(2)]
    T0b = V(al(2048), 2048, BF16, "p (g c) -> p g c", g=32)
    W2b = [V(al(2048), 2048, BF16, "p (k c) -> p k c", k=32) for _ in range(2)]
    rho8 = V(al(32), 32)
    for i_, (m_, rn) in enumerate(((W1b[0], "W1b"), (W1b[1], "W1b1"), (T0b, "T0b"), (W2b[0], "W2b"), (W2b[1], "W2b1"))):
        P.dma("sp", lambda e, i_=i_, m_=m_: e.dma_start(out=m_.rearrange("p k c -> p (k c)"), in_=s5["s5m_d"][i_]), reads=["s5mall"], writes=[rn])
    P.dma("sp", lambda e: e.dma_start(out=rho8, in_=s5["rho_d"]), reads=["s5mall"], writes=["s5"])
    U = V(al(32 * NCH // 2), 32 * NCH // 2, BF16, "p (g c) -> p g c", g=32)
    Ur = V(al(32 * NCH // 2), 32 * NCH // 2, BF16, "p (g c) -> p g c", g=32)
    tb = [V(al(2 * NCH), 2 * NCH, F32, "p (r n) -> p r n", r=2) for _ in range(2)]
    Gin = V(al(2 * NCH), 2 * NCH, F32, "p (r n) -> p r n", r=2)
    Gs = V(al(2 * NCH), 2 * NCH, F32, "p (r n) -> p r n", r=2)
    tt_ = [V(al(NCH), NCH) for _ in range(4)]
    Hb = [[V(al(256), 256, BF16) for _ in range(2)] for _ in range(2)]
    ysb = [V(al(512), 512) for _ in range(2)]
    assert o <= AW, o
    for s_ in range(8):
        P.dma("sp", lambda e, s_=s_: e.dma_start(out=U[16 * s_:16 * s_ + 16], in_=S["u"][:, :, s_, :].rearrange("g i c -> i g c")),
              reads=["uall"], writes=["U%d" % s_])
    Ures = ["U%d" % s_ for s_ in range(8)]
    Ub = U[:, :, :]
    P.op("pool", lambda e: e.tensor_copy(out=Ur[:, :, 0:32], in_=AP(Ub.tensor, Ub.offset + 31, [list(Ub.ap[0]), [NCH, 32], [-1, 32]])),
         reads=Ures, writes=["Ur0"])
    P.op("pool", lambda e: e.tensor_copy(out=Ur[:, :, 32:NCH], in_=AP(Ub.tensor, Ub.offset + NCH - 1, [list(Ub.ap[0]), [NCH, 32], [-1, 512]])),
         reads=Ures, writes=["Ur1"])
    for k in range(16):
        for d_ in range(2):
            dk = d_ * 16 + k
            T = tb[dk % 2]
            tr = "tb%d" % (dk % 2)
            P.dma("sp", lambda e, T=T, dk=dk: e.dma_start(out=T, in_=tab_d[dk]), reads=["taball"], writes=[tr])
            Us = U if d_ == 0 else Ur
            ures = Ures if d_ == 0 else ["Ur0", "Ur1"]
            for mi in range(2):
                g = 2 * k + mi
                for r in range(2):
                    lh = W1b[r][:, dk, mi * 64:(mi + 1) * 64]
                    P.op("pe", lambda e, lh=lh, r=r, mi=mi, g=g, Us=Us: e.matmul(ps[mi * 64:(mi + 1) * 64, r, :], lhsT=lh, rhs=Us[:, g, 0:512], start=True, stop=True),
                         reads=ures + ["W1b", "W1b1"], writes=["ps%d" % r])
                    P.op("pe", lambda e, lh=lh, r=r, mi=mi, g=g, Us=Us: e.matmul(ps[mi * 64:(mi + 1) * 64, 2, r * 32:(r + 1) * 32], lhsT=lh, rhs=Us[:, g, 512:NCH], start=True, stop=True),
                         reads=ures + ["W1b", "W1b1"], writes=["ps2"])
            for (c0, c1, sre, sim, pr) in ((0, 512, ps[:, 0, :], ps[:, 1, :], ["ps0", "ps1"]), (512, NCH, ps[:, 2, 0:32], ps[:, 2, 32:64], ["ps2"])):
                TRs, TIs = T[:, 0, c0:c1], T[:, 1, c0:c1]
                t1_, t2_, t3_, t4_ = [x[:, c0:c1] for x in tt_]
                P.op("dve", lambda e, t1_=t1_, sre=sre, TRs=TRs: e.tensor_tensor(out=t1_, in0=sre, in1=TRs, op=ALU.mult), reads=pr + [tr], writes=["tt0"])
                P.op("dve", lambda e, t2_=t2_, sim=sim, TIs=TIs: e.tensor_tensor(out=t2_, in0=sim, in1=TIs, op=ALU.mult), reads=pr + [tr], writes=["tt1"])
                P.op("dve", lambda e, t3_=t3_, sim=sim, TRs=TRs: e.tensor_tensor(out=t3_, in0=sim, in1=TRs, op=ALU.mult), reads=pr + [tr], writes=["tt2"])
                P.op("dve", lambda e, t4_=t4_, sre=sre, TIs=TIs: e.tensor_tensor(out=t4_, in0=sre, in1=TIs, op=ALU.mult), reads=pr + [tr], writes=["tt3"])
            P.op("pool", lambda e: e.tensor_tensor(out=Gin[:, 0, :], in0=tt_[0], in1=tt_[1], op=ALU.subtract), reads=["tt0", "tt1"], writes=["Gin0"])
            P.op("pool", lambda e: e.tensor_tensor(out=Gin[:, 1, :], in0=tt_[2], in1=tt_[3], op=ALU.add), reads=["tt2", "tt3"], writes=["Gin1"])
            rb = bc_free(rho8[:, dk:dk + 1], NCH).rearrange("p a n -> p (a n)") if False else AP(rho8.tensor, rho8.offset + dk, [list(rho8.ap[0]), [0, NCH]])
            for r in range(2):
                P.op("dve", lambda e, r=r, rb=rb: e.tensor_tensor_scan(out=Gs[:, r, :], data0=rb, data1=Gin[:, r, :], initial=0.0, op0=ALU.mult, op1=ALU.add),
                     reads=["Gin%d" % r, "s5"], writes=["Gs%d" % r])
            TRs, TIs = T[:, 0, 31:543], T[:, 1, 31:543]
            gr, gi = Gs[:, 0, 31:543], Gs[:, 1, 31:543]
            u1, u2, u3, u4 = [x[:, 0:512] for x in tt_]
            P.op("pool", lambda e, TRs=TRs, gr=gr, u1=u1: e.tensor_tensor(out=u1, in0=TRs, in1=gr, op=ALU.mult), reads=[tr, "Gs0", "Gin0", "Gin1"], writes=["tt0"])
            P.op("pool", lambda e, TIs=TIs, gi=gi, u2=u2: e.tensor_tensor(out=u2, in0=TIs, in1=gi, op=ALU.mult), reads=[tr, "Gs1", "Gin0", "Gin1"], writes=["tt1"])
            P.op("pool", lambda e, TRs=TRs, gi=gi, u3=u3: e.tensor_tensor(out=u3, in0=TRs, in1=gi, op=ALU.mult), reads=[tr, "Gs1", "Gin0", "Gin1"], writes=["tt2"])
            P.op("pool", lambda e, TIs=TIs, gr=gr, u4=u4: e.tensor_tensor(out=u4, in0=TIs, in1=gr, op=ALU.mult), reads=[tr, "Gs0", "Gin0", "Gin1"], writes=["tt3"])
            outs_ = []
            for r in range(2):
                hb = Hb[d_][r]
                if d_ == 0:
                    outs_.append(hb)
                else:
                    outs_.append(AP(hb.tensor, hb.offset + 511, [list(hb.ap[0]), [-1, 512]]))
            P.op("pool", lambda e, o_=outs_[0], u1=u1, u2=u2: e.tensor_tensor(out=o_, in0=u1, in1=u2, op=ALU.add), reads=["tt0", "tt1"], writes=["Hb%d0" % d_])
            P.op("pool", lambda e, o_=outs_[1], u3=u3, u4=u4: e.tensor_tensor(out=o_, in0=u3, in1=u4, op=ALU.subtract), reads=["tt2", "tt3"], writes=["Hb%d1" % d_])
        for mi in range(2):
            g = 2 * k + mi
            yb = ps[:, 4 + g % 2, :]
            yr = "ps%d" % (4 + g % 2)
            P.op("pe", lambda e, g=g, yb=yb: e.matmul(yb, lhsT=T0b[:, g, :], rhs=U[:, g, 32:NCH], start=True, stop=False),
                 reads=Ures + ["T0b"], writes=[yr])
            n_ = 0
            for d_ in range(2):
                for r in range(2):
                    n_ += 1
                    lh = W2b[r][mi * 64:(mi + 1) * 64, d_ * 16 + k, :]
                    rh = Hb[d_][r][mi * 64:(mi + 1) * 64, :]
                    P.op("pe", lambda e, lh=lh, rh=rh, yb=yb, n_=n_: e.matmul(yb, lhsT=lh, rhs=rh, start=False, stop=(n_ == 4)),
                         reads=["W2b", "W2b1", "Hb%d%d" % (d_, r)], writes=[yr])
            yt = ysb[g % 2]
            P.op("act", lambda e, yt=yt, yb=yb: e.activation(out=yt, in_=yb, func=AF.Copy), reads=[yr], writes=["ysb%d" % (g % 2)])
            P.dma("pool", lambda e, yt=yt, g=g: e.dma_start(out=S["y"][g].rearrange("c t n -> (c t) n"), in_=yt),
                  reads=["ysb%d" % (g % 2)], writes=[("yd", g)], sem="st_y%d" % (g % 2), soft=["yall"])


KB.phaseS5 = kb_phaseS5
KB.phaseS5main = kb_phaseS5main


MASK_E = -4000.0


def att_plan():
    sets = {}
    plan = []
    for a in range(32):
        rows = (2 * a, 2 * a + 1)
        r0 = [min(max(r - 4, 0), 56) for r in rows]
        lo, hi = min(r0), max(r0) + 7
        tiles = []
        for kt in range(lo // 2, hi // 2 + 1):
            ent = []
            for il in range(2):
                for rl in range(2):
                    i_abs = 2 * kt + il
                    ok = r0[rl] <= i_abs <= r0[rl] + 7
                    ent.append((i_abs - rows[rl] + 7) if ok else None)
            key = tuple(ent)
            if key not in sets:
                sets[key] = len(sets)
            tiles.append((kt, sets[key]))
        plan.append(tiles)
    return plan, sets


def kb_phaseATT(self, ins):
    nc, P, V = self.nc, self.P, self.V
    ps = self.psum
    if not hasattr(self, "S"):
        self.S = {}
    S = self.S
    for nm, shp in (("q", [4, 128, NLAT]), ("k", [4, 128, NT]), ("v", [NT, 512])):
        if nm + "_d_in" in self.debug:
            S[nm] = self.dram_in(nm + "_d", shp, BF16)
    S["yna"] = self.dram_tmp("yna_d", [4, 128, NLAT], BF16)
    tt_d = self.dram_tmp("tt_d", [8, 15, 64, 64], BF16)
    ttm_d = self.dram_tmp("ttm_d", [64, 8, 64], BF16)
    plan, sets = att_plan()
    NS = len(sets)
    o = PERS

    def al(n):
        nonlocal o
        r = o
        o += n
        return r
    ident, identb, onesb = self.ident, self.identb, self.onesb
    kT = V(al(4 * NT // 2), 4 * NT // 2, BF16, "p (m t) -> p m t", m=4)
    Vt = V(al(34 * 256), 34 * 256, BF16, "p (k f) -> p k f", k=34)
    TB = V(al(NS * 8 * 64), NS * 8 * 64, BF16, "p (s h c) -> p s h c", s=NS, h=8)
    qT = [V(al(256), 256, BF16, "p (m t) -> p m t", m=4) for _ in range(2)]
    PT = [V(al(448), 448, BF16, "p (i t) -> p i t", i=7) for _ in range(2)]
    rc = V(al(128), 128)
    yst = [V(al(256), 256, BF16, "p (m t) -> p m t", m=4) for _ in range(2)]
    o_tmp = o
    for m in range(4):
        P.dma("sp", lambda e, m=m: e.dma_start(out=kT[:, m, :], in_=S["k"][m]), reads=["kall"], writes=["kT%d" % m])
    vv = S["v"].rearrange("(k p) f -> p k f", p=128)
    for i in range(2):
        P.dma("sp", lambda e, i=i: e.dma_start(out=Vt[:, i * 17:(i + 1) * 17, :], in_=vv[:, i * 17:(i + 1) * 17, :]), reads=["vall"], writes=["Vt%d" % i])
    L = "att"
    stg = V(al(32), 32)
    Lm = V(al(120), 120)
    E = V(al(4096), 4096, F32, "p (j c) -> p j c", j=64)
    E2 = V(al(4096), 4096, F32, "p (j c) -> p j c", j=64)
    E3 = V(al(4096), 4096, F32, "p (j c) -> p j c", j=64)
    rsel = V(al(2), 2)
    ttb = [V(al(256), 256, BF16) for _ in range(2)]
    mtile = V(al(256), 256, BF16)
    assert o <= AW, o

    def dv(fn, eng="dve", extra_r=(), extra_w=()):
        P.op(eng, fn, reads=[L] + list(extra_r), writes=[L] + list(extra_w))
    dv(lambda e: e.memset(stg, 1.0), "pool")
    P.dma("sp", lambda e: e.dma_start(out=stg[0:120, 0:31], in_=ins["na_rpb"].rearrange("h a b -> (h a) b")), reads=[L], writes=["attstg"])
    P.op("pe", lambda e: e.transpose(out=ps[0:32, 6, 0:120], in_=stg[0:120, :], identity=ident[0:120, 0:120]), reads=["attstg", L, "ident"], writes=["psT6"])
    P.op("dve", lambda e: e.tensor_copy(out=Lm[0:32, :], in_=ps[0:32, 6, 0:120]), reads=["psT6", L], writes=[L])
    E_, E2_, E3_ = E[0:32], E2[0:32], E3[0:32]
    dv(lambda e: e.iota(E_, pattern=[[1, 64], [-1, 64]], base=15, channel_multiplier=-1, allow_small_or_imprecise_dtypes=True), "pool")
    dv(lambda e: e.tensor_scalar(out=E_, in0=E_, scalar1=0.0, scalar2=None, op0=ALU.is_equal))
    dv(lambda e: e.iota(E2_, pattern=[[0, 64], [1, 64]], base=0, channel_multiplier=0, allow_small_or_imprecise_dtypes=True), "pool")
    dv(lambda e: e.tensor_scalar(out=E2_, in0=E2_, scalar1=-8.0, scalar2=0.0, op0=ALU.add, op1=ALU.max))
    dv(lambda e: e.tensor_scalar(out=E2_, in0=E2_, scalar1=48.0, scalar2=None, op0=ALU.min))
    dv(lambda e: e.iota(E3_, pattern=[[1, 64], [0, 64]], base=0, channel_multiplier=0, allow_small_or_imprecise_dtypes=True), "pool")
    dv(lambda e: e.tensor_tensor(out=E3_, in0=E3_, in1=E2_, op=ALU.subtract))
    dv(lambda e: e.tensor_scalar(out=E2_, in0=E3_, scalar1=0.0, scalar2=None, op0=ALU.is_ge))
    dv(lambda e: e.tensor_scalar(out=E3_, in0=E3_, scalar1=15.0, scalar2=None, op0=ALU.is_le))
    dv(lambda e: e.tensor_tensor(out=E2_, in0=E2_, in1=E3_, op=ALU.mult))
    dv(lambda e: e.tensor_tensor(out=E_, in0=E_, in1=E2_, op=ALU.mult))
    dv(lambda e: e.tensor_scalar(out=E2_, in0=E2_, scalar1=-MASK_E, scalar2=MASK_E, op0=ALU.mult, op1=ALU.add))
    rs = rsel[0:32, :]
    dv(lambda e: e.iota(rs[:, 0:1], pattern=[[0, 1]], base=0, channel_multiplier=1, allow_small_or_imprecise_dtypes=True), "pool")
    dv(lambda e: e.tensor_scalar(out=rs[:, 0:1], in0=rs[:, 0:1], scalar1=31.0, scalar2=None, op0=ALU.is_lt))
    dv(lambda e: e.tensor_scalar(out=rs[:, 1:2], in0=rs[:, 0:1], scalar1=-1.0, scalar2=1.0, op0=ALU.mult, op1=ALU.add))
    dv(lambda e: e.tensor_scalar(out=E_, in0=E_, scalar1=rs[:, 0:1], scalar2=None, op0=ALU.mult))
    dv(lambda e: e.scalar_tensor_tensor(out=E_, in0=E2_, scalar=rs[:, 1:2], in1=E_, op0=ALU.mult, op1=ALU.add))
    Ef = E_.rearrange("p j c -> p (j c)")
    ttv = tt_d.rearrange("h a j c -> (h a) (j c)")
    for i in range(8):
        bank = ps[0:120, i % 2, :]
        P.op("pe", lambda e, i=i, bank=bank: e.matmul(bank, lhsT=Lm[0:32, :], rhs=Ef[:, i * 512:(i + 1) * 512], start=True, stop=True),
             reads=[L], writes=["ps%d" % (i % 2)])
        tb_ = ttb[i % 2][0:120, :]
        P.op("act", lambda e, tb_=tb_, bank=bank: e.activation(out=tb_, in_=bank, func=AF.Identity, scale=8.0), reads=["ps%d" % (i % 2)], writes=["ttb%d" % (i % 2)])
        P.dma("sp", lambda e, i=i, tb_=tb_: e.dma_start(out=ttv[:, i * 512:(i + 1) * 512], in_=tb_), reads=["ttb%d" % (i % 2)], writes=[("ttd", i)],
              sem="st_tt%d" % (i % 2), soft=["ttall"])
    dv(lambda e: e.memset(mtile, 8.0 * MASK_E), "pool")
    msrc = AP(mtile.tensor, mtile.offset, [[mtile.ap[0][0], 64], [1, 256]])
    P.dma("sp", lambda e: e.dma_start(out=ttm_d.rearrange("j h c -> j (h c)")[:, 0:256], in_=msrc),
          reads=[L], writes=["ttm_a"], sem="st_ttm")
    P.dma("sp", lambda e: e.dma_start(out=ttm_d.rearrange("j h c -> j (h c)")[:, 256:512], in_=msrc),
          reads=[L], writes=["ttm_b"], sem="st_ttm2")
    self.P.barrier()
    for key, si in sets.items():
        n_ = 0
        for il in range(2):
            for rl in range(2):
                di = key[il * 2 + rl]
                dst = TB[il * 64:(il + 1) * 64, si, :, rl * 64:(rl + 1) * 64]
                if di is None:
                    src = ttm_d
                else:
                    src = tt_d[:, di, :, :].rearrange("h j c -> j h c")
                P.dma("sp", lambda e, dst=dst, src=src: e.dma_start(out=dst, in_=src), writes=[("TB", si, il, rl)], sem="ld_tb%d" % (n_ % 4), soft=["TBall"])
                n_ += 1
    self.P.barrier()

    SCALE = 0.125
    tasks = []
    for a in range(32):
        tiles = [(256 + kt * 128, 2 + kt, si) for (kt, si) in plan[a]] + [(0, 0, None), (128, 1, None)]
        for m in range(4):
            for hh in range(2):
                tasks.append(dict(a=a, m=m, hh=hh, h=2 * m + hh, tiles=tiles, n=len(tasks)))

    def emit_q(a):
        qt = qT[a % 2]
        P.dma("sp", lambda e, a=a, qt=qt: e.dma_start(out=qt, in_=S["q"][:, :, a * 128:(a + 1) * 128].rearrange("m p t -> p m t")),
              reads=["qall"], writes=["qT%d" % (a % 2)])

    def emit_qk(t):
        a, m, hh, h, tiles, n = t["a"], t["m"], t["hh"], t["h"], t["tiles"], t["n"]
        qt = qT[a % 2]
        qr = "qT%d" % (a % 2)
        sb_ = (n % 2) * 2
        st2 = ps[:, sb_:sb_ + 2, :].rearrange("p a b -> p (a b)")
        pt_ = PT[n % 2]
        ptr = "PT%d" % (n % 2)
        nt_ = len(tiles)
        for i, (tok, vt, si) in enumerate(tiles):
            bres = "ps%d" % (sb_ + i // 4)
            o_ap = st2[:, i * 128:(i + 1) * 128]
            P.op("pe", lambda e, o_ap=o_ap, hh=hh, m=m, tok=tok, qt=qt, si=si: e.matmul(
                o_ap, lhsT=kT[hh * 64:(hh + 1) * 64, m, tok:tok + 128], rhs=qt[hh * 64:(hh + 1) * 64, m, :],
                start=True, stop=(si is None), skip_group_check=True), reads=["kT%d" % m, qr], writes=[bres])
            if si is not None:
                P.op("pe", lambda e, o_ap=o_ap, si=si, h=h: e.matmul(o_ap, lhsT=identb, rhs=TB[:, si, h, :], start=False, stop=True, skip_group_check=True),
                     reads=["identb"], writes=[bres])
        for bnk in range((nt_ + 3) // 4):
            c0, c1 = bnk * 4, min(nt_, bnk * 4 + 4)
            P.op("act", lambda e, c0=c0, c1=c1, st2=st2, pt_=pt_: e.activation(
                out=pt_[:, c0:c1, :].rearrange("p i t -> p (i t)"), in_=st2[:, c0 * 128:c1 * 128], func=AF.Exp, scale=SCALE),
                reads=["ps%d" % (sb_ + bnk)], writes=[ptr + "b%d" % bnk])

    def emit_pv(t):
        a, m, hh, h, tiles, n = t["a"], t["m"], t["hh"], t["h"], t["tiles"], t["n"]
        pt_ = PT[n % 2]
        ptr = "PT%d" % (n % 2)
        nt_ = len(tiles)
        ob = ps[:, 4 + (a * 4 + m) % 2, :]
        obr = "ps%d" % (4 + (a * 4 + m) % 2)
        for i, (tok, vt, si) in enumerate(tiles):
            P.op("pe", lambda e, i=i, vt=vt, h=h, hh=hh, ob=ob, pt_=pt_: e.matmul(
                ob[hh * 64:(hh + 1) * 64, 0:128], lhsT=Vt[:, vt, h * 64:(h + 1) * 64], rhs=pt_[:, i, :], start=(i == 0), stop=(i == nt_ - 1), skip_group_check=True),
                reads=["Vt0", "Vt1", ptr + "b%d" % (i // 4)], writes=[obr])
        for i, (tok, vt, si) in enumerate(tiles):
            P.op("pe", lambda e, i=i, hh=hh, ob=ob, pt_=pt_: e.matmul(
                ob[hh * 64:(hh + 1) * 64, 128:256], lhsT=onesb, rhs=pt_[:, i, :], start=(i == 0), stop=(i == nt_ - 1), skip_group_check=True),
                reads=["onesb", ptr + "b%d" % (i // 4)], writes=[obr])
        if hh == 1:
            ys = yst[a % 2]
            P.op("dve", lambda e, ob=ob: e.reciprocal(out=rc, in_=ob[:, 128:256]), reads=[obr], writes=["rc"])
            P.op("dve", lambda e, ob=ob, ys=ys, m=m: e.tensor_tensor(out=ys[:, m, :], in0=ob[:, 0:128], in1=rc, op=ALU.mult), reads=[obr, "rc"], writes=["yst%d_%d" % (a % 2, m)])
            if m == 3:
                P.dma("pool", lambda e, a=a, ys=ys: e.dma_start(out=S["yna"][:, :, a * 128:(a + 1) * 128].rearrange("m p t -> p m t"), in_=ys),
                      reads=["yst%d_%d" % (a % 2, mm) for mm in range(4)], writes=[("ynad", a)], sem="st_yna%d" % (a % 2), soft=["ynaall"])

    emit_q(0)
    emit_qk(tasks[0])
    for ti, t in enumerate(tasks):
        if ti + 1 < len(tasks):
            tn = tasks[ti + 1]
            if tn["m"] == 0 and tn["hh"] == 0:
                emit_q(tn["a"])
            emit_qk(tn)
        emit_pv(t)


KB.phaseATT = kb_phaseATT


def kb_phaseC(self, ins):
    nc, P, V = self.nc, self.P, self.V
    ps = self.psum
    S, Wd = self.S, self.W
    for nm, shp, dt_ in (("xs1", [8, 128, NLAT], F32), ("yna", [4, 128, NLAT], BF16), ("y", [32, 16, 8, 512], F32)):
        if nm + "_d_in" in self.debug:
            S[nm] = self.dram_in(nm + "_d", shp, dt_)
    o = PERS
    R = {}

    def al(n):
        nonlocal o
        r = o
        o += n
        return r
    xsA = V(al(4096), 4096, F32, "p (k t) -> p k t", k=8)
    xsB = V(al(4096), 4096, F32, "p (k t) -> p k t", k=8)
    h3T = V(al(2048), 2048, BF16, "p (k t) -> p k t", k=8)
    R["actT"] = V(al(5632), 5632, BF16, "p (j t) -> p j t", j=NJ)
    R["sg"] = [V(al(512), 512) for _ in range(2)]
    R["zn"] = [V(al(1024), 1024) for _ in range(2)]
    R["mv"] = V(al(32), 32, F32, "p (s c) -> p s c", c=8)
    R["bst"] = [V(al(16), 16) for _ in range(2)]
    R["wup"] = [V(al(1024), 1024, BF16, "p (k c) -> p k c", k=8) for _ in range(4)]
    R["wdn"] = [V(al(1408), 1408, BF16, "p (j c) -> p j c", j=NJ) for _ in range(3)]
    ypre = V(al(2048), 2048, F32, "p (m t) -> p m t", m=4)
    gT = V(al(2048), 2048, F32, "p (m t) -> p m t", m=4)
    gtmp = V(al(2048), 2048, F32, "p (m t) -> p m t", m=4)
    gTb = V(al(1024), 1024, BF16, "p (m t) -> p m t", m=4)
    sig = [V(al(512), 512) for _ in range(2)]
    ymix = V(al(2048), 2048, BF16, "p (k t) -> p k t", k=8)
    wout = [V(al(512), 512, BF16, "p (k c) -> p k c", k=8) for _ in range(8)]
    wglu = [V(al(256), 256, BF16, "p (k c) -> p k c", k=4) for _ in range(4)]
    G2f = V(al(1024), 1024)
    B2f = V(al(1024), 1024)
    ost = [V(al(1024), 1024) for _ in range(2)]
    assert o <= AW, o
    self.wup_cnt = 0
    self.wdn_cnt = 0
    scl = self.scl
    for i in range(8):
        P.dma("sp", lambda e, i=i: e.dma_start(out=wout[i], in_=Wd["out"][i]), reads=["Wall"], writes=["wout%d" % i])
    for i in range(4):
        P.dma("sp", lambda e, i=i: e.dma_start(out=wglu[i], in_=Wd["glu"][i]), reads=["Wall"], writes=["wglu%d" % i])
    lg, lb = ins["ln_g"], ins["ln_b"]
    P.dma("sp", lambda e: e.dma_start(out=G2f, in_=AP(lg.tensor, lg.offset + 16 * 128, [[0, 128], [1, 1024]])), writes=["G2f"])
    P.dma("sp", lambda e: e.dma_start(out=B2f, in_=AP(lb.tensor, lb.offset + 16 * 128, [[0, 128], [1, 1024]])), writes=["B2f"])
    GC = 2.0 * math.sqrt(2.0 / math.pi)
    nblk = NLAT // 512
    if "short" in self.debug:
        nblk = 1
    for blk in range(nblk):
        t0 = blk * 512
        c0 = t0 // 8
        for m in range(4):
            P.dma("sp", lambda e, m=m, c0=c0: e.dma_start(out=ypre[:, m, :].rearrange("p (t n) -> p t n", t=8),
                                                          in_=S["y"][m * 8:(m + 1) * 8, :, :, c0:c0 + 64].rearrange("g c t n -> (g c) t n")),
                  reads=["yall"], writes=["ypre%d" % m])
        for m in range(4):
            xin = ypre[:, m, :].rearrange("p (t n) -> p t n", t=8)
            xo = gT[:, m, :].rearrange("p (n t) -> p t n", t=8)
            tm = gtmp[:, m, :].rearrange("p (n t) -> p t n", t=8)
            P.op("pool", lambda e, xin=xin, xo=xo: e.tensor_copy(out=xo, in_=xin), reads=["ypre%d" % m], writes=["gT%d" % m])
            xt_ = gT[:, m, :]
            tm_ = gtmp[:, m, :]
            P.op("dve", lambda e, xt_=xt_, tm_=tm_: e.tensor_tensor(out=tm_, in0=xt_, in1=xt_, op=ALU.mult), reads=["gT%d" % m], writes=["gtmp%d" % m])
            P.op("dve", lambda e, tm_=tm_: e.tensor_scalar(out=tm_, in0=tm_, scalar1=0.044715, scalar2=1.0, op0=ALU.mult, op1=ALU.add), reads=["gtmp%d" % m], writes=["gtmp%d" % m])
            P.op("dve", lambda e, xt_=xt_, tm_=tm_: e.tensor_tensor(out=tm_, in0=tm_, in1=xt_, op=ALU.mult), reads=["gtmp%d" % m, "gT%d" % m], writes=["gtmp%d" % m])
            P.op("act", lambda e, tm_=tm_: e.activation(out=tm_, in_=tm_, func=AF.Sigmoid, scale=GC), reads=["gtmp%d" % m], writes=["gtmp%d" % m])
            P.op("dve", lambda e, xt_=xt_, tm_=tm_: e.tensor_tensor(out=xt_, in0=xt_, in1=tm_, op=ALU.mult), reads=["gtmp%d" % m, "gT%d" % m], writes=["gT%d" % m])
            P.op("pool", lambda e, xt_=xt_, m=m: e.tensor_copy(out=gTb[:, m, :], in_=xt_), reads=["gT%d" % m], writes=["gTb%d" % m])
        for mo in range(4):
            pb = self.bank(mo % 2)
            for m in range(4):
                P.op("pe", lambda e, mo=mo, m=m, pb=pb: e.matmul(pb, lhsT=wglu[mo][:, m, :], rhs=gTb[:, m, :], start=(m == 0), stop=(m == 3)),
                     reads=["wglu%d" % mo, "gTb%d" % m], writes=["ps%d" % (mo % 2)])
            sg_ = sig[mo % 2]
            P.op("act", lambda e, sg_=sg_, pb=pb, mo=mo: e.activation(out=sg_, in_=pb, func=AF.Sigmoid, bias=self.bgluT[:, mo:mo + 1], scale=1.0),
                 reads=["ps%d" % (mo % 2), "bgluT"], writes=["sig%d" % (mo % 2)])
            P.op("dve", lambda e, sg_=sg_, mo=mo: e.tensor_tensor(out=ymix[:, 4 + mo, :], in0=gT[:, mo, :], in1=sg_, op=ALU.mult),
                 reads=["sig%d" % (mo % 2), "gT%d" % mo], writes=["ymix%d" % (4 + mo)])
        for m in range(4):
            P.dma("sp", lambda e, m=m, t0=t0: e.dma_start(out=ymix[:, m, :], in_=S["yna"][m, :, t0:t0 + 512]), reads=["ynaall"], writes=["ymix%d" % m])
        for kk in range(8):
            P.dma("sp", lambda e, kk=kk, t0=t0: e.dma_start(out=xsA[:, kk, :], in_=S["xs1"][kk, :, t0:t0 + 512]), reads=["xs1all"], writes=["Cxs%d" % kk])
        for i in range(8):
            pb = self.bank(2 + i % 2)
            for kc in range(8):
                P.op("pe", lambda e, i=i, kc=kc, pb=pb: e.matmul(pb, lhsT=wout[i][:, kc, :], rhs=ymix[:, kc, :], start=(kc == 0), stop=(kc == 7)),
                     reads=["wout%d" % i, "ymix%d" % kc], writes=["ps%d" % (2 + i % 2)])
            P.op("dve", lambda e, i=i, pb=pb: e.scalar_tensor_tensor(out=xsA[:, i, :], in0=pb, scalar=scl[:, S_M5, i:i + 1], in1=xsA[:, i, :],
                                                                    op0=ALU.mult, op1=ALU.add),
                 reads=["ps%d" % (2 + i % 2), "Cxs%d" % i, "scal"], writes=["Cxs%d" % i])
        self.ln("C", xsA, 512, R, [(h3T, scl[:, S_G2S, :], scl[:, S_B2S, :], "DhT"), (xsB, scl[:, S_AG1, :], scl[:, S_AB1, :], "Dxs")])
        self.ffn("D", h3T, 512, xsB, Wd["up2"], Wd["dn2"], scl[:, S_HG8, :], R)

        def tok_cb(s, zn, znr, t0=t0):
            ot = ost[s % 2]
            P.op("dve", lambda e, ot=ot, zn=zn: e.tensor_tensor(out=ot, in0=zn, in1=G2f, op=ALU.mult), reads=[znr + "h0", znr + "h1", "G2f"], writes=["ost%d" % (s % 2)])
            P.op("pool", lambda e, ot=ot: e.tensor_tensor(out=ot, in0=ot, in1=B2f, op=ALU.add), reads=["ost%d" % (s % 2), "B2f"], writes=["ost%d" % (s % 2)])
            P.dma("pool", lambda e, ot=ot, s=s, t0=t0: e.dma_start(out=self.out[t0 + s * 128:t0 + (s + 1) * 128, :], in_=ot),
                  reads=["ost%d" % (s % 2)], writes=[("outd", t0, s)], sem="out%d" % (s % 2))
        R["tok_cb"] = tok_cb
        self.ln("D", xsB, 512, R, None)


KB.phaseC = kb_phaseC
```

```python
import contextlib
import math
import numpy as np
import concourse.bass as bass
import concourse.mybir as mybir
from concourse.bass_utils import run_bass_kernel_spmd
from concourse.ap import AP

F32 = mybir.dt.float32
BF16 = mybir.dt.bfloat16
I32 = mybir.dt.int32
AF = mybir.ActivationFunctionType
ALU = mybir.AluOpType
AX = mybir.AxisListType

ENGS = ("pe", "act", "dve", "pool", "sp")


class Op:
    __slots__ = ("eng", "fn", "reads", "writes", "pos", "is_dma", "dsem", "dcount",
                 "waits", "signal", "count", "vc")

    def __init__(self, eng, fn, reads, writes, is_dma=False, dsem=None):
        self.eng = eng
        self.fn = fn
        self.reads = tuple(reads)
        self.writes = tuple(writes)
        self.is_dma = is_dma
        self.dsem = dsem
        self.dcount = 0
        self.waits = {}
        self.signal = False
        self.count = 0
        self.vc = None


class Prog:
    def __init__(self):
        self.streams = {e: [] for e in ENGS}
        self.last_writer = {}
        self.readers = {}
        self.dma_counts = {}
        self.key2phys = {}
        self.free_phys = []
        self.free_sw = []
        self.sw_phys = set()
        self.nphys = 0
        self.final_waits = []

    def op(self, eng, fn, reads=(), writes=()):
        ex = [r for r in reads if isinstance(r, str) and r.startswith("ps") and r not in writes]
        if ex:
            writes = list(writes) + ex
        o = Op(eng, fn, reads, writes)
        self._add(o, ())
        return o

    def dma(self, eng, fn, reads=(), writes=(), sem=None, soft=()):
        if sem is None:
            sem = "d_" + str(writes[0])
        if sem not in self.key2phys:
            fl = self.free_sw if eng == "pool" else self.free_phys
            if fl:
                self.key2phys[sem] = fl.pop(0)
            else:
                self.key2phys[sem] = self.nphys
                self.nphys += 1
                if eng == "pool":
                    self.sw_phys.add(self.key2phys[sem])
        sem = self.key2phys[sem]
        assert (sem in self.sw_phys) == (eng == "pool"), "semaphore shared between SW and HW DGE"
        o = Op(eng, fn, reads, writes, is_dma=True, dsem=sem)
        self.dma_counts[sem] = self.dma_counts.get(sem, 0) + 16
        o.dcount = self.dma_counts[sem]
        self._add(o, soft)
        return o

    @staticmethod
    def _tok(o):
        if o.is_dma:
            return (o.dsem, o.dcount)
        return (o.eng, o.pos)

    def _add(self, o, soft):
        st = self.streams[o.eng]
        o.pos = len(st) + 1
        st.append(o)
        vc = dict(st[-2].vc) if len(st) > 1 else {}
        deps = []
        for r in o.reads:
            w = self.last_writer.get(r)
            if w is not None:
                deps.append(w)
        for w_ in o.writes:
            w = self.last_writer.get(w_)
            if w is not None:
                deps.append(w)
            deps.extend(self.readers.get(w_, ()))
        deps.sort(key=lambda d_: -self._tok(d_)[1])
        for d in deps:
            if d is o:
                continue
            k, v = self._tok(d)
            if (not d.is_dma) and d.eng == o.eng and o.eng == "pe":
                continue
            if vc.get(k, 0) >= v:
                continue
            if o.waits.get(k, 0) < v:
                o.waits[k] = v
            if not d.is_dma:
                d.signal = True
            for kk, vv in d.vc.items():
                if vc.get(kk, 0) < vv:
                    vc[kk] = vv
            if vc.get(k, 0) < v:
                vc[k] = v
        o.vc = vc
        for r in o.reads:
            self.readers.setdefault(r, []).append(o)
        for w_ in o.writes:
            self.last_writer[w_] = o
            self.readers[w_] = []
        for w_ in soft:
            self.last_writer[w_] = o

    def barrier(self):
        lasts = []
        for e in ENGS:
            if e == "sp":
                continue
            st = self.streams[e]
            for o_ in reversed(st):
                if not o_.is_dma:
                    lasts.append(o_)
                    break
        dmas = dict(self.dma_counts)
        for e in ENGS:
            o = Op(e, None, (), ())
            st = self.streams[e]
            o.pos = len(st) + 1
            vc = dict(st[-1].vc) if st else {}
            for d in lasts:
                if d.eng == e:
                    if e == "pe":
                        continue
                k, v = self._tok(d)
                if vc.get(k, 0) < v:
                    o.waits[k] = v
                    d.signal = True
                    vc[k] = v
            for k, v in dmas.items():
                if vc.get(k, 0) < v:
                    o.waits[k] = v
                    vc[k] = v
            o.vc = vc
            st.append(o)
        self.last_writer = {}
        self.readers = {}
        allp = set(self.free_phys) | set(self.free_sw) | set(self.key2phys.values())
        self.free_phys = sorted(p for p in allp if p not in self.sw_phys)
        self.free_sw = sorted(p for p in allp if p in self.sw_phys)
        self.key2phys = {}

    def finish_wait(self, eng, sem_keys):
        self.final_waits.append((eng, list(sem_keys)))

    def emit(self, nc):
        for e in ENGS:
            c = 0
            for o in self.streams[e]:
                if o.signal and not o.is_dma:
                    c += 1
                o.count = c
        pos2count = {e: {o.pos: o.count for o in self.streams[e]} for e in ENGS}
        with contextlib.ExitStack() as es:
            sems = {}
            for e in ENGS:
                sems[e] = es.enter_context(nc.semaphore("s_" + e))
            for k in range(self.nphys):
                sems[k] = es.enter_context(nc.semaphore("q%d" % k))
            block = es.enter_context(nc.Block())
            engmap = {"pe": "tensor", "act": "scalar", "dve": "vector", "pool": "gpsimd",
                      "sp": "sync"}

            def make(e):
                def body(engine):
                    for o in self.streams[e]:
                        for k, v in o.waits.items():
                            if k in ENGS:
                                engine.wait_ge(sems[k], pos2count[k][v])
                            else:
                                engine.wait_ge(sems[k], v)
                        if o.fn is None:
                            if o.signal:
                                engine.nop().then_inc(sems[e], 1)
                            continue
                        ins = o.fn(engine)
                        if o.is_dma:
                            ins.then_inc(sems[o.dsem], 16)
                        elif o.signal:
                            ins.then_inc(sems[e], 1)
                    for (fe, keys) in self.final_waits:
                        if fe == e:
                            for k in keys:
                                engine.wait_ge(sems[k], self.dma_counts[k])
                return body

            for e in ENGS:
                if self.streams[e] or any(fe == e for fe, _ in self.final_waits):
                    getattr(block, engmap[e])(make(e))


D = 1024
NLAT = 4096
NCTX = 256
NT = NLAT + NCTX
DFF = 2816
NJ = DFF // 128
NCH = NT // 8
ALPHA = 2.0 ** 0.25
EPS = 1e-6
AW = 51200
MASKV = -240000.0
TWO_PI = 2.0 * math.pi


class KB:
    def __init__(self, debug=None):
        self.debug = debug or ()
        self.nc = bass.Bass("TRN2", target_bir_lowering=False)
        self.P = Prog()
        self.es = contextlib.ExitStack()
        self.uid = 0

    def dram_in(self, name, shape, dt=F32):
        return self.nc.dram_tensor(name, list(shape), dt, kind="ExternalInput").ap()

    def dram_out(self, name, shape, dt=F32):
        return self.nc.dram_tensor(name, list(shape), dt, kind="ExternalOutput").ap()

    def dram_tmp(self, name, shape, dt=F32):
        if name in self.debug:
            return self.nc.dram_tensor(name, list(shape), dt, kind="ExternalOutput").ap()
        return self.nc.dram_tensor(name, list(shape), dt).ap()

    def V(self, off, n, dt=F32, pat=None, **kw):
        v = self.arena[:, off:off + n]
        if dt != F32:
            v = v.bitcast(dt)
        if pat is not None:
            v = v.rearrange(pat, **kw)
        return v

    def name(self, s):
        self.uid += 1
        return "%s%d" % (s, self.uid)


def bc_free(ap2, n):
    return AP(ap2.tensor, ap2.offset, [list(a) for a in ap2.ap] + [[0, n]])


def mk_ap(base, dims, off=0):
    return AP(base.tensor, base.offset + off, [list(base.ap[0])] + [list(d) for d in dims])


O_IDENT = 0
O_IDENTB = 128
O_ONESB = 192
O_MODT = 224
O_BADA = 368
O_LNG = 440
O_LNB = 464
O_BGLU = 488
O_M05 = 496
O_ST = 500
O_SCL = 520
O_SCC = 648
PERS = 1024
S_SC1P, S_SH0, S_HG2, S_G1S, S_B1S, S_AG0, S_AB0, S_M5, S_G2S, S_B2S, S_AG1, S_AB1, S_HG8, S_G2, S_B2 = range(15)


def kb_setup(self, ins):
    nc, P = self.nc, self.P
    self.arena = self.es.enter_context(nc.sbuf_tensor("arena", [128, AW], F32))
    self.psum = self.es.enter_context(nc.psum_tensor("psum", [128, 8, 512], F32))
    ps = self.psum
    V = self.V
    self.ident = ident = V(O_IDENT, 128)
    self.identb = identb = V(O_IDENTB, 64, BF16)
    self.onesb = onesb = V(O_ONESB, 32, BF16)
    self.modT = modT = V(O_MODT, 144, F32, "p (c v) -> p c v", v=2)
    badaT = V(O_BADA, 72)
    self.lngT = lngT = V(O_LNG, 24)
    self.lnbT = lnbT = V(O_LNB, 24)
    self.bgluT = bgluT = V(O_BGLU, 4)
    self.m05 = m05 = V(O_M05, 1)
    sT = V(O_ST, 16)
    self.scl = scl = V(O_SCL, 128, F32, "p (s k) -> p s k", k=8)
    self.scc = scc = V(O_SCC, 64, F32, "p (s k) -> p s k", k=8)

    P.op("pool", lambda e: e.memset(ident, 0.0), writes=["ident"])
    P.op("pool", lambda e: e.affine_select(out=ident, in_=ident, pattern=[[-1, 128]],
                                           compare_op=ALU.not_equal, fill=1.0, base=0,
                                           channel_multiplier=1), reads=["ident"], writes=["ident"])
    P.op("pool", lambda e: e.tensor_copy(out=identb, in_=ident), reads=["ident"], writes=["identb"])
    P.op("pool", lambda e: e.memset(onesb, 1.0), writes=["onesb"])
    P.op("pool", lambda e: e.memset(m05, -0.5), writes=["m05"])

    T0 = PERS
    st1 = V(T0, 128)
    st2 = V(T0 + 128, 128)
    P.dma("sp", lambda e: e.dma_start(out=st1[0:72, :], in_=ins["b_ada"]), writes=["st1"])
    P.dma("sp", lambda e: e.dma_start(out=st2[0:24, :], in_=ins["ln_g"]), writes=["st2a"])
    P.dma("sp", lambda e: e.dma_start(out=st2[24:48, :], in_=ins["ln_b"]), writes=["st2b"])
    P.dma("sp", lambda e: e.dma_start(out=st2[48:52, :], in_=ins["s5_b_glu"]), writes=["st2c"])
    P.dma("sp", lambda e: e.dma_start(out=st2[52:60, :], in_=ins["c"]), writes=["st2d"])
    P.dma("sp", lambda e: e.dma_start(out=st2[60:68, :], in_=ins["c_ctx"]), writes=["st2e"])
    pt = ps[:, 6, 0:256]
    P.op("pe", lambda e: e.transpose(out=pt[:, 0:72], in_=st1[0:72, :], identity=ident[0:72, 0:72]),
         reads=["st1", "ident"], writes=["psT6"])
    P.op("pe", lambda e: e.transpose(out=pt[:, 128:196], in_=st2[0:68, :], identity=ident[0:68, 0:68]),
         reads=["st2a", "st2b", "st2c", "st2d", "st2e", "ident"], writes=["psT6"])
    P.op("dve", lambda e: e.tensor_copy(out=badaT, in_=pt[:, 0:72]), reads=["psT6"], writes=["badaT"])
    P.op("dve", lambda e: e.tensor_copy(out=lngT, in_=pt[:, 128:152]), reads=["psT6"], writes=["lngT"])
    P.op("dve", lambda e: e.tensor_copy(out=lnbT, in_=pt[:, 152:176]), reads=["psT6"], writes=["lnbT"])
    P.op("dve", lambda e: e.tensor_copy(out=bgluT, in_=pt[:, 176:180]), reads=["psT6"], writes=["bgluT"])
    P.op("act", lambda e: e.activation(out=sT, in_=pt[:, 180:196], func=AF.Silu), reads=["psT6"], writes=["sT"])

    wb = [V(T0 + 256 + i * 4096, 4096, F32, "p (k c) -> p k c", k=8) for i in range(2)]
    w_ada = ins["w_ada"].rearrange("(k p) c -> p k c", p=128)
    modps = ps[:, 0, 0:144]
    for bi in range(18):
        t = wb[bi % 2]
        P.dma("sp", lambda e, t=t, bi=bi: e.dma_start(out=t, in_=w_ada[:, :, bi * 512:(bi + 1) * 512]),
              writes=["wada%d" % (bi % 2)])
        for fc in range(4):
            col = (bi * 4 + fc) * 2
            for kk in range(8):
                rhs = mk_ap(sT, [[8, 2]], off=kk)
                P.op("pe", lambda e, t=t, fc=fc, kk=kk, col=col, rhs=rhs: e.matmul(
                    modps[:, col:col + 2], lhsT=t[:, kk, fc * 128:(fc + 1) * 128], rhs=rhs,
                    start=(kk == 0), stop=(kk == 7)),
                    reads=["wada%d" % (bi % 2), "sT"], writes=["ps0"])
    P.op("dve", lambda e: e.tensor_tensor(out=modT, in0=modps.rearrange("p (c v) -> p c v", v=2),
                                          in1=bc_free(badaT, 2), op=ALU.add),
         reads=["ps0", "badaT"], writes=["modT"])

    def m(i, v):
        return modT[:, 8 * i:8 * i + 8, v]

    def G(i):
        return lngT[:, 8 * i:8 * i + 8]

    def B(i):
        return lnbT[:, 8 * i:8 * i + 8]

    def dv(fn, reads=("modT", "lngT", "lnbT", "scal")):
        P.op("dve", fn, reads=list(reads), writes=["scal"])

    for v, sc in ((0, scl), (1, scc)):
        dv(lambda e, sc=sc, v=v: e.tensor_scalar(out=sc[:, S_SC1P, :], in0=m(1, v), scalar1=1.0, scalar2=None, op0=ALU.add))
        dv(lambda e, sc=sc, v=v: e.tensor_copy(out=sc[:, S_SH0, :], in_=m(0, v)))
        dv(lambda e, sc=sc, v=v: e.tensor_scalar(out=sc[:, S_HG2, :], in0=m(2, v), scalar1=0.5, scalar2=None, op0=ALU.mult))
        dv(lambda e, sc=sc, v=v: e.tensor_scalar(out=sc[:, S_G1S, :], in0=m(4, v), scalar1=1.0, scalar2=None, op0=ALU.add))
        dv(lambda e, sc=sc, v=v: e.tensor_tensor(out=sc[:, S_B1S, :], in0=sc[:, S_G1S, :], in1=B(0), op=ALU.mult))
        dv(lambda e, sc=sc, v=v: e.tensor_tensor(out=sc[:, S_B1S, :], in0=sc[:, S_B1S, :], in1=m(3, v), op=ALU.add))
        dv(lambda e, sc=sc, v=v: e.tensor_tensor(out=sc[:, S_G1S, :], in0=sc[:, S_G1S, :], in1=G(0), op=ALU.mult))
    sc = scl
    dv(lambda e: e.tensor_scalar(out=sc[:, S_AG0, :], in0=G(0), scalar1=ALPHA, scalar2=None, op0=ALU.mult))
    dv(lambda e: e.tensor_scalar(out=sc[:, S_AB0, :], in0=B(0), scalar1=ALPHA, scalar2=None, op0=ALU.mult))
    dv(lambda e: e.tensor_copy(out=sc[:, S_M5, :], in_=m(5, 0)))
    dv(lambda e: e.tensor_scalar(out=sc[:, S_G2S, :], in0=m(7, 0), scalar1=1.0, scalar2=None, op0=ALU.add))
    dv(lambda e: e.tensor_tensor(out=sc[:, S_B2S, :], in0=sc[:, S_G2S, :], in1=B(1), op=ALU.mult))
    dv(lambda e: e.tensor_tensor(out=sc[:, S_B2S, :], in0=sc[:, S_B2S, :], in1=m(6, 0), op=ALU.add))
    dv(lambda e: e.tensor_tensor(out=sc[:, S_G2S, :], in0=sc[:, S_G2S, :], in1=G(1), op=ALU.mult))
    dv(lambda e: e.tensor_scalar(out=sc[:, S_AG1, :], in0=G(1), scalar1=ALPHA, scalar2=None, op0=ALU.mult))
    dv(lambda e: e.tensor_scalar(out=sc[:, S_AB1, :], in0=B(1), scalar1=ALPHA, scalar2=None, op0=ALU.mult))
    dv(lambda e: e.tensor_scalar(out=sc[:, S_HG8, :], in0=m(8, 0), scalar1=0.5, scalar2=None, op0=ALU.mult))
    dv(lambda e: e.tensor_copy(out=sc[:, S_G2, :], in_=G(2)))
    dv(lambda e: e.tensor_copy(out=sc[:, S_B2, :], in_=B(2)))


KB.setup = kb_setup


def kb_convert(self, ins):
    nc, P, V = self.nc, self.P, self.V
    W = self.W = {}
    W["up1"] = self.dram_tmp("wup1_d", [NJ, 128, 8, 256], BF16)
    W["up2"] = self.dram_tmp("wup2_d", [NJ, 128, 8, 256], BF16)
    W["dn1"] = self.dram_tmp("wdn1_d", [8, 128, NJ, 128], BF16)
    W["dn2"] = self.dram_tmp("wdn2_d", [8, 128, NJ, 128], BF16)
    W["in"] = self.dram_tmp("win_d", [16, 128, 8, 128], BF16)
    W["out"] = self.dram_tmp("wout_d", [8, 128, 8, 128], BF16)
    W["glu"] = self.dram_tmp("wglu_d", [4, 128, 4, 128], BF16)
    if "noconv" in self.debug:
        return
    T0 = getattr(self, "conv_off", PERS)
    CMAX = 2816
    assert T0 + 3 * CMAX <= AW
    srcb = [V(T0 + i * CMAX, CMAX) for i in range(2)]
    dstb = [V(T0 + 2 * CMAX + i * (CMAX // 2), CMAX // 2, BF16) for i in range(2)]
    self.cv_n = 0
    engs = ["act", "act", "act"]
    if "nopool" in self.debug:
        engs = ["act", "dve", "dve"]

    def conv(src, nrt, C, dst, mode):
        for rt in range(nrt):
            n = self.cv_n
            self.cv_n += 1
            sb = srcb[n % 2][:, 0:C]
            db = dstb[n % 2][:, 0:C]
            P.dma("sp", lambda e, sb=sb, rt=rt: e.dma_start(out=sb, in_=src[rt * 128:(rt + 1) * 128, :]),
                  writes=["cvs%d" % (n % 2)])
            eng = engs[n % 3]
            i_ap, o_ap = sb, db
            if eng == "act":
                P.op("act", lambda e, i_ap=i_ap, o_ap=o_ap: e.activation(out=o_ap, in_=i_ap, func=AF.Copy),
                     reads=["cvs%d" % (n % 2)], writes=["cvd%d" % (n % 2)])
            else:
                P.op(eng, lambda e, i_ap=i_ap, o_ap=o_ap: e.tensor_copy(out=o_ap, in_=i_ap),
                     reads=["cvs%d" % (n % 2)], writes=["cvd%d" % (n % 2)])
            cw = 128
            if mode == "upa":
                d_ap = dst[:, :, rt, 0:128].rearrange("m p c -> p m c")
            elif mode == "upg":
                d_ap = dst[:, :, rt, 128:256].rearrange("m p c -> p m c")
            else:
                d_ap = dst[:, :, rt, :].rearrange("m p c -> p m c")
            s_ap = db.rearrange("p (m c) -> p m c", c=cw)
            P.dma("sp", lambda e, d_ap=d_ap, s_ap=s_ap: e.dma_start(out=d_ap, in_=s_ap),
                  reads=["cvd%d" % (n % 2)], writes=[("W", id(dst), n)], sem="wconv%d" % (n % 2), soft=["Wall"])

    conv(ins["ffn1_w_up"][:, 0:DFF], 8, DFF, W["up1"], "upa")
    conv(ins["ffn1_w_up"][:, DFF:2 * DFF], 8, DFF, W["up1"], "upg")
    if "cv1" in self.debug:
        return
    conv(ins["ffn1_w_down"], NJ, 1024, W["dn1"], "plain")
    if "cv2" in self.debug:
        return
    conv(ins["w_in"], 8, 2048, W["in"], "plain")
    if "cv3" in self.debug:
        return
    conv(ins["w_out"], 8, 1024, W["out"], "plain")
    if "cv4" in self.debug:
        return
    conv(ins["s5_w_glu"], 4, 512, W["glu"], "plain")
    if "cv5" in self.debug:
        return
    conv(ins["ffn2_w_up"][:, 0:DFF], 8, DFF, W["up2"], "upa")
    conv(ins["ffn2_w_up"][:, DFF:2 * DFF], 8, DFF, W["up2"], "upg")
    conv(ins["ffn2_w_down"], NJ, 1024, W["dn2"], "plain")


KB.convert = kb_convert


def kb_bank(self, b, n=512):
    return self.psum[:, b, 0:n]


def kb_ffn(self, key, hT, nb, xsT, wup_d, wdn_d, hg, R, pump=None, hkey=None, mid=None):
    P = self.P
    actT = R["actT"]
    if hkey is None:
        hkey = key
    for j in range(NJ):
        if pump is not None and j >= 2:
            next(pump, None)
        slot = self.wup_cnt % 4
        self.wup_cnt += 1
        wt = R["wup"][slot]
        P.dma("sp", lambda e, wt=wt, j=j: e.dma_start(out=wt, in_=wup_d[j]), reads=["Wall"],
              writes=["wup%d" % slot])
        pa = self.bank(j % 2, nb)
        pg = self.bank(2 + j % 2, nb)
        for kk in range(8):
            P.op("pe", lambda e, wt=wt, kk=kk, pa=pa: e.matmul(pa, lhsT=wt[:, kk, 0:128], rhs=hT[:, kk, :],
                                                              start=(kk == 0), stop=(kk == 7)),
                 reads=["wup%d" % slot, hkey + "hT%d" % kk], writes=["ps%d" % (j % 2)])
        for kk in range(8):
            P.op("pe", lambda e, wt=wt, kk=kk, pg=pg: e.matmul(pg, lhsT=wt[:, kk, 128:256], rhs=hT[:, kk, :],
                                                              start=(kk == 0), stop=(kk == 7)),
                 reads=["wup%d" % slot, hkey + "hT%d" % kk], writes=["ps%d" % (2 + j % 2)])
        sgb = R["sg"][j % 2][:, 0:nb]
        P.op("act", lambda e, sgb=sgb, pg=pg: e.activation(out=sgb, in_=pg, func=AF.Silu),
             reads=["ps%d" % (2 + j % 2)], writes=["sg%d" % (j % 2)])
        P.op("dve", lambda e, sgb=sgb, pa=pa, j=j: e.tensor_tensor(out=actT[:, j, 0:nb], in0=pa, in1=sgb, op=ALU.mult),
             reads=["ps%d" % (j % 2), "sg%d" % (j % 2)], writes=["actT%d" % j])
    if pump is not None:
        for _ in pump:
            pass
    if mid is not None:
        mid()
    for i in range(8):
        slot = self.wdn_cnt % 3
        self.wdn_cnt += 1
        wd = R["wdn"][slot]
        P.dma("sp", lambda e, wd=wd, i=i: e.dma_start(out=wd, in_=wdn_d[i]), reads=["Wall"],
              writes=["wdn%d" % slot])
        po = self.bank(4 + i % 2, nb)
        for j in range(NJ):
            P.op("pe", lambda e, wd=wd, j=j, po=po: e.matmul(po, lhsT=wd[:, j, :], rhs=actT[:, j, 0:nb],
                                                            start=(j == 0), stop=(j == NJ - 1)),
                 reads=["wdn%d" % slot, "actT%d" % j], writes=["ps%d" % (4 + i % 2)])
        P.op("dve", lambda e, po=po, i=i: e.scalar_tensor_tensor(out=xsT[:, i, :], in0=po, scalar=hg[:, i:i + 1],
                                                                in1=xsT[:, i, :], op0=ALU.mult, op1=ALU.add),
             reads=["ps%d" % (4 + i % 2), key + "xs%d" % i, "scal"], writes=[key + "xs%d" % i])


def kb_ln(self, key, zT, nb, R, outs):
    for _ in self.ln_gen(key, zT, nb, R, outs, None):
        pass


def kb_ln_gen(self, key, zT, nb, R, outs, fixed):
    P = self.P
    S = nb // 128
    mv = R["mv"]
    FW = [((0, 1), ("ps0", "ps1")), ((2, 3), ("ps2", "ps3"))]
    BK = [((4, 5), ("ps4", "ps5")), ((6, 7), ("psT6", "psT7"))]
    for s in range(S):
        zn = R["zn"][s % 2]
        znr = "zn%d" % (s % 2)
        (fa, _), fr = FW[s % 2]
        (ba, _), br = BK[s % 2]
        if fixed is not None:
            (fa, _), fr = fixed[0]
            (ba, _), br = fixed[1]
        psF = self.psum[:, fa:fa + 2, :].rearrange("p a b -> p (a b)")
        psB = self.psum[:, ba:ba + 2, :].rearrange("p a b -> p (a b)")
        for kk in range(8):
            P.op("pe", lambda e, kk=kk, s=s, psF=psF: e.transpose(out=psF[:, kk * 128:(kk + 1) * 128],
                                                                  in_=zT[:, kk, s * 128:(s + 1) * 128], identity=self.ident),
                 reads=[key + "xs%d" % kk, "ident"], writes=[fr[kk // 4]])
        st = R["bst"][s % 2]
        sr = "bst%d" % (s % 2)
        P.op("dve", lambda e, st=st, psF=psF: e.bn_stats(out=st[:, 0:6], in_=psF[:, 0:512]), reads=[fr[0]], writes=[sr])
        P.op("dve", lambda e, st=st, psF=psF: e.bn_stats(out=st[:, 6:12], in_=psF[:, 512:1024]), reads=[fr[1]], writes=[sr])
        mr = "mv%d" % s
        P.op("dve", lambda e, st=st, s=s: e.bn_aggr(out=mv[:, s, 0:2], in_=st[:, 0:12]), reads=[sr], writes=[mr])
        P.op("dve", lambda e, s=s: e.tensor_scalar(out=mv[:, s, 2:3], in0=mv[:, s, 1:2], scalar1=EPS, scalar2=None, op0=ALU.add),
             reads=[mr], writes=[mr])
        P.op("pool", lambda e, s=s: e.tensor_tensor(out=mv[:, s, 3:4], in0=mv[:, s, 2:3], in1=self.m05, op=ALU.pow),
             reads=[mr, "m05"], writes=[mr])
        P.op("dve", lambda e, s=s: e.tensor_scalar(out=mv[:, s, 4:5], in0=mv[:, s, 0:1], scalar1=mv[:, s, 3:4], scalar2=-1.0,
                                                   op0=ALU.mult, op1=ALU.mult), reads=[mr], writes=[mr])
        for hb in range(2):
            P.op("act", lambda e, s=s, zn=zn, hb=hb, psF=psF: e.activation(out=zn[:, hb * 512:(hb + 1) * 512], in_=psF[:, hb * 512:(hb + 1) * 512],
                                                                          func=AF.Identity, scale=mv[:, s, 3:4], bias=mv[:, s, 4:5]),
                 reads=[fr[hb], mr], writes=[znr + "h%d" % hb])
        if outs is None:
            R["tok_cb"](s, zn, znr)
            yield
            continue
        yield
        for kk in range(8):
            P.op("pe", lambda e, kk=kk, zn=zn, psB=psB: e.transpose(out=psB[:, kk * 128:(kk + 1) * 128],
                                                                    in_=zn[:, kk * 128:(kk + 1) * 128], identity=self.ident),
                 reads=[znr + "h%d" % (kk // 4), "ident"], writes=[br[kk // 4]])
        for (dst, gcol, bcol, wres) in outs:
            for kk in range(8):
                o_ap = dst[:, kk, s * 128:(s + 1) * 128]
                i_ap = psB[:, kk * 128:(kk + 1) * 128]
                pr = br[kk // 4]
                g1 = gcol[:, kk:kk + 1]
                b1 = bcol[:, kk:kk + 1]
                if kk < 4:
                    P.op("act", lambda e, o_ap=o_ap, i_ap=i_ap, g1=g1, b1=b1: e.activation(
                        out=o_ap, in_=i_ap, func=AF.Identity, scale=g1, bias=b1),
                        reads=[pr, "scal"], writes=[wres + "%d" % kk])
                else:
                    P.op("dve", lambda e, o_ap=o_ap, i_ap=i_ap, g1=g1, b1=b1: e.tensor_scalar(
                        out=o_ap, in0=i_ap, scalar1=g1, scalar2=b1, op0=ALU.mult, op1=ALU.add),
                        reads=[pr, "scal"], writes=[wres + "%d" % kk])
        yield


KB.ln_gen = kb_ln_gen
KB.bank = kb_bank
KB.ffn = kb_ffn
KB.ln = kb_ln


def kb_phaseA(self, ins):
    nc, P, V = self.nc, self.P, self.V
    ps = self.psum
    S = self.S = {}
    S["xs1"] = self.dram_tmp("xs1_d", [8, 128, NLAT], F32)
    S["q"] = self.dram_tmp("q_d", [4, 128, NLAT], BF16)
    S["k"] = self.dram_tmp("k_d", [4, 128, NT], BF16)
    S["v"] = self.dram_tmp("v_d", [NT, 512], BF16)
    S["u"] = self.dram_tmp("u_d", [32, 16, 8, NCH], BF16)
    o = PERS
    R = {}

    def al(n):
        nonlocal o
        r = o
        o += n
        return r
    xtok = [V(al(1024), 1024) for _ in range(2)]
    xsT = V(al(4096), 4096, F32, "p (k t) -> p k t", k=8)
    xsTb = V(al(4096), 4096, F32, "p (k t) -> p k t", k=8)
    hT = V(al(2048), 2048, BF16, "p (k t) -> p k t", k=8)
    R["actT"] = V(al(5632), 5632, BF16, "p (j t) -> p j t", j=NJ)
    R["sg"] = [V(al(512), 512) for _ in range(2)]
    R["zn"] = [V(al(1024), 1024) for _ in range(2)]
    xs1st = V(al(4096), 4096, F32, "p (k t) -> p k t", k=8)
    h2T = V(al(2048), 2048, BF16, "p (k t) -> p k t", k=8)
    qkst = V(al(2048), 2048, BF16, "p (m t) -> p m t", m=8)
    vst = [V(al(256), 256, BF16) for _ in range(2)]
    ust = V(al(1024), 1024, BF16, "p (m t) -> p m t", m=4)
    R["mv"] = V(al(32), 32, F32, "p (s c) -> p s c", c=8)
    R["bst"] = [V(al(16), 16) for _ in range(2)]
    R["wup"] = [V(al(1024), 1024, BF16, "p (k c) -> p k c", k=8) for _ in range(4)]
    R["wdn"] = [V(al(1408), 1408, BF16, "p (j c) -> p j c", j=NJ) for _ in range(3)]
    wint = [V(al(512), 512, BF16, "p (k c) -> p k c", k=8) for _ in range(4)]
    wv = V(al(2048), 2048, BF16, "p (k m c) -> p k m c", k=8, m=4)
    assert o <= AW, o
    self.wup_cnt = 0
    self.wdn_cnt = 0
    win_cnt = 0
    psT = ps[:, 6:8, :].rearrange("p a b -> p (a b)")
    Wd = self.W
    for m in range(4):
        P.dma("sp", lambda e, m=m: e.dma_start(out=wv[:, :, m, :], in_=Wd["in"][8 + m]), reads=["Wall"], writes=["wv%d" % m])

    if "A0" in self.debug:
        return
    blocks = [(0, NCTX, True)] + [(NCTX + i * 512, 512, False) for i in range(NLAT // 512)]
    if "short" in self.debug:
        blocks = blocks[:2]
    xsT2 = [xsT, xsTb]
    PUMP_BANKS = (((4, 5), ("ps4", "ps5")), ((6, 7), ("psT6", "psT7")))

    def x_stage(blk, buf):
        (t0, nb, isctx) = blk
        sc = self.scc if isctx else self.scl
        xs = xsT2[buf]
        src = ins["ctx"] if isctx else ins["x"]
        r0 = t0 if isctx else t0 - NCTX
        for s in range(nb // 128):
            xt = xtok[s % 2]
            if s % 2 == 0:
                psT = ps[:, 6:8, :].rearrange("p a b -> p (a b)")
                xbr = ("psT6", "psT7")
            else:
                psT = ps[:, 4:6, :].rearrange("p a b -> p (a b)")
                xbr = ("ps4", "ps5")
            x_ap = src[r0 + s * 128:r0 + (s + 1) * 128, :]
            P.dma("sp", lambda e, xt=xt, x_ap=x_ap: e.dma_start(out=xt, in_=x_ap), writes=["xtok%d" % (s % 2)])
            for kk in range(8):
                P.op("pe", lambda e, kk=kk, xt=xt, psT=psT: e.transpose(out=psT[:, kk * 128:(kk + 1) * 128],
                                                                        in_=xt[:, kk * 128:(kk + 1) * 128], identity=self.ident),
                     reads=["xtok%d" % (s % 2), "ident"], writes=[xbr[kk // 4]])
            for kk in range(8):
                i_ap = psT[:, kk * 128:(kk + 1) * 128]
                pr = xbr[kk // 4]
                o1 = xs[:, kk, s * 128:(s + 1) * 128]
                o2 = hT[:, kk, s * 128:(s + 1) * 128]
                s1 = sc[:, S_SC1P, kk:kk + 1]
                s2 = sc[:, S_SH0, kk:kk + 1]
                if kk < 4:
                    P.op("act", lambda e, i_ap=i_ap, o1=o1: e.activation(out=o1, in_=i_ap, func=AF.Identity, scale=ALPHA),
                         reads=[pr], writes=["A%dxs%d" % (buf, kk)])
                    P.op("act", lambda e, i_ap=i_ap, o2=o2, s1=s1, s2=s2: e.activation(out=o2, in_=i_ap, func=AF.Identity, scale=s1, bias=s2),
                         reads=[pr, "scal"], writes=["AhT%d" % kk])
                else:
                    P.op("dve", lambda e, i_ap=i_ap, o1=o1: e.tensor_scalar(out=o1, in0=i_ap, scalar1=ALPHA, scalar2=None, op0=ALU.mult),
                         reads=[pr], writes=["A%dxs%d" % (buf, kk)])
                    P.op("dve", lambda e, i_ap=i_ap, o2=o2, s1=s1, s2=s2: e.tensor_scalar(out=o2, in0=i_ap, scalar1=s1, scalar2=s2, op0=ALU.mult, op1=ALU.add),
                         reads=[pr, "scal"], writes=["AhT%d" % kk])

    def ln_stage(blk, buf, fixed):
        (t0, nb, isctx) = blk
        sc = self.scc if isctx else self.scl
        outs = [(h2T, sc[:, S_G1S, :], sc[:, S_B1S, :], "Ah2T")]
        if not isctx:
            outs.append((xs1st, sc[:, S_AG0, :], sc[:, S_AB0, :], "Axs1"))
        return self.ln_gen("A%d" % buf, xsT2[buf], nb, R, outs, fixed)

    def win_stage(blk):
        nonlocal win_cnt
        (t0, nb, isctx) = blk
        Sn = nb // 128
        r0 = t0 if isctx else t0 - NCTX
        if not isctx:
            for kk in range(8):
                P.dma("pool", lambda e, kk=kk, r0=r0, nb=nb: e.dma_start(out=S["xs1"][kk, :, r0:r0 + nb], in_=xs1st[:, kk, 0:nb]),
                      reads=["Axs1%d" % kk], writes=[("xs1d", t0, kk)], sem="st_xs1_%d" % kk, soft=["xs1all"])
        for mi, m in enumerate([0, 1, 2, 3, 4, 5, 6, 7, 12, 13, 14, 15]):
            if isctx and m < 4:
                continue
            slot = win_cnt % 4
            win_cnt += 1
            wt = wint[slot]
            P.dma("sp", lambda e, wt=wt, m=m: e.dma_start(out=wt, in_=Wd["in"][m]), reads=["Wall"], writes=["win%d" % slot])
            pb = self.bank(mi % 4, nb)
            for kk in range(8):
                P.op("pe", lambda e, wt=wt, kk=kk, pb=pb, nb=nb: e.matmul(pb, lhsT=wt[:, kk, :], rhs=h2T[:, kk, 0:nb],
                                                                         start=(kk == 0), stop=(kk == 7)),
                     reads=["win%d" % slot, "Ah2T%d" % kk], writes=["ps%d" % (mi % 4)])
            if m < 8:
                eng = ("act", "dve")[mi % 2]
                o_ap = qkst[:, m, 0:nb]
                if eng == "act":
                    P.op("act", lambda e, o_ap=o_ap, pb=pb: e.activation(out=o_ap, in_=pb, func=AF.Copy),
                         reads=["ps%d" % (mi % 4)], writes=["qkst%d" % m])
                else:
                    P.op("dve", lambda e, o_ap=o_ap, pb=pb: e.tensor_copy(out=o_ap, in_=pb),
                         reads=["ps%d" % (mi % 4)], writes=["qkst%d" % m])
                if m < 4:
                    P.dma("pool", lambda e, m=m, r0=r0, o_ap=o_ap, nb=nb: e.dma_start(out=S["q"][m, :, r0:r0 + nb], in_=o_ap),
                          reads=["qkst%d" % m], writes=[("qd", t0, m)], sem="st_qk%d" % m, soft=["qall"])
                else:
                    P.dma("pool", lambda e, m=m, t0=t0, o_ap=o_ap, nb=nb: e.dma_start(out=S["k"][m - 4, :, t0:t0 + nb], in_=o_ap),
                          reads=["qkst%d" % m], writes=[("kd", t0, m)], sem="st_qk%d" % m, soft=["kall"])
            else:
                mu = m - 12
                nc8 = nb // 8
                i_ap = pb.rearrange("p (c s) -> p s c", s=8)
                o_ap = ust[:, mu, 0:nb].rearrange("p (s c) -> p s c", s=8)
                P.op("dve", lambda e, o_ap=o_ap, i_ap=i_ap: e.tensor_copy(out=o_ap, in_=i_ap),
                     reads=["ps%d" % (mi % 4)], writes=["ust%d" % mu])
                c0 = t0 // 8
                d_ap = S["u"][mu * 8:(mu + 1) * 8, :, :, c0:c0 + nc8].rearrange("g i s c -> (g i) s c")
                P.dma("pool", lambda e, d_ap=d_ap, o_ap=o_ap: e.dma_start(out=d_ap, in_=o_ap),
                      reads=["ust%d" % mu], writes=[("ud", t0, mu)], sem="st_u%d" % mu, soft=["uall"])
        for s in range(Sn):
            pv = self.bank(4 + s % 2, 512)
            for kk in range(8):
                P.op("pe", lambda e, kk=kk, s=s, pv=pv: e.matmul(pv, lhsT=h2T[:, kk, s * 128:(s + 1) * 128],
                                                                rhs=wv[:, kk, :, :].rearrange("p m c -> p (m c)"),
                                                                start=(kk == 0), stop=(kk == 7)),
                     reads=["wv0", "wv1", "wv2", "wv3", "Ah2T%d" % kk], writes=["ps%d" % (4 + s % 2)])
            vs = vst[s % 2]
            P.op("act", lambda e, vs=vs, pv=pv: e.activation(out=vs, in_=pv, func=AF.Copy),
                 reads=["ps%d" % (4 + s % 2)], writes=["vst%d" % (s % 2)])
            P.dma("pool", lambda e, vs=vs, s=s, t0=t0: e.dma_start(out=S["v"][t0 + s * 128:t0 + (s + 1) * 128, :], in_=vs),
                  reads=["vst%d" % (s % 2)], writes=[("vd", t0, s)], sem="st_v%d" % (s % 2), soft=["vall"])

    prev = None
    for idx, blk in enumerate(blocks):
        buf = idx % 2
        (t0, nb, isctx) = blk
        sc = self.scc if isctx else self.scl
        x_stage(blk, buf)
        pump = ln_stage(prev[0], prev[1], PUMP_BANKS) if prev is not None else None
        mid = (lambda pb_=prev[0]: win_stage(pb_)) if prev is not None else None
        self.ffn("A%d" % buf, hT[:, :, 0:nb], nb, xsT2[buf][:, :, 0:nb], Wd["up1"], Wd["dn1"], sc[:, S_HG2, :], R,
                 pump=pump, hkey="A", mid=mid)
        prev = (blk, buf)
    for _ in ln_stage(prev[0], prev[1], None):
        pass
    win_stage(prev[0])


KB.phaseA = kb_phaseA


IN_SHAPES = {
    "x": [NLAT, D], "c": [8, 128], "ctx": [NCTX, D], "c_ctx": [8, 128],
    "w_ada": [D, 9 * D], "b_ada": [72, 128], "ln_g": [24, 128], "ln_b": [24, 128],
    "ffn1_w_up": [D, 2 * DFF], "ffn1_w_down": [DFF, D], "w_in": [D, 2048],
    "na_rpb": [8, 15, 31], "s5_a_re": [2, 32, 64], "s5_a_im": [2, 32, 64], "s5_log_dt": [2, 32],
    "s5_b_re": [2, 32, 64, 16], "s5_b_im": [2, 32, 64, 16], "s5_c_re": [2, 32, 16, 64],
    "s5_c_im": [2, 32, 16, 64], "s5_d": [32, 16], "s5_w_glu": [512, 512], "s5_b_glu": [4, 128],
    "w_out": [D, D], "ffn2_w_up": [D, 2 * DFF], "ffn2_w_down": [DFF, D],
}


def build_nc(debug=None, stages=("setup", "convert", "A", "S5", "ATT", "C")):
    kb = KB(debug)
    ins = {k: kb.dram_in(k, shp) for k, shp in IN_SHAPES.items()}
    out = kb.dram_out("out", [NLAT, D])
    kb.out = out
    with kb.es:
        kb.setup(ins)
        kb.P.barrier()
        kb.conv_off = AW - 3 * 2816
        if "S5" in stages:
            kb.phaseS5(ins, mid_cb=(lambda: kb.convert(ins)) if "convert" in stages else None)
            assert kb.s5["o_end"] <= kb.conv_off, kb.s5["o_end"]
        elif "convert" in stages:
            kb.convert(ins)
        kb.P.barrier()
        if "A" in stages:
            kb.phaseA(ins)
            kb.P.barrier()
        if "S5" in stages:
            kb.phaseS5main(ins)
            kb.P.barrier()
        if "ATT" in stages:
            kb.phaseATT(ins)
            kb.P.barrier()
        if "C" in stages:
            kb.phaseC(ins)
        if "dump_mod" in kb.debug:
            dm = kb.dram_out("dbg_mod", [128, 144])
            kb.P.dma("sp", lambda e: e.dma_start(out=dm, in_=kb.V(O_MODT, 144)), reads=["modT"], writes=["dbgmod"], sem="out")
        if "C" not in stages:
            z = kb.V(AW - 1024, 1024)
            kb.P.op("pool", lambda e: e.memset(z, 0.0), writes=["zz"])
            kb.P.dma("sp", lambda e: e.dma_start(out=out[0:128, :], in_=z), reads=["zz"], writes=["outz"], sem="out")
        kb.P.finish_wait("sp", list(range(kb.P.nphys)))
        kb.P.emit(kb.nc)
    return kb.nc


def make_in_maps(inputs):
    f = lambda a: np.ascontiguousarray(np.asarray(a, dtype=np.float32))
    shared = {}
    for k in IN_SHAPES:
        if k in ("x", "c", "ctx"):
            continue
        a = f(inputs[k])
        if a.ndim >= 1 and a.shape[0] == 1 and k not in ("c_ctx",):
            a = a[0]
        shared[k] = np.ascontiguousarray(a.reshape(IN_SHAPES[k]))
    maps = []
    for b in range(8):
        m = dict(shared)
        m["x"] = f(inputs["x"][b])
        m["ctx"] = f(inputs["ctx"][b])
        m["c"] = f(inputs["c"][b]).reshape(8, 128)
        maps.append(m)
    return maps


_NC_CACHE = {}


def kernel(**inputs):
    if "nc" not in _NC_CACHE:
        _NC_CACHE["nc"] = build_nc()
    nc = _NC_CACHE["nc"]
    maps = make_in_maps(inputs)
    res = run_bass_kernel_spmd(nc, maps, core_ids=list(range(8)))
    out = np.stack([np.asarray(r["out"], dtype=np.float32) for r in res.results], axis=0)
    return out


def kb_phaseS5(self, ins, mid_cb=None):
    nc, P, V = self.nc, self.P, self.V
    ps = self.psum
    if not hasattr(self, "S"):
        self.S = {}
    S = self.S
    tab_d = self.dram_tmp("tab_d", [32, 128, 2, NCH], F32)
    o = PERS

    def al(n):
        nonlocal o
        r = o
        o += n
        return r
    ident = self.ident
    L = "s5"

    def dv(fn, eng="dve", extra_r=(), extra_w=()):
        P.op(eng, fn, reads=[L] + list(extra_r), writes=[L] + list(extra_w))

    def TT(out, a, b, op, eng="dve"):
        dv(lambda e: e.tensor_tensor(out=out, in0=a, in1=b, op=op), eng)

    def TS(out, a, s1, op0, s2=None, op1=None, eng="dve"):
        if op1 is None:
            dv(lambda e: e.tensor_scalar(out=out, in0=a, scalar1=s1, scalar2=None, op0=op0), eng)
        else:
            dv(lambda e: e.tensor_scalar(out=out, in0=a, scalar1=s1, scalar2=s2, op0=op0, op1=op1), eng)

    def ACT(out, a, func, scale=1.0, bias=0.0):
        dv(lambda e: e.activation(out=out, in_=a, func=func, scale=scale, bias=bias), "act")

    W1b = [V(al(2048), 2048, BF16, "p (k c) -> p k c", k=32) for _ in range(2)]
    T0b = V(al(2048), 2048, BF16, "p (g c) -> p g c", g=32)
    W2b = [V(al(2048), 2048, BF16, "p (k c) -> p k c", k=32) for _ in range(2)]
    rho8 = V(al(32), 32)
    o_mark = o
    st = V(al(384), 384)
    A = V(al(32 * 12), 32 * 12, F32, "p (n k) -> p n k", k=32)
    a_re, a_im, ldt, dtt, x1, er, th, y_, fr, cs, sn, tmp = [A[:, i, :] for i in range(12)]
    A2 = V(al(32 * 12), 32 * 12, F32, "p (n k) -> p n k", k=32)
    ab_re, ab_im, k_re, k_im, den, t1, t2, t3, w_re, w_im, iv_re, iv_im = [A2[:, i, :] for i in range(12)]
    yi = V(al(32), 32).bitcast(I32)
    PaR = V(al(512), 512, F32, "p (k j) -> p k j", j=16)
    PaI = V(al(512), 512, F32, "p (k j) -> p k j", j=16)
    Bre = V(al(512), 512, F32, "p (k c) -> p k c", c=16)
    Bim = V(al(512), 512, F32, "p (k c) -> p k c", c=16)
    Bbre = V(al(512), 512, F32, "p (k c) -> p k c", c=16)
    Bbim = V(al(512), 512, F32, "p (k c) -> p k c", c=16)
    Cre = V(al(512), 512, F32, "p (k c) -> p k c", c=16)
    Cim = V(al(512), 512, F32, "p (k c) -> p k c", c=16)
    tB = V(al(512), 512, F32, "p (k c) -> p k c", c=16)
    cst = V(al(128), 128)
    dcol = V(al(32), 32)
    maskt = [V(al(512), 512, F32, "p (r c) -> p r c", r=4) for _ in range(2)]
    identt = V(al(512), 512, F32, "p (r c) -> p r c", r=4)

    are_v = ins["s5_a_re"].rearrange("d (k m) p -> (d k) (m p)", m=2)
    aim_v = ins["s5_a_im"].rearrange("d (k m) p -> (d k) (m p)", m=2)
    P.dma("sp", lambda e: e.dma_start(out=st[0:32, 0:128], in_=are_v), writes=["s5st0"])
    P.dma("sp", lambda e: e.dma_start(out=st[0:32, 128:256], in_=aim_v), writes=["s5st1"])
    ldt2 = V(al(2), 2)
    P.dma("sp", lambda e: e.dma_start(out=ldt2[0:32, :], in_=ins["s5_log_dt"].rearrange("d (k m) -> (d k) m", m=2)), writes=["s5st2"])
    P.op("dve", lambda e: e.tensor_copy(out=st[0:32, 256:384].rearrange("p (m q) -> p m q", m=2), in_=bc_free(ldt2[0:32, :], 64)),
         reads=["s5st2", L], writes=[L])
    for mi in range(2):
        bre_v = ins["s5_b_re"].rearrange("d (k m) p c -> m p (d k) c", m=2)
        bim_v = ins["s5_b_im"].rearrange("d (k m) p c -> m p (d k) c", m=2)
        P.dma("sp", lambda e, mi=mi, bre_v=bre_v: e.dma_start(out=Bre[mi * 64:(mi + 1) * 64], in_=bre_v[mi]), writes=["s5b%d" % mi])
        P.dma("sp", lambda e, mi=mi, bim_v=bim_v: e.dma_start(out=Bim[mi * 64:(mi + 1) * 64], in_=bim_v[mi]), writes=["s5bi%d" % mi])
    dst16 = V(al(16), 16)
    dst128 = V(al(128), 128)
    P.dma("sp", lambda e: e.dma_start(out=dst16[0:32, :], in_=ins["s5_d"]), writes=["s5d16"])
    P.op("dve", lambda e: e.tensor_copy(out=dst128[0:32, :].rearrange("p (s c) -> p s c", s=8),
                                        in_=AP(dst16.tensor, dst16.offset, [[dst16.ap[0][0], 32], [0, 8], [1, 16]])),
         reads=["s5d16", L], writes=[L])
    pt = ps[:, 6, :]
    P.op("pe", lambda e: e.transpose(out=pt[:, 256:288], in_=dst128[0:32, :], identity=ident[0:32, 0:32]),
         reads=[L, "ident"], writes=["psT6"])
    P.op("dve", lambda e: e.tensor_copy(out=dcol, in_=pt[:, 256:288]), reads=["psT6", L], writes=[L])
    for i in range(3):
        P.op("pe", lambda e, i=i: e.transpose(out=pt[:, i * 32:(i + 1) * 32], in_=st[0:32, i * 128:(i + 1) * 128], identity=ident[0:32, 0:32]),
             reads=["s5st0", "s5st1", L, "ident"], writes=["psT6"])
    P.op("dve", lambda e: e.tensor_copy(out=A[:, 0:3, :], in_=pt[:, 0:96].rearrange("p (n k) -> p n k", n=3)), reads=["psT6", L], writes=[L])
    for ri, (src, dst) in enumerate(((ins["s5_c_re"], Cre), (ins["s5_c_im"], Cim))):
        cv = src.rearrange("d (k m) c p -> d k c m p", m=2)
        for d_ in range(2):
            for j in range(2):
                for kl in range(8):
                    P.dma("sp", lambda e, cv=cv, d_=d_, j=j, kl=kl: e.dma_start(
                        out=cst[kl * 16:(kl + 1) * 16, :].rearrange("p (m q) -> p m q", m=2), in_=cv[d_, 8 * j + kl]),
                        reads=[L], writes=["s5cst%d" % kl])
                P.op("pe", lambda e: e.transpose(out=pt[:, 128:256], in_=cst, identity=ident),
                     reads=["s5cst%d" % kl for kl in range(8)] + ["ident"], writes=["psT6"])
                dk0 = d_ * 16 + 8 * j
                P.op("dve", lambda e, dst=dst, dk0=dk0: e.tensor_copy(out=dst[:, dk0:dk0 + 8, :].rearrange("p k c -> p (k c)"), in_=pt[:, 128:256]),
                     reads=["psT6", L], writes=[L] + ["s5cst%d" % kl for kl in range(8)])
    ACT(dtt, ldt, AF.Exp)
    TT(x1, a_re, dtt, ALU.mult)
    ACT(er, x1, AF.Exp)
    ACT(rho8, x1, AF.Exp, scale=8.0)
    TT(th, a_im, dtt, ALU.mult)
    for (dstt, off) in ((sn, 0.5), (cs, 0.75)):
        TS(y_, th, 1.0 / TWO_PI, ALU.mult, off, ALU.add)
        dv(lambda e: e.tensor_copy(out=yi, in_=y_))
        dv(lambda e: e.tensor_copy(out=fr, in_=yi))
        TT(fr, y_, fr, ALU.subtract)
        TS(tmp, fr, 0.0, ALU.is_lt)
        TT(fr, fr, tmp, ALU.add)
        TS(tmp, fr, 1.0, ALU.is_ge)
        TT(fr, fr, tmp, ALU.subtract)
        TS(fr, fr, TWO_PI, ALU.mult, -math.pi, ALU.add)
        TS(fr, fr, math.pi, ALU.min, -math.pi, ALU.max)
        ACT(dstt, fr, AF.Sin)
    if mid_cb is not None:
        mid_cb()
    TT(ab_re, er, cs, ALU.mult)
    TT(ab_im, er, sn, ALU.mult)
    TS(t1, ab_re, -1.0, ALU.add)
    TT(den, a_re, a_re, ALU.mult)
    TT(t2, a_im, a_im, ALU.mult)
    TT(den, den, t2, ALU.add)
    dv(lambda e: e.reciprocal(out=den, in_=den))
    TT(k_re, t1, a_re, ALU.mult)
    TT(t2, ab_im, a_im, ALU.mult)
    TT(k_re, k_re, t2, ALU.add)
    TT(k_re, k_re, den, ALU.mult)
    TT(k_im, ab_im, a_re, ALU.mult)
    TT(t2, t1, a_im, ALU.mult)
    TT(k_im, k_im, t2, ALU.subtract)
    TT(k_im, k_im, den, ALU.mult)
    TT(t1, er, er, ALU.mult)
    dv(lambda e: e.reciprocal(out=t1, in_=t1))
    TT(iv_re, ab_re, t1, ALU.mult)
    TT(iv_im, ab_im, t1, ALU.mult)
    TS(iv_im, iv_im, -1.0, ALU.mult)

    def cmul(o_re, o_im, x_re, x_im, y_re, y_im):
        TT(t1, x_re, y_re, ALU.mult)
        TT(t2, x_im, y_im, ALU.mult)
        TT(t3, x_re, y_im, ALU.mult)
        TT(o_im, x_im, y_re, ALU.mult)
        TT(o_im, o_im, t3, ALU.add)
        TT(o_re, t1, t2, ALU.subtract)
    dv(lambda e: e.memset(PaR[:, :, 7], 1.0))
    dv(lambda e: e.memset(PaI[:, :, 7], 0.0))
    dv(lambda e: e.tensor_copy(out=PaR[:, :, 8], in_=ab_re))
    dv(lambda e: e.tensor_copy(out=PaI[:, :, 8], in_=ab_im))
    for jj in range(9, 16):
        cmul(PaR[:, :, jj], PaI[:, :, jj], PaR[:, :, jj - 1], PaI[:, :, jj - 1], ab_re, ab_im)
    for jj in range(6, -1, -1):
        cmul(PaR[:, :, jj], PaI[:, :, jj], PaR[:, :, jj + 1], PaI[:, :, jj + 1], iv_re, iv_im)
    dv(lambda e: e.reciprocal(out=t1, in_=rho8))
    TT(w_re, PaR[:, :, 15], t1, ALU.mult)
    TT(w_im, PaI[:, :, 15], t1, ALU.mult)
    TS(w_im, w_im, -1.0, ALU.mult)
    for (o_, x_, kx, y_b, ky, op) in ((Bbre, Bre, k_re, Bim, k_im, ALU.subtract), (Bbim, Bim, k_re, Bre, k_im, ALU.add)):
        dv(lambda e, o_=o_, x_=x_, kx=kx: e.tensor_tensor(out=o_, in0=x_, in1=bc_free(kx, 16), op=ALU.mult),
           extra_r=["s5b0", "s5b1", "s5bi0", "s5bi1"])
        dv(lambda e, y_b=y_b, ky=ky: e.tensor_tensor(out=tB, in0=y_b, in1=bc_free(ky, 16), op=ALU.mult))
        dv(lambda e, o_=o_, op=op: e.tensor_tensor(out=o_, in0=o_, in1=tB, op=op))
    for d_ in range(2):
        dv(lambda e, d_=d_: e.memset(maskt[d_], 1.0), "pool")
        if d_ == 0:
            pat, base, cm = [[0, 4], [16, 8], [0, 16]], 15, -1
        else:
            pat, base, cm = [[0, 4], [-16, 8], [0, 16]], 0, 1
        dv(lambda e, d_=d_, pat=pat, base=base, cm=cm: e.affine_select(
            out=maskt[d_], in_=maskt[d_], pattern=pat, compare_op=ALU.is_ge, fill=0.0, base=base, channel_multiplier=cm), "pool")
    dv(lambda e: e.tensor_copy(out=identt, in_=AP(ident.tensor, ident.offset, [list(ident.ap[0]), [0, 4], [1, 128]])), "pool", extra_r=["ident"])

    o_fm = o
    Fm = [V(al(2048), 2048, F32, "p (k s c) -> p k s c", k=16, s=8) for _ in range(2)]
    W2m = [V(al(2048), 2048, F32, "p (k t c) -> p k t c", k=16, t=8) for _ in range(2)]
    W2p = [V(al(2048), 2048, F32, "p (k t c) -> p k t c", k=16, t=8) for _ in range(2)]
    tF = V(al(2048), 2048, F32, "p (k s c) -> p k s c", k=16, s=8)
    T0acc = V(al(4096), 4096, F32, "p (g c) -> p g c", g=32)
    tM = V(al(512), 512)

    def pw(Pt, dk0, j0, jstep):
        b = Pt[:, dk0:dk0 + 16, :]
        return AP(b.tensor, b.offset + j0, [list(b.ap[0]), [16, 16], [jstep, 8], [0, 16]])

    def bx(Bt, dk0):
        b = Bt[:, dk0:dk0 + 16, :]
        return AP(b.tensor, b.offset, [list(b.ap[0]), [16, 16], [0, 8], [1, 16]])

    def outer(o_re, o_im, dk0, j0, jstep, Xre, Xim, neg_im):
        dv(lambda e: e.tensor_tensor(out=o_re, in0=pw(PaR, dk0, j0, jstep), in1=bx(Xre, dk0), op=ALU.mult))
        dv(lambda e: e.tensor_tensor(out=tF, in0=pw(PaI, dk0, j0, jstep), in1=bx(Xim, dk0), op=ALU.mult))
        dv(lambda e: e.tensor_tensor(out=o_re, in0=o_re, in1=tF, op=ALU.subtract))
        dv(lambda e: e.tensor_tensor(out=o_im, in0=pw(PaR, dk0, j0, jstep), in1=bx(Xim, dk0), op=ALU.mult))
        dv(lambda e: e.tensor_tensor(out=tF, in0=pw(PaI, dk0, j0, jstep), in1=bx(Xre, dk0), op=ALU.mult))
        dv(lambda e: e.tensor_tensor(out=o_im, in0=o_im, in1=tF, op=ALU.add))
        if neg_im:
            dv(lambda e: e.tensor_scalar(out=o_im, in0=o_im, scalar1=-1.0, scalar2=None, op0=ALU.mult))

    for d_ in range(2):
        dk0 = d_ * 16
        if d_ == 0:
            outer(Fm[0], Fm[1], dk0, 14, -1, Bbre, Bbim, False)
            outer(W2m[0], W2m[1], dk0, 8, 1, Cre, Cim, True)
            outer(W2p[0], W2p[1], dk0, 0, 1, Cre, Cim, True)
        else:
            outer(Fm[0], Fm[1], dk0, 7, 1, Bbre, Bbim, False)
            outer(W2m[0], W2m[1], dk0, 15, -1, Cre, Cim, True)
            outer(W2p[0], W2p[1], dk0, 7, -1, Cre, Cim, True)
        for r in range(2):
            dv(lambda e, r=r, dk0=dk0: e.tensor_copy(out=W2b[r][:, dk0:dk0 + 16, :].rearrange("p k (c t) -> p k t c", t=8), in_=W2m[r]),
               "pool", extra_w=["W2b"])
        for r in range(2):
            for kb4 in range(4):
                bank = ps[:, kb4 % 2, :]
                for q in range(4):
                    k_ = kb4 * 4 + q
                    P.op("pe", lambda e, r=r, k_=k_, q=q, bank=bank: e.transpose(
                        out=bank[:, q * 128:(q + 1) * 128], in_=Fm[r][:, k_, :, :].rearrange("p s c -> p (s c)"), identity=ident),
                        reads=[L, "ident"], writes=["ps%d" % (kb4 % 2)])
                P.op("dve", lambda e, r=r, kb4=kb4, dk0=dk0, bank=bank: e.tensor_copy(
                    out=W1b[r][:, dk0 + kb4 * 4:dk0 + kb4 * 4 + 4, :].rearrange("p k c -> p (k c)"), in_=bank),
                    reads=["ps%d" % (kb4 % 2)], writes=["W1b"])
        for gb in range(8):
            mi = gb // 4
            bank = ps[:, 2 + gb % 2, :]
            g0 = 2 * 4 * (gb % 4) + mi
            for q in range(4):
                k_ = 4 * (gb % 4) + q
                for r in range(2):
                    P.op("pe", lambda e, r=r, k_=k_, mi=mi, q=q, bank=bank: e.matmul(
                        bank[:, q * 128:(q + 1) * 128],
                        lhsT=Fm[r][mi * 64:(mi + 1) * 64, k_, :, :].rearrange("p s c -> p (s c)"),
                        rhs=W2p[r][mi * 64:(mi + 1) * 64, k_, :, :].rearrange("p t c -> p (t c)"),
                        start=(r == 0), stop=(r == 1)),
                        reads=[L], writes=["ps%d" % (2 + gb % 2)])
            tb0 = T0acc[:, g0, :]
            acc = AP(tb0.tensor, tb0.offset, [list(tb0.ap[0]), [256, 4], [1, 128]])
            bk = bank.rearrange("p (r c) -> p r c", r=4)
            tM3 = tM.rearrange("p (r c) -> p r c", r=4)
            if d_ == 0:
                P.op("dve", lambda e, acc=acc, bk=bk: e.tensor_tensor(out=acc, in0=bk, in1=maskt[0], op=ALU.mult),
                     reads=["ps%d" % (2 + gb % 2), L], writes=["T0acc"])
            else:
                P.op("dve", lambda e, bk=bk, tM3=tM3: e.tensor_tensor(out=tM3, in0=bk, in1=maskt[1], op=ALU.mult),
                     reads=["ps%d" % (2 + gb % 2), L, "T0acc"], writes=["tM"])
                P.op("dve", lambda e, acc=acc, tM3=tM3: e.tensor_tensor(out=acc, in0=acc, in1=tM3, op=ALU.add),
                     reads=["tM", "T0acc"], writes=["T0acc"])
                dc0 = dcol[:, g0:g0 + 1]
                dcb = AP(dc0.tensor, dc0.offset, [list(dc0.ap[0]), [2, 4], [0, 128]])
                P.op("dve", lambda e, dcb=dcb, tM3=tM3: e.tensor_tensor(out=tM3, in0=identt, in1=dcb, op=ALU.mult),
                     reads=["tM", L], writes=["tM"])
                t0b = T0b[:, g0, :]
                o4 = AP(t0b.tensor, t0b.offset, [list(t0b.ap[0]), [256, 4], [1, 8], [8, 16]])
                a4 = AP(tb0.tensor, tb0.offset, [list(tb0.ap[0]), [256, 4], [16, 8], [1, 16]])
                P.op("dve", lambda e, o4=o4, a4=a4: e.tensor_tensor(out=o4, in0=a4, in1=tM.rearrange("p (r t c) -> p r t c", r=4, t=8), op=ALU.add),
                     reads=["tM", "T0acc"], writes=["T0b", "tM"])
        P.op("dve", lambda e: e.memset(tM[:, 0:1], 0.0), reads=["ps0", "ps1", "ps2", "ps3", "tM", L], writes=[L, "tM"])

    P.op("dve", lambda e: e.memset(tM[:, 0:1], 0.0), reads=["ps0", "ps1", "ps2", "ps3", "tM", "T0acc", "T0b", "W1b", "W2b", L], writes=[L, "tM", "T0acc"])
    o = o_fm
    wp = [V(al(64), 64, F32, "p (r k) -> p r k", r=2) for _ in range(10)]
    dv(lambda e: e.tensor_copy(out=wp[0][:, 0, :], in_=w_re))
    dv(lambda e: e.tensor_copy(out=wp[0][:, 1, :], in_=w_im))
    for q in range(1, 10):
        a_, b_ = wp[q - 1][:, 0, :], wp[q - 1][:, 1, :]
        TT(t1, a_, a_, ALU.mult)
        TT(t2, b_, b_, ALU.mult)
        TT(wp[q][:, 0, :], t1, t2, ALU.subtract)
        TT(t1, a_, b_, ALU.mult)
        TS(wp[q][:, 1, :], t1, 2.0, ALU.mult)
    TBL = V(al(8 * 2 * NCH), 8 * 2 * NCH, F32, "p (k r n) -> p k r n", k=8, r=2)
    tq = [V(al(2048), 2048) for _ in range(2)]
    for bt in range(4):
        k0 = bt * 8
        dv(lambda e: e.memset(TBL[:, :, 0, 0:1], 1.0))
        dv(lambda e: e.memset(TBL[:, :, 1, 0:1], 0.0))
        dv(lambda e, k0=k0: e.tensor_copy(out=TBL[:, :, 0, 1], in_=wp[0][:, 0, k0:k0 + 8]))
        dv(lambda e, k0=k0: e.tensor_copy(out=TBL[:, :, 1, 1], in_=wp[0][:, 1, k0:k0 + 8]))
        for q in range(1, 10):
            sz = 2 ** q
            n = min(sz, NCH - sz)
            lo_r, lo_i = TBL[:, :, 0, 0:n], TBL[:, :, 1, 0:n]
            hi_r, hi_i = TBL[:, :, 0, sz:sz + n], TBL[:, :, 1, sz:sz + n]
            wr = bc_free(wp[q][:, 0, k0:k0 + 8], n)
            wi = bc_free(wp[q][:, 1, k0:k0 + 8], n)
            u1 = tq[0][:, 0:8 * n].rearrange("p (k n) -> p k n", k=8)
            u2 = tq[1][:, 0:8 * n].rearrange("p (k n) -> p k n", k=8)
            TT(u1, lo_r, wr, ALU.mult)
            TT(u2, lo_i, wi, ALU.mult)
            TT(hi_r, u1, u2, ALU.subtract)
            TT(u1, lo_r, wi, ALU.mult)
            TT(u2, lo_i, wr, ALU.mult)
            TT(hi_i, u1, u2, ALU.add)
        P.dma("pool", lambda e, k0=k0: e.dma_start(out=tab_d[k0:k0 + 8].rearrange("k p r n -> p k r n"), in_=TBL),
              reads=[L], writes=[("tabd", bt)], sem="st_tab", soft=["taball"])
        dv(lambda e: e.memset(tq[0][:, 0:1], 0.0), extra_r=[("tabd", bt)])
    s5m_d = self.dram_tmp("s5m_d", [5, 128, 4096], BF16)
    rho_d = self.dram_tmp("rho_d", [128, 32], F32)
    for i_, m_ in enumerate((W1b[0], W1b[1], T0b, W2b[0], W2b[1])):
        P.dma("pool", lambda e, i_=i_, m_=m_: e.dma_start(out=s5m_d[i_], in_=m_.rearrange("p k c -> p (k c)")),
              reads=["W1b", "T0b", "W2b", L], writes=[("s5md", i_)], sem="st_s5m", soft=["s5mall"])
    P.dma("pool", lambda e: e.dma_start(out=rho_d, in_=rho8), reads=[L], writes=["rhod"], sem="st_s5m", soft=["s5mall"])
    self.s5 = dict(s5m_d=s5m_d, rho_d=rho_d, tab_d=tab_d, o_end=o, L=L)


def kb_phaseS5main(self, ins):
    nc, P, V = self.nc, self.P, self.V
    ps = self.psum
    S = self.S
    s5 = self.s5
    tab_d = s5["tab_d"]
    o_mark = PERS
    if "u_in" in self.debug:
        S["u"] = self.dram_in("u_d", [32, 16, 8, NCH], BF16)
    S["y"] = self.dram_tmp("y_d", [32, 16, 8, 512], F32)

    def al(n):
        nonlocal o
        r = o
        o += n
        return r
    o = o_mark
    W1b = [V(al(2048), 2048, BF16, "p (k c) -> p k c", k=32) for _ in range(2)]
    T0b = V(al(2048), 2048, BF16, "p (g c) -> p g c", g=32)
    W2b = [V(al(2048), 2048, BF16, "p (k c) -> p k c", k=32) for _ in range(2)]
    rho8 = V(al(32), 32)
    for i_, (m_, rn) in enumerate(((W1b[0], "W1b"), (W1b[1], "W1b1"), (T0b, "T0b"), (W2b[0], "W2b"), (W2b[1], "W2b1"))):
        P.dma("sp", lambda e, i_=i_, m_=m_: e.dma_start(out=m_.rearrange("p k c -> p (k c)"), in_=s5["s5m_d"][i_]), reads=["s5mall"], writes=[rn])
    P.dma("sp", lambda e: e.dma_start(out=rho8, in_=s5["rho_d"]), reads=["s5mall"], writes=["s5"])
    U = V(al(32 * NCH // 2), 32 * NCH // 2, BF16, "p (g c) -> p g c", g=32)
    Ur = V(al(32 * NCH // 2), 32 * NCH // 2, BF16, "p (g c) -> p g c", g=32)
    tb = [V(al(2 * NCH), 2 * NCH, F32, "p (r n) -> p r n", r=2) for _ in range(2)]
    Gin = V(al(2 * NCH), 2 * NCH, F32, "p (r n) -> p r n", r=2)
    Gs = V(al(2 * NCH), 2 * NCH, F32, "p (r n) -> p r n", r=2)
    tt_ = [V(al(NCH), NCH) for _ in range(4)]
    Hb = [[V(al(256), 256, BF16) for _ in range(2)] for _ in range(2)]
    ysb = [V(al(512), 512) for _ in range(2)]
    assert o <= AW, o
    for s_ in range(8):
        P.dma("sp", lambda e, s_=s_: e.dma_start(out=U[16 * s_:16 * s_ + 16], in_=S["u"][:, :, s_, :].rearrange("g i c -> i g c")),
              reads=["uall"], writes=["U%d" % s_])
    Ures = ["U%d" % s_ for s_ in range(8)]
    Ub = U[:, :, :]
    P.op("pool", lambda e: e.tensor_copy(out=Ur[:, :, 0:32], in_=AP(Ub.tensor, Ub.offset + 31, [list(Ub.ap[0]), [NCH, 32], [-1, 32]])),
         reads=Ures, writes=["Ur0"])
    P.op("pool", lambda e: e.tensor_copy(out=Ur[:, :, 32:NCH], in_=AP(Ub.tensor, Ub.offset + NCH - 1, [list(Ub.ap[0]), [NCH, 32], [-1, 512]])),
         reads=Ures, writes=["Ur1"])
    for k in range(16):
        for d_ in range(2):
            dk = d_ * 16 + k
            T = tb[dk % 2]
            tr = "tb%d" % (dk % 2)
            P.dma("sp", lambda e, T=T, dk=dk: e.dma_start(out=T, in_=tab_d[dk]), reads=["taball"], writes=[tr])
            Us = U if d_ == 0 else Ur
            ures = Ures if d_ == 0 else ["Ur0", "Ur1"]
            for mi in range(2):
                g = 2 * k + mi
                for r in range(2):
                    lh = W1b[r][:, dk, mi * 64:(mi + 1) * 64]
                    P.op("pe", lambda e, lh=lh, r=r, mi=mi, g=g, Us=Us: e.matmul(ps[mi * 64:(mi + 1) * 64, r, :], lhsT=lh, rhs=Us[:, g, 0:512], start=True, stop=True),
                         reads=ures + ["W1b", "W1b1"], writes=["ps%d" % r])
                    P.op("pe", lambda e, lh=lh, r=r, mi=mi, g=g, Us=Us: e.matmul(ps[mi * 64:(mi + 1) * 64, 2, r * 32:(r + 1) * 32], lhsT=lh, rhs=Us[:, g, 512:NCH], start=True, stop=True),
                         reads=ures + ["W1b", "W1b1"], writes=["ps2"])
            for (c0, c1, sre, sim, pr) in ((0, 512, ps[:, 0, :], ps[:, 1, :], ["ps0", "ps1"]), (512, NCH, ps[:, 2, 0:32], ps[:, 2, 32:64], ["ps2"])):
                TRs, TIs = T[:, 0, c0:c1], T[:, 1, c0:c1]
                t1_, t2_, t3_, t4_ = [x[:, c0:c1] for x in tt_]
                P.op("dve", lambda e, t1_=t1_, sre=sre, TRs=TRs: e.tensor_tensor(out=t1_, in0=sre, in1=TRs, op=ALU.mult), reads=pr + [tr], writes=["tt0"])
                P.op("dve", lambda e, t2_=t2_, sim=sim, TIs=TIs: e.tensor_tensor(out=t2_, in0=sim, in1=TIs, op=ALU.mult), reads=pr + [tr], writes=["tt1"])
                P.op("dve", lambda e, t3_=t3_, sim=sim, TRs=TRs: e.tensor_tensor(out=t3_, in0=sim, in1=TRs, op=ALU.mult), reads=pr + [tr], writes=["tt2"])
                P.op("dve", lambda e, t4_=t4_, sre=sre, TIs=TIs: e.tensor_tensor(out=t4_, in0=sre, in1=TIs, op=ALU.mult), reads=pr + [tr], writes=["tt3"])
            P.op("pool", lambda e: e.tensor_tensor(out=Gin[:, 0, :], in0=tt_[0], in1=tt_[1], op=ALU.subtract), reads=["tt0", "tt1"], writes=["Gin0"])
            P.op("pool", lambda e: e.tensor_tensor(out=Gin[:, 1, :], in0=tt_[2], in1=tt_[3], op=ALU.add), reads=["tt2", "tt3"], writes=["Gin1"])
            rb = bc_free(rho8[:, dk:dk + 1], NCH).rearrange("p a n -> p (a n)") if False else AP(rho8.tensor, rho8.offset + dk, [list(rho8.ap[0]), [0, NCH]])
            for r in range(2):
                P.op("dve", lambda e, r=r, rb=rb: e.tensor_tensor_scan(out=Gs[:, r, :], data0=rb, data1=Gin[:, r, :], initial=0.0, op0=ALU.mult, op1=ALU.add),
                     reads=["Gin%d" % r, "s5"], writes=["Gs%d" % r])
            TRs, TIs = T[:, 0, 31:543], T[:, 1, 31:543]
            gr, gi = Gs[:, 0, 31:543], Gs[:, 1, 31:543]
            u1, u2, u3, u4 = [x[:, 0:512] for x in tt_]
            P.op("pool", lambda e, TRs=TRs, gr=gr, u1=u1: e.tensor_tensor(out=u1, in0=TRs, in1=gr, op=ALU.mult), reads=[tr, "Gs0", "Gin0", "Gin1"], writes=["tt0"])
            P.op("pool", lambda e, TIs=TIs, gi=gi, u2=u2: e.tensor_tensor(out=u2, in0=TIs, in1=gi, op=ALU.mult), reads=[tr, "Gs1", "Gin0", "Gin1"], writes=["tt1"])
            P.op("pool", lambda e, TRs=TRs, gi=gi, u3=u3: e.tensor_tensor(out=u3, in0=TRs, in1=gi, op=ALU.mult), reads=[tr, "Gs1", "Gin0", "Gin1"], writes=["tt2"])
            P.op("pool", lambda e, TIs=TIs, gr=gr, u4=u4: e.tensor_tensor(out=u4, in0=TIs, in1=gr, op=ALU.mult), reads=[tr, "Gs0", "Gin0", "Gin1"], writes=["tt3"])
            outs_ = []
            for r in range(2):
                hb = Hb[d_][r]
                if d_ == 0:
                    outs_.append(hb)
                else:
                    outs_.append(AP(hb.tensor, hb.offset + 511, [list(hb.ap[0]), [-1, 512]]))
            P.op("pool", lambda e, o_=outs_[0], u1=u1, u2=u2: e.tensor_tensor(out=o_, in0=u1, in1=u2, op=ALU.add), reads=["tt0", "tt1"], writes=["Hb%d0" % d_])
            P.op("pool", lambda e, o_=outs_[1], u3=u3, u4=u4: e.tensor_tensor(out=o_, in0=u3, in1=u4, op=ALU.subtract), reads=["tt2", "tt3"], writes=["Hb%d1" % d_])
        for mi in range(2):
            g = 2 * k + mi
            yb = ps[:, 4 + g % 2, :]
            yr = "ps%d" % (4 + g % 2)
            P.op("pe", lambda e, g=g, yb=yb: e.matmul(yb, lhsT=T0b[:, g, :], rhs=U[:, g, 32:NCH], start=True, stop=False),
                 reads=Ures + ["T0b"], writes=[yr])
            n_ = 0
            for d_ in range(2):
                for r in range(2):
                    n_ += 1
                    lh = W2b[r][mi * 64:(mi + 1) * 64, d_ * 16 + k, :]
                    rh = Hb[d_][r][mi * 64:(mi + 1) * 64, :]
                    P.op("pe", lambda e, lh=lh, rh=rh, yb=yb, n_=n_: e.matmul(yb, lhsT=lh, rhs=rh, start=False, stop=(n_ == 4)),
                         reads=["W2b", "W2b1", "Hb%d%d" % (d_, r)], writes=[yr])
            yt = ysb[g % 2]
            P.op("act", lambda e, yt=yt, yb=yb: e.activation(out=yt, in_=yb, func=AF.Copy), reads=[yr], writes=["ysb%d" % (g % 2)])
            P.dma("pool", lambda e, yt=yt, g=g: e.dma_start(out=S["y"][g].rearrange("c t n -> (c t) n"), in_=yt),
                  reads=["ysb%d" % (g % 2)], writes=[("yd", g)], sem="st_y%d" % (g % 2), soft=["yall"])


KB.phaseS5 = kb_phaseS5
KB.phaseS5main = kb_phaseS5main


MASK_E = -4000.0


def att_plan():
    sets = {}
    plan = []
    for a in range(32):
        rows = (2 * a, 2 * a + 1)
        r0 = [min(max(r - 4, 0), 56) for r in rows]
        lo, hi = min(r0), max(r0) + 7
        tiles = []
        for kt in range(lo // 2, hi // 2 + 1):
            ent = []
            for il in range(2):
                for rl in range(2):
                    i_abs = 2 * kt + il
                    ok = r0[rl] <= i_abs <= r0[rl] + 7
                    ent.append((i_abs - rows[rl] + 7) if ok else None)
            key = tuple(ent)
            if key not in sets:
                sets[key] = len(sets)
            tiles.append((kt, sets[key]))
        plan.append(tiles)
    return plan, sets


def kb_phaseATT(self, ins):
    nc, P, V = self.nc, self.P, self.V
    ps = self.psum
    if not hasattr(self, "S"):
        self.S = {}
    S = self.S
    for nm, shp in (("q", [4, 128, NLAT]), ("k", [4, 128, NT]), ("v", [NT, 512])):
        if nm + "_d_in" in self.debug:
            S[nm] = self.dram_in(nm + "_d", shp, BF16)
    S["yna"] = self.dram_tmp("yna_d", [4, 128, NLAT], BF16)
    tt_d = self.dram_tmp("tt_d", [8, 15, 64, 64], BF16)
    ttm_d = self.dram_tmp("ttm_d", [64, 8, 64], BF16)
    plan, sets = att_plan()
    NS = len(sets)
    o = PERS

    def al(n):
        nonlocal o
        r = o
        o += n
        return r
    ident, identb, onesb = self.ident, self.identb, self.onesb
    kT = V(al(4 * NT // 2), 4 * NT // 2, BF16, "p (m t) -> p m t", m=4)
    Vt = V(al(34 * 256), 34 * 256, BF16, "p (k f) -> p k f", k=34)
    TB = V(al(NS * 8 * 64), NS * 8 * 64, BF16, "p (s h c) -> p s h c", s=NS, h=8)
    qT = [V(al(256), 256, BF16, "p (m t) -> p m t", m=4) for _ in range(2)]
    PT = [V(al(448), 448, BF16, "p (i t) -> p i t", i=7) for _ in range(2)]
    rc = V(al(128), 128)
    yst = [V(al(256), 256, BF16, "p (m t) -> p m t", m=4) for _ in range(2)]
    o_tmp = o
    for m in range(4):
        P.dma("sp", lambda e, m=m: e.dma_start(out=kT[:, m, :], in_=S["k"][m]), reads=["kall"], writes=["kT%d" % m])
    vv = S["v"].rearrange("(k p) f -> p k f", p=128)
    for i in range(2):
        P.dma("sp", lambda e, i=i: e.dma_start(out=Vt[:, i * 17:(i + 1) * 17, :], in_=vv[:, i * 17:(i + 1) * 17, :]), reads=["vall"], writes=["Vt%d" % i])
    L = "att"
    stg = V(al(32), 32)
    Lm = V(al(120), 120)
    E = V(al(4096), 4096, F32, "p (j c) -> p j c", j=64)
    E2 = V(al(4096), 4096, F32, "p (j c) -> p j c", j=64)
    E3 = V(al(4096), 4096, F32, "p (j c) -> p j c", j=64)
    rsel = V(al(2), 2)
    ttb = [V(al(256), 256, BF16) for _ in range(2)]
    mtile = V(al(256), 256, BF16)
    assert o <= AW, o

    def dv(fn, eng="dve", extra_r=(), extra_w=()):
        P.op(eng, fn, reads=[L] + list(extra_r), writes=[L] + list(extra_w))
    dv(lambda e: e.memset(stg, 1.0), "pool")
    P.dma("sp", lambda e: e.dma_start(out=stg[0:120, 0:31], in_=ins["na_rpb"].rearrange("h a b -> (h a) b")), reads=[L], writes=["attstg"])
    P.op("pe", lambda e: e.transpose(out=ps[0:32, 6, 0:120], in_=stg[0:120, :], identity=ident[0:120, 0:120]), reads=["attstg", L, "ident"], writes=["psT6"])
    P.op("dve", lambda e: e.tensor_copy(out=Lm[0:32, :], in_=ps[0:32, 6, 0:120]), reads=["psT6", L], writes=[L])
    E_, E2_, E3_ = E[0:32], E2[0:32], E3[0:32]
    dv(lambda e: e.iota(E_, pattern=[[1, 64], [-1, 64]], base=15, channel_multiplier=-1, allow_small_or_imprecise_dtypes=True), "pool")
    dv(lambda e: e.tensor_scalar(out=E_, in0=E_, scalar1=0.0, scalar2=None, op0=ALU.is_equal))
    dv(lambda e: e.iota(E2_, pattern=[[0, 64], [1, 64]], base=0, channel_multiplier=0, allow_small_or_imprecise_dtypes=True), "pool")
    dv(lambda e: e.tensor_scalar(out=E2_, in0=E2_, scalar1=-8.0, scalar2=0.0, op0=ALU.add, op1=ALU.max))
    dv(lambda e: e.tensor_scalar(out=E2_, in0=E2_, scalar1=48.0, scalar2=None, op0=ALU.min))
    dv(lambda e: e.iota(E3_, pattern=[[1, 64], [0, 64]], base=0, channel_multiplier=0, allow_small_or_imprecise_dtypes=True), "pool")
    dv(lambda e: e.tensor_tensor(out=E3_, in0=E3_, in1=E2_, op=ALU.subtract))
    dv(lambda e: e.tensor_scalar(out=E2_, in0=E3_, scalar1=0.0, scalar2=None, op0=ALU.is_ge))
    dv(lambda e: e.tensor_scalar(out=E3_, in0=E3_, scalar1=15.0, scalar2=None, op0=ALU.is_le))
    dv(lambda e: e.tensor_tensor(out=E2_, in0=E2_, in1=E3_, op=ALU.mult))
    dv(lambda e: e.tensor_tensor(out=E_, in0=E_, in1=E2_, op=ALU.mult))
    dv(lambda e: e.tensor_scalar(out=E2_, in0=E2_, scalar1=-MASK_E, scalar2=MASK_E, op0=ALU.mult, op1=ALU.add))
    rs = rsel[0:32, :]
    dv(lambda e: e.iota(rs[:, 0:1], pattern=[[0, 1]], base=0, channel_multiplier=1, allow_small_or_imprecise_dtypes=True), "pool")
    dv(lambda e: e.tensor_scalar(out=rs[:, 0:1], in0=rs[:, 0:1], scalar1=31.0, scalar2=None, op0=ALU.is_lt))
    dv(lambda e: e.tensor_scalar(out=rs[:, 1:2], in0=rs[:, 0:1], scalar1=-1.0, scalar2=1.0, op0=ALU.mult, op1=ALU.add))
    dv(lambda e: e.tensor_scalar(out=E_, in0=E_, scalar1=rs[:, 0:1], scalar2=None, op0=ALU.mult))
    dv(lambda e: e.scalar_tensor_tensor(out=E_, in0=E2_, scalar=rs[:, 1:2], in1=E_, op0=ALU.mult, op1=ALU.add))
    Ef = E_.rearrange("p j c -> p (j c)")
    ttv = tt_d.rearrange("h a j c -> (h a) (j c)")
    for i in range(8):
        bank = ps[0:120, i % 2, :]
        P.op("pe", lambda e, i=i, bank=bank: e.matmul(bank, lhsT=Lm[0:32, :], rhs=Ef[:, i * 512:(i + 1) * 512], start=True, stop=True),
             reads=[L], writes=["ps%d" % (i % 2)])
        tb_ = ttb[i % 2][0:120, :]
        P.op("act", lambda e, tb_=tb_, bank=bank: e.activation(out=tb_, in_=bank, func=AF.Identity, scale=8.0), reads=["ps%d" % (i % 2)], writes=["ttb%d" % (i % 2)])
        P.dma("sp", lambda e, i=i, tb_=tb_: e.dma_start(out=ttv[:, i * 512:(i + 1) * 512], in_=tb_), reads=["ttb%d" % (i % 2)], writes=[("ttd", i)],
              sem="st_tt%d" % (i % 2), soft=["ttall"])
    dv(lambda e: e.memset(mtile, 8.0 * MASK_E), "pool")
    msrc = AP(mtile.tensor, mtile.offset, [[mtile.ap[0][0], 64], [1, 256]])
    P.dma("sp", lambda e: e.dma_start(out=ttm_d.rearrange("j h c -> j (h c)")[:, 0:256], in_=msrc),
          reads=[L], writes=["ttm_a"], sem="st_ttm")
    P.dma("sp", lambda e: e.dma_start(out=ttm_d.rearrange("j h c -> j (h c)")[:, 256:512], in_=msrc),
          reads=[L], writes=["ttm_b"], sem="st_ttm2")
    self.P.barrier()
    for key, si in sets.items():
        n_ = 0
        for il in range(2):
            for rl in range(2):
                di = key[il * 2 + rl]
                dst = TB[il * 64:(il + 1) * 64, si, :, rl * 64:(rl + 1) * 64]
                if di is None:
                    src = ttm_d
                else:
                    src = tt_d[:, di, :, :].rearrange("h j c -> j h c")
                P.dma("sp", lambda e, dst=dst, src=src: e.dma_start(out=dst, in_=src), writes=[("TB", si, il, rl)], sem="ld_tb%d" % (n_ % 4), soft=["TBall"])
                n_ += 1
    self.P.barrier()

    SCALE = 0.125
    tasks = []
    for a in range(32):
        tiles = [(256 + kt * 128, 2 + kt, si) for (kt, si) in plan[a]] + [(0, 0, None), (128, 1, None)]
        for m in range(4):
            for hh in range(2):
                tasks.append(dict(a=a, m=m, hh=hh, h=2 * m + hh, tiles=tiles, n=len(tasks)))

    def emit_q(a):
        qt = qT[a % 2]
        P.dma("sp", lambda e, a=a, qt=qt: e.dma_start(out=qt, in_=S["q"][:, :, a * 128:(a + 1) * 128].rearrange("m p t -> p m t")),
              reads=["qall"], writes=["qT%d" % (a % 2)])

    def emit_qk(t):
        a, m, hh, h, tiles, n = t["a"], t["m"], t["hh"], t["h"], t["tiles"], t["n"]
        qt = qT[a % 2]
        qr = "qT%d" % (a % 2)
        sb_ = (n % 2) * 2
        st2 = ps[:, sb_:sb_ + 2, :].rearrange("p a b -> p (a b)")
        pt_ = PT[n % 2]
        ptr = "PT%d" % (n % 2)
        nt_ = len(tiles)
        for i, (tok, vt, si) in enumerate(tiles):
            bres = "ps%d" % (sb_ + i // 4)
            o_ap = st2[:, i * 128:(i + 1) * 128]
            P.op("pe", lambda e, o_ap=o_ap, hh=hh, m=m, tok=tok, qt=qt, si=si: e.matmul(
                o_ap, lhsT=kT[hh * 64:(hh + 1) * 64, m, tok:tok + 128], rhs=qt[hh * 64:(hh + 1) * 64, m, :],
                start=True, stop=(si is None), skip_group_check=True), reads=["kT%d" % m, qr], writes=[bres])
            if si is not None:
                P.op("pe", lambda e, o_ap=o_ap, si=si, h=h: e.matmul(o_ap, lhsT=identb, rhs=TB[:, si, h, :], start=False, stop=True, skip_group_check=True),
                     reads=["identb"], writes=[bres])
        for bnk in range((nt_ + 3) // 4):
            c0, c1 = bnk * 4, min(nt_, bnk * 4 + 4)
            P.op("act", lambda e, c0=c0, c1=c1, st2=st2, pt_=pt_: e.activation(
                out=pt_[:, c0:c1, :].rearrange("p i t -> p (i t)"), in_=st2[:, c0 * 128:c1 * 128], func=AF.Exp, scale=SCALE),
                reads=["ps%d" % (sb_ + bnk)], writes=[ptr + "b%d" % bnk])

    def emit_pv(t):
        a, m, hh, h, tiles, n = t["a"], t["m"], t["hh"], t["h"], t["tiles"], t["n"]
        pt_ = PT[n % 2]
        ptr = "PT%d" % (n % 2)
        nt_ = len(tiles)
        ob = ps[:, 4 + (a * 4 + m) % 2, :]
        obr = "ps%d" % (4 + (a * 4 + m) % 2)
        for i, (tok, vt, si) in enumerate(tiles):
            P.op("pe", lambda e, i=i, vt=vt, h=h, hh=hh, ob=ob, pt_=pt_: e.matmul(
                ob[hh * 64:(hh + 1) * 64, 0:128], lhsT=Vt[:, vt, h * 64:(h + 1) * 64], rhs=pt_[:, i, :], start=(i == 0), stop=(i == nt_ - 1), skip_group_check=True),
                reads=["Vt0", "Vt1", ptr + "b%d" % (i // 4)], writes=[obr])
        for i, (tok, vt, si) in enumerate(tiles):
            P.op("pe", lambda e, i=i, hh=hh, ob=ob, pt_=pt_: e.matmul(
                ob[hh * 64:(hh + 1) * 64, 128:256], lhsT=onesb, rhs=pt_[:, i, :], start=(i == 0), stop=(i == nt_ - 1), skip_group_check=True),
                reads=["onesb", ptr + "b%d" % (i // 4)], writes=[obr])
        if hh == 1:
            ys = yst[a % 2]
            P.op("dve", lambda e, ob=ob: e.reciprocal(out=rc, in_=ob[:, 128:256]), reads=[obr], writes=["rc"])
            P.op("dve", lambda e, ob=ob, ys=ys, m=m: e.tensor_tensor(out=ys[:, m, :], in0=ob[:, 0:128], in1=rc, op=ALU.mult), reads=[obr, "rc"], writes=["yst%d_%d" % (a % 2, m)])
            if m == 3:
                P.dma("pool", lambda e, a=a, ys=ys: e.dma_start(out=S["yna"][:, :, a * 128:(a + 1) * 128].rearrange("m p t -> p m t"), in_=ys),
                      reads=["yst%d_%d" % (a % 2, mm) for mm in range(4)], writes=[("ynad", a)], sem="st_yna%d" % (a % 2), soft=["ynaall"])

    emit_q(0)
    emit_qk(tasks[0])
    for ti, t in enumerate(tasks):
        if ti + 1 < len(tasks):
            tn = tasks[ti + 1]
            if tn["m"] == 0 and tn["hh"] == 0:
                emit_q(tn["a"])
            emit_qk(tn)
        emit_pv(t)


KB.phaseATT = kb_phaseATT


def kb_phaseC(self, ins):
    nc, P, V = self.nc, self.P, self.V
    ps = self.psum
    S, Wd = self.S, self.W
    for nm, shp, dt_ in (("xs1", [8, 128, NLAT], F32), ("yna", [4, 128, NLAT], BF16), ("y", [32, 16, 8, 512], F32)):
        if nm + "_d_in" in self.debug:
            S[nm] = self.dram_in(nm + "_d", shp, dt_)
    o = PERS
    R = {}

    def al(n):
        nonlocal o
        r = o
        o += n
        return r
    xsA = V(al(4096), 4096, F32, "p (k t) -> p k t", k=8)
    xsB0 = V(al(4096), 4096, F32, "p (k t) -> p k t", k=8)
    xsB1 = V(al(4096), 4096, F32, "p (k t) -> p k t", k=8)
    h3T = V(al(2048), 2048, BF16, "p (k t) -> p k t", k=8)
    R["actT"] = V(al(5632), 5632, BF16, "p (j t) -> p j t", j=NJ)
    R["sg"] = [V(al(512), 512) for _ in range(2)]
    R["zn"] = [V(al(1024), 1024) for _ in range(2)]
    R["mv"] = V(al(32), 32, F32, "p (s c) -> p s c", c=8)
    R["bst"] = [V(al(16), 16) for _ in range(2)]
    R["wup"] = [V(al(1024), 1024, BF16, "p (k c) -> p k c", k=8) for _ in range(4)]
    R["wdn"] = [V(al(1408), 1408, BF16, "p (j c) -> p j c", j=NJ) for _ in range(3)]
    ypre = V(al(2048), 2048, F32, "p (m t) -> p m t", m=4)
    gT = V(al(2048), 2048, F32, "p (m t) -> p m t", m=4)
    gtmp = V(al(1024), 1024, F32, "p (m t) -> p m t", m=2)
    gTb = V(al(1024), 1024, BF16, "p (m t) -> p m t", m=4)
    sig = [V(al(512), 512) for _ in range(2)]
    ymix = V(al(2048), 2048, BF16, "p (k t) -> p k t", k=8)
    wout = [V(al(512), 512, BF16, "p (k c) -> p k c", k=8) for _ in range(8)]
    wglu = [V(al(256), 256, BF16, "p (k c) -> p k c", k=4) for _ in range(4)]
    G2f = V(al(1024), 1024)
    B2f = V(al(1024), 1024)
    ost = [V(al(1024), 1024) for _ in range(2)]
    assert o <= AW, o
    self.wup_cnt = 0
    self.wdn_cnt = 0
    scl = self.scl
    for i in range(8):
        P.dma("sp", lambda e, i=i: e.dma_start(out=wout[i], in_=Wd["out"][i]), reads=["Wall"], writes=["wout%d" % i])
    for i in range(4):
        P.dma("sp", lambda e, i=i: e.dma_start(out=wglu[i], in_=Wd["glu"][i]), reads=["Wall"], writes=["wglu%d" % i])
    lg, lb = ins["ln_g"], ins["ln_b"]
    P.dma("sp", lambda e: e.dma_start(out=G2f, in_=AP(lg.tensor, lg.offset + 16 * 128, [[0, 128], [1, 1024]])), writes=["G2f"])
    P.dma("sp", lambda e: e.dma_start(out=B2f, in_=AP(lb.tensor, lb.offset + 16 * 128, [[0, 128], [1, 1024]])), writes=["B2f"])
    GC = 2.0 * math.sqrt(2.0 / math.pi)
    nblk = NLAT // 512
    if "short" in self.debug:
        nblk = 1
    pend = None
    for blk in range(nblk):
        t0 = blk * 512
        c0 = t0 // 8
        xsB = (xsB0, xsB1)[blk % 2]
        dk_ = "D%d" % (blk % 2)
        for m in range(4):
            P.dma("sp", lambda e, m=m, c0=c0: e.dma_start(out=ypre[:, m, :].rearrange("p (t n) -> p t n", t=8),
                                                          in_=S["y"][m * 8:(m + 1) * 8, :, :, c0:c0 + 64].rearrange("g c t n -> (g c) t n")),
                  reads=["yall"], writes=["ypre%d" % m])
        for m in range(4):
            xin = ypre[:, m, :].rearrange("p (t n) -> p t n", t=8)
            xo = gT[:, m, :].rearrange("p (n t) -> p t n", t=8)
            tm = gtmp[:, m % 2, :].rearrange("p (n t) -> p t n", t=8)
            P.op("pool", lambda e, xin=xin, xo=xo: e.tensor_copy(out=xo, in_=xin), reads=["ypre%d" % m], writes=["gT%d" % m])
            xt_ = gT[:, m, :]
            tm_ = gtmp[:, m % 2, :]
            P.op("dve", lambda e, xt_=xt_, tm_=tm_: e.tensor_tensor(out=tm_, in0=xt_, in1=xt_, op=ALU.mult), reads=["gT%d" % m], writes=["gtmp%d" % (m % 2)])
            P.op("dve", lambda e, tm_=tm_: e.tensor_scalar(out=tm_, in0=tm_, scalar1=0.044715, scalar2=1.0, op0=ALU.mult, op1=ALU.add), reads=["gtmp%d" % (m % 2)], writes=["gtmp%d" % (m % 2)])
            P.op("dve", lambda e, xt_=xt_, tm_=tm_: e.tensor_tensor(out=tm_, in0=tm_, in1=xt_, op=ALU.mult), reads=["gtmp%d" % (m % 2), "gT%d" % m], writes=["gtmp%d" % (m % 2)])
            P.op("act", lambda e, tm_=tm_: e.activation(out=tm_, in_=tm_, func=AF.Sigmoid, scale=GC), reads=["gtmp%d" % (m % 2)], writes=["gtmp%d" % (m % 2)])
            P.op("dve", lambda e, xt_=xt_, tm_=tm_: e.tensor_tensor(out=xt_, in0=xt_, in1=tm_, op=ALU.mult), reads=["gtmp%d" % (m % 2), "gT%d" % m], writes=["gT%d" % m])
            P.op("pool", lambda e, xt_=xt_, m=m: e.tensor_copy(out=gTb[:, m, :], in_=xt_), reads=["gT%d" % m], writes=["gTb%d" % m])
        for mo in range(4):
            pb = self.bank(mo % 2)
            for m in range(4):
                P.op("pe", lambda e, mo=mo, m=m, pb=pb: e.matmul(pb, lhsT=wglu[mo][:, m, :], rhs=gTb[:, m, :], start=(m == 0), stop=(m == 3)),
                     reads=["wglu%d" % mo, "gTb%d" % m], writes=["ps%d" % (mo % 2)])
            sg_ = sig[mo % 2]
            P.op("act", lambda e, sg_=sg_, pb=pb, mo=mo: e.activation(out=sg_, in_=pb, func=AF.Sigmoid, bias=self.bgluT[:, mo:mo + 1], scale=1.0),
                 reads=["ps%d" % (mo % 2), "bgluT"], writes=["sig%d" % (mo % 2)])
            P.op("dve", lambda e, sg_=sg_, mo=mo: e.tensor_tensor(out=ymix[:, 4 + mo, :], in0=gT[:, mo, :], in1=sg_, op=ALU.mult),
                 reads=["sig%d" % (mo % 2), "gT%d" % mo], writes=["ymix%d" % (4 + mo)])
        for m in range(4):
            P.dma("sp", lambda e, m=m, t0=t0: e.dma_start(out=ymix[:, m, :], in_=S["yna"][m, :, t0:t0 + 512]), reads=["ynaall"], writes=["ymix%d" % m])
        for kk in range(8):
            P.dma("sp", lambda e, kk=kk, t0=t0: e.dma_start(out=xsA[:, kk, :], in_=S["xs1"][kk, :, t0:t0 + 512]), reads=["xs1all"], writes=["Cxs%d" % kk])
        for i in range(8):
            pb = self.bank(2 + i % 2)
            for kc in range(8):
                P.op("pe", lambda e, i=i, kc=kc, pb=pb: e.matmul(pb, lhsT=wout[i][:, kc, :], rhs=ymix[:, kc, :], start=(kc == 0), stop=(kc == 7)),
                     reads=["wout%d" % i, "ymix%d" % kc], writes=["ps%d" % (2 + i % 2)])
            P.op("dve", lambda e, i=i, pb=pb: e.scalar_tensor_tensor(out=xsA[:, i, :], in0=pb, scalar=scl[:, S_M5, i:i + 1], in1=xsA[:, i, :],
                                                                    op0=ALU.mult, op1=ALU.add),
                 reads=["ps%d" % (2 + i % 2), "Cxs%d" % i, "scal"], writes=["Cxs%d" % i])
        self.ln("C", xsA, 512, R, [(h3T, scl[:, S_G2S, :], scl[:, S_B2S, :], "DhT"), (xsB, scl[:, S_AG1, :], scl[:, S_AB1, :], dk_ + "xs")])
        self.ffn(dk_, h3T, 512, xsB, Wd["up2"], Wd["dn2"], scl[:, S_HG8, :], R, pump=pend, hkey="D")

        def tok_cb(s, zn, znr, t0=t0):
            ot = ost[s % 2]
            P.op("dve", lambda e, ot=ot, zn=zn: e.tensor_tensor(out=ot, in0=zn, in1=G2f, op=ALU.mult), reads=[znr + "h0", znr + "h1", "G2f"], writes=["ost%d" % (s % 2)])
            P.op("pool", lambda e, ot=ot: e.tensor_tensor(out=ot, in0=ot, in1=B2f, op=ALU.add), reads=["ost%d" % (s % 2), "B2f"], writes=["ost%d" % (s % 2)])
            P.dma("pool", lambda e, ot=ot, s=s, t0=t0: e.dma_start(out=self.out[t0 + s * 128:t0 + (s + 1) * 128, :], in_=ot),
                  reads=["ost%d" % (s % 2)], writes=[("outd", t0, s)], sem="out%d" % (s % 2))
        pend = self.ln_gen(dk_, xsB, 512, dict(R, tok_cb=tok_cb), None, (((4, 5), ("ps4", "ps5")), ((6, 7), ("psT6", "psT7"))))
    for _ in pend:
        pass


KB.phaseC = kb_phaseC
```
